# Optimizing a Trainium2 kernel written in Bass

```python
import math
import jax
import jax.numpy as jnp
from jax import lax
import numpy as np

D_MODEL = 1024
BATCH = 8
SEQ = 2048
DEPTH = 4

D_HEAD = 64
N_MIXERS = 4
MIX_WIDTH = 256
NSA_HEADS = 4
NSA_ROT = D_HEAD // 4
CMP_BLOCK = 32
CMP_STRIDE = 16
SEL_BLOCK = 64
SEL_TOPN = 16
WINDOW = 512
FOX_HEADS = 4
MLA_HEADS = 4
MLA_Q_RANK = 256
MLA_KV_RANK = 128
MLA_NOPE = 64
MLA_ROPE = 32
MLA_V = 64
DIFF_HEADS = 4
DIFF_QK = 32
DIFF_V = 64
DIFF_ROT = DIFF_QK // 4
D_FF = 2816
CONV_WIDTH = 3
Q_BLOCK = 128
ROPE_THETA = 500000.0
EPS = 1e-6
FORCE_SCORE = 1e4
NEG_BIG = -1e30

IN_SPLITS = (
    NSA_HEADS * D_HEAD, D_HEAD, D_HEAD, D_HEAD, D_HEAD, D_HEAD, D_HEAD, 3 * NSA_HEADS,
    FOX_HEADS * D_HEAD, FOX_HEADS * D_HEAD, FOX_HEADS * D_HEAD, FOX_HEADS,
    MLA_Q_RANK, MLA_KV_RANK, MLA_ROPE,
    DIFF_HEADS * 2 * DIFF_QK, DIFF_HEADS * 2 * DIFF_QK, DIFF_HEADS * DIFF_V,
)
IN_WIDTH = sum(IN_SPLITS)

kernel_name = 'hybrid_nsa_fox_mla_diff_trunk'


def rms_norm(x, g=None, eps=EPS):
    xf = x.astype(jnp.float32)
    y = xf * lax.rsqrt(jnp.mean(xf * xf, axis=-1, keepdims=True) + eps)
    if g is not None:
        y = y * g.astype(jnp.float32)
    return y.astype(x.dtype)


def rope(x, pos, n_rot):
    half = n_rot // 2
    inv_freq = ROPE_THETA ** (-jnp.arange(half, dtype=jnp.float32) / half)
    ang = pos.astype(jnp.float32)[:, None, :, None] * inv_freq
    cos = jnp.cos(ang).astype(x.dtype)
    sin = jnp.sin(ang).astype(x.dtype)
    x1 = x[..., :half]
    x2 = x[..., half:n_rot]
    return jnp.concatenate([x1 * cos - x2 * sin, x2 * cos + x1 * sin, x[..., n_rot:]], axis=-1)


def heads(t, n):
    b, s, _ = t.shape
    return t.reshape(b, s, n, -1).transpose(0, 2, 1, 3)


def merge_heads(t):
    b, h, s, d = t.shape
    return t.transpose(0, 2, 1, 3).reshape(b, s, h * d)


def split_cols(t, sizes):
    idx = []
    acc = 0
    for s in sizes[:-1]:
        acc += s
        idx.append(acc)
    return jnp.split(t, idx, axis=-1)


def chunk_seq(t, axis):
    s = t.shape
    t = t.reshape(s[:axis] + (s[axis] // Q_BLOCK, Q_BLOCK) + s[axis + 1:])
    return jnp.moveaxis(t, axis, 0)


def unchunk_seq(t, axis):
    t = jnp.moveaxis(t, 0, axis)
    s = t.shape
    return t.reshape(s[:axis] + (s[axis] * s[axis + 1],) + s[axis + 2:])


def causal_attention(q, k, v, q_cum=None, k_cum=None):
    S = q.shape[2]
    scale = q.shape[-1] ** -0.5
    kpos = jnp.arange(S)

    def body(args):
        i, qb = args[0], args[1]
        s = jnp.einsum('bhqd,bhkd->bhqk', qb, k).astype(jnp.float32) * scale
        if q_cum is not None:
            s = s + args[2][..., None] - k_cum[:, :, None, :]
        qpos = i * Q_BLOCK + jnp.arange(Q_BLOCK)
        s = jnp.where(kpos[None, :] <= qpos[:, None], s, -jnp.inf)
        p = jax.nn.softmax(s, axis=-1).astype(v.dtype)
        return jnp.einsum('bhqk,bhkd->bhqd', p, v)

    xs = (jnp.arange(S // Q_BLOCK), chunk_seq(q, 2))
    if q_cum is not None:
        xs = xs + (chunk_seq(q_cum, 2),)
    return unchunk_seq(lax.map(body, xs), 2)


def compress_blocks(k, pe, w1, w2):
    b, s, d = k.shape
    halves = k.reshape(b, s // CMP_STRIDE, CMP_STRIDE, d)
    blocks = jnp.concatenate([halves[:, :-1], halves[:, 1:]], axis=2) + pe
    hid = jax.nn.silu(blocks.reshape(b, blocks.shape[1], CMP_BLOCK * d) @ w1)
    return hid @ w2


def selected_block_attention(q, k_blk, v_blk, sel_idx):
    b, h, S, d = q.shape
    scale = d ** -0.5
    gather = jax.vmap(lambda blocks, ix: blocks[ix])

    def body(args):
        i, qb, ib = args
        ksel = gather(k_blk, ib)
        vsel = gather(v_blk, ib)
        s = jnp.einsum('bhqd,bqnkd->bhqnk', qb, ksel).astype(jnp.float32) * scale
        qpos = i * Q_BLOCK + jnp.arange(Q_BLOCK)
        kpos = ib[..., None] * SEL_BLOCK + jnp.arange(SEL_BLOCK)
        mask = (kpos <= qpos[None, :, None, None])[:, None]
        s = jnp.where(mask, s, -jnp.inf)
        shp = s.shape
        p = jax.nn.softmax(s.reshape(shp[0], shp[1], shp[2], -1), axis=-1).reshape(shp).astype(vsel.dtype)
        return jnp.einsum('bhqnk,bqnkd->bhqd', p, vsel)

    out = lax.map(body, (jnp.arange(S // Q_BLOCK), chunk_seq(q, 2), chunk_seq(sel_idx, 1)))
    return unchunk_seq(out, 2)


def window_attention(q, k, v):
    b, h, S, d = q.shape
    scale = d ** -0.5
    kp = jnp.pad(k, ((0, 0), (WINDOW, 0), (0, 0)))
    vp = jnp.pad(v, ((0, 0), (WINDOW, 0), (0, 0)))
    span = Q_BLOCK + WINDOW

    def body(args):
        i, qb = args
        start = i * Q_BLOCK
        kb = lax.dynamic_slice_in_dim(kp, start, span, axis=1)
        vb = lax.dynamic_slice_in_dim(vp, start, span, axis=1)
        s = jnp.einsum('bhqd,bkd->bhqk', qb, kb).astype(jnp.float32) * scale
        qpos = start + jnp.arange(Q_BLOCK)
        kpos = start - WINDOW + jnp.arange(span)
        dist = qpos[:, None] - kpos[None, :]
        mask = (dist >= 0) & (dist < WINDOW) & (kpos[None, :] >= 0)
        p = jax.nn.softmax(jnp.where(mask, s, -jnp.inf), axis=-1).astype(vb.dtype)
        return jnp.einsum('bhqk,bkd->bhqd', p, vb)

    out = lax.map(body, (jnp.arange(S // Q_BLOCK), chunk_seq(q, 2)))
    return unchunk_seq(out, 2)


def nsa_mixer(parts, pos, qk_g, cmp_pe, cmp_w1, cmp_w2):
    q, k_c, v_c, k_s, v_s, k_w, v_w, g = parts
    b, S, _ = q.shape
    t = jnp.arange(S)
    q = rope(rms_norm(heads(q, NSA_HEADS), qk_g[0]), pos, NSA_ROT)
    k_cmp = rms_norm(compress_blocks(k_c, cmp_pe[0], cmp_w1[0], cmp_w2[0]), qk_g[1])
    v_cmp = compress_blocks(v_c, cmp_pe[1], cmp_w1[1], cmp_w2[1])
    n_cmp = k_cmp.shape[1]
    cmp_end = jnp.arange(n_cmp) * CMP_STRIDE + (CMP_BLOCK - 1)
    vis = cmp_end[None, :] <= t[:, None]
    s = jnp.einsum('bhsd,bcd->bhsc', q, k_cmp).astype(jnp.float32) * (D_HEAD ** -0.5)
    p_cmp = jax.nn.softmax(jnp.where(vis, s, NEG_BIG), axis=-1)
    p_cmp = jnp.where(jnp.any(vis, axis=-1)[:, None], p_cmp, 0.0)
    o_cmp = jnp.einsum('bhsc,bcd->bhsd', p_cmp.astype(v_cmp.dtype), v_cmp)
    n_blk = S // SEL_BLOCK
    n_sel = min(SEL_TOPN, n_blk)
    cs = np.arange(n_cmp)[:, None] * CMP_STRIDE
    bs = np.arange(n_blk)[None, :] * SEL_BLOCK
    overlap = jnp.asarray(((cs < bs + SEL_BLOCK) & (cs + CMP_BLOCK > bs)).astype(np.float32))
    importance = jnp.einsum('bhsc,cj->bsj', p_cmp, overlap)
    blk = jnp.arange(n_blk)[None, :]
    cur = (t // SEL_BLOCK)[:, None]
    forced = (blk == 0) | (blk == cur) | (blk == cur - 1)
    score = jnp.where(blk * SEL_BLOCK > t[:, None], -1.0, jnp.where(forced, FORCE_SCORE, importance))
    _, sel_idx = lax.top_k(score, n_sel)
    k_s = rope(rms_norm(k_s[:, None], qk_g[2]), pos, NSA_ROT)[:, 0]
    o_slc = selected_block_attention(q, k_s.reshape(b, n_blk, SEL_BLOCK, D_HEAD),
                                     v_s.reshape(b, n_blk, SEL_BLOCK, D_HEAD), sel_idx)
    k_w = rope(rms_norm(k_w[:, None], qk_g[3]), pos, NSA_ROT)[:, 0]
    o_win = window_attention(q, k_w, v_w)
    gt = jax.nn.sigmoid(g.astype(jnp.float32)).astype(q.dtype)
    gt = gt.reshape(b, S, 3, NSA_HEADS).transpose(2, 0, 3, 1)[..., None]
    return merge_heads(gt[0] * o_cmp + gt[1] * o_slc + gt[2] * o_win)


def fox_mixer(parts, qk_g, f_b):
    q, k, v, f = parts
    q = rms_norm(heads(q, FOX_HEADS), qk_g[0])
    k = rms_norm(heads(k, FOX_HEADS), qk_g[1])
    v = heads(v, FOX_HEADS)
    log_f = jax.nn.log_sigmoid(f.astype(jnp.float32) + f_b.astype(jnp.float32))
    cum = lax.cumsum(log_f, axis=1).transpose(0, 2, 1)
    return merge_heads(causal_attention(q, k, v, cum, cum))


def mla_mixer(parts, pos, cq_g, ckv_g, w_uq, w_ukv, qk_g):
    c_q, c_kv, k_rope = parts
    b, S, _ = c_q.shape
    q = heads(rms_norm(c_q, cq_g) @ w_uq, MLA_HEADS)
    kv = heads(rms_norm(c_kv, ckv_g) @ w_ukv, MLA_HEADS)
    k_nope, v = kv[..., :MLA_NOPE], kv[..., MLA_NOPE:]
    k = jnp.concatenate([jnp.broadcast_to(k_rope[:, None], (b, MLA_HEADS, S, MLA_ROPE)), k_nope], axis=-1)
    q = rope(rms_norm(q, qk_g[0]), pos, MLA_ROPE)
    k = rope(rms_norm(k, qk_g[1]), pos, MLA_ROPE)
    return merge_heads(causal_attention(q, k, v))


def diff_mixer(parts, pos, lam_init, qk_g, lam_vec, out_g):
    q, k, v = parts
    b, S, _ = q.shape

    def halves(t):
        return t.reshape(b, S, DIFF_HEADS, 2, DIFF_QK).transpose(3, 0, 2, 1, 4)

    q = rope(rms_norm(halves(q), qk_g[0]), pos, DIFF_ROT)
    k = rope(rms_norm(halves(k), qk_g[1]), pos, DIFF_ROT)
    v = heads(v, DIFF_HEADS)
    lv = lam_vec.astype(jnp.float32)
    lam = jnp.exp(jnp.sum(lv[0] * lv[1])) - jnp.exp(jnp.sum(lv[2] * lv[3])) + lam_init
    o = causal_attention(q[0], k[0], v) - lam.astype(v.dtype) * causal_attention(q[1], k[1], v)
    return merge_heads(rms_norm(o, out_g) * (1.0 - lam_init))


def token_mixer(h, pos, lam_init, w_in, nsa_qk_g, nsa_cmp_pe, nsa_cmp_w1, nsa_cmp_w2, fox_qk_g, fox_f_b,
                mla_cq_g, mla_ckv_g, mla_w_uq, mla_w_ukv, mla_qk_g, diff_qk_g, diff_lambda, diff_out_g,
                br_w, gate_w, gate_b, w_out):
    b, S, D = h.shape
    parts = split_cols(h @ w_in, IN_SPLITS)
    o_nsa = nsa_mixer(parts[0:8], pos, nsa_qk_g, nsa_cmp_pe, nsa_cmp_w1, nsa_cmp_w2)
    o_fox = fox_mixer(parts[8:12], fox_qk_g, fox_f_b)
    o_mla = mla_mixer(parts[12:15], pos, mla_cq_g, mla_ckv_g, mla_w_uq, mla_w_ukv, mla_qk_g)
    o_diff = diff_mixer(parts[15:18], pos, lam_init, diff_qk_g, diff_lambda, diff_out_g)
    branches = jnp.stack([o_nsa, o_fox, o_mla, o_diff], axis=2)
    y = jnp.einsum('bsmc,mcd->bsmd', branches, br_w)
    gates = jax.nn.sigmoid((h @ gate_w + gate_b).astype(jnp.float32)).astype(h.dtype)
    gates = gates.reshape(b, S, N_MIXERS, D)
    return jnp.sum(gates * y, axis=2) @ w_out


def conv_ffn(h, w_up, conv_w, conv_b, w_down):
    g, v = jnp.split(h @ w_up, 2, axis=-1)
    g = lax.conv_general_dilated(g, conv_w[:, None, :].astype(g.dtype), window_strides=(1,),
                                 padding=[(CONV_WIDTH - 1, 0)], dimension_numbers=('NWC', 'WIO', 'NWC'),
                                 feature_group_count=g.shape[-1]) + conv_b
    return (jax.nn.silu(g) * v) @ w_down


def setup_inputs(seed: int = 0) -> dict:
    key = jax.random.key(seed)
    ks = iter(jax.random.split(key, 40))

    def nrm(shape, scale):
        return jax.random.normal(next(ks), shape, jnp.float32) * scale

    def gain(shape):
        return 1.0 + nrm(shape, 0.02)

    L, D = DEPTH, D_MODEL
    x = nrm((BATCH, SEQ, D), 1.0)
    c = nrm((BATCH, D), 1.0)
    positions = (jax.random.randint(next(ks), (BATCH, 1), 0, 4096, dtype=jnp.int32)
                 + jnp.arange(SEQ, dtype=jnp.int32)[None, :])
    return {
        'x': x,
        'c': c,
        'positions': positions,
        'ada_w': nrm((L, D, 6 * D), 0.5 * D ** -0.5),
        'ada_b': nrm((L, 6 * D), 0.02),
        'w_in': nrm((L, D, IN_WIDTH), D ** -0.5),
        'nsa_qk_g': gain((L, 4, D_HEAD)),
        'nsa_cmp_pe': nrm((L, 2, CMP_BLOCK, D_HEAD), 0.1),
        'nsa_cmp_w1': nrm((L, 2, CMP_BLOCK * D_HEAD, D_HEAD), (CMP_BLOCK * D_HEAD) ** -0.5),
        'nsa_cmp_w2': nrm((L, 2, D_HEAD, D_HEAD), D_HEAD ** -0.5),
        'fox_qk_g': gain((L, 2, D_HEAD)),
        'fox_f_b': 3.0 + nrm((L, FOX_HEADS), 0.5),
        'mla_cq_g': gain((L, MLA_Q_RANK)),
        'mla_ckv_g': gain((L, MLA_KV_RANK)),
        'mla_w_uq': nrm((L, MLA_Q_RANK, MLA_HEADS * (MLA_ROPE + MLA_NOPE)), MLA_Q_RANK ** -0.5),
        'mla_w_ukv': nrm((L, MLA_KV_RANK, MLA_HEADS * (MLA_NOPE + MLA_V)), MLA_KV_RANK ** -0.5),
        'mla_qk_g': gain((L, 2, MLA_ROPE + MLA_NOPE)),
        'diff_qk_g': gain((L, 2, DIFF_QK)),
        'diff_lambda': nrm((L, 4, DIFF_QK), 0.1),
        'diff_out_g': gain((L, DIFF_V)),
        'br_w': nrm((L, N_MIXERS, MIX_WIDTH, D), MIX_WIDTH ** -0.5),
        'gate_w': nrm((L, D, N_MIXERS * D), D ** -0.5),
        'gate_b': nrm((L, N_MIXERS * D), 0.02),
        'w_out': nrm((L, D, D), D ** -0.5),
        'ffn_w_up': nrm((L, D, 2 * D_FF), D ** -0.5),
        'ffn_conv_w': nrm((L, CONV_WIDTH, D_FF), CONV_WIDTH ** -0.5),
        'ffn_conv_b': nrm((L, D_FF), 0.02),
        'ffn_w_down': nrm((L, D_FF, D), D_FF ** -0.5),
    }


def reference(x, c, positions, ada_w, ada_b, w_in, nsa_qk_g, nsa_cmp_pe, nsa_cmp_w1, nsa_cmp_w2,
              fox_qk_g, fox_f_b, mla_cq_g, mla_ckv_g, mla_w_uq, mla_w_ukv, mla_qk_g,
              diff_qk_g, diff_lambda, diff_out_g, br_w, gate_w, gate_b, w_out,
              ffn_w_up, ffn_conv_w, ffn_conv_b, ffn_w_down):
    for l in range(DEPTH):
        mod = jax.nn.silu(c) @ ada_w[l] + ada_b[l]
        sh1, sc1, g1, sh2, sc2, g2 = jnp.split(mod[:, None, :], 6, axis=-1)
        lam_init = 0.8 - 0.6 * math.exp(-0.3 * l)
        h = rms_norm(x) * (1.0 + sc1) + sh1
        x = x + g1 * token_mixer(h, positions, lam_init, w_in[l], nsa_qk_g[l], nsa_cmp_pe[l], nsa_cmp_w1[l],
                                 nsa_cmp_w2[l], fox_qk_g[l], fox_f_b[l], mla_cq_g[l], mla_ckv_g[l],
                                 mla_w_uq[l], mla_w_ukv[l], mla_qk_g[l], diff_qk_g[l], diff_lambda[l],
                                 diff_out_g[l], br_w[l], gate_w[l], gate_b[l], w_out[l])
        h = rms_norm(x) * (1.0 + sc2) + sh2
        x = x + g2 * conv_ffn(h, ffn_w_up[l], ffn_conv_w[l], ffn_conv_b[l], ffn_w_down[l])
    return x
```

```python
import contextlib
import math
import numpy as np
import ml_dtypes
import concourse.bass as bass
import concourse.mybir as mybir
from concourse.bass_utils import run_bass_kernel_spmd

F32 = mybir.dt.float32
BF16 = mybir.dt.bfloat16
I32 = mybir.dt.int32
AF = mybir.ActivationFunctionType
ALU = mybir.AluOpType

S = 2048
D = 1024
NQ = 4
QT = 512
NKT = 16
DEPTH = 4
EPS = 1e-6
DFF = 2816
NFF = 22
THETA = 500000.0
BIG = 30000.0
IN_W = 2608


class Res:
    __slots__ = ("name", "w", "r", "dsem", "dcnt")

    def __init__(self, name):
        self.name = name
        self.w = None
        self.r = {}
        self.dsem = None
        self.dcnt = 0


class KB:
    ENG = ("pe", "act", "dve", "pool", "sp")

    def __init__(self, nc, stack):
        self.nc = nc
        self.stack = stack
        self.e = {"pe": nc.tensor, "act": nc.scalar, "dve": nc.vector, "pool": nc.gpsimd, "sp": nc.sync}
        self.sem = {k: stack.enter_context(nc.semaphore("s_" + k)) for k in self.ENG}
        self.cnt = {k: 0 for k in self.ENG}
        self.seen = {k: {} for k in self.ENG}
        self.same_engine_raw = True
        self.n_wait = 0

    def sb(self, name, shape, dt, stack=None):
        self.uid = getattr(self, "uid", 0) + 1
        return (stack or self.stack).enter_context(self.nc.sbuf_tensor(f"{name}_u{self.uid}", list(shape), dt))

    def ps(self, name, shape, dt=F32):
        return self.stack.enter_context(self.nc.psum_tensor(name, list(shape), dt))

    def newsem(self, name):
        self.uid = getattr(self, "uid", 0) + 1
        return self.stack.enter_context(self.nc.semaphore(f"{name}_u{self.uid}"))

    def _need(self, eng, deps):
        for key, sem, val in deps:
            if self.seen[eng].get(key, 0) >= val:
                continue
            self.e[eng].wait_ge(sem, val)
            self.n_wait += 1
            self.seen[eng][key] = val

    def _deps(self, eng, reads, writes):
        d = {}

        def add(w):
            if w is None:
                return
            e, i = w
            if e == "dma":
                sem, val = i
                key = ("dma", id(sem))
                if d.get(key, (None, 0))[1] < val:
                    d[key] = (sem, val)
            else:
                if e == eng and (eng == "pe" or not self.same_engine_raw):
                    return
                if d.get(e, (None, 0))[1] < i:
                    d[e] = (self.sem[e], i)

        for r in reads:
            add(r.w)
        for w in writes:
            add(w.w)
            for e, i in w.r.items():
                if e == eng:
                    continue
                if e == "dma":
                    add(("dma", i))
                else:
                    add((e, i))
        return [(k, s, v) for k, (s, v) in d.items()]

    def op(self, eng, fn, reads=(), writes=()):
        self._need(eng, self._deps(eng, reads, writes))
        ins = fn(self.e[eng])
        ins.then_inc(self.sem[eng], 1)
        self.cnt[eng] += 1
        idx = self.cnt[eng]
        for r in reads:
            r.r[eng] = idx
        for w in writes:
            w.w = (eng, idx)
            w.r = {}
        return ins

    def dma_in(self, q, res, fn, reads=()):
        if res.dsem is None:
            res.dsem = self.newsem("d_" + res.name)
        self._need(q, self._deps(q, reads, [res]))
        inss = fn(self.e[q])
        if not isinstance(inss, (list, tuple)):
            inss = [inss]
        for ins in inss:
            ins.then_inc(res.dsem, 16)
            res.dcnt += 16
        res.w = ("dma", (res.dsem, res.dcnt))
        res.r = {}

    def dma_out(self, q, res_list, fn, sem):
        self._need(q, self._deps(q, res_list, []))
        inss = fn(self.e[q])
        if not isinstance(inss, (list, tuple)):
            inss = [inss]
        for ins in inss:
            ins.then_inc(sem[0], 16)
            sem[1] += 16
        for r in res_list:
            r.r["dma"] = (sem[0], sem[1])

    def barrier(self):
        for a in ("pe", "act", "dve", "pool"):
            deps = []
            for b in ("pe", "act", "dve", "pool"):
                if a != b and self.cnt[b] > 0:
                    deps.append((b, self.sem[b], self.cnt[b]))
            self._need(a, deps)


CB = {}
CF = {}
SM = {}


def _alloc(tab, name, n, cur):
    tab[name] = (cur, cur + n)
    return cur + n


def _layout():
    c = 0
    for name, n in (("ident", 128), ("ones", 128), ("b64", 128), ("b32", 128), ("sw_nsa", 128),
                    ("sw_mla", 128), ("sw_dif", 128), ("selrow", 512), ("esel", 2048), ("ovl", 64), ("tabA", 512), ("tabB", 512)):
        c = _alloc(CB, name, n, c)
    CB["_n"] = c
    c = 0
    for name, n in (("ident", 128), ("invf", 1), ("sgn", 1), ("negpi", 1), ("epsc", 8), ("lamc", 2 * DEPTH)):
        c = _alloc(CF, name, n, c)
    CF["_n"] = c
    c = 0
    for name, n in (("ada_b", 48), ("gate_b", 32), ("conv_w", 66), ("conv_b", 22),
                    ("nsa_gq", 1), ("nsa_gkc", 1), ("nsa_gks", 1), ("nsa_gkw", 1),
                    ("fox_gq", 1), ("fox_gk", 1), ("fox_fb", 1),
                    ("mla_cqg", 2), ("mla_ckvg", 1), ("mla_gq", 1), ("mla_gk", 1),
                    ("dif_gq", 1), ("dif_gk", 1), ("dif_og", 1), ("dif_lam", 128)):
        c = _alloc(SM, name, n, c)
    SM["_n"] = c


_layout()

DER = {}
_c = 0
for _name, _n in (("a1", 8), ("a2", 8), ("nsa_gq", 1), ("nsa_gkc", 1), ("nsa_gks", 1), ("nsa_gkw", 1),
                  ("fox_gq", 1), ("fox_gk", 1), ("negfb", 1), ("mla_cqg", 2), ("mla_ckvg", 1), ("mla_gq", 1),
                  ("mla_gk", 1), ("dif_gq", 1), ("dif_gk", 1), ("dif_og", 1), ("neglam", 1), ("t0", 4), ("t1", 4)):
    _c = _alloc(DER, _name, _n, _c)
DER["_n"] = _c


def _rope_partner(r, head, n_rot):
    rr = r % head
    half = n_rot // 2
    base = r - rr
    if rr < half:
        return base + rr + half, rr, -1.0
    if rr < n_rot:
        return base + rr - half, rr - half, 1.0
    return None, None, 0.0


def make_consts(layer_ids=tuple(range(DEPTH))):
    cb = np.zeros((128, CB["_n"]), np.float32)
    cf = np.zeros((128, CF["_n"]), np.float32)
    p = np.arange(128)
    cb[:, CB["ident"][0]:CB["ident"][1]] = np.eye(128)
    cb[:, CB["ones"][0]:CB["ones"][1]] = 1.0
    cb[:, CB["b64"][0]:CB["b64"][1]] = (p[:, None] // 64 == p[None, :] // 64)
    cb[:, CB["b32"][0]:CB["b32"][1]] = (p[:, None] // 32 == p[None, :] // 32)
    for ci, (nm, head, nrot) in enumerate((("sw_nsa", 64, 16), ("sw_mla", 96, 32), ("sw_dif", 32, 8))):
        sw = np.zeros((128, 128), np.float32)
        half = nrot // 2
        inv = (np.float32(THETA) ** (-np.arange(half, dtype=np.float32) / np.float32(half))).astype(np.float32)
        for m in range(128):
            if nm == "sw_mla":
                if 64 <= m < 96:
                    rr = m - 64
                    sw[64 + (rr + 16 if rr < 16 else rr - 16), m] = 1.0
                continue
            pr, fi, sg = _rope_partner(m, head, nrot)
            if pr is not None:
                sw[pr, m] = 1.0
        for r in range(32):
            pr, fi, sg = _rope_partner(r, head, nrot)
            if pr is not None:
                cf[32 * ci + r, CF["invf"][0]] = inv[fi]
                cf[32 * ci + r, CF["sgn"][0]] = sg
        cb[:, CB[nm][0]:CB[nm][1]] = sw
    for h in range(4):
        cb[32 * h, CB["selrow"][0] + 128 * h: CB["selrow"][0] + 128 * (h + 1)] = 1.0
    for kt in range(16):
        for pp in range(128):
            cb[2 * kt + pp // 64, CB["esel"][0] + kt * 128 + pp] = BIG
    cs = np.arange(127)[:, None] * 16
    bs = np.arange(32)[None, :] * 64
    ov = ((cs < bs + 64) & (cs + 32 > bs)).astype(np.float32)
    cb[0:127, CB["ovl"][0]:CB["ovl"][0] + 32] = ov
    cb[0:127, CB["ovl"][0] + 32:CB["ovl"][0] + 64] = 1.0
    cf[:, CF["ident"][0]:CF["ident"][1]] = np.eye(128)
    cf[:, CF["negpi"][0]] = -math.pi
    for i_, v_ in enumerate((1024 * EPS, 64 * EPS, 96 * EPS, 32 * EPS, 256 * EPS, 128 * EPS, 1.0, 1e-30)):
        cf[:, CF["epsc"][0] + i_] = v_
    for i_, l_ in enumerate(layer_ids):
        lam_init = 0.8 - 0.6 * math.exp(-0.3 * l_)
        cf[:, CF["lamc"][0] + 2 * i_] = 8.0 * (1.0 - lam_init)
        cf[:, CF["lamc"][0] + 2 * i_ + 1] = -lam_init
    t = (np.arange(16)[None, :, None] * 128 + p[:, None, None])
    j = np.arange(32)[None, None, :]
    cur = t // 64
    future = (j * 64 > t)
    forced = ((j == 0) | (j == cur) | (j == cur - 1)) & (~future)
    A = (~future & ~forced).astype(np.float32)
    Bt = np.where(future, -1.0, np.where(forced, 1e4, 0.0)).astype(np.float32)
    cb[:, CB["tabA"][0]:CB["tabA"][1]] = A.reshape(128, 512)
    cb[:, CB["tabB"][0]:CB["tabB"][1]] = Bt.reshape(128, 512)
    return cb.astype(ml_dtypes.bfloat16), cf


def make_smalls(inp, layer_ids=tuple(range(DEPTH))):
    sm = np.zeros((128, len(layer_ids) * SM["_n"]), np.float32)
    for li_, l in enumerate(layer_ids):
        o = li_ * SM["_n"]

        def put(name, arr):
            a, b = SM[name]
            arr = np.asarray(arr, np.float32)
            if arr.ndim == 1:
                arr = arr[:, None]
            sm[:arr.shape[0], o + a:o + a + arr.shape[1]] = arr

        put("ada_b", inp["ada_b"][l].reshape(48, 128).T)
        put("gate_b", inp["gate_b"][l].reshape(32, 128).T)
        cw = inp["ffn_conv_w"][l]
        put("conv_w", np.concatenate([cw[t].reshape(22, 128).T for t in range(3)], axis=1))
        put("conv_b", inp["ffn_conv_b"][l].reshape(22, 128).T)
        g = inp["nsa_qk_g"][l]
        put("nsa_gq", np.tile(g[0], 2)); put("nsa_gkc", np.tile(g[1], 2))
        put("nsa_gks", np.tile(g[2], 2)); put("nsa_gkw", np.tile(g[3], 2))
        g = inp["fox_qk_g"][l]
        put("fox_gq", np.tile(g[0], 2)); put("fox_gk", np.tile(g[1], 2))
        fb = np.zeros(128, np.float32)
        fb[0::32] = inp["fox_f_b"][l]
        put("fox_fb", fb)
        put("mla_cqg", inp["mla_cq_g"][l].reshape(2, 128).T)
        put("mla_ckvg", inp["mla_ckv_g"][l])
        gq_, gk_ = inp["mla_qk_g"][l, 0], inp["mla_qk_g"][l, 1]
        put("mla_gq", np.concatenate([gq_[32:], gq_[:32]])); put("mla_gk", np.concatenate([gk_[32:], gk_[:32]]))
        put("dif_gq", np.tile(inp["diff_qk_g"][l, 0], 4)); put("dif_gk", np.tile(inp["diff_qk_g"][l, 1], 4))
        put("dif_og", np.tile(inp["diff_out_g"][l], 2))
        put("dif_lam", np.tile(inp["diff_lambda"][l].reshape(1, 128), (128, 1)))
    return sm


C_NSA_Q, C_NSA_KC, C_NSA_KS, C_NSA_VS, C_NSA_KW, C_NSA_VW, C_NSA_G = 0, 256, 384, 448, 512, 576, 640
C_FOX_Q, C_FOX_K, C_FOX_V, C_FOX_F = 652, 908, 1164, 1420
C_MLA_CQ, C_MLA_CKV, C_MLA_KR = 1424, 1680, 1808
C_DIF_Q, C_DIF_K, C_DIF_V = 1840, 2096, 2352


class Prog:
    def __init__(self, n_layers=DEPTH, dbg=(), wdepth=DEPTH):
        self.n_layers = n_layers
        self.wd = wdepth
        self.dbg = set(dbg)
        self.nc = bass.Bass("TRN2", target_bir_lowering=False)
        self.stack = contextlib.ExitStack()
        self.dbg_out = {}

    def dram_in(self, name, shape, dt=F32):
        return self.nc.dram_tensor(name, list(shape), dt, kind="ExternalInput").ap()

    def dram_out(self, name, shape, dt=F32):
        return self.nc.dram_tensor(name, list(shape), dt, kind="ExternalOutput").ap()

    def mm(self, out, lhsT, rhs, start, stop, rin, rout):
        self.kb.op("pe", lambda e: e.matmul(out, lhsT=lhsT, rhs=rhs, start=start, stop=stop), rin, [rout])

    def cbf(self, name, rows=slice(0, 128), c0=0, c1=None):
        a, b = CB[name]
        if c1 is None:
            c1 = b - a
        return self.t_cb[rows, a + c0:a + c1]

    def cf(self, name, c0=0, c1=None, rows=slice(0, 128)):
        a, b = CF[name]
        if c1 is None:
            c1 = b - a
        return self.t_cf[rows, a + c0:a + c1]

    def sm(self, l, name, c0=0, c1=None, rows=slice(0, 128)):
        a, b = SM[name]
        if c1 is None:
            c1 = b - a
        o = l * SM["_n"]
        return self.t_sm[rows, o + a + c0:o + a + c1]

    def der(self, name, c0=0, c1=None, rows=slice(0, 128), par=None):
        a, b = DER[name]
        if c1 is None:
            c1 = b - a
        return self.t_ders[self.cur if par is None else par][rows, a + c0:a + c1]

    @property
    def t_mod(self):
        return self.t_mods[self.cur]

    @property
    def Rmod(self):
        return self.Rmods[self.cur]

    @property
    def Rder(self):
        return self.Rders[self.cur]

    def dump(self, name, ap, res, shape):
        if name not in self.dbg:
            return
        d = self.dram_out("dbg_" + name, shape, F32 if ap.dtype == F32 else BF16)
        self.dbg_out[name] = d
        self.kb.dma_out("sp", res, lambda e: e.dma_start(out=d, in_=ap), self.osem)

    def build(self):
        nc = self.nc
        st = self.stack
        kb = self.kb = KB(nc, st)
        self.osem = [kb.newsem("osem"), 0]
        self.x_d = self.dram_in("x", [S, D])
        self.cT_d = self.dram_in("cT", [128, 8])
        self.pos_d = self.dram_in("pos", [1, S], I32)
        self.ada_w = self.dram_in("ada_w", [self.wd, D, 6 * D])
        self.w_in = self.dram_in("w_in", [self.wd, D, IN_W])
        self.wg_rep = self.dram_in("wg_rep", [self.wd, D, 768])
        self.wf_pad = self.dram_in("wf_pad", [self.wd, D, 128])
        self.cmp_w1 = self.dram_in("cmp_w1", [self.wd, 2, 2048, 64])
        self.cmp_w2 = self.dram_in("cmp_w2", [self.wd, 2, 64, 64])
        self.cmp_pe = self.dram_in("cmp_pe", [self.wd, 2, 64, 32])
        self.w_uq = self.dram_in("w_uq", [self.wd, 256, 384])
        self.w_ukv = self.dram_in("w_ukv", [self.wd, 128, 512])
        self.br_w = self.dram_in("br_w", [self.wd, 4, 256, D])
        self.gate_w = self.dram_in("gate_w", [self.wd, D, 4 * D])
        self.w_out = self.dram_in("w_out", [self.wd, D, D])
        self.w_up = self.dram_in("w_up", [self.wd, D, 2 * DFF])
        self.w_down = self.dram_in("w_down", [self.wd, DFF, D])
        self.sm_d = self.dram_in("smalls", [128, self.wd * SM["_n"]])
        self.cb_d = self.dram_in("cbf", [128, CB["_n"]], BF16)
        self.cf_d = self.dram_in("cf32", [128, CF["_n"]])
        self.y_d = self.dram_out("y", [S, D])

        self.xT = [kb.sb(f"xT{k}", [128, S], F32) for k in range(8)]
        self.hT = [kb.sb(f"hT{k}", [128, S], BF16) for k in range(8)]
        self.Rx = [[Res(f"x{k}_{q}") for q in range(NQ)] for k in range(8)]
        self.Rh = [[Res(f"h{k}_{q}") for q in range(NQ)] for k in range(8)]
        self.ropeC = kb.sb("ropeC", [128, S], BF16)
        self.ropeS = kb.sb("ropeS", [128, S], BF16)
        self.Rrope = Res("rope")
        self.t_cb = kb.sb("t_cb", [128, CB["_n"]], BF16)
        self.t_cf = kb.sb("t_cf", [128, CF["_n"]], F32)
        self.t_sm = kb.sb("t_sm", [128, self.wd * SM["_n"]], F32)
        self.t_mods = [kb.sb(f"t_mod{i}", [128, 48], F32) for i in range(2)]
        self.t_ders = [kb.sb(f"t_der{i}", [128, DER["_n"]], F32) for i in range(2)]
        self.adab = [kb.sb(f"adab{i}", [128, 512], BF16) for i in range(2)]
        self.Radab = [Res(f"adab{i}") for i in range(2)]
        self.Rmods = [Res("mod0"), Res("mod1")]
        self.Rders = [Res("der0"), Res("der1")]
        self.cur = 0
        self.t_scb = kb.sb("t_scb", [128, 8], BF16)
        self.Rcb, self.Rcf, self.Rsm, self.Rscb = (Res(n) for n in ("cb", "cf", "sm", "scb"))
        self.WB = [kb.sb(f"WB{i}", [128, 8, 512], BF16) for i in range(2)]
        self.RWB = [Res(f"WB{i}") for i in range(2)]
        self.wb_i = 0
        self.ps = [kb.ps(f"ps{i}", [128, 512]) for i in range(8)]
        self.Rps = [Res(f"ps{i}") for i in range(8)]

        kb.dma_in("sp", self.Rcb, lambda e: e.dma_start(out=self.t_cb[:], in_=self.cb_d))
        kb.dma_in("sp", self.Rcf, lambda e: e.dma_start(out=self.t_cf[:], in_=self.cf_d))
        kb.dma_in("sp", self.Rsm, lambda e: e.dma_start(out=self.t_sm[:], in_=self.sm_d))

        self.ada_gen = None
        self.prologue()
        for l in range(self.n_layers):
            self.layer(l)
        self.epilogue()
        kb.e["sp"].wait_ge(self.osem[0], self.osem[1])
        self.stack.close()
        return nc

    def mark(self, name):
        if not hasattr(self, 'marks'):
            self.marks = []
        self.marks.append((name, self.kb.cnt['pe']))

    def next_wb(self):
        i = self.wb_i
        self.wb_i ^= 1
        return self.WB[i], self.RWB[i]

    def load_w(self, dram_ap_pkn, ncols, nk=8):
        wb, r = self.next_wb()
        self.kb.dma_in("pool", r, lambda e: e.dma_start(out=wb[:, 0:nk, 0:ncols], in_=dram_ap_pkn))
        return wb, r

    def win_cols(self, l, c0, n):
        return self.w_in[l].rearrange("(k p) n -> p k n", p=128)[:, :, c0:c0 + n]

    def prologue(self):
        kb = self.kb
        with contextlib.ExitStack() as ps_:
            xs = [kb.sb(f"xstage{i}", [128, 4, D], F32, ps_) for i in range(2)]
            Rxs = [Res(f"xstage{i}") for i in range(2)]
            posi = kb.sb("posi", [128, S], I32, ps_)
            posf = kb.sb("posf", [128, S], F32, ps_)
            tA = kb.sb("tA", [128, S], F32, ps_)
            tB = kb.sb("tB", [128, S], F32, ps_)
            Rposi, Rposf, RtA, RtB = Res("posi"), Res("posf"), Res("tA"), Res("tB")
            ct = kb.sb("ct", [128, 8], F32, ps_)
            Rct = Res("ct")
            kb.dma_in("sp", Rct, lambda e: e.dma_start(out=ct[:], in_=self.cT_d))
            kb.op("act", lambda e: e.activation(out=self.t_scb[:], in_=ct[:], func=AF.Silu), [Rct], [self.Rscb])
            self.ada_gen = self.g_ada(0)
            xv = self.x_d.rearrange("(g t p) d -> g p t d", t=4, p=128)
            for g in range(2):
                kb.dma_in("sp", Rxs[g], lambda e: e.dma_start(out=xs[g][:], in_=xv[g]))
            for g in range(4):
                xg, Rxg = xs[g % 2], Rxs[g % 2]
                for k in range(8):
                    self.ada_step(3)
                    pb = self.ps[k % 4]
                    for t in range(4):
                        kb.op("pe", lambda e: e.transpose(out=pb[:, t * 128:(t + 1) * 128], in_=xg[:, t, k * 128:(k + 1) * 128],
                                                          identity=self.cf("ident")), [Rxg, self.Rcf], [self.Rps[k % 4]])
                    eng = "act" if k % 2 == 0 else "dve"
                    if eng == "act":
                        kb.op("act", lambda e: e.activation(out=self.xT[k][:, g * 512:(g + 1) * 512], in_=pb[:], func=AF.Copy),
                              [self.Rps[k % 4]], [self.Rx[k][g]])
                    else:
                        kb.op("dve", lambda e: e.tensor_copy(out=self.xT[k][:, g * 512:(g + 1) * 512], in_=pb[:]),
                              [self.Rps[k % 4]], [self.Rx[k][g]])
                if g + 2 < 4:
                    kb.dma_in("sp", Rxg, lambda e: e.dma_start(out=xg[:], in_=xv[g + 2]))
            kb.dma_in("sp", Rposi, lambda e: e.dma_start(out=posi[:], in_=self.pos_d.partition_broadcast(128)))
            kb.op("dve", lambda e: e.tensor_copy(out=posf[:], in_=posi[:]), [Rposi], [Rposf])
            twopi = 2.0 * math.pi
            invf = self.cf("invf")
            sgn = self.cf("sgn")
            ti = posi
            for which in range(2):
                kb.op("dve", lambda e: e.tensor_scalar(out=tA[:], in0=posf[:], scalar1=invf, scalar2=1.0 / twopi, op0=ALU.mult, op1=ALU.mult),
                      [Rposf, self.Rcf, RtB], [RtA])
                if which == 1:
                    kb.op("dve", lambda e: e.tensor_scalar(out=tA[:], in0=tA[:], scalar1=0.25, scalar2=None, op0=ALU.add), [RtA], [RtA])
                kb.op("dve", lambda e: e.tensor_copy(out=ti[:], in_=tA[:]), [RtA, Rposf], [Rposi])
                kb.op("dve", lambda e: e.tensor_copy(out=tB[:], in_=ti[:]), [Rposi], [RtB])
                kb.op("dve", lambda e: e.tensor_tensor(out=tA[:], in0=tA[:], in1=tB[:], op=ALU.subtract), [RtA, RtB], [RtA])
                kb.op("act", lambda e: e.activation(out=tB[:], in_=tA[:], func=AF.Sin, scale=twopi), [RtA], [RtB])
                if which == 0:
                    kb.op("dve", lambda e: e.tensor_scalar(out=self.ropeS[:], in0=tB[:], scalar1=sgn, scalar2=None, op0=ALU.mult),
                          [RtB, self.Rcf], [self.Rrope])
                else:
                    kb.op("dve", lambda e: e.tensor_copy(out=self.ropeC[:], in_=tB[:]), [RtB], [self.Rrope])
            self.dump("ropeC", self.ropeC[:], [self.Rrope], [128, S])
            self.dump("ropeS", self.ropeS[:], [self.Rrope], [128, S])
            self.dump("xT0", self.xT[0][:], self.Rx[0], [128, S])
            kb.barrier()
            if self.dbg:
                kb.e["act"].wait_ge(self.osem[0], self.osem[1])
                kb.barrier()

    def epilogue(self):
        kb = self.kb
        with contextlib.ExitStack() as ps_:
            ys = [kb.sb(f"ystage{i}", [128, 2, D], F32, ps_) for i in range(2)]
            Rys = [Res(f"ystage{i}") for i in range(2)]
            yv = self.y_d.rearrange("(g t p) d -> g p t d", t=2, p=128)
            for g in range(8):
                q = g // 2
                for t in range(2):
                    tt = g * 2 + t
                    for k in range(8):
                        pb = self.ps[(k // 4) + 2 * (tt % 2)]
                        kb.op("pe", lambda e: e.transpose(out=pb[:, (k % 4) * 128:(k % 4 + 1) * 128], in_=self.xT[k][:, tt * 128:(tt + 1) * 128],
                                                          identity=self.cf("ident")), [self.Rx[k][q], self.Rcf], [self.Rps[(k // 4) + 2 * (tt % 2)]])
                    for hh in range(2):
                        bi = hh + 2 * (tt % 2)
                        if hh == 0:
                            kb.op("act", lambda e: e.activation(out=ys[g % 2][:, t, hh * 512:(hh + 1) * 512], in_=self.ps[bi][:], func=AF.Copy),
                                  [self.Rps[bi]], [Rys[g % 2]])
                        else:
                            kb.op("dve", lambda e: e.tensor_copy(out=ys[g % 2][:, t, hh * 512:(hh + 1) * 512], in_=self.ps[bi][:]),
                                  [self.Rps[bi]], [Rys[g % 2]])
                kb.dma_out("sp", [Rys[g % 2]], lambda e: e.dma_start(out=yv[g], in_=ys[g % 2][:]), self.osem)

    def layer(self, l):
        self.cur = l % 2
        while self.ada_gen is not None:
            self.ada_step()
        self.norm_mod(l, 0)
        kb = self.kb
        with contextlib.ExitStack() as ls:
            self.oT = [None] * 4
            self.RoT = [None] * 4
            for m, fn in self.mixer_order():
                self.oT[m] = [kb.sb(f"oT{m}_{c}", [128, S], BF16, ls) for c in range(2)]
                self.RoT[m] = [[Res(f"oT{m}_{c}_{q}") for q in range(NQ)] for c in range(2)]
                with contextlib.ExitStack() as ms:
                    fn(l, ms)
                    kb.barrier()
                if l == 0:
                    for c in range(2):
                        self.dump(f"o{m}_{c}", self.oT[m][c][:], self.RoT[m][c], [128, S])
            if self.dbg:
                kb.e["act"].wait_ge(self.osem[0], self.osem[1])
                kb.barrier()
            if getattr(self, "mixers", None) is None:
                self.merge(l)
                self.dump(f"x1_l{l}", self.xT[0][:], self.Rx[0], [128, S])
        if getattr(self, "mixers", None) is None:
            self.norm_mod(l, 1)
            self.ffn(l)
            self.dump(f"x2_l{l}", self.xT[0][:], self.Rx[0], [128, S])
            if self.dbg:
                kb.e["act"].wait_ge(self.osem[0], self.osem[1])
                kb.barrier()

    def gate_tile(self, wg, rwg, blk, q, gt, Rgt):
        kb = self.kb
        pb, Rpb = self.ps[3], self.Rps[3]
        self.proj_T(pb, Rpb, slice(0, 128), lambda k: wg[:, k, blk * 128:(blk + 1) * 128], rwg, q)
        kb.op("act", lambda e: e.activation(out=gt[:], in_=pb[:], func=AF.Exp, scale=-1.0), [Rpb], [Rgt])
        kb.op("dve", lambda e: e.tensor_scalar(out=gt[:], in0=gt[:], scalar1=1.0, scalar2=None, op0=ALU.add), [Rgt], [Rgt])
        kb.op("dve", lambda e: e.reciprocal(out=gt[:], in_=gt[:]), [Rgt], [Rgt])

    def nsa(self, l, ms):
        kb = self.kb
        self.mark('nsa_prelude')
        KS2 = kb.sb("KS2", [128, S], BF16, ms)
        KW2 = kb.sb("KW2", [128, S], BF16, ms)
        RKS = [Res(f"KS2_{q}") for q in range(NQ)]
        RKW = [Res(f"KW2_{q}") for q in range(NQ)]
        VSW = kb.sb("VSW", [128, NKT, 384], BF16, ms)
        RVSW = Res("VSW")
        KCMP = kb.sb("KCMP", [128, 128], BF16, ms)
        VCMP = kb.sb("VCMP", [128, 192], BF16, ms)
        RKCMP, RVCMP = Res("KCMP"), Res("VCMP")
        IMP = kb.sb("IMP", [128, 512], F32, ms)
        RIMP = Res("IMP")
        SELM = kb.sb("SELM", [32, S], BF16, ms)
        RSELM = [Res(f"SELM{q}") for q in range(NQ)]
        self.mixer_scratch(ms, nq=2, nk=0, va=False)
        kb.op("pool", lambda e: e.memset(VSW[:, :, 64:128], 1.0), [], [RVSW])
        kb.op("pool", lambda e: e.memset(VSW[:, :, 256:320], 1.0), [], [RVSW])
        kb.op("pool", lambda e: e.memset(VCMP[:], 0.0), [], [RVCMP])
        kb.op("pool", lambda e: e.memset(VCMP[:, 64:128], 1.0), [RVCMP], [RVCMP])
        kb.op("pool", lambda e: e.memset(KCMP[:], 0.0), [], [RKCMP])
        kb.op("pool", lambda e: e.memset(IMP[:], 0.0), [], [RIMP])
        pstat, Rpstat = self.ps[5], self.Rps[5]
        wA, rwA = self.load_w(self.win_cols(l, C_NSA_KC, 384), 384)
        wq, rwq = self.load_w(self.win_cols(l, C_NSA_Q, 256), 256)
        with contextlib.ExitStack() as pscope:
            KVC = self.QTt[1]
            RKVC = [Res(f"KVC{q}") for q in range(NQ)]
            W1 = kb.sb("W1", [128, 32, 64], BF16, pscope)
            W2 = kb.sb("W2", [128, 64], BF16, pscope)
            PEt = kb.sb("PEt", [128, 32], BF16, pscope)
            HID = kb.sb("HID", [128, 128], BF16, pscope)
            RW1, RW2, RPE, RHID = Res("W1"), Res("W2"), Res("PEt"), Res("HID")
            kb.dma_in("pool", RW1, lambda e: [e.dma_start(out=W1[64 * i:64 * i + 64, :, :], in_=self.cmp_w1[l, i].rearrange("(j d) o -> d j o", d=64)) for i in range(2)])
            kb.dma_in("pool", RW2, lambda e: [e.dma_start(out=W2[64 * i:64 * i + 64, :], in_=self.cmp_w2[l, i]) for i in range(2)])
            kb.dma_in("pool", RPE, lambda e: [e.dma_start(out=PEt[64 * i:64 * i + 64, :], in_=self.cmp_pe[l, i]) for i in range(2)])
            for q in range(NQ):
                cs = slice(q * QT, (q + 1) * QT)
                for (c0, dstt, Rd, gname) in ((128, KS2, RKS, "nsa_gks"), (256, KW2, RKW, "nsa_gkw")):
                    for half in range(2):
                        self.proj_T(self.ps[3], self.Rps[3], slice(64 * half, 64 * half + 64), lambda k: wA[:, k, c0:c0 + 64], rwA, q)
                    self.headnorm(3, 128, self.cbf("b64"), self.cf("epsc", 1, 2), self.der(gname), dstt[:, cs], Rd[q], cs, rope=(0, "sw_nsa", [0, 64]))
                self.proj_T(self.ps[4], self.Rps[4], slice(0, 128), lambda k: wA[:, k, 0:128], rwA, q)
                kb.op("act", lambda e: e.activation(out=KVC[:, cs], in_=self.ps[4][:], func=AF.Copy), [self.Rps[4]], [RKVC[q]])
            for g in range(4):
                pb, Rpb = self.ps[3 + g % 2], self.Rps[3 + g % 2]
                for t in range(4):
                    kt = g * 4 + t
                    for b in range(2):
                        for k in range(8):
                            self.mm(pb[:, (t * 2 + b) * 64:(t * 2 + b + 1) * 64], self.hT[k][:, kt * 128:(kt + 1) * 128], wA[:, k, 192 + 128 * b:256 + 128 * b],
                                    k == 0, k == 7, [rwA, self.Rh[k][g]], Rpb)
                src = pb[:].rearrange("p (t b c) -> p t b c", t=4, b=2)
                for sidx in (0, 2):
                    dstv = VSW[:, g * 4:(g + 1) * 4, :].rearrange("p t (b s c) -> p t b s c", b=2, s=3)[:, :, :, sidx, :]
                    kb.op("dve" if sidx == 0 else "act", (lambda e: e.tensor_copy(out=dstv, in_=src)) if sidx == 0 else
                          (lambda e: e.activation(out=dstv, in_=src, func=AF.Copy)), [Rpb], [RVSW])
            pH, RpH = self.ps[6], self.Rps[6]
            for i in range(2):
                rr = slice(64 * i, 64 * i + 64)
                n = 0
                for j in range(32):
                    self.mm(pH[rr, 0:127], W1[rr, j, :], KVC[rr, j:j + 16 * 126 + 1:16], n == 0, False, [RW1] + RKVC, RpH)
                    n += 1
                    self.mm(pH[rr, 0:127], W1[rr, j, :], PEt[rr, j:j + 1].to_broadcast([64, 127]), False, j == 31, [RW1, RPE], RpH)
            kb.op("act", lambda e: e.activation(out=HID[:, 0:127], in_=pH[:, 0:127], func=AF.Silu), [RpH], [RHID])
            for half in range(2):
                self.mm(self.ps[3][64 * half:64 * half + 64, 0:127], W2[0:64, :], HID[0:64, 0:127], True, True, [RW2, RHID], self.Rps[3])
            kb.op("act", lambda e: e.activation(out=self.sqb[0][:, 0:127], in_=self.ps[3][:, 0:127], func=AF.Square), [self.Rps[3]], [self.Rsqb[0]])
            self.mm(pstat[:, 0:127], self.cbf("b64"), self.sqb[0][:, 0:127], True, True, [self.Rsqb[0], self.Rcb], Rpstat)
            self.rstd_from(pstat[:, 0:127], self.rstd[0][:, 0:127], self.cf("epsc", 1, 2), Rpstat, self.Rrstd[0])
            kb.op("dve", lambda e: e.scalar_tensor_tensor(out=KCMP[:, 0:127], in0=self.ps[3][:, 0:127], scalar=self.der("nsa_gkc"), in1=self.rstd[0][:, 0:127],
                                                          op0=ALU.mult, op1=ALU.mult), [self.Rps[3], self.Rrstd[0], self.Rder, RKCMP], [RKCMP])
            self.mm(self.ps[4][0:127, 0:64], HID[64:128, 0:127], W2[64:128, :], True, True, [RW2, RHID], self.Rps[4])
            kb.op("dve", lambda e: e.tensor_copy(out=VCMP[0:127, 0:64], in_=self.ps[4][0:127, 0:64]), [self.Rps[4], RVCMP], [RVCMP])
            kb.op("act", lambda e: e.activation(out=VCMP[0:127, 128:192], in_=self.ps[4][0:127, 0:64], func=AF.Copy), [self.Rps[4], RVCMP], [RVCMP])
            self.dump("nsaKS", KS2[:], RKS, [128, S])
            self.dump("nsaKCMP", KCMP[:], [RKCMP], [128, 128])
            self.dump("nsaVCMP", VCMP[:], [RVCMP], [128, 192])
            kb.barrier()
            if self.dbg:
                kb.e["act"].wait_ge(self.osem[0], self.osem[1])
                kb.barrier()
        GT = [kb.sb(f"GT{i}", [128, QT], F32, ms) for i in range(2)]
        RGT = [Res(f"GT{i}") for i in range(2)]
        self.mark('nsa_q')
        wg0, rwg0 = self.load_w(self.wg_rep[l].rearrange("(k p) n -> p k n", p=128)[:, :, 0:256], 256)
        for u in range(2):
            for q in range(NQ):
                cs = slice(q * QT, (q + 1) * QT)
                self.proj_T(self.ps[3], self.Rps[3], slice(0, 128), lambda k: wq[:, k, u * 128:(u + 1) * 128], rwq, q)
                self.headnorm(3, 128, self.cbf("b64"), self.cf("epsc", 1, 2), self.der("nsa_gq"), self.QTt[u][:, cs], self.RQT[u][q], cs, rope=(0, "sw_nsa", [0, 64]))
        self.mark('nsa_cmp')
        wg1, rwg1 = self.load_w(self.wg_rep[l].rearrange("(k p) n -> p k n", p=128)[:, :, 256:768], 512)
        pI, RpI = self.ps[5], self.Rps[5]
        gi = 0
        for u in range(2):
            for q in range(NQ):
                cs = slice(q * QT, (q + 1) * QT)
                gt, Rgt = GT[gi], RGT[gi]
                gi ^= 1
                self.pend_flush()
                self.gate_tile(wg0, rwg0, u, q, gt, Rgt)
                for hh in range(2):
                    rb = slice(64 * hh, 64 * hh + 64)
                    ob = 6 + hh
                    rins = [self.RQT[u][q], RKCMP, RVCMP]

                    def imp_mm(pi, hh=hh):
                        for t in range(4):
                            self.mm(pI[:, (hh * 4 + t) * 64:(hh * 4 + t + 1) * 64], self.PT[pi][:, t * 128:(t + 1) * 128], self.cbf("ovl"), True, True,
                                    [self.RPT[pi], self.Rcb], RpI)

                    def fin_cmp(ob=ob, hh=hh, u=u, cs=cs, q=q, gt=gt, Rgt=Rgt):
                        i, nr = self.finalize(ob, hh, None, None, eps=1e-30)
                        kb.op("pool", lambda e: e.tensor_tensor(out=self.rec[i][nr, :], in0=self.rec[i][nr, :], in1=gt[nr, :], op=ALU.mult), [self.Rrec[i], Rgt], [self.Rrec[i]])
                        kb.op("dve", lambda e: e.tensor_tensor(out=self.oT[0][u][nr, cs], in0=self.ps[ob][nr, :], in1=self.rec[i][nr, :], op=ALU.mult),
                              [self.Rps[ob], self.Rrec[i]], [self.RoT[0][u][q]])

                    self.attn_map([(0, 0, QT, ("vis", 0, q * QT - 31))],
                                  lambda c0, c1, rb=rb, u=u, q=q: self.QTt[u][rb, q * QT + c0:q * QT + c1],
                                  lambda kt, rb=rb: KCMP[rb, :],
                                  lambda kt, hh=hh: VCMP[:, 64 * hh:64 * hh + 128],
                                  0.125, rins, ob, fin=fin_cmp, after_p=imp_mm)
                for hh in range(2):
                    for t in range(4):
                        tt = q * 4 + t
                        base = (hh * 4 + t) * 64
                        rcol = self.rstd[0][:, 0:1]
                        kb.op("dve", lambda e: e.tensor_scalar(out=rcol, in0=pI[:, base + 32:base + 33], scalar1=1e-30, scalar2=None, op0=ALU.add), [RpI], [self.Rrstd[0]])
                        kb.op("dve", lambda e: e.reciprocal(out=rcol, in_=rcol), [self.Rrstd[0]], [self.Rrstd[0]])
                        kb.op("dve", lambda e: e.scalar_tensor_tensor(out=IMP[:, tt * 32:(tt + 1) * 32], in0=pI[:, base:base + 32], scalar=rcol,
                                                                      in1=IMP[:, tt * 32:(tt + 1) * 32], op0=ALU.mult, op1=ALU.add),
                              [RpI, self.Rrstd[0], RIMP], [RIMP])
        self.pend_flush()
        self.dump("nsaIMP", IMP[:], [RIMP], [128, 512])
        self.mark('nsa_topk')
        SC = self.rec[0]
        RSC = self.Rrec[0]
        kb.op("dve", lambda e: e.tensor_tensor(out=SC[:], in0=IMP[:], in1=self.cbf("tabA"), op=ALU.mult), [RIMP, self.Rcb], [RSC])
        kb.op("dve", lambda e: e.tensor_tensor(out=SC[:], in0=SC[:], in1=self.cbf("tabB"), op=ALU.add), [RSC, self.Rcb], [RSC])
        m8 = self.rstd[1]
        Rm8 = self.Rrstd[1]
        sc2 = self.rec[1]
        Rsc2 = self.Rrec[1]
        selm = self.sqb[0]
        Rselm = self.Rsqb[0]
        pT, RpT = self.ps[5], self.Rps[5]
        for tt in range(16):
            sl = slice(tt * 32, (tt + 1) * 32)
            kb.op("dve", lambda e: e.max(out=m8[:, 0:8], in_=SC[:, sl]), [RSC], [Rm8])
            kb.op("dve", lambda e: e.match_replace(out=sc2[:, 0:32], in_to_replace=m8[:, 0:8], in_values=SC[:, sl], imm_value=-2.0), [RSC, Rm8], [Rsc2])
            kb.op("dve", lambda e: e.max(out=m8[:, 8:16], in_=sc2[:, 0:32]), [Rsc2], [Rm8])
            kb.op("dve", lambda e: e.tensor_scalar(out=selm[:, sl], in0=SC[:, sl], scalar1=m8[:, 15:16], scalar2=-1.0, op0=ALU.is_ge, op1=ALU.add),
                  [RSC, Rm8], [Rselm])
            self.mm(pT[0:32, (tt % 4) * 128:(tt % 4 + 1) * 128], selm[:, sl], self.cbf("ident"), True, True, [Rselm, self.Rcb], RpT)
            if tt % 4 == 3:
                qq = tt // 4
                kb.op("act", lambda e: e.activation(out=SELM[0:32, qq * QT:(qq + 1) * QT], in_=pT[0:32, :], func=AF.Copy), [RpT], [RSELM[qq]])
        self.dump("nsaSELM", SELM[:], RSELM, [32, S])
        self.mark('nsa_slcwin')
        for u in range(2):
            for q in range(NQ):
                cs = slice(q * QT, (q + 1) * QT)
                self.gate_tile(wg1, rwg1, u, q, GT[0], RGT[0])
                self.gate_tile(wg1, rwg1, 2 + u, q, GT[1], RGT[1])
                for hh in range(2):
                    rb = slice(64 * hh, 64 * hh + 64)
                    rins = [self.RQT[u][q], RVSW, RSELM[q], self.Rcb] + RKS
                    shared = {}

                    def fin_s(hh=hh, shared=shared):
                        i_s, nr = self.finalize(6, hh, None, None)
                        kb.op("pool", lambda e: e.tensor_tensor(out=self.rec[i_s][nr, :], in0=self.rec[i_s][nr, :], in1=GT[0][nr, :], op=ALU.mult),
                              [self.Rrec[i_s], RGT[0]], [self.Rrec[i_s]])
                        kb.op("dve", lambda e: e.tensor_tensor(out=self.rec[i_s][nr, :], in0=self.ps[6][nr, :], in1=self.rec[i_s][nr, :], op=ALU.mult),
                              [self.Rps[6], self.Rrec[i_s]], [self.Rrec[i_s]])
                        shared["i_s"] = i_s

                    def fin_w(hh=hh, shared=shared, u=u, cs=cs, q=q):
                        i_s = shared["i_s"]
                        i_w, nr = self.finalize(7, hh, None, None)
                        kb.op("pool", lambda e: e.tensor_tensor(out=self.rec[i_w][nr, :], in0=self.rec[i_w][nr, :], in1=GT[1][nr, :], op=ALU.mult),
                              [self.Rrec[i_w], RGT[1]], [self.Rrec[i_w]])
                        kb.op("dve", lambda e: e.tensor_tensor(out=self.rec[i_w][nr, :], in0=self.ps[7][nr, :], in1=self.rec[i_w][nr, :], op=ALU.mult),
                              [self.Rps[7], self.Rrec[i_w]], [self.Rrec[i_w]])
                        kb.op("pool", lambda e: e.tensor_tensor(out=self.rec[i_w][nr, :], in0=self.rec[i_w][nr, :], in1=self.rec[i_s][nr, :], op=ALU.add),
                              [self.Rrec[i_w], self.Rrec[i_s]], [self.Rrec[i_w]])
                        kb.op("pool", lambda e: e.tensor_tensor(out=self.oT[0][u][nr, cs], in0=self.oT[0][u][nr, cs], in1=self.rec[i_w][nr, :], op=ALU.add),
                              [self.Rrec[i_w], self.RoT[0][u][q]], [self.RoT[0][u][q]])

                    self.attn_map(self.causal_tiles(q),
                                  lambda c0, c1, rb=rb, u=u, q=q: self.QTt[u][rb, q * QT + c0:q * QT + c1],
                                  lambda kt, rb=rb: KS2[rb, kt * 128:(kt + 1) * 128],
                                  lambda kt, hh=hh: VSW[:, kt, 64 * hh:64 * hh + 128],
                                  0.125, rins, 6,
                                  extra=lambda kt, c0, c1, q=q: (self.cbf("esel", slice(0, 32), kt * 128, (kt + 1) * 128), SELM[0:32, q * QT + c0:q * QT + c1]),
                                  fin=fin_s)
                    tiles = [(4 * q + j, 128 * j, QT, ("causal", 0, 0)) for j in range(4)]
                    if q > 0:
                        tiles += [(4 * q - 4 + j, 0, 128 * (j + 1), ("lower", 128 * j, 0)) for j in range(4)]
                    rins = [self.RQT[u][q], RVSW] + RKW
                    self.attn_map(tiles,
                                  lambda c0, c1, rb=rb, u=u, q=q: self.QTt[u][rb, q * QT + c0:q * QT + c1],
                                  lambda kt, rb=rb: KW2[rb, kt * 128:(kt + 1) * 128],
                                  lambda kt, hh=hh: VSW[:, kt, 192 + 64 * hh:192 + 64 * hh + 128],
                                  0.125, rins, 7, fin=fin_w)
                self.pend_flush()

    def mixer_order(self):
        sel = getattr(self, 'mixers', None) or [0, 2, 1, 3]
        fns = {0: self.nsa, 1: self.fox, 2: self.mla, 3: self.diff}
        return [(m, fns[m]) for m in sel]

    def g_ada(self, l):
        kb = self.kb
        par = l % 2
        t_mod, Rmod, Rder = self.t_mods[par], self.Rmods[par], self.Rders[par]
        pb, Rpb = self.ps[7], self.Rps[7]
        av = self.ada_w[l].rearrange("(k p) n -> p k n", p=128)
        avk = self.ada_w[l].rearrange("(k p) n -> k p n", p=128)
        n = 0
        for k in range(8):
            for cb_ in range(12):
                wb, rw = self.adab[n % 2], self.Radab[n % 2]
                kb.dma_in("pool", rw, lambda e: e.dma_start(out=wb[:], in_=avk[k][:, cb_ * 512:(cb_ + 1) * 512]))
                for jj in range(4):
                    j = cb_ * 4 + jj
                    kb.op("pe", lambda e: e.matmul(pb[:, j:j + 1], lhsT=wb[:, jj * 128:(jj + 1) * 128], rhs=self.t_scb[:, k:k + 1],
                                                   start=(k == 0 and j == 0), stop=(k == 7), skip_group_check=True), [rw, self.Rscb], [Rpb])
                n += 1
                yield
        kb.op("dve", lambda e: e.tensor_tensor(out=t_mod[:], in0=pb[:, 0:48], in1=self.sm(l, "ada_b"), op=ALU.add),
              [Rpb, self.Rsm], [Rmod])
        d = lambda *a, **kw: self.der(*a, par=par, **kw)

        def ts(dst, src, s1, s2=None, op0=ALU.mult, op1=None):
            if s2 is None:
                kb.op("dve", lambda e: e.tensor_scalar(out=dst, in0=src, scalar1=s1, scalar2=None, op0=op0),
                      [Rmod, self.Rsm, Rder, self.Rcf], [Rder])
            else:
                kb.op("dve", lambda e: e.tensor_scalar(out=dst, in0=src, scalar1=s1, scalar2=s2, op0=op0, op1=op1),
                      [Rmod, self.Rsm, Rder, self.Rcf], [Rder])

        ts(d("a1"), t_mod[:, 8:16], 1.0, 32.0, ALU.add, ALU.mult)
        ts(d("a2"), t_mod[:, 32:40], 1.0, 32.0, ALU.add, ALU.mult)
        for nm, sc in (("nsa_gq", 8.0), ("nsa_gkc", 8.0), ("nsa_gks", 8.0), ("nsa_gkw", 8.0), ("fox_gq", 8.0), ("fox_gk", 8.0),
                       ("mla_cqg", 16.0), ("mla_ckvg", math.sqrt(128.0)), ("mla_gq", math.sqrt(96.0)), ("mla_gk", math.sqrt(96.0)),
                       ("dif_gq", math.sqrt(32.0)), ("dif_gk", math.sqrt(32.0)), ("dif_og", self.cf("lamc", 2 * l, 2 * l + 1))):
            ts(d(nm), self.sm(l, nm), sc)
        ts(d("negfb"), self.sm(l, "fox_fb"), -1.0)
        yield
        if not hasattr(self, "t_lt"):
            self.t_lt = kb.sb("t_lt", [128, 64], F32)
            self.Rlt = Res("lt")
        lt = self.t_lt
        kb.op("dve", lambda e: e.tensor_tensor(out=lt[:, 0:32], in0=self.sm(l, "dif_lam", 0, 32), in1=self.sm(l, "dif_lam", 32, 64), op=ALU.mult),
              [self.Rsm], [self.Rlt])
        kb.op("dve", lambda e: e.tensor_tensor(out=lt[:, 32:64], in0=self.sm(l, "dif_lam", 64, 96), in1=self.sm(l, "dif_lam", 96, 128), op=ALU.mult),
              [self.Rsm], [self.Rlt])
        kb.op("dve", lambda e: e.tensor_reduce(out=d("t0", 0, 2), in_=lt[:].rearrange("p (a b) -> p a b", a=2), axis=mybir.AxisListType.X, op=ALU.add),
              [self.Rlt], [Rder])
        kb.op("act", lambda e: e.activation(out=d("t1", 0, 2), in_=d("t0", 0, 2), func=AF.Exp), [Rder], [Rder])
        kb.op("dve", lambda e: e.tensor_tensor(out=d("t0", 2, 3), in0=d("t1", 1, 2), in1=d("t1", 0, 1), op=ALU.subtract), [Rder], [Rder])
        kb.op("dve", lambda e: e.tensor_scalar(out=d("neglam"), in0=d("t0", 2, 3), scalar1=self.cf("lamc", 2 * l + 1, 2 * l + 2), scalar2=None, op0=ALU.add),
              [Rder, self.Rcf], [Rder])
        yield

    def ada_step(self, n=1):
        for _ in range(n):
            if self.ada_gen is not None:
                try:
                    next(self.ada_gen)
                except StopIteration:
                    self.ada_gen = None

    def norm_mod(self, l, which):
        self.mark('norm')
        kb = self.kb
        acol = "a1" if which == 0 else "a2"
        shc = 0 if which == 0 else 24
        with contextlib.ExitStack() as ns:
            sq = [kb.sb(f"nsq{i}", [128, QT], BF16, ns) for i in range(2)]
            Rsq = [Res(f"nsq{i}") for i in range(2)]
            rstd = kb.sb("nrstd", [128, QT], F32, ns)
            Rrstd = Res("nrstd")
            tmp = [kb.sb(f"ntmp{i}", [128, QT], F32, ns) for i in range(2)]
            Rtmp = [Res(f"ntmp{i}") for i in range(2)]
            for q in range(NQ):
                cs = slice(q * QT, (q + 1) * QT)
                pb, Rpb = self.ps[q % 2], self.Rps[q % 2]
                for k in range(8):
                    eng = "pool" if k % 2 == 0 else "dve"
                    kb.op(eng, lambda e: e.tensor_tensor(out=sq[k % 2][:], in0=self.xT[k][:, cs], in1=self.xT[k][:, cs], op=ALU.mult),
                          [self.Rx[k][q]], [Rsq[k % 2]])
                    self.mm(pb[:], self.cbf("ones"), sq[k % 2][:], k == 0, k == 7, [Rsq[k % 2], self.Rcb], Rpb)
                kb.op("act", lambda e: e.activation(out=rstd[:], in_=pb[:], func=AF.Ln, bias=self.cf("epsc", 0, 1), scale=1.0), [Rpb, self.Rcf], [Rrstd])
                kb.op("act", lambda e: e.activation(out=rstd[:], in_=rstd[:], func=AF.Exp, scale=-0.5), [Rrstd], [Rrstd])
                for k in range(8):
                    kb.op("dve", lambda e: e.tensor_tensor(out=tmp[k % 2][:], in0=self.xT[k][:, cs], in1=rstd[:], op=ALU.mult),
                          [self.Rx[k][q], Rrstd], [Rtmp[k % 2]])
                    kb.op("act", lambda e: e.activation(out=self.hT[k][:, cs], in_=tmp[k % 2][:], func=AF.Identity,
                                                        scale=self.der(acol, k, k + 1), bias=self.t_mod[:, shc + k:shc + k + 1]),
                          [Rtmp[k % 2], self.Rder, self.Rmod], [self.Rh[k][q]])
            kb.barrier()
        self.dump(f"h{which}_0", self.hT[0][:], self.Rh[0], [128, S])
        self.dump(f"h{which}_7", self.hT[7][:], self.Rh[7], [128, S])

    def mixer_scratch(self, ms, nq=1, nk=1, va=True):
        kb = self.kb
        self.sqb = [kb.sb(f"sqb{i}", [128, QT], BF16, ms) for i in range(2)]
        self.Rsqb = [Res(f"sqb{i}") for i in range(2)]
        self.rstd = [kb.sb(f"rstd{i}", [128, QT], F32, ms) for i in range(2)]
        self.Rrstd = [Res(f"rstd{i}") for i in range(2)]
        self.rt1 = [kb.sb(f"rt1_{i}", [128, QT], BF16, ms) for i in range(2)]
        self.Rrt1 = [Res(f"rt1_{i}") for i in range(2)]
        self.rt2 = [kb.sb(f"rt2_{i}", [128, QT], BF16, ms) for i in range(2)]
        self.Rrt2 = [Res(f"rt2_{i}") for i in range(2)]
        self.PT = [kb.sb(f"PT{i}", [128, QT], BF16, ms) for i in range(6)]
        self.RPT = [Res(f"PT{i}") for i in range(6)]
        self.rec = [kb.sb(f"rec{i}", [128, QT], F32, ms) for i in range(2)]
        self.Rrec = [Res(f"rec{i}") for i in range(2)]
        self.QTt = [kb.sb(f"QTt{i}", [128, S], BF16, ms) for i in range(nq)]
        self.RQT = [[Res(f"QT{i}_{q}") for q in range(NQ)] for i in range(nq)]
        self.KTt = [kb.sb(f"KTt{i}", [128, S], BF16, ms) for i in range(nk)]
        self.RKT = [[Res(f"KT{i}_{q}") for q in range(NQ)] for i in range(nk)]
        self.pend = []
        self.bg = None
        self.hn_i = 0
        self.pt_i = 0
        self.sb_i = 0
        self.rec_i = 0
        if va:
            self.VA = kb.sb("VA", [128, NKT, 256], BF16, ms)
            self.RVA = [Res(f"VA{g}") for g in range(4)]
            kb.op("pool", lambda e: e.memset(self.VA[:, :, 64:192], 1.0), [], self.RVA)

    def headnorm(self, *a, **kw):
        for _ in self.g_headnorm(*a, **kw):
            pass

    def g_headnorm(self, src_bank, nrows, blk, neps, gcol, dst, Rdst, cs, rope=None):
        kb = self.kb
        i = self.hn_i
        self.hn_i ^= 1
        rs = slice(0, nrows)
        src, Rsrc = self.ps[src_bank][rs, :], self.Rps[src_bank]
        pstat, Rpstat = self.ps[5], self.Rps[5]
        kb.op("act", lambda e: e.activation(out=self.sqb[i][rs, :], in_=src, func=AF.Square), [Rsrc], [self.Rsqb[i]])
        self.mm(pstat[rs, :], blk, self.sqb[i][rs, :], True, True, [self.Rsqb[i], self.Rcb], Rpstat)
        yield
        kb.op("act", lambda e: e.activation(out=self.rstd[i][rs, :], in_=pstat[rs, :], func=AF.Ln, bias=neps, scale=1.0), [Rpstat, self.Rcf], [self.Rrstd[i]])
        kb.op("act", lambda e: e.activation(out=self.rstd[i][rs, :], in_=self.rstd[i][rs, :], func=AF.Exp, scale=-0.5), [self.Rrstd[i]], [self.Rrstd[i]])
        yield
        kb.op("dve", lambda e: e.scalar_tensor_tensor(out=dst, in0=src, scalar=gcol, in1=self.rstd[i][rs, :], op0=ALU.mult, op1=ALU.mult),
              [Rsrc, self.Rrstd[i], self.Rder], [Rdst])
        if rope is None:
            yield
            return
        ci, swname, wins = rope
        pA, RpA = pstat, Rpstat
        pB, RpB = self.ps[src_bank], self.Rps[src_bank]
        self.mm(pA[rs, :], self.cbf("ident", rs, 0, nrows), dst, True, True, [Rdst, self.Rcb], RpA)
        self.mm(pB[rs, :], self.cbf(swname, rs, 0, nrows), dst, True, True, [Rdst, self.Rcb], RpB)
        yield
        tab = slice(32 * ci, 32 * ci + 32)
        for w0 in wins:
            ws = slice(w0, w0 + 32)
            kb.op("dve", lambda e: e.tensor_tensor(out=self.rt1[i][ws, :], in0=pA[ws, :], in1=self.ropeC[tab, cs], op=ALU.mult),
                  [RpA, self.Rrope], [self.Rrt1[i]])
            kb.op("dve", lambda e: e.tensor_tensor(out=self.rt2[i][ws, :], in0=pB[ws, :], in1=self.ropeS[tab, cs], op=ALU.mult),
                  [RpB, self.Rrope], [self.Rrt2[i]])
            kb.op("pool", lambda e: e.tensor_tensor(out=dst[ws, :], in0=self.rt1[i][ws, :], in1=self.rt2[i][ws, :], op=ALU.add),
                  [self.Rrt1[i], self.Rrt2[i]], [Rdst])
        yield

    def bg_step(self):
        if self.bg is not None:
            try:
                next(self.bg)
            except StopIteration:
                self.bg = None

    def bg_drain(self):
        while self.bg is not None:
            self.bg_step()

    def g_vgroup(self, g, w_ap_k, rw, ncol, evac, nk=8, lhs_fn=None, rl=None):
        pb, Rpb = self.ps[3 + g % 2], self.Rps[3 + g % 2]
        for t in range(4):
            kt = g * 4 + t
            for k in range(nk):
                lhs = self.hT[k][:, kt * 128:(kt + 1) * 128] if lhs_fn is None else lhs_fn(k, kt)
                rr = self.Rh[k][g] if rl is None else rl(k, g)
                self.mm(pb[:, t * ncol:(t + 1) * ncol], lhs, w_ap_k(k), k == 0, k == nk - 1, [rw, rr], Rpb)
            if t % 2 == 1:
                yield
        evac(g, pb, Rpb)
        yield

    def attn_map(self, tiles, q_ap, k_ap, va_ap, scale, rins, o_bank, extra=None, bias=None, fin=None, after_p=None):
        kb = self.kb
        nt = len(tiles)
        for ti, (kt, c0, c1, mask) in enumerate(tiles):
            n = c1 - c0
            si = self.sb_i
            self.sb_i = (self.sb_i + 1) % 5
            pi = self.pt_i
            self.pt_i = (self.pt_i + 1) % 6
            pS, RpS = self.ps[si], self.Rps[si]
            self.mm(pS[:, 0:n], k_ap(kt), q_ap(c0, c1), True, extra is None, rins, RpS)
            if extra is not None:
                l2, r2 = extra(kt, c0, c1)
                self.mm(pS[:, 0:n], l2, r2, False, True, rins, RpS)
            b = bias(kt) if bias is not None else None
            if b is None:
                kb.op("act", lambda e: e.activation(out=self.PT[pi][:, 0:n], in_=pS[:, 0:n], func=AF.Exp, scale=scale), [RpS], [self.RPT[pi]])
            else:
                kb.op("act", lambda e: e.activation(out=self.PT[pi][:, 0:n], in_=pS[:, 0:n], func=AF.Exp, scale=scale, bias=b),
                      [RpS] + list(rins), [self.RPT[pi]])
            if mask is not None:
                kind, m0, base = mask
                if kind == "causal":
                    kb.op("pool", lambda e: e.affine_select(out=self.PT[pi][:, m0:m0 + 128], in_=self.PT[pi][:, m0:m0 + 128], pattern=[[1, 128]],
                                                            compare_op=ALU.is_ge, fill=0.0, base=0, channel_multiplier=-1),
                          [self.RPT[pi]], [self.RPT[pi]])
                elif kind == "lower":
                    kb.op("pool", lambda e: e.affine_select(out=self.PT[pi][:, m0:m0 + 128], in_=self.PT[pi][:, m0:m0 + 128], pattern=[[-1, 128]],
                                                            compare_op=ALU.is_ge, fill=0.0, base=-1, channel_multiplier=1),
                          [self.RPT[pi]], [self.RPT[pi]])
                elif kind == "vis":
                    kb.op("pool", lambda e: e.affine_select(out=self.PT[pi][:, 0:n], in_=self.PT[pi][:, 0:n], pattern=[[1, n]],
                                                            compare_op=ALU.is_ge, fill=0.0, base=base, channel_multiplier=-16),
                          [self.RPT[pi]], [self.RPT[pi]])
            if after_p is not None:
                after_p(pi)

            def pv(kt=kt, c0=c0, c1=c1, n=n, pi=pi, first=(ti == 0), last=(ti == nt - 1)):
                self.mm(self.ps[o_bank][:, c0:c1], va_ap(kt), self.PT[pi][:, 0:n], first, last,
                        [self.RPT[pi]] + list(rins), self.Rps[o_bank])

            self.pend.append((pv, fin if ti == nt - 1 else None))
            while len(self.pend) > self.LA:
                self.pend_pop()

    LA = 4

    def pend_pop(self):
        pv, fin = self.pend.pop(0)
        pv()
        if fin is not None:
            fin()

    def pend_flush(self):
        while self.pend:
            self.pend_pop()

    @staticmethod
    def causal_tiles(q):
        tiles = [(kt, 0, QT, None) for kt in range(4 * q)]
        for j in range(4):
            tiles.append((4 * q + j, 128 * j, QT, ("causal", 0, 0)))
        return tiles

    def finalize(self, o_bank, parity, dst, Rdst, eps=None):
        kb = self.kb
        i = self.rec_i
        self.rec_i ^= 1
        pO, RpO = self.ps[o_bank], self.Rps[o_bank]
        nr = slice(0, 64) if parity == 0 else slice(64, 128)
        dr = slice(64, 128) if parity == 0 else slice(0, 64)
        if eps is None:
            kb.op("dve", lambda e: e.reciprocal(out=self.rec[i][nr, :], in_=pO[dr, :]), [RpO], [self.Rrec[i]])
        else:
            kb.op("dve", lambda e: e.tensor_scalar(out=self.rec[i][nr, :], in0=pO[dr, :], scalar1=eps, scalar2=None, op0=ALU.add), [RpO], [self.Rrec[i]])
            kb.op("dve", lambda e: e.reciprocal(out=self.rec[i][nr, :], in_=self.rec[i][nr, :]), [self.Rrec[i]], [self.Rrec[i]])
        if dst is not None:
            kb.op("dve", lambda e: e.tensor_tensor(out=dst, in0=pO[nr, :], in1=self.rec[i][nr, :], op=ALU.mult), [RpO, self.Rrec[i]], [Rdst])
        return i, nr

    def proj_T(self, pb, Rpb, rows, w_ap_k, rw, q, nk=8, rhs_fn=None, rrhs=None):
        cs = slice(q * QT, (q + 1) * QT)
        for k in range(nk):
            rhs = self.hT[k][:, cs] if rhs_fn is None else rhs_fn(k)
            rr = self.Rh[k][q] if rrhs is None else rrhs(k)
            self.mm(pb[rows, :], w_ap_k(k), rhs, k == 0, k == nk - 1, [rw, rr], Rpb)

    def v_proj(self, w_ap_k, rw, ncol, evac, nk=8, lhs_fn=None, rl=None):
        for g in range(4):
            pb, Rpb = self.ps[3 + g % 2], self.Rps[3 + g % 2]
            for t in range(4):
                kt = g * 4 + t
                for k in range(nk):
                    lhs = self.hT[k][:, kt * 128:(kt + 1) * 128] if lhs_fn is None else lhs_fn(k, kt)
                    rr = self.Rh[k][g] if rl is None else rl(k, g)
                    self.mm(pb[:, t * ncol:(t + 1) * ncol], lhs, w_ap_k(k), k == 0, k == nk - 1, [rw, rr], Rpb)
            evac(g, pb, Rpb)

    def fox(self, l, fs):
        kb = self.kb
        self.mark('fox_prelude')
        fs0 = fs
        DQ = kb.sb("foxDQ", [128, S], BF16, fs)
        RDQ = [Res(f"foxDQ{q}") for q in range(NQ)]
        Dk = kb.sb("foxDk", [128, NKT, 4], F32, fs)
        RDk = Res("foxDk")
        with contextlib.ExitStack() as fs:
            Dt = kb.sb("foxD", [128, S], F32, fs)
            RD = [Res(f"foxD{q}") for q in range(NQ)]
            ft = [kb.sb(f"foxft{i}", [128, QT], F32, fs) for i in range(2)]
            Rft = [Res(f"foxft{i}") for i in range(2)]
            onesf = kb.sb("foxones", [128, QT], F32, fs)
            Rones = Res("foxones")
            kb.op("pool", lambda e: e.memset(onesf[:], 1.0), [], [Rones])
            wf, rwf = self.load_w(self.wf_pad[l].rearrange("(k p) n -> p k n", p=128), 128)
            for q in range(NQ):
                cs = slice(q * QT, (q + 1) * QT)
                pb, Rpb = self.ps[3 + q % 2], self.Rps[3 + q % 2]
                self.proj_T(pb, Rpb, slice(0, 128), lambda k: wf[:, k, 0:128], rwf, q)
                kb.op("act", lambda e: e.activation(out=ft[0][:], in_=pb[:], func=AF.Exp, scale=-1.0, bias=self.der("negfb")),
                      [Rpb, self.Rder], [Rft[0]])
                kb.op("act", lambda e: e.activation(out=ft[1][:], in_=ft[0][:], func=AF.Ln, bias=self.cf("epsc", 6, 7), scale=1.0), [Rft[0], self.Rcf], [Rft[1]])
                if q == 0:
                    kb.op("dve", lambda e: e.tensor_tensor_scan(out=Dt[:, cs], data0=onesf[:], data1=ft[1][:], initial=0.0, op0=ALU.mult, op1=ALU.add),
                          [Rones, Rft[1]], [RD[q]])
                else:
                    kb.op("dve", lambda e: e.tensor_tensor_scan(out=Dt[:, cs], data0=onesf[:], data1=ft[1][:], initial=Dt[:, q * QT - 1:q * QT],
                                                                op0=ALU.mult, op1=ALU.add), [Rones, Rft[1], RD[q - 1]], [RD[q]])
                kb.op("pool", lambda e: e.tensor_scalar(out=DQ[:, cs], in0=Dt[:, cs], scalar1=-8.0, scalar2=None, op0=ALU.mult), [RD[q]], [RDQ[q]])
                pt, Rpt = self.ps[5], self.Rps[5]
                for t in range(4):
                    kb.op("pe", lambda e: e.transpose(out=pt[:, t * 128:(t + 1) * 128], in_=Dt[:, q * QT + t * 128:q * QT + (t + 1) * 128],
                                                      identity=self.cf("ident")), [RD[q], self.Rcf], [Rpt])
                kb.op("dve", lambda e: e.tensor_copy(out=Dk[:, q * 4:(q + 1) * 4, :], in_=pt[:].rearrange("p (t h r) -> p t h r", t=4, h=4)[:, :, :, 0]),
                      [Rpt], [RDk])
            self.dump("foxD", Dt[:], RD, [128, S])
            kb.barrier()
            if self.dbg:
                kb.e["act"].wait_ge(self.osem[0], self.osem[1])
                kb.barrier()
        if True:
            self.mixer_scratch(fs0)
            self.mark('fox_units')
            fox_w = []
            for u in range(2):
                wqk_, rwqk_ = self.next_wb()
                kb.dma_in("pool", rwqk_, lambda e: [e.dma_start(out=wqk_[:, :, 0:128], in_=self.win_cols(l, C_FOX_Q + u * 128, 128)),
                                                    e.dma_start(out=wqk_[:, :, 128:256], in_=self.win_cols(l, C_FOX_K + u * 128, 128)),
                                                    e.dma_start(out=wqk_[:, :, 256:384], in_=self.win_cols(l, C_FOX_V + u * 128, 128))])
                fox_w.append((wqk_, rwqk_))
            for u in range(2):
                wqk, rwqk = fox_w[u]

                def evac(g, pb, Rpb):
                    src = pb[:].rearrange("p (t a c) -> p t a c", t=4, a=2)
                    dstv = self.VA[:, g * 4:(g + 1) * 4, :].rearrange("p t (a c) -> p t a c", a=4)[:, :, 0::3, :]
                    kb.op("dve", lambda e: e.tensor_copy(out=dstv, in_=src), [Rpb], [self.RVA[g]])

                def pre(q):
                    cs = slice(q * QT, (q + 1) * QT)
                    self.proj_T(self.ps[3], self.Rps[3], slice(0, 128), lambda k: wqk[:, k, 0:128], rwqk, q)
                    yield
                    yield from self.g_headnorm(3, 128, self.cbf("b64"), self.cf("epsc", 1, 2), self.der("fox_gq"), self.QTt[0][:, cs], self.RQT[0][q], cs)
                    self.proj_T(self.ps[4], self.Rps[4], slice(0, 128), lambda k: wqk[:, k, 128:256], rwqk, q)
                    yield
                    yield from self.g_headnorm(4, 128, self.cbf("b64"), self.cf("epsc", 1, 2), self.der("fox_gk"), self.KTt[0][:, cs], self.RKT[0][q], cs)
                    yield from self.g_vgroup(q, lambda k: wqk[:, k, 256:384], rwqk, 128, evac)

                self.bg = pre(0)
                self.bg_drain()
                for q in range(NQ):
                    cs = slice(q * QT, (q + 1) * QT)
                    if q + 1 < NQ:
                        self.bg = pre(q + 1)
                        self.bg_drain()
                    for hh in range(2):
                        h = 2 * u + hh
                        rb = slice(64 * hh, 64 * hh + 64)
                        ob = 6 + hh
                        rins = [self.RQT[0][q], RDk, RDQ[q], self.Rcb] + self.RKT[0][:q + 1] + self.RVA[:q + 1]
                        self.attn_map(
                            self.causal_tiles(q),
                            lambda c0, c1, rb=rb, q=q: self.QTt[0][rb, q * QT + c0:q * QT + c1],
                            lambda kt, rb=rb: self.KTt[0][rb, kt * 128:(kt + 1) * 128],
                            lambda kt, hh=hh: self.VA[:, kt, 128 * hh:128 * hh + 128],
                            0.125, rins, ob,
                            extra=lambda kt, c0, c1, h=h, q=q: (self.cbf("selrow", slice(0, 128), 128 * h, 128 * h + 128), DQ[:, q * QT + c0:q * QT + c1]),
                            bias=lambda kt, h=h: Dk[:, kt, h:h + 1],
                            fin=lambda ob=ob, hh=hh, u=u, rb=rb, cs=cs, q=q: self.finalize(ob, hh, self.oT[1][u][rb, cs], self.RoT[1][u][q]))
                    self.bg_drain()
                self.pend_flush()

    def merge(self, l):
        self.mark('merge')
        kb = self.kb
        with contextlib.ExitStack() as ms:
            MT = [kb.sb(f"MT{i}", [128, S], BF16, ms) for i in range(4)]
            RMT = [[Res(f"MT{i}_{q}") for q in range(NQ)] for i in range(4)]
            brw = [kb.sb(f"brw{i}", [128, 2, 4, 128], BF16, ms) for i in range(2)]
            Rbrw = [Res(f"brw{i}") for i in range(2)]
            wo = [kb.sb(f"wo{i}", [128, 4, 128], BF16, ms) for i in range(2)]
            Rwo = [Res(f"wo{i}") for i in range(2)]
            sig = [kb.sb(f"sig{i}", [128, QT], F32, ms) for i in range(2)]
            Rsig = [Res(f"sig{i}") for i in range(2)]
            tmp = [kb.sb(f"mtmp{i}", [128, QT], F32, ms) for i in range(2)]
            Rtmp = [Res(f"mtmp{i}") for i in range(2)]
            acc = [kb.sb(f"macc{i}", [128, QT], F32, ms) for i in range(2)]
            Racc = [Res(f"macc{i}") for i in range(2)]
            gwv = self.gate_w[l].rearrange("(k p) (m n) -> p k m n", p=128, m=4)
            brv = self.br_w[l].rearrange("m (k p) n -> p k m n", p=128)
            wov = self.w_out[l].rearrange("(k p) n -> p k n", p=128)
            n_it = 0
            loaded = {}

            def load_dc(dc):
                if dc in loaded or dc >= 8:
                    return
                wb_, rgw_ = self.next_wb()
                kb.dma_in("pool", rgw_, lambda e: [e.dma_start(out=wb_[:, :, m_ * 128:(m_ + 1) * 128], in_=gwv[:, :, m_, dc * 128:(dc + 1) * 128]) for m_ in range(4)])
                bi_ = dc % 2
                kb.dma_in("pool", Rbrw[bi_], lambda e: [e.dma_start(out=brw[bi_][:, :, m_, :], in_=brv[:, :, m_, dc * 128:(dc + 1) * 128]) for m_ in range(4)])
                loaded[dc] = (wb_, rgw_)

            wo_loaded = {}

            def load_wo(grp, dout):
                key = grp * 8 + dout
                if key in wo_loaded or dout >= 8:
                    return
                wi_ = key % 2
                kb.dma_in("pool", Rwo[wi_], lambda e: e.dma_start(out=wo[wi_][:], in_=wov[:, grp * 4:(grp + 1) * 4, dout * 128:(dout + 1) * 128]))
                wo_loaded[key] = wi_

            for grp in range(2):
                for dcl in range(4):
                    dc = grp * 4 + dcl
                    load_dc(dc)
                    if dcl < 3:
                        load_dc(dc + 1)
                    wb, rgw = loaded[dc]
                    bi = dc % 2
                    for q in range(NQ):
                        cs = slice(q * QT, (q + 1) * QT)
                        ai = n_it % 2
                        n_it += 1
                        for m in range(4):
                            pg, Rpg = self.ps[m % 2], self.Rps[m % 2]
                            py, Rpy = self.ps[2 + m % 2], self.Rps[2 + m % 2]
                            for k in range(8):
                                self.mm(pg[:], wb[:, k, m * 128:(m + 1) * 128], self.hT[k][:, cs], k == 0, k == 7, [rgw, self.Rh[k][q]], Rpg)
                            for k in range(2):
                                self.mm(py[:], brw[bi][:, k, m, :], self.oT[m][k][:, cs], k == 0, k == 1, [Rbrw[bi], self.RoT[m][k][q]], Rpy)
                            si = m % 2
                            kb.op("act", lambda e: e.activation(out=sig[si][:], in_=pg[:], func=AF.Sigmoid, bias=self.sm(l, "gate_b", m * 8 + dc, m * 8 + dc + 1), scale=1.0),
                                  [Rpg, self.Rsm], [Rsig[si]])
                            if m == 0:
                                kb.op("dve", lambda e: e.tensor_tensor(out=acc[ai][:], in0=py[:], in1=sig[si][:], op=ALU.mult), [Rpy, Rsig[si]], [Racc[ai]])
                            else:
                                kb.op("dve", lambda e: e.tensor_tensor(out=tmp[si][:], in0=py[:], in1=sig[si][:], op=ALU.mult), [Rpy, Rsig[si]], [Rtmp[si]])
                                if m < 3:
                                    kb.op("dve", lambda e: e.tensor_tensor(out=acc[ai][:], in0=acc[ai][:], in1=tmp[si][:], op=ALU.add), [Racc[ai], Rtmp[si]], [Racc[ai]])
                                else:
                                    kb.op("dve", lambda e: e.tensor_tensor(out=MT[dcl][:, cs], in0=acc[ai][:], in1=tmp[si][:], op=ALU.add),
                                          [Racc[ai], Rtmp[si]], [RMT[dcl][q]])
                if l == 0 and grp == 0:
                    self.dump("merged0", MT[0][:], RMT[0], [128, S])
                for dout in range(8):
                    load_wo(grp, dout)
                    load_wo(grp, dout + 1)
                    if dout == 7 and grp == 0:
                        load_dc(4)
                    wi = wo_loaded[grp * 8 + dout]
                    for q in range(NQ):
                        cs = slice(q * QT, (q + 1) * QT)
                        pb, Rpb = self.ps[4 + (dout * NQ + q) % 2], self.Rps[4 + (dout * NQ + q) % 2]
                        for dcl in range(4):
                            self.mm(pb[:], wo[wi][:, dcl, :], MT[dcl][:, cs], dcl == 0, dcl == 3, [Rwo[wi], RMT[dcl][q]], Rpb)
                        kb.op("dve", lambda e: e.scalar_tensor_tensor(out=self.xT[dout][:, cs], in0=pb[:], scalar=self.t_mod[:, 16 + dout:17 + dout],
                                                                      in1=self.xT[dout][:, cs], op0=ALU.mult, op1=ALU.add),
                              [Rpb, self.Rmod, self.Rx[dout][q]], [self.Rx[dout][q]])
            kb.barrier()

    def ffn(self, l):
        kb = self.kb
        self.mark('ffn')
        NJ = NFF // 2
        with contextlib.ExitStack() as fs:
            AT = [kb.sb(f"AT{j}", [128, S], BF16, fs) for j in range(NJ)]
            RAT = [[Res(f"AT{j}_{q}") for q in range(NQ)] for j in range(NJ)]
            G = [kb.sb(f"G{i}", [128, QT + 2], F32, fs) for i in range(2)]
            RG = [Res(f"G{i}") for i in range(2)]
            GC = [kb.sb(f"GC{i}", [128, QT], F32, fs) for i in range(2)]
            RGC = [Res(f"GC{i}") for i in range(2)]
            wd = [kb.sb(f"wd{i}", [128, NJ, 128], BF16, fs) for i in range(3)]
            Rwd = [Res(f"wd{i}") for i in range(3)]
            wd_n = 0
            gn = 0
            upv = self.w_up[l].rearrange("(k p) n -> p k n", p=128)
            dnv = self.w_down[l].rearrange("(j p) n -> p j n", p=128)
            if l + 1 < self.n_layers:
                self.ada_gen = self.g_ada(l + 1)
            up_loaded = {}

            def load_up(j0):
                if j0 in up_loaded or j0 >= NFF:
                    return
                nj_ = 1 if (j0 % NJ) == NJ - 1 else 2
                wb_, rwu_ = self.next_wb()
                kb.dma_in("pool", rwu_, lambda e: [e.dma_start(out=wb_[:, :, 0:128 * nj_], in_=upv[:, :, j0 * 128:(j0 + nj_) * 128]),
                                                   e.dma_start(out=wb_[:, :, 256:256 + 128 * nj_], in_=upv[:, :, DFF + j0 * 128:DFF + (j0 + nj_) * 128])])
                up_loaded[j0] = (wb_, rwu_)

            wd_loaded = {}

            def load_wd(ps__, dout):
                key = ps__ * 8 + dout
                if key in wd_loaded or dout >= 8:
                    return
                wi_ = key % 3
                kb.dma_in("pool", Rwd[wi_], lambda e: e.dma_start(out=wd[wi_][:], in_=dnv[:, ps__ * NJ:(ps__ + 1) * NJ, dout * 128:(dout + 1) * 128]))
                wd_loaded[key] = wi_

            for ps_ in range(2):
                wb, rwu = None, None
                for jj in range(NJ):
                    j = ps_ * NJ + jj
                    self.ada_step(3)
                    if jj % 2 == 0:
                        load_up(j)
                        nxt = j + 2
                        if jj + 2 < NJ:
                            load_up(nxt)
                        elif ps_ == 0:
                            pass
                        wb, rwu = up_loaded[j]
                    if jj == NJ - 1:
                        load_wd(ps_, 0)
                    co = 128 * (jj % 2)
                    w0 = self.sm(l, "conv_w", 0 * NFF + j, 0 * NFF + j + 1)
                    w1 = self.sm(l, "conv_w", 1 * NFF + j, 1 * NFF + j + 1)
                    w2 = self.sm(l, "conv_w", 2 * NFF + j, 2 * NFF + j + 1)
                    cb_ = self.sm(l, "conv_b", j, j + 1)
                    for q in range(NQ):
                        gi = gn % 2
                        gn += 1
                        cs = slice(q * QT, (q + 1) * QT)
                        pg, Rpg = self.ps[gi], self.Rps[gi]
                        pv, Rpv = self.ps[2 + gi], self.Rps[2 + gi]
                        for k in range(8):
                            self.mm(pg[:], wb[:, k, co:co + 128], self.hT[k][:, cs], k == 0, k == 7, [rwu, self.Rh[k][q]], Rpg)
                        for k in range(8):
                            self.mm(pv[:], wb[:, k, 256 + co:256 + co + 128], self.hT[k][:, cs], k == 0, k == 7, [rwu, self.Rh[k][q]], Rpv)
                        if q == 0:
                            kb.op("dve", lambda e: e.memset(G[gi][:, 0:2], 0.0), [], [RG[gi]])
                        else:
                            kb.op("dve", lambda e: e.tensor_copy(out=G[gi][:, 0:2], in_=G[1 - gi][:, QT:QT + 2]), [RG[1 - gi]], [RG[gi]])
                        kb.op("act", lambda e: e.activation(out=G[gi][:, 2:2 + QT], in_=pg[:], func=AF.Copy), [Rpg], [RG[gi]])
                        kb.op("dve", lambda e: e.tensor_scalar(out=GC[gi][:], in0=G[gi][:, 2:2 + QT], scalar1=w2, scalar2=cb_, op0=ALU.mult, op1=ALU.add),
                              [RG[gi], self.Rsm], [RGC[gi]])
                        kb.op("dve", lambda e: e.scalar_tensor_tensor(out=GC[gi][:], in0=G[gi][:, 1:1 + QT], scalar=w1, in1=GC[gi][:], op0=ALU.mult, op1=ALU.add),
                              [RG[gi], self.Rsm, RGC[gi]], [RGC[gi]])
                        kb.op("dve", lambda e: e.scalar_tensor_tensor(out=GC[gi][:], in0=G[gi][:, 0:QT], scalar=w0, in1=GC[gi][:], op0=ALU.mult, op1=ALU.add),
                              [RG[gi], self.Rsm, RGC[gi]], [RGC[gi]])
                        kb.op("act", lambda e: e.activation(out=GC[gi][:], in_=GC[gi][:], func=AF.Silu), [RGC[gi]], [RGC[gi]])
                        kb.op("dve", lambda e: e.tensor_tensor(out=AT[jj][:, cs], in0=pv[:], in1=GC[gi][:], op=ALU.mult),
                              [Rpv, RGC[gi]], [RAT[jj][q]])
                for dout in range(8):
                    self.ada_step(3)
                    load_wd(ps_, dout)
                    load_wd(ps_, dout + 1)
                    if dout == 6 and ps_ == 0:
                        load_up(NJ)
                    wi = wd_loaded[ps_ * 8 + dout]
                    for q in range(NQ):
                        cs = slice(q * QT, (q + 1) * QT)
                        pb, Rpb = self.ps[4 + q % 2], self.Rps[4 + q % 2]
                        for jj in range(NJ):
                            self.mm(pb[:], wd[wi][:, jj, :], AT[jj][:, cs], jj == 0, jj == NJ - 1, [Rwd[wi], RAT[jj][q]], Rpb)
                        kb.op("dve", lambda e: e.scalar_tensor_tensor(out=self.xT[dout][:, cs], in0=pb[:], scalar=self.t_mod[:, 40 + dout:41 + dout],
                                                                      in1=self.xT[dout][:, cs], op0=ALU.mult, op1=ALU.add),
                              [Rpb, self.Rmod, self.Rx[dout][q]], [self.Rx[dout][q]])
            kb.barrier()

    def rstd_from(self, pstat_ap, out_ap, neps, Rin, Rout):
        kb = self.kb
        kb.op("act", lambda e: e.activation(out=out_ap, in_=pstat_ap, func=AF.Ln, bias=neps, scale=1.0), [Rin, self.Rcf], [Rout])
        kb.op("act", lambda e: e.activation(out=out_ap, in_=out_ap, func=AF.Exp, scale=-0.5), [Rout], [Rout])

    def mla(self, l, ms):
        kb = self.kb
        self.mark('mla_prelude')
        cqn = [kb.sb(f"cqn{i}", [128, S], BF16, ms) for i in range(2)]
        Rcqn = [[Res(f"cqn{i}_{q}") for q in range(NQ)] for i in range(2)]
        ckvn = kb.sb("ckvn", [128, S], BF16, ms)
        Rckvn = [Res(f"ckvn{q}") for q in range(NQ)]
        wuq = kb.sb("wuq", [128, 2, 384], BF16, ms)
        wukv = kb.sb("wukv", [128, 512], BF16, ms)
        Rwuq, Rwukv = Res("wuq"), Res("wukv")
        kb.dma_in("pool", Rwuq, lambda e: e.dma_start(out=wuq[:], in_=self.w_uq[l].rearrange("(k p) n -> p k n", p=128)))
        kb.dma_in("pool", Rwukv, lambda e: e.dma_start(out=wukv[:], in_=self.w_ukv[l]))
        self.mixer_scratch(ms)
        wc, rwc = self.load_w(self.win_cols(l, C_MLA_CQ, 416), 416)
        pstat, Rpstat = self.ps[5], self.Rps[5]
        for q in range(NQ):
            cs = slice(q * QT, (q + 1) * QT)
            for c in range(2):
                self.proj_T(self.ps[3 + c], self.Rps[3 + c], slice(0, 128), lambda k: wc[:, k, c * 128:(c + 1) * 128], rwc, q)
                kb.op("act", lambda e: e.activation(out=self.sqb[c][:], in_=self.ps[3 + c][:], func=AF.Square), [self.Rps[3 + c]], [self.Rsqb[c]])
                self.mm(pstat[:], self.cbf("ones"), self.sqb[c][:], c == 0, c == 1, [self.Rsqb[c], self.Rcb], Rpstat)
            self.rstd_from(pstat[:], self.rstd[0][:], self.cf("epsc", 4, 5), Rpstat, self.Rrstd[0])
            for c in range(2):
                kb.op("dve", lambda e: e.scalar_tensor_tensor(out=cqn[c][:, cs], in0=self.ps[3 + c][:], scalar=self.der("mla_cqg", c, c + 1),
                                                              in1=self.rstd[0][:], op0=ALU.mult, op1=ALU.mult),
                      [self.Rps[3 + c], self.Rrstd[0], self.Rder], [Rcqn[c][q]])
            self.proj_T(self.ps[3], self.Rps[3], slice(0, 128), lambda k: wc[:, k, 256:384], rwc, q)
            self.headnorm(3, 128, self.cbf("ones"), self.cf("epsc", 5, 6), self.der("mla_ckvg"), ckvn[:, cs], Rckvn[q], cs)
        self.mark('mla_heads')
        r96 = slice(0, 96)
        for h in range(4):
            hh = h % 2
            vcol = 0 if hh == 0 else 192

            def evac(g, pb, Rpb, vcol=vcol):
                kb.op("dve", lambda e: e.tensor_copy(out=self.VA[:, g * 4:(g + 1) * 4, vcol:vcol + 64], in_=pb[:, 0:256].rearrange("p (t c) -> p t c", t=4)),
                      [Rpb], [self.RVA[g]])

            def pre(q, h=h):
                cs = slice(q * QT, (q + 1) * QT)
                for k in range(2):
                    self.mm(self.ps[3][0:64, :], wuq[:, k, 96 * h + 32:96 * h + 96], cqn[k][:, cs], k == 0, k == 1, [Rwuq, Rcqn[k][q]], self.Rps[3])
                for k in range(2):
                    self.mm(self.ps[3][64:96, :], wuq[:, k, 96 * h:96 * h + 32], cqn[k][:, cs], k == 0, k == 1, [Rwuq, Rcqn[k][q]], self.Rps[3])
                yield
                yield from self.g_headnorm(3, 96, self.cbf("ones", r96, 0, 96), self.cf("epsc", 2, 3, r96), self.der("mla_gq", 0, 1, r96), self.QTt[0][r96, cs],
                                           self.RQT[0][q], cs, rope=(1, "sw_mla", [64]))
                self.mm(self.ps[4][0:64, :], wukv[:, 128 * h:128 * h + 64], ckvn[:, cs], True, True, [Rwukv, Rckvn[q]], self.Rps[4])
                self.proj_T(self.ps[4], self.Rps[4], slice(64, 96), lambda k: wc[:, k, 384:416], rwc, q)
                yield
                yield from self.g_headnorm(4, 96, self.cbf("ones", r96, 0, 96), self.cf("epsc", 2, 3, r96), self.der("mla_gk", 0, 1, r96), self.KTt[0][r96, cs],
                                           self.RKT[0][q], cs, rope=(1, "sw_mla", [64]))
                yield from self.g_vgroup(q, lambda k: wukv[:, 128 * h + 64:128 * h + 128], Rwukv, 64, evac, nk=1,
                                         lhs_fn=lambda k, kt: ckvn[:, kt * 128:(kt + 1) * 128], rl=lambda k, g: Rckvn[g])

            self.bg = pre(0)
            self.bg_drain()
            for q in range(NQ):
                cs = slice(q * QT, (q + 1) * QT)
                if q + 1 < NQ:
                    self.bg = pre(q + 1)
                    self.bg_drain()
                rb = slice(64 * hh, 64 * hh + 64)
                ob = 6 + q % 2
                rins = [self.RQT[0][q]] + self.RKT[0][:q + 1] + self.RVA[:q + 1]
                self.attn_map(self.causal_tiles(q),
                              lambda c0, c1, q=q: self.QTt[0][r96, q * QT + c0:q * QT + c1],
                              lambda kt: self.KTt[0][r96, kt * 128:(kt + 1) * 128],
                              lambda kt, hh=hh: self.VA[:, kt, 128 * hh:128 * hh + 128],
                              96.0 ** -0.5, rins, ob,
                              fin=lambda ob=ob, hh=hh, h=h, rb=rb, cs=cs, q=q: self.finalize(ob, hh, self.oT[2][h // 2][rb, cs], self.RoT[2][h // 2][q]))
                self.bg_drain()
            self.pend_flush()

    def diff(self, l, ms):
        kb = self.kb
        self.mark('diff')
        self.mixer_scratch(ms)
        dsq = kb.sb("dsq", [128, QT], BF16, ms)
        Rdsq = Res("dsq")
        r64 = slice(0, 64)
        pstat, Rpstat = self.ps[5], self.Rps[5]
        dif_w = {}

        def load_dif(h_):
            if h_ in dif_w or h_ >= 4:
                return
            w_, r_ = self.next_wb()
            kb.dma_in("pool", r_, lambda e: [e.dma_start(out=w_[:, :, 0:64], in_=self.win_cols(l, C_DIF_Q + 64 * h_, 64)),
                                             e.dma_start(out=w_[:, :, 64:128], in_=self.win_cols(l, C_DIF_K + 64 * h_, 64)),
                                             e.dma_start(out=w_[:, :, 128:192], in_=self.win_cols(l, C_DIF_V + 64 * h_, 64))])
            dif_w[h_] = (w_, r_)

        for h in range(4):
            hh = h % 2
            load_dif(h)
            load_dif(h + 1)
            wqk, rwqk = dif_w[h]
            vcol = 0 if hh == 0 else 192

            def evac(g, pb, Rpb, vcol=vcol):
                kb.op("dve", lambda e: e.tensor_copy(out=self.VA[:, g * 4:(g + 1) * 4, vcol:vcol + 64], in_=pb[:, 0:256].rearrange("p (t c) -> p t c", t=4)),
                      [Rpb], [self.RVA[g]])

            def pre(q, wqk=wqk, rwqk=rwqk):
                cs = slice(q * QT, (q + 1) * QT)
                self.proj_T(self.ps[3], self.Rps[3], r64, lambda k: wqk[:, k, 0:64], rwqk, q)
                yield
                yield from self.g_headnorm(3, 64, self.cbf("b32", r64, 0, 64), self.cf("epsc", 3, 4, r64), self.der("dif_gq", 0, 1, r64), self.QTt[0][r64, cs],
                                           self.RQT[0][q], cs, rope=(2, "sw_dif", [0, 32]))
                self.proj_T(self.ps[4], self.Rps[4], r64, lambda k: wqk[:, k, 64:128], rwqk, q)
                yield
                yield from self.g_headnorm(4, 64, self.cbf("b32", r64, 0, 64), self.cf("epsc", 3, 4, r64), self.der("dif_gk", 0, 1, r64), self.KTt[0][r64, cs],
                                           self.RKT[0][q], cs, rope=(2, "sw_dif", [0, 32]))
                yield from self.g_vgroup(q, lambda k: wqk[:, k, 128:192], rwqk, 64, evac)

            self.bg = pre(0)
            self.bg_drain()
            for q in range(NQ):
                cs = slice(q * QT, (q + 1) * QT)
                if q + 1 < NQ:
                    self.bg = pre(q + 1)
                    self.bg_drain()
                rins = [self.RQT[0][q]] + self.RKT[0][:q + 1] + self.RVA[:q + 1]

                def fin_diff(q=q, cs=cs, hh=hh, h=h):
                    recs = []
                    for a in range(2):
                        i, nr = self.finalize(6 + a, hh, None, None)
                        kb.op("dve", lambda e: e.tensor_tensor(out=self.rec[i][nr, :], in0=self.ps[6 + a][nr, :], in1=self.rec[i][nr, :], op=ALU.mult),
                              [self.Rps[6 + a], self.Rrec[i]], [self.Rrec[i]])
                        recs.append(i)
                    i0, i1 = recs
                    kb.op("dve", lambda e: e.scalar_tensor_tensor(out=self.rec[i0][nr, :], in0=self.rec[i1][nr, :], scalar=self.der("neglam", 0, 1, nr),
                                                                  in1=self.rec[i0][nr, :], op0=ALU.mult, op1=ALU.add),
                          [self.Rrec[i0], self.Rrec[i1], self.Rder], [self.Rrec[i0]])
                    pst, Rpst = self.ps[6], self.Rps[6]
                    kb.op("pool", lambda e: e.tensor_tensor(out=dsq[nr, :], in0=self.rec[i0][nr, :], in1=self.rec[i0][nr, :], op=ALU.mult),
                          [self.Rrec[i0]], [Rdsq])
                    self.mm(pst[nr, :], self.cbf("ones", nr, 0, 64), dsq[nr, :], True, True, [Rdsq, self.Rcb], Rpst)
                    self.rstd_from(pst[nr, :], self.rec[i1][nr, :], self.cf("epsc", 1, 2, nr), Rpst, self.Rrec[i1])
                    kb.op("dve", lambda e: e.scalar_tensor_tensor(out=self.oT[3][h // 2][nr, cs], in0=self.rec[i0][nr, :], scalar=self.der("dif_og", 0, 1, nr),
                                                                  in1=self.rec[i1][nr, :], op0=ALU.mult, op1=ALU.mult),
                          [self.Rrec[i0], self.Rrec[i1], self.Rder], [self.RoT[3][h // 2][q]])

                for a in range(2):
                    ra = slice(32 * a, 32 * a + 32)
                    self.attn_map(self.causal_tiles(q),
                                  lambda c0, c1, ra=ra, q=q: self.QTt[0][ra, q * QT + c0:q * QT + c1],
                                  lambda kt, ra=ra: self.KTt[0][ra, kt * 128:(kt + 1) * 128],
                                  lambda kt, hh=hh: self.VA[:, kt, 128 * hh:128 * hh + 128],
                                  32.0 ** -0.5, rins, 6 + a, fin=(fin_diff if a == 1 else None))
                self.bg_drain()
            self.pend_flush()


def prep_inputs(inp, layer_ids=tuple(range(DEPTH))):
    li = list(layer_ids)
    cb, cf = make_consts(layer_ids)
    sm = make_smalls(inp, layer_ids)

    def W(name):
        return np.ascontiguousarray(np.asarray(inp[name], np.float32)[li])

    w_in = W("w_in")
    gcols = []
    for br in range(3):
        for pr in range(2):
            for hh in range(2):
                gcols += [C_NSA_G + br * 4 + pr * 2 + hh] * 64
    wg_rep = np.ascontiguousarray(w_in[:, :, gcols])
    wf_pad = np.zeros((len(li), D, 128), np.float32)
    for h in range(4):
        wf_pad[:, :, 32 * h] = w_in[:, :, C_FOX_F + h]
    shared = {
        "ada_w": W("ada_w"), "w_in": w_in, "wg_rep": wg_rep, "wf_pad": wf_pad,
        "cmp_w1": W("nsa_cmp_w1"), "cmp_w2": W("nsa_cmp_w2"),
        "cmp_pe": np.ascontiguousarray(np.transpose(W("nsa_cmp_pe"), (0, 1, 3, 2))),
        "w_uq": W("mla_w_uq"), "w_ukv": W("mla_w_ukv"), "br_w": W("br_w"), "gate_w": W("gate_w"),
        "w_out": W("w_out"), "w_up": W("ffn_w_up"), "w_down": W("ffn_w_down"),
        "smalls": sm, "cbf": cb, "cf32": cf,
    }
    maps = []
    for b in range(8):
        m = dict(shared)
        m["x"] = np.ascontiguousarray(inp["x"][b], np.float32)
        m["cT"] = np.ascontiguousarray(np.asarray(inp["c"][b], np.float32).reshape(8, 128).T)
        m["pos"] = np.ascontiguousarray(np.asarray(inp["positions"][b], np.int32).reshape(1, S))
        maps.append(m)
    return maps


FUSED = True


def kernel(**inputs):
    inp = {k: np.asarray(v) for k, v in inputs.items()}
    if FUSED:
        maps = prep_inputs(inp)
        nc = Prog().build()
        res = run_bass_kernel_spmd(nc, maps, core_ids=list(range(8)))
        return np.stack([np.asarray(res.results[b]["y"], np.float32) for b in range(8)], axis=0)
    nc = Prog(n_layers=1, wdepth=1).build()
    x = np.asarray(inp["x"], np.float32)
    for l in range(DEPTH):
        cur = dict(inp)
        cur["x"] = x
        maps = prep_inputs(cur, (l,))
        res = run_bass_kernel_spmd(nc, maps, core_ids=list(range(8)))
        x = np.stack([np.asarray(res.results[b]["y"], np.float32) for b in range(8)], axis=0)
    return x
```

```python
import contextlib
import math
import numpy as np
import ml_dtypes
import concourse.bass as bass
import concourse.mybir as mybir
from concourse.bass_utils import run_bass_kernel_spmd

F32 = mybir.dt.float32
BF16 = mybir.dt.bfloat16
I32 = mybir.dt.int32
AF = mybir.ActivationFunctionType
ALU = mybir.AluOpType

S = 2048
D = 1024
NQ = 4
QT = 512
NKT = 16
DEPTH = 4
EPS = 1e-6
DFF = 2816
NFF = 22
THETA = 500000.0
BIG = 30000.0
IN_W = 2608


class Res:
    __slots__ = ("name", "w", "r", "dsem", "dcnt")

    def __init__(self, name):
        self.name = name
        self.w = None
        self.r = {}
        self.dsem = None
        self.dcnt = 0


class KB:
    ENG = ("pe", "act", "dve", "pool", "sp")

    def __init__(self, nc, stack):
        self.nc = nc
        self.stack = stack
        self.e = {"pe": nc.tensor, "act": nc.scalar, "dve": nc.vector, "pool": nc.gpsimd, "sp": nc.sync}
        self.sem = {k: stack.enter_context(nc.semaphore("s_" + k)) for k in self.ENG}
        self.cnt = {k: 0 for k in self.ENG}
        self.seen = {k: {} for k in self.ENG}
        self.same_engine_raw = True
        self.n_wait = 0

    def sb(self, name, shape, dt, stack=None):
        self.uid = getattr(self, "uid", 0) + 1
        return (stack or self.stack).enter_context(self.nc.sbuf_tensor(f"{name}_u{self.uid}", list(shape), dt))

    def ps(self, name, shape, dt=F32):
        return self.stack.enter_context(self.nc.psum_tensor(name, list(shape), dt))

    def newsem(self, name):
        self.uid = getattr(self, "uid", 0) + 1
        return self.stack.enter_context(self.nc.semaphore(f"{name}_u{self.uid}"))

    def _need(self, eng, deps):
        for key, sem, val in deps:
            if self.seen[eng].get(key, 0) >= val:
                continue
            self.e[eng].wait_ge(sem, val)
            self.n_wait += 1
            self.seen[eng][key] = val

    def _deps(self, eng, reads, writes):
        d = {}

        def add(w):
            if w is None:
                return
            e, i = w
            if e == "dma":
                sem, val = i
                key = ("dma", id(sem))
                if d.get(key, (None, 0))[1] < val:
                    d[key] = (sem, val)
            else:
                if e == eng and (eng == "pe" or not self.same_engine_raw):
                    return
                if d.get(e, (None, 0))[1] < i:
                    d[e] = (self.sem[e], i)

        for r in reads:
            add(r.w)
        for w in writes:
            add(w.w)
            for e, i in w.r.items():
                if e == eng:
                    continue
                if e == "dma":
                    add(("dma", i))
                else:
                    add((e, i))
        return [(k, s, v) for k, (s, v) in d.items()]

    def op(self, eng, fn, reads=(), writes=()):
        self._need(eng, self._deps(eng, reads, writes))
        ins = fn(self.e[eng])
        ins.then_inc(self.sem[eng], 1)
        self.cnt[eng] += 1
        idx = self.cnt[eng]
        for r in reads:
            r.r[eng] = idx
        for w in writes:
            w.w = (eng, idx)
            w.r = {}
        return ins

    def dma_in(self, q, res, fn, reads=()):
        if res.dsem is None:
            res.dsem = self.newsem("d_" + res.name)
        self._need(q, self._deps(q, reads, [res]))
        inss = fn(self.e[q])
        if not isinstance(inss, (list, tuple)):
            inss = [inss]
        for ins in inss:
            ins.then_inc(res.dsem, 16)
            res.dcnt += 16
        res.w = ("dma", (res.dsem, res.dcnt))
        res.r = {}

    def dma_out(self, q, res_list, fn, sem):
        self._need(q, self._deps(q, res_list, []))
        inss = fn(self.e[q])
        if not isinstance(inss, (list, tuple)):
            inss = [inss]
        for ins in inss:
            ins.then_inc(sem[0], 16)
            sem[1] += 16
        for r in res_list:
            r.r["dma"] = (sem[0], sem[1])

    def barrier(self):
        for a in ("pe", "act", "dve", "pool"):
            deps = []
            for b in ("pe", "act", "dve", "pool"):
                if a != b and self.cnt[b] > 0:
                    deps.append((b, self.sem[b], self.cnt[b]))
            self._need(a, deps)


CB = {}
CF = {}
SM = {}


def _alloc(tab, name, n, cur):
    tab[name] = (cur, cur + n)
    return cur + n


def _layout():
    c = 0
    for name, n in (("ident", 128), ("ones", 128), ("b64", 128), ("b32", 128), ("sw_nsa", 128),
                    ("sw_mla", 128), ("sw_dif", 128), ("selrow", 512), ("esel", 2048), ("ovl", 64), ("tabA", 512), ("tabB", 512)):
        c = _alloc(CB, name, n, c)
    CB["_n"] = c
    c = 0
    for name, n in (("ident", 128), ("invf", 1), ("sgn", 1), ("negpi", 1), ("epsc", 8), ("lamc", 2 * DEPTH)):
        c = _alloc(CF, name, n, c)
    CF["_n"] = c
    c = 0
    for name, n in (("ada_b", 48), ("gate_b", 32), ("conv_w", 66), ("conv_b", 22),
                    ("nsa_gq", 1), ("nsa_gkc", 1), ("nsa_gks", 1), ("nsa_gkw", 1),
                    ("fox_gq", 1), ("fox_gk", 1), ("fox_fb", 1),
                    ("mla_cqg", 2), ("mla_ckvg", 1), ("mla_gq", 1), ("mla_gk", 1),
                    ("dif_gq", 1), ("dif_gk", 1), ("dif_og", 1), ("dif_lam", 128)):
        c = _alloc(SM, name, n, c)
    SM["_n"] = c


_layout()

DER = {}
_c = 0
for _name, _n in (("a1", 8), ("a2", 8), ("nsa_gq", 1), ("nsa_gkc", 1), ("nsa_gks", 1), ("nsa_gkw", 1),
                  ("fox_gq", 1), ("fox_gk", 1), ("negfb", 1), ("mla_cqg", 2), ("mla_ckvg", 1), ("mla_gq", 1),
                  ("mla_gk", 1), ("dif_gq", 1), ("dif_gk", 1), ("dif_og", 1), ("neglam", 1), ("t0", 4), ("t1", 4)):
    _c = _alloc(DER, _name, _n, _c)
DER["_n"] = _c


def _rope_partner(r, head, n_rot):
    rr = r % head
    half = n_rot // 2
    base = r - rr
    if rr < half:
        return base + rr + half, rr, -1.0
    if rr < n_rot:
        return base + rr - half, rr - half, 1.0
    return None, None, 0.0


def make_consts(layer_ids=tuple(range(DEPTH))):
    cb = np.zeros((128, CB["_n"]), np.float32)
    cf = np.zeros((128, CF["_n"]), np.float32)
    p = np.arange(128)
    cb[:, CB["ident"][0]:CB["ident"][1]] = np.eye(128)
    cb[:, CB["ones"][0]:CB["ones"][1]] = 1.0
    cb[:, CB["b64"][0]:CB["b64"][1]] = (p[:, None] // 64 == p[None, :] // 64)
    cb[:, CB["b32"][0]:CB["b32"][1]] = (p[:, None] // 32 == p[None, :] // 32)
    for ci, (nm, head, nrot) in enumerate((("sw_nsa", 64, 16), ("sw_mla", 96, 32), ("sw_dif", 32, 8))):
        sw = np.zeros((128, 128), np.float32)
        half = nrot // 2
        inv = (np.float32(THETA) ** (-np.arange(half, dtype=np.float32) / np.float32(half))).astype(np.float32)
        for m in range(128):
            if nm == "sw_mla":
                if 64 <= m < 96:
                    rr = m - 64
                    sw[64 + (rr + 16 if rr < 16 else rr - 16), m] = 1.0
                continue
            pr, fi, sg = _rope_partner(m, head, nrot)
            if pr is not None:
                sw[pr, m] = 1.0
        for r in range(32):
            pr, fi, sg = _rope_partner(r, head, nrot)
            if pr is not None:
                cf[32 * ci + r, CF["invf"][0]] = inv[fi]
                cf[32 * ci + r, CF["sgn"][0]] = sg
        cb[:, CB[nm][0]:CB[nm][1]] = sw
    for h in range(4):
        cb[32 * h, CB["selrow"][0] + 128 * h: CB["selrow"][0] + 128 * (h + 1)] = 1.0
    for kt in range(16):
        for pp in range(128):
            cb[2 * kt + pp // 64, CB["esel"][0] + kt * 128 + pp] = BIG
    cs = np.arange(127)[:, None] * 16
    bs = np.arange(32)[None, :] * 64
    ov = ((cs < bs + 64) & (cs + 32 > bs)).astype(np.float32)
    cb[0:127, CB["ovl"][0]:CB["ovl"][0] + 32] = ov
    cb[0:127, CB["ovl"][0] + 32:CB["ovl"][0] + 64] = 1.0
    cf[:, CF["ident"][0]:CF["ident"][1]] = np.eye(128)
    cf[:, CF["negpi"][0]] = -math.pi
    for i_, v_ in enumerate((1024 * EPS, 64 * EPS, 96 * EPS, 32 * EPS, 256 * EPS, 128 * EPS, 1.0, 1e-30)):
        cf[:, CF["epsc"][0] + i_] = v_
    for i_, l_ in enumerate(layer_ids):
        lam_init = 0.8 - 0.6 * math.exp(-0.3 * l_)
        cf[:, CF["lamc"][0] + 2 * i_] = 8.0 * (1.0 - lam_init)
        cf[:, CF["lamc"][0] + 2 * i_ + 1] = -lam_init
    t = (np.arange(16)[None, :, None] * 128 + p[:, None, None])
    j = np.arange(32)[None, None, :]
    cur = t // 64
    future = (j * 64 > t)
    forced = ((j == 0) | (j == cur) | (j == cur - 1)) & (~future)
    A = (~future & ~forced).astype(np.float32)
    Bt = np.where(future, -1.0, np.where(forced, 1e4, 0.0)).astype(np.float32)
    cb[:, CB["tabA"][0]:CB["tabA"][1]] = A.reshape(128, 512)
    cb[:, CB["tabB"][0]:CB["tabB"][1]] = Bt.reshape(128, 512)
    return cb.astype(ml_dtypes.bfloat16), cf


def make_smalls(inp, layer_ids=tuple(range(DEPTH))):
    sm = np.zeros((128, len(layer_ids) * SM["_n"]), np.float32)
    for li_, l in enumerate(layer_ids):
        o = li_ * SM["_n"]

        def put(name, arr):
            a, b = SM[name]
            arr = np.asarray(arr, np.float32)
            if arr.ndim == 1:
                arr = arr[:, None]
            sm[:arr.shape[0], o + a:o + a + arr.shape[1]] = arr

        put("ada_b", inp["ada_b"][l].reshape(48, 128).T)
        put("gate_b", inp["gate_b"][l].reshape(32, 128).T)
        cw = inp["ffn_conv_w"][l]
        put("conv_w", np.concatenate([cw[t].reshape(22, 128).T for t in range(3)], axis=1))
        put("conv_b", inp["ffn_conv_b"][l].reshape(22, 128).T)
        g = inp["nsa_qk_g"][l]
        put("nsa_gq", np.tile(g[0], 2)); put("nsa_gkc", np.tile(g[1], 2))
        put("nsa_gks", np.tile(g[2], 2)); put("nsa_gkw", np.tile(g[3], 2))
        g = inp["fox_qk_g"][l]
        put("fox_gq", np.tile(g[0], 2)); put("fox_gk", np.tile(g[1], 2))
        fb = np.zeros(128, np.float32)
        fb[0::32] = inp["fox_f_b"][l]
        put("fox_fb", fb)
        put("mla_cqg", inp["mla_cq_g"][l].reshape(2, 128).T)
        put("mla_ckvg", inp["mla_ckv_g"][l])
        gq_, gk_ = inp["mla_qk_g"][l, 0], inp["mla_qk_g"][l, 1]
        put("mla_gq", np.concatenate([gq_[32:], gq_[:32]])); put("mla_gk", np.concatenate([gk_[32:], gk_[:32]]))
        put("dif_gq", np.tile(inp["diff_qk_g"][l, 0], 4)); put("dif_gk", np.tile(inp["diff_qk_g"][l, 1], 4))
        put("dif_og", np.tile(inp["diff_out_g"][l], 2))
        put("dif_lam", np.tile(inp["diff_lambda"][l].reshape(1, 128), (128, 1)))
    return sm


C_NSA_Q, C_NSA_KC, C_NSA_KS, C_NSA_VS, C_NSA_KW, C_NSA_VW, C_NSA_G = 0, 256, 384, 448, 512, 576, 640
C_FOX_Q, C_FOX_K, C_FOX_V, C_FOX_F = 652, 908, 1164, 1420
C_MLA_CQ, C_MLA_CKV, C_MLA_KR = 1424, 1680, 1808
C_DIF_Q, C_DIF_K, C_DIF_V = 1840, 2096, 2352


class Prog:
    def __init__(self, n_layers=DEPTH, dbg=(), wdepth=DEPTH):
        self.n_layers = n_layers
        self.wd = wdepth
        self.dbg = set(dbg)
        self.nc = bass.Bass("TRN2", target_bir_lowering=False)
        self.stack = contextlib.ExitStack()
        self.dbg_out = {}

    def dram_in(self, name, shape, dt=F32):
        return self.nc.dram_tensor(name, list(shape), dt, kind="ExternalInput").ap()

    def dram_out(self, name, shape, dt=F32):
        return self.nc.dram_tensor(name, list(shape), dt, kind="ExternalOutput").ap()

    def mm(self, out, lhsT, rhs, start, stop, rin, rout):
        self.kb.op("pe", lambda e: e.matmul(out, lhsT=lhsT, rhs=rhs, start=start, stop=stop), rin, [rout])

    def cbf(self, name, rows=slice(0, 128), c0=0, c1=None):
        a, b = CB[name]
        if c1 is None:
            c1 = b - a
        return self.t_cb[rows, a + c0:a + c1]

    def cf(self, name, c0=0, c1=None, rows=slice(0, 128)):
        a, b = CF[name]
        if c1 is None:
            c1 = b - a
        return self.t_cf[rows, a + c0:a + c1]

    def sm(self, l, name, c0=0, c1=None, rows=slice(0, 128)):
        a, b = SM[name]
        if c1 is None:
            c1 = b - a
        o = l * SM["_n"]
        return self.t_sm[rows, o + a + c0:o + a + c1]

    def der(self, name, c0=0, c1=None, rows=slice(0, 128), par=None):
        a, b = DER[name]
        if c1 is None:
            c1 = b - a
        return self.t_ders[self.cur if par is None else par][rows, a + c0:a + c1]

    @property
    def t_mod(self):
        return self.t_mods[self.cur]

    @property
    def Rmod(self):
        return self.Rmods[self.cur]

    @property
    def Rder(self):
        return self.Rders[self.cur]

    def dump(self, name, ap, res, shape):
        if name not in self.dbg:
            return
        d = self.dram_out("dbg_" + name, shape, F32 if ap.dtype == F32 else BF16)
        self.dbg_out[name] = d
        self.kb.dma_out("sp", res, lambda e: e.dma_start(out=d, in_=ap), self.osem)

    def build(self):
        nc = self.nc
        st = self.stack
        kb = self.kb = KB(nc, st)
        self.osem = [kb.newsem("osem"), 0]
        self.x_d = self.dram_in("x", [S, D])
        self.cT_d = self.dram_in("cT", [128, 8])
        self.pos_d = self.dram_in("pos", [1, S], I32)
        self.ada_w = self.dram_in("ada_w", [self.wd, D, 6 * D])
        self.w_in = self.dram_in("w_in", [self.wd, D, IN_W])
        self.wg_rep = self.dram_in("wg_rep", [self.wd, D, 768])
        self.wf_pad = self.dram_in("wf_pad", [self.wd, D, 128])
        self.cmp_w1 = self.dram_in("cmp_w1", [self.wd, 2, 2048, 64])
        self.cmp_w2 = self.dram_in("cmp_w2", [self.wd, 2, 64, 64])
        self.cmp_pe = self.dram_in("cmp_pe", [self.wd, 2, 64, 32])
        self.w_uq = self.dram_in("w_uq", [self.wd, 256, 384])
        self.w_ukv = self.dram_in("w_ukv", [self.wd, 128, 512])
        self.br_w = self.dram_in("br_w", [self.wd, 4, 256, D])
        self.gate_w = self.dram_in("gate_w", [self.wd, D, 4 * D])
        self.w_out = self.dram_in("w_out", [self.wd, D, D])
        self.w_up = self.dram_in("w_up", [self.wd, D, 2 * DFF])
        self.w_down = self.dram_in("w_down", [self.wd, DFF, D])
        self.sm_d = self.dram_in("smalls", [128, self.wd * SM["_n"]])
        self.cb_d = self.dram_in("cbf", [128, CB["_n"]], BF16)
        self.cf_d = self.dram_in("cf32", [128, CF["_n"]])
        self.y_d = self.dram_out("y", [S, D])

        self.xT = [kb.sb(f"xT{k}", [128, S], F32) for k in range(8)]
        self.hT = [kb.sb(f"hT{k}", [128, S], BF16) for k in range(8)]
        self.Rx = [[Res(f"x{k}_{q}") for q in range(NQ)] for k in range(8)]
        self.Rh = [[Res(f"h{k}_{q}") for q in range(NQ)] for k in range(8)]
        self.ropeC = kb.sb("ropeC", [128, S], BF16)
        self.ropeS = kb.sb("ropeS", [128, S], BF16)
        self.Rrope = Res("rope")
        self.t_cb = kb.sb("t_cb", [128, CB["_n"]], BF16)
        self.t_cf = kb.sb("t_cf", [128, CF["_n"]], F32)
        self.t_sm = kb.sb("t_sm", [128, self.wd * SM["_n"]], F32)
        self.t_mods = [kb.sb(f"t_mod{i}", [128, 48], F32) for i in range(2)]
        self.t_ders = [kb.sb(f"t_der{i}", [128, DER["_n"]], F32) for i in range(2)]
        self.adab = [kb.sb(f"adab{i}", [128, 512], BF16) for i in range(2)]
        self.Radab = [Res(f"adab{i}") for i in range(2)]
        self.Rmods = [Res("mod0"), Res("mod1")]
        self.Rders = [Res("der0"), Res("der1")]
        self.cur = 0
        self.t_scb = kb.sb("t_scb", [128, 8], BF16)
        self.Rcb, self.Rcf, self.Rsm, self.Rscb = (Res(n) for n in ("cb", "cf", "sm", "scb"))
        self.WB = [kb.sb(f"WB{i}", [128, 8, 512], BF16) for i in range(2)]
        self.RWB = [Res(f"WB{i}") for i in range(2)]
        self.wb_i = 0
        self.ps = [kb.ps(f"ps{i}", [128, 512]) for i in range(8)]
        self.Rps = [Res(f"ps{i}") for i in range(8)]

        kb.dma_in("sp", self.Rcb, lambda e: e.dma_start(out=self.t_cb[:], in_=self.cb_d))
        kb.dma_in("sp", self.Rcf, lambda e: e.dma_start(out=self.t_cf[:], in_=self.cf_d))
        kb.dma_in("sp", self.Rsm, lambda e: e.dma_start(out=self.t_sm[:], in_=self.sm_d))

        self.ada_gen = None
        self.prologue()
        for l in range(self.n_layers):
            self.layer(l)
        self.epilogue()
        kb.e["sp"].wait_ge(self.osem[0], self.osem[1])
        self.stack.close()
        return nc

    def mark(self, name):
        if not hasattr(self, 'marks'):
            self.marks = []
        self.marks.append((name, self.kb.cnt['pe']))

    def next_wb(self):
        i = self.wb_i
        self.wb_i ^= 1
        return self.WB[i], self.RWB[i]

    def load_w(self, dram_ap_pkn, ncols, nk=8):
        wb, r = self.next_wb()
        self.kb.dma_in("pool", r, lambda e: e.dma_start(out=wb[:, 0:nk, 0:ncols], in_=dram_ap_pkn))
        return wb, r

    def win_cols(self, l, c0, n):
        return self.w_in[l].rearrange("(k p) n -> p k n", p=128)[:, :, c0:c0 + n]

    def prologue(self):
        kb = self.kb
        with contextlib.ExitStack() as ps_:
            xs = [kb.sb(f"xstage{i}", [128, 4, D], F32, ps_) for i in range(2)]
            Rxs = [Res(f"xstage{i}") for i in range(2)]
            posi = kb.sb("posi", [128, S], I32, ps_)
            posf = kb.sb("posf", [128, S], F32, ps_)
            tA = kb.sb("tA", [128, S], F32, ps_)
            tB = kb.sb("tB", [128, S], F32, ps_)
            Rposi, Rposf, RtA, RtB = Res("posi"), Res("posf"), Res("tA"), Res("tB")
            ct = kb.sb("ct", [128, 8], F32, ps_)
            Rct = Res("ct")
            kb.dma_in("sp", Rct, lambda e: e.dma_start(out=ct[:], in_=self.cT_d))
            kb.op("act", lambda e: e.activation(out=self.t_scb[:], in_=ct[:], func=AF.Silu), [Rct], [self.Rscb])
            self.ada_gen = self.g_ada(0)
            xv = self.x_d.rearrange("(g t p) d -> g p t d", t=4, p=128)
            for g in range(2):
                kb.dma_in("sp", Rxs[g], lambda e: e.dma_start(out=xs[g][:], in_=xv[g]))
            for g in range(4):
                xg, Rxg = xs[g % 2], Rxs[g % 2]
                for k in range(8):
                    self.ada_step(3)
                    pb = self.ps[k % 4]
                    for t in range(4):
                        kb.op("pe", lambda e: e.transpose(out=pb[:, t * 128:(t + 1) * 128], in_=xg[:, t, k * 128:(k + 1) * 128],
                                                          identity=self.cf("ident")), [Rxg, self.Rcf], [self.Rps[k % 4]])
                    eng = "act" if k % 2 == 0 else "dve"
                    if eng == "act":
                        kb.op("act", lambda e: e.activation(out=self.xT[k][:, g * 512:(g + 1) * 512], in_=pb[:], func=AF.Copy),
                              [self.Rps[k % 4]], [self.Rx[k][g]])
                    else:
                        kb.op("dve", lambda e: e.tensor_copy(out=self.xT[k][:, g * 512:(g + 1) * 512], in_=pb[:]),
                              [self.Rps[k % 4]], [self.Rx[k][g]])
                if g + 2 < 4:
                    kb.dma_in("sp", Rxg, lambda e: e.dma_start(out=xg[:], in_=xv[g + 2]))
            kb.dma_in("sp", Rposi, lambda e: e.dma_start(out=posi[:], in_=self.pos_d.partition_broadcast(128)))
            kb.op("dve", lambda e: e.tensor_copy(out=posf[:], in_=posi[:]), [Rposi], [Rposf])
            twopi = 2.0 * math.pi
            invf = self.cf("invf")
            sgn = self.cf("sgn")
            ti = posi
            for which in range(2):
                kb.op("dve", lambda e: e.tensor_scalar(out=tA[:], in0=posf[:], scalar1=invf, scalar2=1.0 / twopi, op0=ALU.mult, op1=ALU.mult),
                      [Rposf, self.Rcf, RtB], [RtA])
                if which == 1:
                    kb.op("dve", lambda e: e.tensor_scalar(out=tA[:], in0=tA[:], scalar1=0.25, scalar2=None, op0=ALU.add), [RtA], [RtA])
                kb.op("dve", lambda e: e.tensor_copy(out=ti[:], in_=tA[:]), [RtA, Rposf], [Rposi])
                kb.op("dve", lambda e: e.tensor_copy(out=tB[:], in_=ti[:]), [Rposi], [RtB])
                kb.op("dve", lambda e: e.tensor_tensor(out=tA[:], in0=tA[:], in1=tB[:], op=ALU.subtract), [RtA, RtB], [RtA])
                kb.op("act", lambda e: e.activation(out=tB[:], in_=tA[:], func=AF.Sin, scale=twopi), [RtA], [RtB])
                if which == 0:
                    kb.op("dve", lambda e: e.tensor_scalar(out=self.ropeS[:], in0=tB[:], scalar1=sgn, scalar2=None, op0=ALU.mult),
                          [RtB, self.Rcf], [self.Rrope])
                else:
                    kb.op("dve", lambda e: e.tensor_copy(out=self.ropeC[:], in_=tB[:]), [RtB], [self.Rrope])
            self.dump("ropeC", self.ropeC[:], [self.Rrope], [128, S])
            self.dump("ropeS", self.ropeS[:], [self.Rrope], [128, S])
            self.dump("xT0", self.xT[0][:], self.Rx[0], [128, S])
            kb.barrier()
            if self.dbg:
                kb.e["act"].wait_ge(self.osem[0], self.osem[1])
                kb.barrier()

    def epilogue(self):
        kb = self.kb
        with contextlib.ExitStack() as ps_:
            ys = [kb.sb(f"ystage{i}", [128, 2, D], F32, ps_) for i in range(2)]
            Rys = [Res(f"ystage{i}") for i in range(2)]
            yv = self.y_d.rearrange("(g t p) d -> g p t d", t=2, p=128)
            for g in range(8):
                q = g // 2
                for t in range(2):
                    tt = g * 2 + t
                    for k in range(8):
                        pb = self.ps[(k // 4) + 2 * (tt % 2)]
                        kb.op("pe", lambda e: e.transpose(out=pb[:, (k % 4) * 128:(k % 4 + 1) * 128], in_=self.xT[k][:, tt * 128:(tt + 1) * 128],
                                                          identity=self.cf("ident")), [self.Rx[k][q], self.Rcf], [self.Rps[(k // 4) + 2 * (tt % 2)]])
                    for hh in range(2):
                        bi = hh + 2 * (tt % 2)
                        if hh == 0:
                            kb.op("act", lambda e: e.activation(out=ys[g % 2][:, t, hh * 512:(hh + 1) * 512], in_=self.ps[bi][:], func=AF.Copy),
                                  [self.Rps[bi]], [Rys[g % 2]])
                        else:
                            kb.op("dve", lambda e: e.tensor_copy(out=ys[g % 2][:, t, hh * 512:(hh + 1) * 512], in_=self.ps[bi][:]),
                                  [self.Rps[bi]], [Rys[g % 2]])
                kb.dma_out("sp", [Rys[g % 2]], lambda e: e.dma_start(out=yv[g], in_=ys[g % 2][:]), self.osem)

    def layer(self, l):
        self.cur = l % 2
        while self.ada_gen is not None:
            self.ada_step()
        self.norm_mod(l, 0)
        kb = self.kb
        with contextlib.ExitStack() as ls:
            self.oT = [None] * 4
            self.RoT = [None] * 4
            for m, fn in self.mixer_order():
                self.oT[m] = [kb.sb(f"oT{m}_{c}", [128, S], BF16, ls) for c in range(2)]
                self.RoT[m] = [[Res(f"oT{m}_{c}_{q}") for q in range(NQ)] for c in range(2)]
                with contextlib.ExitStack() as ms:
                    fn(l, ms)
                    kb.barrier()
                if l == 0:
                    for c in range(2):
                        self.dump(f"o{m}_{c}", self.oT[m][c][:], self.RoT[m][c], [128, S])
            if self.dbg:
                kb.e["act"].wait_ge(self.osem[0], self.osem[1])
                kb.barrier()
            if getattr(self, "mixers", None) is None:
                self.merge(l)
                self.dump(f"x1_l{l}", self.xT[0][:], self.Rx[0], [128, S])
        if getattr(self, "mixers", None) is None:
            self.norm_mod(l, 1)
            self.ffn(l)
            self.dump(f"x2_l{l}", self.xT[0][:], self.Rx[0], [128, S])
            if self.dbg:
                kb.e["act"].wait_ge(self.osem[0], self.osem[1])
                kb.barrier()

    def gate_tile(self, wg, rwg, blk, q, gt, Rgt):
        kb = self.kb
        pb, Rpb = self.ps[3], self.Rps[3]
        self.proj_T(pb, Rpb, slice(0, 128), lambda k: wg[:, k, blk * 128:(blk + 1) * 128], rwg, q)
        kb.op("act", lambda e: e.activation(out=gt[:], in_=pb[:], func=AF.Exp, scale=-1.0), [Rpb], [Rgt])
        kb.op("dve", lambda e: e.tensor_scalar(out=gt[:], in0=gt[:], scalar1=1.0, scalar2=None, op0=ALU.add), [Rgt], [Rgt])
        kb.op("dve", lambda e: e.reciprocal(out=gt[:], in_=gt[:]), [Rgt], [Rgt])

    def nsa(self, l, ms):
        kb = self.kb
        self.mark('nsa_prelude')
        KS2 = kb.sb("KS2", [128, S], BF16, ms)
        KW2 = kb.sb("KW2", [128, S], BF16, ms)
        RKS = [Res(f"KS2_{q}") for q in range(NQ)]
        RKW = [Res(f"KW2_{q}") for q in range(NQ)]
        VSW = kb.sb("VSW", [128, NKT, 384], BF16, ms)
        RVSW = Res("VSW")
        KCMP = kb.sb("KCMP", [128, 128], BF16, ms)
        VCMP = kb.sb("VCMP", [128, 192], BF16, ms)
        RKCMP, RVCMP = Res("KCMP"), Res("VCMP")
        IMP = kb.sb("IMP", [128, 512], F32, ms)
        RIMP = Res("IMP")
        SELM = kb.sb("SELM", [32, S], BF16, ms)
        RSELM = [Res(f"SELM{q}") for q in range(NQ)]
        self.mixer_scratch(ms, nq=2, nk=0, va=False)
        kb.op("pool", lambda e: e.memset(VSW[:, :, 64:128], 1.0), [], [RVSW])
        kb.op("pool", lambda e: e.memset(VSW[:, :, 256:320], 1.0), [], [RVSW])
        kb.op("pool", lambda e: e.memset(VCMP[:], 0.0), [], [RVCMP])
        kb.op("pool", lambda e: e.memset(VCMP[:, 64:128], 1.0), [RVCMP], [RVCMP])
        kb.op("pool", lambda e: e.memset(KCMP[:], 0.0), [], [RKCMP])
        kb.op("pool", lambda e: e.memset(IMP[:], 0.0), [], [RIMP])
        pstat, Rpstat = self.ps[5], self.Rps[5]
        wA, rwA = self.load_w(self.win_cols(l, C_NSA_KC, 384), 384)
        wq, rwq = self.load_w(self.win_cols(l, C_NSA_Q, 256), 256)
        with contextlib.ExitStack() as pscope:
            KVC = self.QTt[1]
            RKVC = [Res(f"KVC{q}") for q in range(NQ)]
            W1 = kb.sb("W1", [128, 32, 64], BF16, pscope)
            W2 = kb.sb("W2", [128, 64], BF16, pscope)
            PEt = kb.sb("PEt", [128, 32], BF16, pscope)
            HID = kb.sb("HID", [128, 128], BF16, pscope)
            RW1, RW2, RPE, RHID = Res("W1"), Res("W2"), Res("PEt"), Res("HID")
            kb.dma_in("pool", RW1, lambda e: [e.dma_start(out=W1[64 * i:64 * i + 64, :, :], in_=self.cmp_w1[l, i].rearrange("(j d) o -> d j o", d=64)) for i in range(2)])
            kb.dma_in("pool", RW2, lambda e: [e.dma_start(out=W2[64 * i:64 * i + 64, :], in_=self.cmp_w2[l, i]) for i in range(2)])
            kb.dma_in("pool", RPE, lambda e: [e.dma_start(out=PEt[64 * i:64 * i + 64, :], in_=self.cmp_pe[l, i]) for i in range(2)])
            for q in range(NQ):
                cs = slice(q * QT, (q + 1) * QT)
                for (c0, dstt, Rd, gname) in ((128, KS2, RKS, "nsa_gks"), (256, KW2, RKW, "nsa_gkw")):
                    for half in range(2):
                        self.proj_T(self.ps[3], self.Rps[3], slice(64 * half, 64 * half + 64), lambda k: wA[:, k, c0:c0 + 64], rwA, q)
                    self.headnorm(3, 128, self.cbf("b64"), self.cf("epsc", 1, 2), self.der(gname), dstt[:, cs], Rd[q], cs, rope=(0, "sw_nsa", [0, 64]))
                self.proj_T(self.ps[4], self.Rps[4], slice(0, 128), lambda k: wA[:, k, 0:128], rwA, q)
                kb.op("act", lambda e: e.activation(out=KVC[:, cs], in_=self.ps[4][:], func=AF.Copy), [self.Rps[4]], [RKVC[q]])
            for g in range(4):
                pb, Rpb = self.ps[3 + g % 2], self.Rps[3 + g % 2]
                for t in range(4):
                    kt = g * 4 + t
                    for b in range(2):
                        for k in range(8):
                            self.mm(pb[:, (t * 2 + b) * 64:(t * 2 + b + 1) * 64], self.hT[k][:, kt * 128:(kt + 1) * 128], wA[:, k, 192 + 128 * b:256 + 128 * b],
                                    k == 0, k == 7, [rwA, self.Rh[k][g]], Rpb)
                src = pb[:].rearrange("p (t b c) -> p t b c", t=4, b=2)
                for sidx in (0, 2):
                    dstv = VSW[:, g * 4:(g + 1) * 4, :].rearrange("p t (b s c) -> p t b s c", b=2, s=3)[:, :, :, sidx, :]
                    kb.op("dve" if sidx == 0 else "act", (lambda e: e.tensor_copy(out=dstv, in_=src)) if sidx == 0 else
                          (lambda e: e.activation(out=dstv, in_=src, func=AF.Copy)), [Rpb], [RVSW])
            pH, RpH = self.ps[6], self.Rps[6]
            for i in range(2):
                rr = slice(64 * i, 64 * i + 64)
                n = 0
                for j in range(32):
                    self.mm(pH[rr, 0:127], W1[rr, j, :], KVC[rr, j:j + 16 * 126 + 1:16], n == 0, False, [RW1] + RKVC, RpH)
                    n += 1
                    self.mm(pH[rr, 0:127], W1[rr, j, :], PEt[rr, j:j + 1].to_broadcast([64, 127]), False, j == 31, [RW1, RPE], RpH)
            kb.op("act", lambda e: e.activation(out=HID[:, 0:127], in_=pH[:, 0:127], func=AF.Silu), [RpH], [RHID])
            for half in range(2):
                self.mm(self.ps[3][64 * half:64 * half + 64, 0:127], W2[0:64, :], HID[0:64, 0:127], True, True, [RW2, RHID], self.Rps[3])
            kb.op("act", lambda e: e.activation(out=self.sqb[0][:, 0:127], in_=self.ps[3][:, 0:127], func=AF.Square), [self.Rps[3]], [self.Rsqb[0]])
            self.mm(pstat[:, 0:127], self.cbf("b64"), self.sqb[0][:, 0:127], True, True, [self.Rsqb[0], self.Rcb], Rpstat)
            self.rstd_from(pstat[:, 0:127], self.rstd[0][:, 0:127], self.cf("epsc", 1, 2), Rpstat, self.Rrstd[0])
            kb.op("dve", lambda e: e.scalar_tensor_tensor(out=KCMP[:, 0:127], in0=self.ps[3][:, 0:127], scalar=self.der("nsa_gkc"), in1=self.rstd[0][:, 0:127],
                                                          op0=ALU.mult, op1=ALU.mult), [self.Rps[3], self.Rrstd[0], self.Rder, RKCMP], [RKCMP])
            self.mm(self.ps[4][0:127, 0:64], HID[64:128, 0:127], W2[64:128, :], True, True, [RW2, RHID], self.Rps[4])
            kb.op("dve", lambda e: e.tensor_copy(out=VCMP[0:127, 0:64], in_=self.ps[4][0:127, 0:64]), [self.Rps[4], RVCMP], [RVCMP])
            kb.op("act", lambda e: e.activation(out=VCMP[0:127, 128:192], in_=self.ps[4][0:127, 0:64], func=AF.Copy), [self.Rps[4], RVCMP], [RVCMP])
            self.dump("nsaKS", KS2[:], RKS, [128, S])
            self.dump("nsaKCMP", KCMP[:], [RKCMP], [128, 128])
            self.dump("nsaVCMP", VCMP[:], [RVCMP], [128, 192])
            kb.barrier()
            if self.dbg:
                kb.e["act"].wait_ge(self.osem[0], self.osem[1])
                kb.barrier()
        GT = [kb.sb(f"GT{i}", [128, QT], F32, ms) for i in range(2)]
        RGT = [Res(f"GT{i}") for i in range(2)]
        self.mark('nsa_q')
        wg0, rwg0 = self.load_w(self.wg_rep[l].rearrange("(k p) n -> p k n", p=128)[:, :, 0:256], 256)
        for u in range(2):
            for q in range(NQ):
                cs = slice(q * QT, (q + 1) * QT)
                self.proj_T(self.ps[3], self.Rps[3], slice(0, 128), lambda k: wq[:, k, u * 128:(u + 1) * 128], rwq, q)
                self.headnorm(3, 128, self.cbf("b64"), self.cf("epsc", 1, 2), self.der("nsa_gq"), self.QTt[u][:, cs], self.RQT[u][q], cs, rope=(0, "sw_nsa", [0, 64]))
        self.mark('nsa_cmp')
        wg1, rwg1 = self.load_w(self.wg_rep[l].rearrange("(k p) n -> p k n", p=128)[:, :, 256:768], 512)
        pI, RpI = self.ps[5], self.Rps[5]
        gi = 0
        for u in range(2):
            for q in range(NQ):
                cs = slice(q * QT, (q + 1) * QT)
                gt, Rgt = GT[gi], RGT[gi]
                gi ^= 1
                self.pend_flush()
                self.gate_tile(wg0, rwg0, u, q, gt, Rgt)
                for hh in range(2):
                    rb = slice(64 * hh, 64 * hh + 64)
                    ob = 6 + hh
                    rins = [self.RQT[u][q], RKCMP, RVCMP]

                    def imp_mm(pi, hh=hh):
                        for t in range(4):
                            self.mm(pI[:, (hh * 4 + t) * 64:(hh * 4 + t + 1) * 64], self.PT[pi][:, t * 128:(t + 1) * 128], self.cbf("ovl"), True, True,
                                    [self.RPT[pi], self.Rcb], RpI)

                    def fin_cmp(ob=ob, hh=hh, u=u, cs=cs, q=q, gt=gt, Rgt=Rgt):
                        i, nr = self.finalize(ob, hh, None, None, eps=1e-30)
                        kb.op("pool", lambda e: e.tensor_tensor(out=self.rec[i][nr, :], in0=self.rec[i][nr, :], in1=gt[nr, :], op=ALU.mult), [self.Rrec[i], Rgt], [self.Rrec[i]])
                        kb.op("dve", lambda e: e.tensor_tensor(out=self.oT[0][u][nr, cs], in0=self.ps[ob][nr, :], in1=self.rec[i][nr, :], op=ALU.mult),
                              [self.Rps[ob], self.Rrec[i]], [self.RoT[0][u][q]])

                    self.attn_map([(0, 0, QT, ("vis", 0, q * QT - 31))],
                                  lambda c0, c1, rb=rb, u=u, q=q: self.QTt[u][rb, q * QT + c0:q * QT + c1],
                                  lambda kt, rb=rb: KCMP[rb, :],
                                  lambda kt, hh=hh: VCMP[:, 64 * hh:64 * hh + 128],
                                  0.125, rins, ob, fin=fin_cmp, after_p=imp_mm)
                for hh in range(2):
                    for t in range(4):
                        tt = q * 4 + t
                        base = (hh * 4 + t) * 64
                        rcol = self.rstd[0][:, 0:1]
                        kb.op("dve", lambda e: e.tensor_scalar(out=rcol, in0=pI[:, base + 32:base + 33], scalar1=1e-30, scalar2=None, op0=ALU.add), [RpI], [self.Rrstd[0]])
                        kb.op("dve", lambda e: e.reciprocal(out=rcol, in_=rcol), [self.Rrstd[0]], [self.Rrstd[0]])
                        kb.op("dve", lambda e: e.scalar_tensor_tensor(out=IMP[:, tt * 32:(tt + 1) * 32], in0=pI[:, base:base + 32], scalar=rcol,
                                                                      in1=IMP[:, tt * 32:(tt + 1) * 32], op0=ALU.mult, op1=ALU.add),
                              [RpI, self.Rrstd[0], RIMP], [RIMP])
        self.pend_flush()
        self.dump("nsaIMP", IMP[:], [RIMP], [128, 512])
        self.mark('nsa_topk')
        SC = self.rec[0]
        RSC = self.Rrec[0]
        kb.op("dve", lambda e: e.tensor_tensor(out=SC[:], in0=IMP[:], in1=self.cbf("tabA"), op=ALU.mult), [RIMP, self.Rcb], [RSC])
        kb.op("dve", lambda e: e.tensor_tensor(out=SC[:], in0=SC[:], in1=self.cbf("tabB"), op=ALU.add), [RSC, self.Rcb], [RSC])
        m8 = self.rstd[1]
        Rm8 = self.Rrstd[1]
        sc2 = self.rec[1]
        Rsc2 = self.Rrec[1]
        selm = self.sqb[0]
        Rselm = self.Rsqb[0]
        pT, RpT = self.ps[5], self.Rps[5]
        for tt in range(16):
            sl = slice(tt * 32, (tt + 1) * 32)
            kb.op("dve", lambda e: e.max(out=m8[:, 0:8], in_=SC[:, sl]), [RSC], [Rm8])
            kb.op("dve", lambda e: e.match_replace(out=sc2[:, 0:32], in_to_replace=m8[:, 0:8], in_values=SC[:, sl], imm_value=-2.0), [RSC, Rm8], [Rsc2])
            kb.op("dve", lambda e: e.max(out=m8[:, 8:16], in_=sc2[:, 0:32]), [Rsc2], [Rm8])
            kb.op("dve", lambda e: e.tensor_scalar(out=selm[:, sl], in0=SC[:, sl], scalar1=m8[:, 15:16], scalar2=-1.0, op0=ALU.is_ge, op1=ALU.add),
                  [RSC, Rm8], [Rselm])
            self.mm(pT[0:32, (tt % 4) * 128:(tt % 4 + 1) * 128], selm[:, sl], self.cbf("ident"), True, True, [Rselm, self.Rcb], RpT)
            if tt % 4 == 3:
                qq = tt // 4
                kb.op("act", lambda e: e.activation(out=SELM[0:32, qq * QT:(qq + 1) * QT], in_=pT[0:32, :], func=AF.Copy), [RpT], [RSELM[qq]])
        self.dump("nsaSELM", SELM[:], RSELM, [32, S])
        self.mark('nsa_slcwin')
        for u in range(2):
            for q in range(NQ):
                cs = slice(q * QT, (q + 1) * QT)
                self.gate_tile(wg1, rwg1, u, q, GT[0], RGT[0])
                self.gate_tile(wg1, rwg1, 2 + u, q, GT[1], RGT[1])
                for hh in range(2):
                    rb = slice(64 * hh, 64 * hh + 64)
                    rins = [self.RQT[u][q], RVSW, RSELM[q], self.Rcb] + RKS
                    shared = {}

                    def fin_s(hh=hh, shared=shared):
                        i_s, nr = self.finalize(6, hh, None, None)
                        kb.op("pool", lambda e: e.tensor_tensor(out=self.rec[i_s][nr, :], in0=self.rec[i_s][nr, :], in1=GT[0][nr, :], op=ALU.mult),
                              [self.Rrec[i_s], RGT[0]], [self.Rrec[i_s]])
                        kb.op("dve", lambda e: e.tensor_tensor(out=self.rec[i_s][nr, :], in0=self.ps[6][nr, :], in1=self.rec[i_s][nr, :], op=ALU.mult),
                              [self.Rps[6], self.Rrec[i_s]], [self.Rrec[i_s]])
                        shared["i_s"] = i_s

                    def fin_w(hh=hh, shared=shared, u=u, cs=cs, q=q):
                        i_s = shared["i_s"]
                        i_w, nr = self.finalize(7, hh, None, None)
                        kb.op("pool", lambda e: e.tensor_tensor(out=self.rec[i_w][nr, :], in0=self.rec[i_w][nr, :], in1=GT[1][nr, :], op=ALU.mult),
                              [self.Rrec[i_w], RGT[1]], [self.Rrec[i_w]])
                        kb.op("dve", lambda e: e.tensor_tensor(out=self.rec[i_w][nr, :], in0=self.ps[7][nr, :], in1=self.rec[i_w][nr, :], op=ALU.mult),
                              [self.Rps[7], self.Rrec[i_w]], [self.Rrec[i_w]])
                        kb.op("pool", lambda e: e.tensor_tensor(out=self.rec[i_w][nr, :], in0=self.rec[i_w][nr, :], in1=self.rec[i_s][nr, :], op=ALU.add),
                              [self.Rrec[i_w], self.Rrec[i_s]], [self.Rrec[i_w]])
                        kb.op("pool", lambda e: e.tensor_tensor(out=self.oT[0][u][nr, cs], in0=self.oT[0][u][nr, cs], in1=self.rec[i_w][nr, :], op=ALU.add),
                              [self.Rrec[i_w], self.RoT[0][u][q]], [self.RoT[0][u][q]])

                    self.attn_map(self.causal_tiles(q),
                                  lambda c0, c1, rb=rb, u=u, q=q: self.QTt[u][rb, q * QT + c0:q * QT + c1],
                                  lambda kt, rb=rb: KS2[rb, kt * 128:(kt + 1) * 128],
                                  lambda kt, hh=hh: VSW[:, kt, 64 * hh:64 * hh + 128],
                                  0.125, rins, 6,
                                  extra=lambda kt, c0, c1, q=q: (self.cbf("esel", slice(0, 32), kt * 128, (kt + 1) * 128), SELM[0:32, q * QT + c0:q * QT + c1]),
                                  fin=fin_s)
                    tiles = [(4 * q + j, 128 * j, QT, ("causal", 0, 0)) for j in range(4)]
                    if q > 0:
                        tiles += [(4 * q - 4 + j, 0, 128 * (j + 1), ("lower", 128 * j, 0)) for j in range(4)]
                    rins = [self.RQT[u][q], RVSW] + RKW
                    self.attn_map(tiles,
                                  lambda c0, c1, rb=rb, u=u, q=q: self.QTt[u][rb, q * QT + c0:q * QT + c1],
                                  lambda kt, rb=rb: KW2[rb, kt * 128:(kt + 1) * 128],
                                  lambda kt, hh=hh: VSW[:, kt, 192 + 64 * hh:192 + 64 * hh + 128],
                                  0.125, rins, 7, fin=fin_w)
                self.pend_flush()

    def mixer_order(self):
        sel = getattr(self, 'mixers', None) or [0, 2, 1, 3]
        fns = {0: self.nsa, 1: self.fox, 2: self.mla, 3: self.diff}
        return [(m, fns[m]) for m in sel]

    def g_ada(self, l):
        kb = self.kb
        par = l % 2
        t_mod, Rmod, Rder = self.t_mods[par], self.Rmods[par], self.Rders[par]
        pb, Rpb = self.ps[7], self.Rps[7]
        av = self.ada_w[l].rearrange("(k p) n -> p k n", p=128)
        avk = self.ada_w[l].rearrange("(k p) n -> k p n", p=128)
        n = 0
        for k in range(8):
            for cb_ in range(12):
                wb, rw = self.adab[n % 2], self.Radab[n % 2]
                kb.dma_in("pool", rw, lambda e: e.dma_start(out=wb[:], in_=avk[k][:, cb_ * 512:(cb_ + 1) * 512]))
                for jj in range(4):
                    j = cb_ * 4 + jj
                    kb.op("pe", lambda e: e.matmul(pb[:, j:j + 1], lhsT=wb[:, jj * 128:(jj + 1) * 128], rhs=self.t_scb[:, k:k + 1],
                                                   start=(k == 0 and j == 0), stop=(k == 7), skip_group_check=True), [rw, self.Rscb], [Rpb])
                n += 1
                yield
        kb.op("dve", lambda e: e.tensor_tensor(out=t_mod[:], in0=pb[:, 0:48], in1=self.sm(l, "ada_b"), op=ALU.add),
              [Rpb, self.Rsm], [Rmod])
        d = lambda *a, **kw: self.der(*a, par=par, **kw)

        def ts(dst, src, s1, s2=None, op0=ALU.mult, op1=None):
            if s2 is None:
                kb.op("dve", lambda e: e.tensor_scalar(out=dst, in0=src, scalar1=s1, scalar2=None, op0=op0),
                      [Rmod, self.Rsm, Rder, self.Rcf], [Rder])
            else:
                kb.op("dve", lambda e: e.tensor_scalar(out=dst, in0=src, scalar1=s1, scalar2=s2, op0=op0, op1=op1),
                      [Rmod, self.Rsm, Rder, self.Rcf], [Rder])

        ts(d("a1"), t_mod[:, 8:16], 1.0, 32.0, ALU.add, ALU.mult)
        ts(d("a2"), t_mod[:, 32:40], 1.0, 32.0, ALU.add, ALU.mult)
        for nm, sc in (("nsa_gq", 8.0), ("nsa_gkc", 8.0), ("nsa_gks", 8.0), ("nsa_gkw", 8.0), ("fox_gq", 8.0), ("fox_gk", 8.0),
                       ("mla_cqg", 16.0), ("mla_ckvg", math.sqrt(128.0)), ("mla_gq", math.sqrt(96.0)), ("mla_gk", math.sqrt(96.0)),
                       ("dif_gq", math.sqrt(32.0)), ("dif_gk", math.sqrt(32.0)), ("dif_og", self.cf("lamc", 2 * l, 2 * l + 1))):
            ts(d(nm), self.sm(l, nm), sc)
        ts(d("negfb"), self.sm(l, "fox_fb"), -1.0)
        yield
        if not hasattr(self, "t_lt"):
            self.t_lt = kb.sb("t_lt", [128, 64], F32)
            self.Rlt = Res("lt")
        lt = self.t_lt
        kb.op("dve", lambda e: e.tensor_tensor(out=lt[:, 0:32], in0=self.sm(l, "dif_lam", 0, 32), in1=self.sm(l, "dif_lam", 32, 64), op=ALU.mult),
              [self.Rsm], [self.Rlt])
        kb.op("dve", lambda e: e.tensor_tensor(out=lt[:, 32:64], in0=self.sm(l, "dif_lam", 64, 96), in1=self.sm(l, "dif_lam", 96, 128), op=ALU.mult),
              [self.Rsm], [self.Rlt])
        kb.op("dve", lambda e: e.tensor_reduce(out=d("t0", 0, 2), in_=lt[:].rearrange("p (a b) -> p a b", a=2), axis=mybir.AxisListType.X, op=ALU.add),
              [self.Rlt], [Rder])
        kb.op("act", lambda e: e.activation(out=d("t1", 0, 2), in_=d("t0", 0, 2), func=AF.Exp), [Rder], [Rder])
        kb.op("dve", lambda e: e.tensor_tensor(out=d("t0", 2, 3), in0=d("t1", 1, 2), in1=d("t1", 0, 1), op=ALU.subtract), [Rder], [Rder])
        kb.op("dve", lambda e: e.tensor_scalar(out=d("neglam"), in0=d("t0", 2, 3), scalar1=self.cf("lamc", 2 * l + 1, 2 * l + 2), scalar2=None, op0=ALU.add),
              [Rder, self.Rcf], [Rder])
        yield

    def ada_step(self, n=1):
        for _ in range(n):
            if self.ada_gen is not None:
                try:
                    next(self.ada_gen)
                except StopIteration:
                    self.ada_gen = None

    def norm_mod(self, l, which):
        self.mark('norm')
        kb = self.kb
        acol = "a1" if which == 0 else "a2"
        shc = 0 if which == 0 else 24
        with contextlib.ExitStack() as ns:
            sq = [kb.sb(f"nsq{i}", [128, QT], BF16, ns) for i in range(2)]
            Rsq = [Res(f"nsq{i}") for i in range(2)]
            rstd = kb.sb("nrstd", [128, QT], F32, ns)
            Rrstd = Res("nrstd")
            tmp = [kb.sb(f"ntmp{i}", [128, QT], F32, ns) for i in range(2)]
            Rtmp = [Res(f"ntmp{i}") for i in range(2)]
            for q in range(NQ):
                cs = slice(q * QT, (q + 1) * QT)
                pb, Rpb = self.ps[q % 2], self.Rps[q % 2]
                for k in range(8):
                    eng = "pool" if k % 2 == 0 else "dve"
                    kb.op(eng, lambda e: e.tensor_tensor(out=sq[k % 2][:], in0=self.xT[k][:, cs], in1=self.xT[k][:, cs], op=ALU.mult),
                          [self.Rx[k][q]], [Rsq[k % 2]])
                    self.mm(pb[:], self.cbf("ones"), sq[k % 2][:], k == 0, k == 7, [Rsq[k % 2], self.Rcb], Rpb)
                kb.op("act", lambda e: e.activation(out=rstd[:], in_=pb[:], func=AF.Ln, bias=self.cf("epsc", 0, 1), scale=1.0), [Rpb, self.Rcf], [Rrstd])
                kb.op("act", lambda e: e.activation(out=rstd[:], in_=rstd[:], func=AF.Exp, scale=-0.5), [Rrstd], [Rrstd])
                for k in range(8):
                    kb.op("dve", lambda e: e.tensor_tensor(out=tmp[k % 2][:], in0=self.xT[k][:, cs], in1=rstd[:], op=ALU.mult),
                          [self.Rx[k][q], Rrstd], [Rtmp[k % 2]])
                    kb.op("act", lambda e: e.activation(out=self.hT[k][:, cs], in_=tmp[k % 2][:], func=AF.Identity,
                                                        scale=self.der(acol, k, k + 1), bias=self.t_mod[:, shc + k:shc + k + 1]),
                          [Rtmp[k % 2], self.Rder, self.Rmod], [self.Rh[k][q]])
            kb.barrier()
        self.dump(f"h{which}_0", self.hT[0][:], self.Rh[0], [128, S])
        self.dump(f"h{which}_7", self.hT[7][:], self.Rh[7], [128, S])

    def mixer_scratch(self, ms, nq=1, nk=1, va=True):
        kb = self.kb
        self.sqb = [kb.sb(f"sqb{i}", [128, QT], BF16, ms) for i in range(2)]
        self.Rsqb = [Res(f"sqb{i}") for i in range(2)]
        self.rstd = [kb.sb(f"rstd{i}", [128, QT], F32, ms) for i in range(2)]
        self.Rrstd = [Res(f"rstd{i}") for i in range(2)]
        self.rt1 = [kb.sb(f"rt1_{i}", [128, QT], BF16, ms) for i in range(2)]
        self.Rrt1 = [Res(f"rt1_{i}") for i in range(2)]
        self.rt2 = [kb.sb(f"rt2_{i}", [128, QT], BF16, ms) for i in range(2)]
        self.Rrt2 = [Res(f"rt2_{i}") for i in range(2)]
        self.PT = [kb.sb(f"PT{i}", [128, QT], BF16, ms) for i in range(6)]
        self.RPT = [Res(f"PT{i}") for i in range(6)]
        self.rec = [kb.sb(f"rec{i}", [128, QT], F32, ms) for i in range(2)]
        self.Rrec = [Res(f"rec{i}") for i in range(2)]
        self.QTt = [kb.sb(f"QTt{i}", [128, S], BF16, ms) for i in range(nq)]
        self.RQT = [[Res(f"QT{i}_{q}") for q in range(NQ)] for i in range(nq)]
        self.KTt = [kb.sb(f"KTt{i}", [128, S], BF16, ms) for i in range(nk)]
        self.RKT = [[Res(f"KT{i}_{q}") for q in range(NQ)] for i in range(nk)]
        self.pend = []
        self.bg = None
        self.hn_i = 0
        self.pt_i = 0
        self.sb_i = 0
        self.rec_i = 0
        if va:
            self.VA = kb.sb("VA", [128, NKT, 256], BF16, ms)
            self.RVA = [Res(f"VA{g}") for g in range(4)]
            kb.op("pool", lambda e: e.memset(self.VA[:, :, 64:192], 1.0), [], self.RVA)

    def headnorm(self, *a, **kw):
        for _ in self.g_headnorm(*a, **kw):
            pass

    def g_headnorm(self, src_bank, nrows, blk, neps, gcol, dst, Rdst, cs, rope=None):
        kb = self.kb
        i = self.hn_i
        self.hn_i ^= 1
        rs = slice(0, nrows)
        src, Rsrc = self.ps[src_bank][rs, :], self.Rps[src_bank]
        pstat, Rpstat = self.ps[5], self.Rps[5]
        kb.op("act", lambda e: e.activation(out=self.sqb[i][rs, :], in_=src, func=AF.Square), [Rsrc], [self.Rsqb[i]])
        self.mm(pstat[rs, :], blk, self.sqb[i][rs, :], True, True, [self.Rsqb[i], self.Rcb], Rpstat)
        yield
        kb.op("act", lambda e: e.activation(out=self.rstd[i][rs, :], in_=pstat[rs, :], func=AF.Ln, bias=neps, scale=1.0), [Rpstat, self.Rcf], [self.Rrstd[i]])
        kb.op("act", lambda e: e.activation(out=self.rstd[i][rs, :], in_=self.rstd[i][rs, :], func=AF.Exp, scale=-0.5), [self.Rrstd[i]], [self.Rrstd[i]])
        yield
        kb.op("dve", lambda e: e.scalar_tensor_tensor(out=dst, in0=src, scalar=gcol, in1=self.rstd[i][rs, :], op0=ALU.mult, op1=ALU.mult),
              [Rsrc, self.Rrstd[i], self.Rder], [Rdst])
        if rope is None:
            yield
            return
        ci, swname, wins = rope
        pA, RpA = pstat, Rpstat
        pB, RpB = self.ps[src_bank], self.Rps[src_bank]
        self.mm(pA[rs, :], self.cbf("ident", rs, 0, nrows), dst, True, True, [Rdst, self.Rcb], RpA)
        self.mm(pB[rs, :], self.cbf(swname, rs, 0, nrows), dst, True, True, [Rdst, self.Rcb], RpB)
        yield
        tab = slice(32 * ci, 32 * ci + 32)
        for w0 in wins:
            ws = slice(w0, w0 + 32)
            kb.op("dve", lambda e: e.tensor_tensor(out=self.rt1[i][ws, :], in0=pA[ws, :], in1=self.ropeC[tab, cs], op=ALU.mult),
                  [RpA, self.Rrope], [self.Rrt1[i]])
            kb.op("dve", lambda e: e.tensor_tensor(out=self.rt2[i][ws, :], in0=pB[ws, :], in1=self.ropeS[tab, cs], op=ALU.mult),
                  [RpB, self.Rrope], [self.Rrt2[i]])
            kb.op("pool", lambda e: e.tensor_tensor(out=dst[ws, :], in0=self.rt1[i][ws, :], in1=self.rt2[i][ws, :], op=ALU.add),
                  [self.Rrt1[i], self.Rrt2[i]], [Rdst])
        yield

    def bg_step(self):
        if self.bg is not None:
            try:
                next(self.bg)
            except StopIteration:
                self.bg = None

    def bg_drain(self):
        while self.bg is not None:
            self.bg_step()

    def g_vgroup(self, g, w_ap_k, rw, ncol, evac, nk=8, lhs_fn=None, rl=None):
        pb, Rpb = self.ps[3 + g % 2], self.Rps[3 + g % 2]
        for t in range(4):
            kt = g * 4 + t
            for k in range(nk):
                lhs = self.hT[k][:, kt * 128:(kt + 1) * 128] if lhs_fn is None else lhs_fn(k, kt)
                rr = self.Rh[k][g] if rl is None else rl(k, g)
                self.mm(pb[:, t * ncol:(t + 1) * ncol], lhs, w_ap_k(k), k == 0, k == nk - 1, [rw, rr], Rpb)
            if t % 2 == 1:
                yield
        evac(g, pb, Rpb)
        yield

    def attn_map(self, tiles, q_ap, k_ap, va_ap, scale, rins, o_bank, extra=None, bias=None, fin=None, after_p=None):
        kb = self.kb
        nt = len(tiles)
        for ti, (kt, c0, c1, mask) in enumerate(tiles):
            n = c1 - c0
            si = self.sb_i
            self.sb_i = (self.sb_i + 1) % 5
            pi = self.pt_i
            self.pt_i = (self.pt_i + 1) % 6
            pS, RpS = self.ps[si], self.Rps[si]
            self.mm(pS[:, 0:n], k_ap(kt), q_ap(c0, c1), True, extra is None, rins, RpS)
            if extra is not None:
                l2, r2 = extra(kt, c0, c1)
                self.mm(pS[:, 0:n], l2, r2, False, True, rins, RpS)
            b = bias(kt) if bias is not None else None
            if b is None:
                kb.op("act", lambda e: e.activation(out=self.PT[pi][:, 0:n], in_=pS[:, 0:n], func=AF.Exp, scale=scale), [RpS], [self.RPT[pi]])
            else:
                kb.op("act", lambda e: e.activation(out=self.PT[pi][:, 0:n], in_=pS[:, 0:n], func=AF.Exp, scale=scale, bias=b),
                      [RpS] + list(rins), [self.RPT[pi]])
            if mask is not None:
                kind, m0, base = mask
                if kind == "causal":
                    kb.op("pool", lambda e: e.affine_select(out=self.PT[pi][:, m0:m0 + 128], in_=self.PT[pi][:, m0:m0 + 128], pattern=[[1, 128]],
                                                            compare_op=ALU.is_ge, fill=0.0, base=0, channel_multiplier=-1),
                          [self.RPT[pi]], [self.RPT[pi]])
                elif kind == "lower":
                    kb.op("pool", lambda e: e.affine_select(out=self.PT[pi][:, m0:m0 + 128], in_=self.PT[pi][:, m0:m0 + 128], pattern=[[-1, 128]],
                                                            compare_op=ALU.is_ge, fill=0.0, base=-1, channel_multiplier=1),
                          [self.RPT[pi]], [self.RPT[pi]])
                elif kind == "vis":
                    kb.op("pool", lambda e: e.affine_select(out=self.PT[pi][:, 0:n], in_=self.PT[pi][:, 0:n], pattern=[[1, n]],
                                                            compare_op=ALU.is_ge, fill=0.0, base=base, channel_multiplier=-16),
                          [self.RPT[pi]], [self.RPT[pi]])
            if after_p is not None:
                after_p(pi)

            def pv(kt=kt, c0=c0, c1=c1, n=n, pi=pi, first=(ti == 0), last=(ti == nt - 1)):
                self.mm(self.ps[o_bank][:, c0:c1], va_ap(kt), self.PT[pi][:, 0:n], first, last,
                        [self.RPT[pi]] + list(rins), self.Rps[o_bank])

            self.pend.append((pv, fin if ti == nt - 1 else None))
            while len(self.pend) > self.LA:
                self.pend_pop()

    LA = 4

    def pend_pop(self):
        pv, fin = self.pend.pop(0)
        pv()
        if fin is not None:
            fin()

    def pend_flush(self):
        while self.pend:
            self.pend_pop()

    @staticmethod
    def causal_tiles(q):
        tiles = [(kt, 0, QT, None) for kt in range(4 * q)]
        for j in range(4):
            tiles.append((4 * q + j, 128 * j, QT, ("causal", 0, 0)))
        return tiles

    def finalize(self, o_bank, parity, dst, Rdst, eps=None):
        kb = self.kb
        i = self.rec_i
        self.rec_i ^= 1
        pO, RpO = self.ps[o_bank], self.Rps[o_bank]
        nr = slice(0, 64) if parity == 0 else slice(64, 128)
        dr = slice(64, 128) if parity == 0 else slice(0, 64)
        if eps is None:
            kb.op("dve", lambda e: e.reciprocal(out=self.rec[i][nr, :], in_=pO[dr, :]), [RpO], [self.Rrec[i]])
        else:
            kb.op("dve", lambda e: e.tensor_scalar(out=self.rec[i][nr, :], in0=pO[dr, :], scalar1=eps, scalar2=None, op0=ALU.add), [RpO], [self.Rrec[i]])
            kb.op("dve", lambda e: e.reciprocal(out=self.rec[i][nr, :], in_=self.rec[i][nr, :]), [self.Rrec[i]], [self.Rrec[i]])
        if dst is not None:
            kb.op("dve", lambda e: e.tensor_tensor(out=dst, in0=pO[nr, :], in1=self.rec[i][nr, :], op=ALU.mult), [RpO, self.Rrec[i]], [Rdst])
        return i, nr

    def proj_T(self, pb, Rpb, rows, w_ap_k, rw, q, nk=8, rhs_fn=None, rrhs=None):
        cs = slice(q * QT, (q + 1) * QT)
        for k in range(nk):
            rhs = self.hT[k][:, cs] if rhs_fn is None else rhs_fn(k)
            rr = self.Rh[k][q] if rrhs is None else rrhs(k)
            self.mm(pb[rows, :], w_ap_k(k), rhs, k == 0, k == nk - 1, [rw, rr], Rpb)

    def v_proj(self, w_ap_k, rw, ncol, evac, nk=8, lhs_fn=None, rl=None):
        for g in range(4):
            pb, Rpb = self.ps[3 + g % 2], self.Rps[3 + g % 2]
            for t in range(4):
                kt = g * 4 + t
                for k in range(nk):
                    lhs = self.hT[k][:, kt * 128:(kt + 1) * 128] if lhs_fn is None else lhs_fn(k, kt)
                    rr = self.Rh[k][g] if rl is None else rl(k, g)
                    self.mm(pb[:, t * ncol:(t + 1) * ncol], lhs, w_ap_k(k), k == 0, k == nk - 1, [rw, rr], Rpb)
            evac(g, pb, Rpb)

    def fox(self, l, fs):
        kb = self.kb
        self.mark('fox_prelude')
        fs0 = fs
        DQ = kb.sb("foxDQ", [128, S], BF16, fs)
        RDQ = [Res(f"foxDQ{q}") for q in range(NQ)]
        Dk = kb.sb("foxDk", [128, NKT, 4], F32, fs)
        RDk = Res("foxDk")
        with contextlib.ExitStack() as fs:
            Dt = kb.sb("foxD", [128, S], F32, fs)
            RD = [Res(f"foxD{q}") for q in range(NQ)]
            ft = [kb.sb(f"foxft{i}", [128, QT], F32, fs) for i in range(2)]
            Rft = [Res(f"foxft{i}") for i in range(2)]
            onesf = kb.sb("foxones", [128, QT], F32, fs)
            Rones = Res("foxones")
            kb.op("pool", lambda e: e.memset(onesf[:], 1.0), [], [Rones])
            wf, rwf = self.load_w(self.wf_pad[l].rearrange("(k p) n -> p k n", p=128), 128)
            for q in range(NQ):
                cs = slice(q * QT, (q + 1) * QT)
                pb, Rpb = self.ps[3 + q % 2], self.Rps[3 + q % 2]
                self.proj_T(pb, Rpb, slice(0, 128), lambda k: wf[:, k, 0:128], rwf, q)
                kb.op("act", lambda e: e.activation(out=ft[0][:], in_=pb[:], func=AF.Exp, scale=-1.0, bias=self.der("negfb")),
                      [Rpb, self.Rder], [Rft[0]])
                kb.op("act", lambda e: e.activation(out=ft[1][:], in_=ft[0][:], func=AF.Ln, bias=self.cf("epsc", 6, 7), scale=1.0), [Rft[0], self.Rcf], [Rft[1]])
                if q == 0:
                    kb.op("dve", lambda e: e.tensor_tensor_scan(out=Dt[:, cs], data0=onesf[:], data1=ft[1][:], initial=0.0, op0=ALU.mult, op1=ALU.add),
                          [Rones, Rft[1]], [RD[q]])
                else:
                    kb.op("dve", lambda e: e.tensor_tensor_scan(out=Dt[:, cs], data0=onesf[:], data1=ft[1][:], initial=Dt[:, q * QT - 1:q * QT],
                                                                op0=ALU.mult, op1=ALU.add), [Rones, Rft[1], RD[q - 1]], [RD[q]])
                kb.op("pool", lambda e: e.tensor_scalar(out=DQ[:, cs], in0=Dt[:, cs], scalar1=-8.0, scalar2=None, op0=ALU.mult), [RD[q]], [RDQ[q]])
                pt, Rpt = self.ps[5], self.Rps[5]
                for t in range(4):
                    kb.op("pe", lambda e: e.transpose(out=pt[:, t * 128:(t + 1) * 128], in_=Dt[:, q * QT + t * 128:q * QT + (t + 1) * 128],
                                                      identity=self.cf("ident")), [RD[q], self.Rcf], [Rpt])
                kb.op("dve", lambda e: e.tensor_copy(out=Dk[:, q * 4:(q + 1) * 4, :], in_=pt[:].rearrange("p (t h r) -> p t h r", t=4, h=4)[:, :, :, 0]),
                      [Rpt], [RDk])
            self.dump("foxD", Dt[:], RD, [128, S])
            kb.barrier()
            if self.dbg:
                kb.e["act"].wait_ge(self.osem[0], self.osem[1])
                kb.barrier()
        if True:
            self.mixer_scratch(fs0)
            self.mark('fox_units')
            fox_w = []
            for u in range(2):
                wqk_, rwqk_ = self.next_wb()
                kb.dma_in("pool", rwqk_, lambda e: [e.dma_start(out=wqk_[:, :, 0:128], in_=self.win_cols(l, C_FOX_Q + u * 128, 128)),
                                                    e.dma_start(out=wqk_[:, :, 128:256], in_=self.win_cols(l, C_FOX_K + u * 128, 128)),
                                                    e.dma_start(out=wqk_[:, :, 256:384], in_=self.win_cols(l, C_FOX_V + u * 128, 128))])
                fox_w.append((wqk_, rwqk_))
            for u in range(2):
                wqk, rwqk = fox_w[u]

                def evac(g, pb, Rpb):
                    src = pb[:].rearrange("p (t a c) -> p t a c", t=4, a=2)
                    dstv = self.VA[:, g * 4:(g + 1) * 4, :].rearrange("p t (a c) -> p t a c", a=4)[:, :, 0::3, :]
                    kb.op("dve", lambda e: e.tensor_copy(out=dstv, in_=src), [Rpb], [self.RVA[g]])

                def pre(q):
                    cs = slice(q * QT, (q + 1) * QT)
                    self.proj_T(self.ps[3], self.Rps[3], slice(0, 128), lambda k: wqk[:, k, 0:128], rwqk, q)
                    yield
                    yield from self.g_headnorm(3, 128, self.cbf("b64"), self.cf("epsc", 1, 2), self.der("fox_gq"), self.QTt[0][:, cs], self.RQT[0][q], cs)
                    self.proj_T(self.ps[4], self.Rps[4], slice(0, 128), lambda k: wqk[:, k, 128:256], rwqk, q)
                    yield
                    yield from self.g_headnorm(4, 128, self.cbf("b64"), self.cf("epsc", 1, 2), self.der("fox_gk"), self.KTt[0][:, cs], self.RKT[0][q], cs)
                    yield from self.g_vgroup(q, lambda k: wqk[:, k, 256:384], rwqk, 128, evac)

                self.bg = pre(0)
                self.bg_drain()
                for q in range(NQ):
                    cs = slice(q * QT, (q + 1) * QT)
                    if q + 1 < NQ:
                        self.bg = pre(q + 1)
                        self.bg_drain()
                    for hh in range(2):
                        h = 2 * u + hh
                        rb = slice(64 * hh, 64 * hh + 64)
                        ob = 6 + hh
                        rins = [self.RQT[0][q], RDk, RDQ[q], self.Rcb] + self.RKT[0][:q + 1] + self.RVA[:q + 1]
                        self.attn_map(
                            self.causal_tiles(q),
                            lambda c0, c1, rb=rb, q=q: self.QTt[0][rb, q * QT + c0:q * QT + c1],
                            lambda kt, rb=rb: self.KTt[0][rb, kt * 128:(kt + 1) * 128],
                            lambda kt, hh=hh: self.VA[:, kt, 128 * hh:128 * hh + 128],
                            0.125, rins, ob,
                            extra=lambda kt, c0, c1, h=h, q=q: (self.cbf("selrow", slice(0, 128), 128 * h, 128 * h + 128), DQ[:, q * QT + c0:q * QT + c1]),
                            bias=lambda kt, h=h: Dk[:, kt, h:h + 1],
                            fin=lambda ob=ob, hh=hh, u=u, rb=rb, cs=cs, q=q: self.finalize(ob, hh, self.oT[1][u][rb, cs], self.RoT[1][u][q]))
                    self.bg_drain()
                self.pend_flush()

    def merge(self, l):
        self.mark('merge')
        kb = self.kb
        with contextlib.ExitStack() as ms:
            MT = [kb.sb(f"MT{i}", [128, S], BF16, ms) for i in range(4)]
            RMT = [[Res(f"MT{i}_{q}") for q in range(NQ)] for i in range(4)]
            brw = [kb.sb(f"brw{i}", [128, 2, 4, 128], BF16, ms) for i in range(2)]
            Rbrw = [Res(f"brw{i}") for i in range(2)]
            wo = [kb.sb(f"wo{i}", [128, 4, 128], BF16, ms) for i in range(2)]
            Rwo = [Res(f"wo{i}") for i in range(2)]
            sig = [kb.sb(f"sig{i}", [128, QT], F32, ms) for i in range(2)]
            Rsig = [Res(f"sig{i}") for i in range(2)]
            tmp = [kb.sb(f"mtmp{i}", [128, QT], F32, ms) for i in range(2)]
            Rtmp = [Res(f"mtmp{i}") for i in range(2)]
            acc = [kb.sb(f"macc{i}", [128, QT], F32, ms) for i in range(2)]
            Racc = [Res(f"macc{i}") for i in range(2)]
            gwv = self.gate_w[l].rearrange("(k p) (m n) -> p k m n", p=128, m=4)
            brv = self.br_w[l].rearrange("m (k p) n -> p k m n", p=128)
            wov = self.w_out[l].rearrange("(k p) n -> p k n", p=128)
            n_it = 0
            loaded = {}

            def load_dc(dc):
                if dc in loaded or dc >= 8:
                    return
                wb_, rgw_ = self.next_wb()
                kb.dma_in("pool", rgw_, lambda e: [e.dma_start(out=wb_[:, :, m_ * 128:(m_ + 1) * 128], in_=gwv[:, :, m_, dc * 128:(dc + 1) * 128]) for m_ in range(4)])
                bi_ = dc % 2
                kb.dma_in("pool", Rbrw[bi_], lambda e: [e.dma_start(out=brw[bi_][:, :, m_, :], in_=brv[:, :, m_, dc * 128:(dc + 1) * 128]) for m_ in range(4)])
                loaded[dc] = (wb_, rgw_)

            wo_loaded = {}

            def load_wo(grp, dout):
                key = grp * 8 + dout
                if key in wo_loaded or dout >= 8:
                    return
                wi_ = key % 2
                kb.dma_in("pool", Rwo[wi_], lambda e: e.dma_start(out=wo[wi_][:], in_=wov[:, grp * 4:(grp + 1) * 4, dout * 128:(dout + 1) * 128]))
                wo_loaded[key] = wi_

            for grp in range(2):
                for dcl in range(4):
                    dc = grp * 4 + dcl
                    load_dc(dc)
                    if dcl < 3:
                        load_dc(dc + 1)
                    wb, rgw = loaded[dc]
                    bi = dc % 2
                    for q in range(NQ):
                        cs = slice(q * QT, (q + 1) * QT)
                        ai = n_it % 2
                        n_it += 1
                        for m in range(4):
                            pg, Rpg = self.ps[m % 2], self.Rps[m % 2]
                            py, Rpy = self.ps[2 + m % 2], self.Rps[2 + m % 2]
                            for k in range(8):
                                self.mm(pg[:], wb[:, k, m * 128:(m + 1) * 128], self.hT[k][:, cs], k == 0, k == 7, [rgw, self.Rh[k][q]], Rpg)
                            for k in range(2):
                                self.mm(py[:], brw[bi][:, k, m, :], self.oT[m][k][:, cs], k == 0, k == 1, [Rbrw[bi], self.RoT[m][k][q]], Rpy)
                            si = m % 2
                            kb.op("act", lambda e: e.activation(out=sig[si][:], in_=pg[:], func=AF.Sigmoid, bias=self.sm(l, "gate_b", m * 8 + dc, m * 8 + dc + 1), scale=1.0),
                                  [Rpg, self.Rsm], [Rsig[si]])
                            if m == 0:
                                kb.op("dve", lambda e: e.tensor_tensor(out=acc[ai][:], in0=py[:], in1=sig[si][:], op=ALU.mult), [Rpy, Rsig[si]], [Racc[ai]])
                            else:
                                kb.op("dve", lambda e: e.tensor_tensor(out=tmp[si][:], in0=py[:], in1=sig[si][:], op=ALU.mult), [Rpy, Rsig[si]], [Rtmp[si]])
                                if m < 3:
                                    kb.op("dve", lambda e: e.tensor_tensor(out=acc[ai][:], in0=acc[ai][:], in1=tmp[si][:], op=ALU.add), [Racc[ai], Rtmp[si]], [Racc[ai]])
                                else:
                                    kb.op("dve", lambda e: e.tensor_tensor(out=MT[dcl][:, cs], in0=acc[ai][:], in1=tmp[si][:], op=ALU.add),
                                          [Racc[ai], Rtmp[si]], [RMT[dcl][q]])
                if l == 0 and grp == 0:
                    self.dump("merged0", MT[0][:], RMT[0], [128, S])
                for dout in range(8):
                    load_wo(grp, dout)
                    load_wo(grp, dout + 1)
                    if dout == 7 and grp == 0:
                        load_dc(4)
                    wi = wo_loaded[grp * 8 + dout]
                    for q in range(NQ):
                        cs = slice(q * QT, (q + 1) * QT)
                        pb, Rpb = self.ps[4 + (dout * NQ + q) % 2], self.Rps[4 + (dout * NQ + q) % 2]
                        for dcl in range(4):
                            self.mm(pb[:], wo[wi][:, dcl, :], MT[dcl][:, cs], dcl == 0, dcl == 3, [Rwo[wi], RMT[dcl][q]], Rpb)
                        kb.op("dve", lambda e: e.scalar_tensor_tensor(out=self.xT[dout][:, cs], in0=pb[:], scalar=self.t_mod[:, 16 + dout:17 + dout],
                                                                      in1=self.xT[dout][:, cs], op0=ALU.mult, op1=ALU.add),
                              [Rpb, self.Rmod, self.Rx[dout][q]], [self.Rx[dout][q]])
            kb.barrier()

    def ffn(self, l):
        kb = self.kb
        self.mark('ffn')
        NJ = NFF // 2
        with contextlib.ExitStack() as fs:
            AT = [kb.sb(f"AT{j}", [128, S], BF16, fs) for j in range(NJ)]
            RAT = [[Res(f"AT{j}_{q}") for q in range(NQ)] for j in range(NJ)]
            G = [kb.sb(f"G{i}", [128, QT + 2], F32, fs) for i in range(2)]
            RG = [Res(f"G{i}") for i in range(2)]
            GC = [kb.sb(f"GC{i}", [128, QT], F32, fs) for i in range(2)]
            RGC = [Res(f"GC{i}") for i in range(2)]
            wd = [kb.sb(f"wd{i}", [128, NJ, 128], BF16, fs) for i in range(3)]
            Rwd = [Res(f"wd{i}") for i in range(3)]
            wd_n = 0
            gn = 0
            upv = self.w_up[l].rearrange("(k p) n -> p k n", p=128)
            dnv = self.w_down[l].rearrange("(j p) n -> p j n", p=128)
            if l + 1 < self.n_layers:
                self.ada_gen = self.g_ada(l + 1)
            up_loaded = {}

            def load_up(j0):
                if j0 in up_loaded or j0 >= NFF:
                    return
                nj_ = 1 if (j0 % NJ) == NJ - 1 else 2
                wb_, rwu_ = self.next_wb()
                kb.dma_in("pool", rwu_, lambda e: [e.dma_start(out=wb_[:, :, 0:128 * nj_], in_=upv[:, :, j0 * 128:(j0 + nj_) * 128]),
                                                   e.dma_start(out=wb_[:, :, 256:256 + 128 * nj_], in_=upv[:, :, DFF + j0 * 128:DFF + (j0 + nj_) * 128])])
                up_loaded[j0] = (wb_, rwu_)

            wd_loaded = {}

            def load_wd(ps__, dout):
                key = ps__ * 8 + dout
                if key in wd_loaded or dout >= 8:
                    return
                wi_ = key % 3
                kb.dma_in("pool", Rwd[wi_], lambda e: e.dma_start(out=wd[wi_][:], in_=dnv[:, ps__ * NJ:(ps__ + 1) * NJ, dout * 128:(dout + 1) * 128]))
                wd_loaded[key] = wi_

            for ps_ in range(2):
                wb, rwu = None, None
                for jj in range(NJ):
                    j = ps_ * NJ + jj
                    self.ada_step(3)
                    if jj % 2 == 0:
                        load_up(j)
                        nxt = j + 2
                        if jj + 2 < NJ:
                            load_up(nxt)
                        elif ps_ == 0:
                            pass
                        wb, rwu = up_loaded[j]
                    if jj == NJ - 1:
                        load_wd(ps_, 0)
                    co = 128 * (jj % 2)
                    w0 = self.sm(l, "conv_w", 0 * NFF + j, 0 * NFF + j + 1)
                    w1 = self.sm(l, "conv_w", 1 * NFF + j, 1 * NFF + j + 1)
                    w2 = self.sm(l, "conv_w", 2 * NFF + j, 2 * NFF + j + 1)
                    cb_ = self.sm(l, "conv_b", j, j + 1)
                    for q in range(NQ):
                        gi = gn % 2
                        gn += 1
                        cs = slice(q * QT, (q + 1) * QT)
                        pg, Rpg = self.ps[gi], self.Rps[gi]
                        pv, Rpv = self.ps[2 + gi], self.Rps[2 + gi]
                        for k in range(8):
                            self.mm(pg[:], wb[:, k, co:co + 128], self.hT[k][:, cs], k == 0, k == 7, [rwu, self.Rh[k][q]], Rpg)
                        for k in range(8):
                            self.mm(pv[:], wb[:, k, 256 + co:256 + co + 128], self.hT[k][:, cs], k == 0, k == 7, [rwu, self.Rh[k][q]], Rpv)
                        if q == 0:
                            kb.op("dve", lambda e: e.memset(G[gi][:, 0:2], 0.0), [], [RG[gi]])
                        else:
                            kb.op("dve", lambda e: e.tensor_copy(out=G[gi][:, 0:2], in_=G[1 - gi][:, QT:QT + 2]), [RG[1 - gi]], [RG[gi]])
                        kb.op("act", lambda e: e.activation(out=G[gi][:, 2:2 + QT], in_=pg[:], func=AF.Copy), [Rpg], [RG[gi]])
                        kb.op("act", lambda e: e.activation(out=GC[gi][:], in_=pg[:], func=AF.Identity, scale=w2, bias=cb_),
                              [Rpg, self.Rsm], [RGC[gi]])
                        kb.op("dve", lambda e: e.scalar_tensor_tensor(out=GC[gi][:], in0=G[gi][:, 1:1 + QT], scalar=w1, in1=GC[gi][:], op0=ALU.mult, op1=ALU.add),
                              [RG[gi], self.Rsm, RGC[gi]], [RGC[gi]])
                        kb.op("dve", lambda e: e.scalar_tensor_tensor(out=GC[gi][:], in0=G[gi][:, 0:QT], scalar=w0, in1=GC[gi][:], op0=ALU.mult, op1=ALU.add),
                              [RG[gi], self.Rsm, RGC[gi]], [RGC[gi]])
                        kb.op("act", lambda e: e.activation(out=GC[gi][:], in_=GC[gi][:], func=AF.Silu), [RGC[gi]], [RGC[gi]])
                        kb.op("dve", lambda e: e.tensor_tensor(out=AT[jj][:, cs], in0=pv[:], in1=GC[gi][:], op=ALU.mult),
                              [Rpv, RGC[gi]], [RAT[jj][q]])
                for dout in range(8):
                    self.ada_step(3)
                    load_wd(ps_, dout)
                    load_wd(ps_, dout + 1)
                    if dout == 6 and ps_ == 0:
                        load_up(NJ)
                    wi = wd_loaded[ps_ * 8 + dout]
                    for q in range(NQ):
                        cs = slice(q * QT, (q + 1) * QT)
                        pb, Rpb = self.ps[4 + q % 2], self.Rps[4 + q % 2]
                        for jj in range(NJ):
                            self.mm(pb[:], wd[wi][:, jj, :], AT[jj][:, cs], jj == 0, jj == NJ - 1, [Rwd[wi], RAT[jj][q]], Rpb)
                        kb.op("dve", lambda e: e.scalar_tensor_tensor(out=self.xT[dout][:, cs], in0=pb[:], scalar=self.t_mod[:, 40 + dout:41 + dout],
                                                                      in1=self.xT[dout][:, cs], op0=ALU.mult, op1=ALU.add),
                              [Rpb, self.Rmod, self.Rx[dout][q]], [self.Rx[dout][q]])
            kb.barrier()

    def rstd_from(self, pstat_ap, out_ap, neps, Rin, Rout):
        kb = self.kb
        kb.op("act", lambda e: e.activation(out=out_ap, in_=pstat_ap, func=AF.Ln, bias=neps, scale=1.0), [Rin, self.Rcf], [Rout])
        kb.op("act", lambda e: e.activation(out=out_ap, in_=out_ap, func=AF.Exp, scale=-0.5), [Rout], [Rout])

    def mla(self, l, ms):
        kb = self.kb
        self.mark('mla_prelude')
        cqn = [kb.sb(f"cqn{i}", [128, S], BF16, ms) for i in range(2)]
        Rcqn = [[Res(f"cqn{i}_{q}") for q in range(NQ)] for i in range(2)]
        ckvn = kb.sb("ckvn", [128, S], BF16, ms)
        Rckvn = [Res(f"ckvn{q}") for q in range(NQ)]
        wuq = kb.sb("wuq", [128, 2, 384], BF16, ms)
        wukv = kb.sb("wukv", [128, 512], BF16, ms)
        Rwuq, Rwukv = Res("wuq"), Res("wukv")
        kb.dma_in("pool", Rwuq, lambda e: e.dma_start(out=wuq[:], in_=self.w_uq[l].rearrange("(k p) n -> p k n", p=128)))
        kb.dma_in("pool", Rwukv, lambda e: e.dma_start(out=wukv[:], in_=self.w_ukv[l]))
        self.mixer_scratch(ms)
        wc, rwc = self.load_w(self.win_cols(l, C_MLA_CQ, 416), 416)
        pstat, Rpstat = self.ps[5], self.Rps[5]
        for q in range(NQ):
            cs = slice(q * QT, (q + 1) * QT)
            for c in range(2):
                self.proj_T(self.ps[3 + c], self.Rps[3 + c], slice(0, 128), lambda k: wc[:, k, c * 128:(c + 1) * 128], rwc, q)
                kb.op("act", lambda e: e.activation(out=self.sqb[c][:], in_=self.ps[3 + c][:], func=AF.Square), [self.Rps[3 + c]], [self.Rsqb[c]])
                self.mm(pstat[:], self.cbf("ones"), self.sqb[c][:], c == 0, c == 1, [self.Rsqb[c], self.Rcb], Rpstat)
            self.rstd_from(pstat[:], self.rstd[0][:], self.cf("epsc", 4, 5), Rpstat, self.Rrstd[0])
            for c in range(2):
                kb.op("dve", lambda e: e.scalar_tensor_tensor(out=cqn[c][:, cs], in0=self.ps[3 + c][:], scalar=self.der("mla_cqg", c, c + 1),
                                                              in1=self.rstd[0][:], op0=ALU.mult, op1=ALU.mult),
                      [self.Rps[3 + c], self.Rrstd[0], self.Rder], [Rcqn[c][q]])
            self.proj_T(self.ps[3], self.Rps[3], slice(0, 128), lambda k: wc[:, k, 256:384], rwc, q)
            self.headnorm(3, 128, self.cbf("ones"), self.cf("epsc", 5, 6), self.der("mla_ckvg"), ckvn[:, cs], Rckvn[q], cs)
        self.mark('mla_heads')
        r96 = slice(0, 96)
        for h in range(4):
            hh = h % 2
            vcol = 0 if hh == 0 else 192

            def evac(g, pb, Rpb, vcol=vcol):
                kb.op("dve", lambda e: e.tensor_copy(out=self.VA[:, g * 4:(g + 1) * 4, vcol:vcol + 64], in_=pb[:, 0:256].rearrange("p (t c) -> p t c", t=4)),
                      [Rpb], [self.RVA[g]])

            def pre(q, h=h):
                cs = slice(q * QT, (q + 1) * QT)
                for k in range(2):
                    self.mm(self.ps[3][0:64, :], wuq[:, k, 96 * h + 32:96 * h + 96], cqn[k][:, cs], k == 0, k == 1, [Rwuq, Rcqn[k][q]], self.Rps[3])
                for k in range(2):
                    self.mm(self.ps[3][64:96, :], wuq[:, k, 96 * h:96 * h + 32], cqn[k][:, cs], k == 0, k == 1, [Rwuq, Rcqn[k][q]], self.Rps[3])
                yield
                yield from self.g_headnorm(3, 96, self.cbf("ones", r96, 0, 96), self.cf("epsc", 2, 3, r96), self.der("mla_gq", 0, 1, r96), self.QTt[0][r96, cs],
                                           self.RQT[0][q], cs, rope=(1, "sw_mla", [64]))
                self.mm(self.ps[4][0:64, :], wukv[:, 128 * h:128 * h + 64], ckvn[:, cs], True, True, [Rwukv, Rckvn[q]], self.Rps[4])
                self.proj_T(self.ps[4], self.Rps[4], slice(64, 96), lambda k: wc[:, k, 384:416], rwc, q)
                yield
                yield from self.g_headnorm(4, 96, self.cbf("ones", r96, 0, 96), self.cf("epsc", 2, 3, r96), self.der("mla_gk", 0, 1, r96), self.KTt[0][r96, cs],
                                           self.RKT[0][q], cs, rope=(1, "sw_mla", [64]))
                yield from self.g_vgroup(q, lambda k: wukv[:, 128 * h + 64:128 * h + 128], Rwukv, 64, evac, nk=1,
                                         lhs_fn=lambda k, kt: ckvn[:, kt * 128:(kt + 1) * 128], rl=lambda k, g: Rckvn[g])

            self.bg = pre(0)
            self.bg_drain()
            for q in range(NQ):
                cs = slice(q * QT, (q + 1) * QT)
                if q + 1 < NQ:
                    self.bg = pre(q + 1)
                    self.bg_drain()
                rb = slice(64 * hh, 64 * hh + 64)
                ob = 6 + q % 2
                rins = [self.RQT[0][q]] + self.RKT[0][:q + 1] + self.RVA[:q + 1]
                self.attn_map(self.causal_tiles(q),
                              lambda c0, c1, q=q: self.QTt[0][r96, q * QT + c0:q * QT + c1],
                              lambda kt: self.KTt[0][r96, kt * 128:(kt + 1) * 128],
                              lambda kt, hh=hh: self.VA[:, kt, 128 * hh:128 * hh + 128],
                              96.0 ** -0.5, rins, ob,
                              fin=lambda ob=ob, hh=hh, h=h, rb=rb, cs=cs, q=q: self.finalize(ob, hh, self.oT[2][h // 2][rb, cs], self.RoT[2][h // 2][q]))
                self.bg_drain()
            self.pend_flush()

    def diff(self, l, ms):
        kb = self.kb
        self.mark('diff')
        self.mixer_scratch(ms)
        dsq = kb.sb("dsq", [128, QT], BF16, ms)
        Rdsq = Res("dsq")
        r64 = slice(0, 64)
        pstat, Rpstat = self.ps[5], self.Rps[5]
        dif_w = {}

        def load_dif(h_):
            if h_ in dif_w or h_ >= 4:
                return
            w_, r_ = self.next_wb()
            kb.dma_in("pool", r_, lambda e: [e.dma_start(out=w_[:, :, 0:64], in_=self.win_cols(l, C_DIF_Q + 64 * h_, 64)),
                                             e.dma_start(out=w_[:, :, 64:128], in_=self.win_cols(l, C_DIF_K + 64 * h_, 64)),
                                             e.dma_start(out=w_[:, :, 128:192], in_=self.win_cols(l, C_DIF_V + 64 * h_, 64))])
            dif_w[h_] = (w_, r_)

        for h in range(4):
            hh = h % 2
            load_dif(h)
            load_dif(h + 1)
            wqk, rwqk = dif_w[h]
            vcol = 0 if hh == 0 else 192

            def evac(g, pb, Rpb, vcol=vcol):
                kb.op("dve", lambda e: e.tensor_copy(out=self.VA[:, g * 4:(g + 1) * 4, vcol:vcol + 64], in_=pb[:, 0:256].rearrange("p (t c) -> p t c", t=4)),
                      [Rpb], [self.RVA[g]])

            def pre(q, wqk=wqk, rwqk=rwqk):
                cs = slice(q * QT, (q + 1) * QT)
                self.proj_T(self.ps[3], self.Rps[3], r64, lambda k: wqk[:, k, 0:64], rwqk, q)
                yield
                yield from self.g_headnorm(3, 64, self.cbf("b32", r64, 0, 64), self.cf("epsc", 3, 4, r64), self.der("dif_gq", 0, 1, r64), self.QTt[0][r64, cs],
                                           self.RQT[0][q], cs, rope=(2, "sw_dif", [0, 32]))
                self.proj_T(self.ps[4], self.Rps[4], r64, lambda k: wqk[:, k, 64:128], rwqk, q)
                yield
                yield from self.g_headnorm(4, 64, self.cbf("b32", r64, 0, 64), self.cf("epsc", 3, 4, r64), self.der("dif_gk", 0, 1, r64), self.KTt[0][r64, cs],
                                           self.RKT[0][q], cs, rope=(2, "sw_dif", [0, 32]))
                yield from self.g_vgroup(q, lambda k: wqk[:, k, 128:192], rwqk, 64, evac)

            self.bg = pre(0)
            self.bg_drain()
            for q in range(NQ):
                cs = slice(q * QT, (q + 1) * QT)
                if q + 1 < NQ:
                    self.bg = pre(q + 1)
                    self.bg_drain()
                rins = [self.RQT[0][q]] + self.RKT[0][:q + 1] + self.RVA[:q + 1]

                def fin_diff(q=q, cs=cs, hh=hh, h=h):
                    recs = []
                    for a in range(2):
                        i, nr = self.finalize(6 + a, hh, None, None)
                        kb.op("dve", lambda e: e.tensor_tensor(out=self.rec[i][nr, :], in0=self.ps[6 + a][nr, :], in1=self.rec[i][nr, :], op=ALU.mult),
                              [self.Rps[6 + a], self.Rrec[i]], [self.Rrec[i]])
                        recs.append(i)
                    i0, i1 = recs
                    kb.op("dve", lambda e: e.scalar_tensor_tensor(out=self.rec[i0][nr, :], in0=self.rec[i1][nr, :], scalar=self.der("neglam", 0, 1, nr),
                                                                  in1=self.rec[i0][nr, :], op0=ALU.mult, op1=ALU.add),
                          [self.Rrec[i0], self.Rrec[i1], self.Rder], [self.Rrec[i0]])
                    pst, Rpst = self.ps[6], self.Rps[6]
                    kb.op("pool", lambda e: e.tensor_tensor(out=dsq[nr, :], in0=self.rec[i0][nr, :], in1=self.rec[i0][nr, :], op=ALU.mult),
                          [self.Rrec[i0]], [Rdsq])
                    self.mm(pst[nr, :], self.cbf("ones", nr, 0, 64), dsq[nr, :], True, True, [Rdsq, self.Rcb], Rpst)
                    self.rstd_from(pst[nr, :], self.rec[i1][nr, :], self.cf("epsc", 1, 2, nr), Rpst, self.Rrec[i1])
                    kb.op("dve", lambda e: e.scalar_tensor_tensor(out=self.oT[3][h // 2][nr, cs], in0=self.rec[i0][nr, :], scalar=self.der("dif_og", 0, 1, nr),
                                                                  in1=self.rec[i1][nr, :], op0=ALU.mult, op1=ALU.mult),
                          [self.Rrec[i0], self.Rrec[i1], self.Rder], [self.RoT[3][h // 2][q]])

                for a in range(2):
                    ra = slice(32 * a, 32 * a + 32)
                    self.attn_map(self.causal_tiles(q),
                                  lambda c0, c1, ra=ra, q=q: self.QTt[0][ra, q * QT + c0:q * QT + c1],
                                  lambda kt, ra=ra: self.KTt[0][ra, kt * 128:(kt + 1) * 128],
                                  lambda kt, hh=hh: self.VA[:, kt, 128 * hh:128 * hh + 128],
                                  32.0 ** -0.5, rins, 6 + a, fin=(fin_diff if a == 1 else None))
                self.bg_drain()
            self.pend_flush()


def prep_inputs(inp, layer_ids=tuple(range(DEPTH))):
    li = list(layer_ids)
    cb, cf = make_consts(layer_ids)
    sm = make_smalls(inp, layer_ids)

    def W(name):
        return np.ascontiguousarray(np.asarray(inp[name], np.float32)[li])

    w_in = W("w_in")
    gcols = []
    for br in range(3):
        for pr in range(2):
            for hh in range(2):
                gcols += [C_NSA_G + br * 4 + pr * 2 + hh] * 64
    wg_rep = np.ascontiguousarray(w_in[:, :, gcols])
    wf_pad = np.zeros((len(li), D, 128), np.float32)
    for h in range(4):
        wf_pad[:, :, 32 * h] = w_in[:, :, C_FOX_F + h]
    shared = {
        "ada_w": W("ada_w"), "w_in": w_in, "wg_rep": wg_rep, "wf_pad": wf_pad,
        "cmp_w1": W("nsa_cmp_w1"), "cmp_w2": W("nsa_cmp_w2"),
        "cmp_pe": np.ascontiguousarray(np.transpose(W("nsa_cmp_pe"), (0, 1, 3, 2))),
        "w_uq": W("mla_w_uq"), "w_ukv": W("mla_w_ukv"), "br_w": W("br_w"), "gate_w": W("gate_w"),
        "w_out": W("w_out"), "w_up": W("ffn_w_up"), "w_down": W("ffn_w_down"),
        "smalls": sm, "cbf": cb, "cf32": cf,
    }
    maps = []
    for b in range(8):
        m = dict(shared)
        m["x"] = np.ascontiguousarray(inp["x"][b], np.float32)
        m["cT"] = np.ascontiguousarray(np.asarray(inp["c"][b], np.float32).reshape(8, 128).T)
        m["pos"] = np.ascontiguousarray(np.asarray(inp["positions"][b], np.int32).reshape(1, S))
        maps.append(m)
    return maps


FUSED = True


def kernel(**inputs):
    inp = {k: np.asarray(v) for k, v in inputs.items()}
    if FUSED:
        maps = prep_inputs(inp)
        nc = Prog().build()
        res = run_bass_kernel_spmd(nc, maps, core_ids=list(range(8)))
        return np.stack([np.asarray(res.results[b]["y"], np.float32) for b in range(8)], axis=0)
    nc = Prog(n_layers=1, wdepth=1).build()
    x = np.asarray(inp["x"], np.float32)
    for l in range(DEPTH):
        cur = dict(inp)
        cur["x"] = x
        maps = prep_inputs(cur, (l,))
        res = run_bass_kernel_spmd(nc, maps, core_ids=list(range(8)))
        x = np.stack([np.asarray(res.results[b]["y"], np.float32) for b in range(8)], axis=0)
    return x
```

```python
import contextlib
import math
import numpy as np
import ml_dtypes
import concourse.bass as bass
import concourse.mybir as mybir
from concourse.bass_utils import run_bass_kernel_spmd

F32 = mybir.dt.float32
BF16 = mybir.dt.bfloat16
I32 = mybir.dt.int32
AF = mybir.ActivationFunctionType
ALU = mybir.AluOpType

S = 2048
D = 1024
NQ = 4
QT = 512
NKT = 16
DEPTH = 4
EPS = 1e-6
DFF = 2816
NFF = 22
THETA = 500000.0
BIG = 30000.0
IN_W = 2608


class Res:
    __slots__ = ("name", "w", "r", "dsem", "dcnt")

    def __init__(self, name):
        self.name = name
        self.w = None
        self.r = {}
        self.dsem = None
        self.dcnt = 0


class KB:
    ENG = ("pe", "act", "dve", "pool", "sp")

    def __init__(self, nc, stack):
        self.nc = nc
        self.stack = stack
        self.e = {"pe": nc.tensor, "act": nc.scalar, "dve": nc.vector, "pool": nc.gpsimd, "sp": nc.sync}
        self.sem = {k: stack.enter_context(nc.semaphore("s_" + k)) for k in self.ENG}
        self.cnt = {k: 0 for k in self.ENG}
        self.seen = {k: {} for k in self.ENG}
        self.same_engine_raw = True
        self.n_wait = 0

    def sb(self, name, shape, dt, stack=None):
        self.uid = getattr(self, "uid", 0) + 1
        return (stack or self.stack).enter_context(self.nc.sbuf_tensor(f"{name}_u{self.uid}", list(shape), dt))

    def ps(self, name, shape, dt=F32):
        return self.stack.enter_context(self.nc.psum_tensor(name, list(shape), dt))

    def newsem(self, name):
        self.uid = getattr(self, "uid", 0) + 1
        return self.stack.enter_context(self.nc.semaphore(f"{name}_u{self.uid}"))

    def _need(self, eng, deps):
        for key, sem, val in deps:
            if self.seen[eng].get(key, 0) >= val:
                continue
            self.e[eng].wait_ge(sem, val)
            self.n_wait += 1
            self.seen[eng][key] = val

    def _deps(self, eng, reads, writes):
        d = {}

        def add(w):
            if w is None:
                return
            e, i = w
            if e == "dma":
                sem, val = i
                key = ("dma", id(sem))
                if d.get(key, (None, 0))[1] < val:
                    d[key] = (sem, val)
            else:
                if e == eng and (eng == "pe" or not self.same_engine_raw):
                    return
                if d.get(e, (None, 0))[1] < i:
                    d[e] = (self.sem[e], i)

        for r in reads:
            add(r.w)
        for w in writes:
            add(w.w)
            for e, i in w.r.items():
                if e == eng:
                    continue
                if e == "dma":
                    add(("dma", i))
                else:
                    add((e, i))
        return [(k, s, v) for k, (s, v) in d.items()]

    def op(self, eng, fn, reads=(), writes=()):
        self._need(eng, self._deps(eng, reads, writes))
        ins = fn(self.e[eng])
        ins.then_inc(self.sem[eng], 1)
        self.cnt[eng] += 1
        idx = self.cnt[eng]
        for r in reads:
            r.r[eng] = idx
        for w in writes:
            w.w = (eng, idx)
            w.r = {}
        return ins

    def dma_in(self, q, res, fn, reads=()):
        if res.dsem is None:
            res.dsem = self.newsem("d_" + res.name)
        self._need(q, self._deps(q, reads, [res]))
        inss = fn(self.e[q])
        if not isinstance(inss, (list, tuple)):
            inss = [inss]
        for ins in inss:
            ins.then_inc(res.dsem, 16)
            res.dcnt += 16
        res.w = ("dma", (res.dsem, res.dcnt))
        res.r = {}

    def dma_out(self, q, res_list, fn, sem):
        self._need(q, self._deps(q, res_list, []))
        inss = fn(self.e[q])
        if not isinstance(inss, (list, tuple)):
            inss = [inss]
        for ins in inss:
            ins.then_inc(sem[0], 16)
            sem[1] += 16
        for r in res_list:
            r.r["dma"] = (sem[0], sem[1])

    def barrier(self):
        for a in ("pe", "act", "dve", "pool"):
            deps = []
            for b in ("pe", "act", "dve", "pool"):
                if a != b and self.cnt[b] > 0:
                    deps.append((b, self.sem[b], self.cnt[b]))
            self._need(a, deps)


CB = {}
CF = {}
SM = {}


def _alloc(tab, name, n, cur):
    tab[name] = (cur, cur + n)
    return cur + n


def _layout():
    c = 0
    for name, n in (("ident", 128), ("ones", 128), ("b64", 128), ("b32", 128), ("sw_nsa", 128),
                    ("sw_mla", 128), ("sw_dif", 128), ("selrow", 512), ("esel", 2048), ("ovl", 64), ("tabA", 512), ("tabB", 512)):
        c = _alloc(CB, name, n, c)
    CB["_n"] = c
    c = 0
    for name, n in (("ident", 128), ("invf", 1), ("sgn", 1), ("negpi", 1), ("epsc", 8), ("lamc", 2 * DEPTH)):
        c = _alloc(CF, name, n, c)
    CF["_n"] = c
    c = 0
    for name, n in (("ada_b", 48), ("gate_b", 32), ("conv_w", 66), ("conv_b", 22),
                    ("nsa_gq", 1), ("nsa_gkc", 1), ("nsa_gks", 1), ("nsa_gkw", 1),
                    ("fox_gq", 1), ("fox_gk", 1), ("fox_fb", 1),
                    ("mla_cqg", 2), ("mla_ckvg", 1), ("mla_gq", 1), ("mla_gk", 1),
                    ("dif_gq", 1), ("dif_gk", 1), ("dif_og", 1), ("dif_lam", 128)):
        c = _alloc(SM, name, n, c)
    SM["_n"] = c


_layout()

DER = {}
_c = 0
for _name, _n in (("a1", 8), ("a2", 8), ("nsa_gq", 1), ("nsa_gkc", 1), ("nsa_gks", 1), ("nsa_gkw", 1),
                  ("fox_gq", 1), ("fox_gk", 1), ("negfb", 1), ("mla_cqg", 2), ("mla_ckvg", 1), ("mla_gq", 1),
                  ("mla_gk", 1), ("dif_gq", 1), ("dif_gk", 1), ("dif_og", 1), ("neglam", 1), ("t0", 4), ("t1", 4)):
    _c = _alloc(DER, _name, _n, _c)
DER["_n"] = _c


def _rope_partner(r, head, n_rot):
    rr = r % head
    half = n_rot // 2
    base = r - rr
    if rr < half:
        return base + rr + half, rr, -1.0
    if rr < n_rot:
        return base + rr - half, rr - half, 1.0
    return None, None, 0.0


def make_consts(layer_ids=tuple(range(DEPTH))):
    cb = np.zeros((128, CB["_n"]), np.float32)
    cf = np.zeros((128, CF["_n"]), np.float32)
    p = np.arange(128)
    cb[:, CB["ident"][0]:CB["ident"][1]] = np.eye(128)
    cb[:, CB["ones"][0]:CB["ones"][1]] = 1.0
    cb[:, CB["b64"][0]:CB["b64"][1]] = (p[:, None] // 64 == p[None, :] // 64)
    cb[:, CB["b32"][0]:CB["b32"][1]] = (p[:, None] // 32 == p[None, :] // 32)
    for ci, (nm, head, nrot) in enumerate((("sw_nsa", 64, 16), ("sw_mla", 96, 32), ("sw_dif", 32, 8))):
        sw = np.zeros((128, 128), np.float32)
        half = nrot // 2
        inv = (np.float32(THETA) ** (-np.arange(half, dtype=np.float32) / np.float32(half))).astype(np.float32)
        for m in range(128):
            if nm == "sw_mla":
                if 64 <= m < 96:
                    rr = m - 64
                    sw[64 + (rr + 16 if rr < 16 else rr - 16), m] = 1.0
                continue
            pr, fi, sg = _rope_partner(m, head, nrot)
            if pr is not None:
                sw[pr, m] = 1.0
        for r in range(32):
            pr, fi, sg = _rope_partner(r, head, nrot)
            if pr is not None:
                cf[32 * ci + r, CF["invf"][0]] = inv[fi]
                cf[32 * ci + r, CF["sgn"][0]] = sg
        cb[:, CB[nm][0]:CB[nm][1]] = sw
    for h in range(4):
        cb[32 * h, CB["selrow"][0] + 128 * h: CB["selrow"][0] + 128 * (h + 1)] = 1.0
    for kt in range(16):
        for pp in range(128):
            cb[2 * kt + pp // 64, CB["esel"][0] + kt * 128 + pp] = BIG
    cs = np.arange(127)[:, None] * 16
    bs = np.arange(32)[None, :] * 64
    ov = ((cs < bs + 64) & (cs + 32 > bs)).astype(np.float32)
    cb[0:127, CB["ovl"][0]:CB["ovl"][0] + 32] = ov
    cb[0:127, CB["ovl"][0] + 32:CB["ovl"][0] + 64] = 1.0
    cf[:, CF["ident"][0]:CF["ident"][1]] = np.eye(128)
    cf[:, CF["negpi"][0]] = -math.pi
    for i_, v_ in enumerate((1024 * EPS, 64 * EPS, 96 * EPS, 32 * EPS, 256 * EPS, 128 * EPS, 1.0, 1e-30)):
        cf[:, CF["epsc"][0] + i_] = v_
    for i_, l_ in enumerate(layer_ids):
        lam_init = 0.8 - 0.6 * math.exp(-0.3 * l_)
        cf[:, CF["lamc"][0] + 2 * i_] = 8.0 * (1.0 - lam_init)
        cf[:, CF["lamc"][0] + 2 * i_ + 1] = -lam_init
    t = (np.arange(16)[None, :, None] * 128 + p[:, None, None])
    j = np.arange(32)[None, None, :]
    cur = t // 64
    future = (j * 64 > t)
    forced = ((j == 0) | (j == cur) | (j == cur - 1)) & (~future)
    A = (~future & ~forced).astype(np.float32)
    Bt = np.where(future, -1.0, np.where(forced, 1e4, 0.0)).astype(np.float32)
    cb[:, CB["tabA"][0]:CB["tabA"][1]] = A.reshape(128, 512)
    cb[:, CB["tabB"][0]:CB["tabB"][1]] = Bt.reshape(128, 512)
    return cb.astype(ml_dtypes.bfloat16), cf


def make_smalls(inp, layer_ids=tuple(range(DEPTH))):
    sm = np.zeros((128, len(layer_ids) * SM["_n"]), np.float32)
    for li_, l in enumerate(layer_ids):
        o = li_ * SM["_n"]

        def put(name, arr):
            a, b = SM[name]
            arr = np.asarray(arr, np.float32)
            if arr.ndim == 1:
                arr = arr[:, None]
            sm[:arr.shape[0], o + a:o + a + arr.shape[1]] = arr

        put("ada_b", inp["ada_b"][l].reshape(48, 128).T)
        put("gate_b", inp["gate_b"][l].reshape(32, 128).T)
        cw = inp["ffn_conv_w"][l]
        put("conv_w", np.concatenate([cw[t].reshape(22, 128).T for t in range(3)], axis=1))
        put("conv_b", inp["ffn_conv_b"][l].reshape(22, 128).T)
        g = inp["nsa_qk_g"][l]
        put("nsa_gq", np.tile(g[0], 2)); put("nsa_gkc", np.tile(g[1], 2))
        put("nsa_gks", np.tile(g[2], 2)); put("nsa_gkw", np.tile(g[3], 2))
        g = inp["fox_qk_g"][l]
        put("fox_gq", np.tile(g[0], 2)); put("fox_gk", np.tile(g[1], 2))
        fb = np.zeros(128, np.float32)
        fb[0::32] = inp["fox_f_b"][l]
        put("fox_fb", fb)
        put("mla_cqg", inp["mla_cq_g"][l].reshape(2, 128).T)
        put("mla_ckvg", inp["mla_ckv_g"][l])
        gq_, gk_ = inp["mla_qk_g"][l, 0], inp["mla_qk_g"][l, 1]
        put("mla_gq", np.concatenate([gq_[32:], gq_[:32]])); put("mla_gk", np.concatenate([gk_[32:], gk_[:32]]))
        put("dif_gq", np.tile(inp["diff_qk_g"][l, 0], 4)); put("dif_gk", np.tile(inp["diff_qk_g"][l, 1], 4))
        put("dif_og", np.tile(inp["diff_out_g"][l], 2))
        put("dif_lam", np.tile(inp["diff_lambda"][l].reshape(1, 128), (128, 1)))
    return sm


C_NSA_Q, C_NSA_KC, C_NSA_KS, C_NSA_VS, C_NSA_KW, C_NSA_VW, C_NSA_G = 0, 256, 384, 448, 512, 576, 640
C_FOX_Q, C_FOX_K, C_FOX_V, C_FOX_F = 652, 908, 1164, 1420
C_MLA_CQ, C_MLA_CKV, C_MLA_KR = 1424, 1680, 1808
C_DIF_Q, C_DIF_K, C_DIF_V = 1840, 2096, 2352


class Prog:
    def __init__(self, n_layers=DEPTH, dbg=(), wdepth=DEPTH):
        self.n_layers = n_layers
        self.wd = wdepth
        self.dbg = set(dbg)
        self.nc = bass.Bass("TRN2", target_bir_lowering=False)
        self.stack = contextlib.ExitStack()
        self.dbg_out = {}

    def dram_in(self, name, shape, dt=F32):
        return self.nc.dram_tensor(name, list(shape), dt, kind="ExternalInput").ap()

    def dram_out(self, name, shape, dt=F32):
        return self.nc.dram_tensor(name, list(shape), dt, kind="ExternalOutput").ap()

    def mm(self, out, lhsT, rhs, start, stop, rin, rout):
        self.kb.op("pe", lambda e: e.matmul(out, lhsT=lhsT, rhs=rhs, start=start, stop=stop), rin, [rout])

    def cbf(self, name, rows=slice(0, 128), c0=0, c1=None):
        a, b = CB[name]
        if c1 is None:
            c1 = b - a
        return self.t_cb[rows, a + c0:a + c1]

    def cf(self, name, c0=0, c1=None, rows=slice(0, 128)):
        a, b = CF[name]
        if c1 is None:
            c1 = b - a
        return self.t_cf[rows, a + c0:a + c1]

    def sm(self, l, name, c0=0, c1=None, rows=slice(0, 128)):
        a, b = SM[name]
        if c1 is None:
            c1 = b - a
        o = l * SM["_n"]
        return self.t_sm[rows, o + a + c0:o + a + c1]

    def der(self, name, c0=0, c1=None, rows=slice(0, 128), par=None):
        a, b = DER[name]
        if c1 is None:
            c1 = b - a
        return self.t_ders[self.cur if par is None else par][rows, a + c0:a + c1]

    @property
    def t_mod(self):
        return self.t_mods[self.cur]

    @property
    def Rmod(self):
        return self.Rmods[self.cur]

    @property
    def Rder(self):
        return self.Rders[self.cur]

    def dump(self, name, ap, res, shape):
        if name not in self.dbg:
            return
        d = self.dram_out("dbg_" + name, shape, F32 if ap.dtype == F32 else BF16)
        self.dbg_out[name] = d
        self.kb.dma_out("sp", res, lambda e: e.dma_start(out=d, in_=ap), self.osem)

    def build(self):
        nc = self.nc
        st = self.stack
        kb = self.kb = KB(nc, st)
        self.osem = [kb.newsem("osem"), 0]
        self.x_d = self.dram_in("x", [S, D])
        self.cT_d = self.dram_in("cT", [128, 8])
        self.pos_d = self.dram_in("pos", [1, S], I32)
        self.ada_w = self.dram_in("ada_w", [self.wd, D, 6 * D])
        self.w_in = self.dram_in("w_in", [self.wd, D, IN_W])
        self.wg_rep = self.dram_in("wg_rep", [self.wd, D, 768])
        self.wf_pad = self.dram_in("wf_pad", [self.wd, D, 128])
        self.cmp_w1 = self.dram_in("cmp_w1", [self.wd, 2, 2048, 64])
        self.cmp_w2 = self.dram_in("cmp_w2", [self.wd, 2, 64, 64])
        self.cmp_pe = self.dram_in("cmp_pe", [self.wd, 2, 64, 32])
        self.w_uq = self.dram_in("w_uq", [self.wd, 256, 384])
        self.w_ukv = self.dram_in("w_ukv", [self.wd, 128, 512])
        self.br_w = self.dram_in("br_w", [self.wd, 4, 256, D])
        self.gate_w = self.dram_in("gate_w", [self.wd, D, 4 * D])
        self.w_out = self.dram_in("w_out", [self.wd, D, D])
        self.w_up = self.dram_in("w_up", [self.wd, D, 2 * DFF])
        self.w_down = self.dram_in("w_down", [self.wd, DFF, D])
        self.sm_d = self.dram_in("smalls", [128, self.wd * SM["_n"]])
        self.cb_d = self.dram_in("cbf", [128, CB["_n"]], BF16)
        self.cf_d = self.dram_in("cf32", [128, CF["_n"]])
        self.y_d = self.dram_out("y", [S, D])

        self.xT = [kb.sb(f"xT{k}", [128, S], F32) for k in range(8)]
        self.hT = [kb.sb(f"hT{k}", [128, S], BF16) for k in range(8)]
        self.Rx = [[Res(f"x{k}_{q}") for q in range(NQ)] for k in range(8)]
        self.Rh = [[Res(f"h{k}_{q}") for q in range(NQ)] for k in range(8)]
        self.ropeC = kb.sb("ropeC", [128, S], BF16)
        self.ropeS = kb.sb("ropeS", [128, S], BF16)
        self.Rrope = Res("rope")
        self.t_cb = kb.sb("t_cb", [128, CB["_n"]], BF16)
        self.t_cf = kb.sb("t_cf", [128, CF["_n"]], F32)
        self.t_sm = kb.sb("t_sm", [128, self.wd * SM["_n"]], F32)
        self.t_mods = [kb.sb(f"t_mod{i}", [128, 48], F32) for i in range(2)]
        self.t_ders = [kb.sb(f"t_der{i}", [128, DER["_n"]], F32) for i in range(2)]
        self.adab = [kb.sb(f"adab{i}", [128, 512], BF16) for i in range(2)]
        self.Radab = [Res(f"adab{i}") for i in range(2)]
        self.Rmods = [Res("mod0"), Res("mod1")]
        self.Rders = [Res("der0"), Res("der1")]
        self.cur = 0
        self.t_scb = kb.sb("t_scb", [128, 8], BF16)
        self.Rcb, self.Rcf, self.Rsm, self.Rscb = (Res(n) for n in ("cb", "cf", "sm", "scb"))
        self.WB = [kb.sb(f"WB{i}", [128, 8, 512], BF16) for i in range(2)]
        self.RWB = [Res(f"WB{i}") for i in range(2)]
        self.wb_i = 0
        self.ps = [kb.ps(f"ps{i}", [128, 512]) for i in range(8)]
        self.Rps = [Res(f"ps{i}") for i in range(8)]

        kb.dma_in("sp", self.Rcb, lambda e: e.dma_start(out=self.t_cb[:], in_=self.cb_d))
        kb.dma_in("sp", self.Rcf, lambda e: e.dma_start(out=self.t_cf[:], in_=self.cf_d))
        kb.dma_in("sp", self.Rsm, lambda e: e.dma_start(out=self.t_sm[:], in_=self.sm_d))

        self.ada_gen = None
        self.prologue()
        for l in range(self.n_layers):
            self.layer(l)
        self.epilogue()
        kb.e["sp"].wait_ge(self.osem[0], self.osem[1])
        self.stack.close()
        return nc

    def mark(self, name):
        if not hasattr(self, 'marks'):
            self.marks = []
        self.marks.append((name, self.kb.cnt['pe']))

    def next_wb(self):
        i = self.wb_i
        self.wb_i ^= 1
        return self.WB[i], self.RWB[i]

    def load_w(self, dram_ap_pkn, ncols, nk=8):
        wb, r = self.next_wb()
        self.kb.dma_in("pool", r, lambda e: e.dma_start(out=wb[:, 0:nk, 0:ncols], in_=dram_ap_pkn))
        return wb, r

    def win_cols(self, l, c0, n):
        return self.w_in[l].rearrange("(k p) n -> p k n", p=128)[:, :, c0:c0 + n]

    def prologue(self):
        kb = self.kb
        with contextlib.ExitStack() as ps_:
            xs = [kb.sb(f"xstage{i}", [128, 4, D], F32, ps_) for i in range(2)]
            Rxs = [Res(f"xstage{i}") for i in range(2)]
            posi = kb.sb("posi", [128, S], I32, ps_)
            posf = kb.sb("posf", [128, S], F32, ps_)
            tA = kb.sb("tA", [128, S], F32, ps_)
            tB = kb.sb("tB", [128, S], F32, ps_)
            Rposi, Rposf, RtA, RtB = Res("posi"), Res("posf"), Res("tA"), Res("tB")
            ct = kb.sb("ct", [128, 8], F32, ps_)
            Rct = Res("ct")
            kb.dma_in("sp", Rct, lambda e: e.dma_start(out=ct[:], in_=self.cT_d))
            kb.op("act", lambda e: e.activation(out=self.t_scb[:], in_=ct[:], func=AF.Silu), [Rct], [self.Rscb])
            self.ada_gen = self.g_ada(0)
            xv = self.x_d.rearrange("(g t p) d -> g p t d", t=4, p=128)
            for g in range(2):
                kb.dma_in("sp", Rxs[g], lambda e: e.dma_start(out=xs[g][:], in_=xv[g]))
            for g in range(4):
                xg, Rxg = xs[g % 2], Rxs[g % 2]
                for k in range(8):
                    self.ada_step(3)
                    pb = self.ps[k % 4]
                    for t in range(4):
                        kb.op("pe", lambda e: e.transpose(out=pb[:, t * 128:(t + 1) * 128], in_=xg[:, t, k * 128:(k + 1) * 128],
                                                          identity=self.cf("ident")), [Rxg, self.Rcf], [self.Rps[k % 4]])
                    eng = "act" if k % 2 == 0 else "dve"
                    if eng == "act":
                        kb.op("act", lambda e: e.activation(out=self.xT[k][:, g * 512:(g + 1) * 512], in_=pb[:], func=AF.Copy),
                              [self.Rps[k % 4]], [self.Rx[k][g]])
                    else:
                        kb.op("dve", lambda e: e.tensor_copy(out=self.xT[k][:, g * 512:(g + 1) * 512], in_=pb[:]),
                              [self.Rps[k % 4]], [self.Rx[k][g]])
                if g + 2 < 4:
                    kb.dma_in("sp", Rxg, lambda e: e.dma_start(out=xg[:], in_=xv[g + 2]))
            kb.dma_in("sp", Rposi, lambda e: e.dma_start(out=posi[:], in_=self.pos_d.partition_broadcast(128)))
            kb.op("dve", lambda e: e.tensor_copy(out=posf[:], in_=posi[:]), [Rposi], [Rposf])
            twopi = 2.0 * math.pi
            invf = self.cf("invf")
            sgn = self.cf("sgn")
            ti = posi
            for which in range(2):
                kb.op("dve", lambda e: e.tensor_scalar(out=tA[:], in0=posf[:], scalar1=invf, scalar2=1.0 / twopi, op0=ALU.mult, op1=ALU.mult),
                      [Rposf, self.Rcf, RtB], [RtA])
                if which == 1:
                    kb.op("dve", lambda e: e.tensor_scalar(out=tA[:], in0=tA[:], scalar1=0.25, scalar2=None, op0=ALU.add), [RtA], [RtA])
                kb.op("dve", lambda e: e.tensor_copy(out=ti[:], in_=tA[:]), [RtA, Rposf], [Rposi])
                kb.op("dve", lambda e: e.tensor_copy(out=tB[:], in_=ti[:]), [Rposi], [RtB])
                kb.op("dve", lambda e: e.tensor_tensor(out=tA[:], in0=tA[:], in1=tB[:], op=ALU.subtract), [RtA, RtB], [RtA])
                kb.op("act", lambda e: e.activation(out=tB[:], in_=tA[:], func=AF.Sin, scale=twopi), [RtA], [RtB])
                if which == 0:
                    kb.op("dve", lambda e: e.tensor_scalar(out=self.ropeS[:], in0=tB[:], scalar1=sgn, scalar2=None, op0=ALU.mult),
                          [RtB, self.Rcf], [self.Rrope])
                else:
                    kb.op("dve", lambda e: e.tensor_copy(out=self.ropeC[:], in_=tB[:]), [RtB], [self.Rrope])
            self.dump("ropeC", self.ropeC[:], [self.Rrope], [128, S])
            self.dump("ropeS", self.ropeS[:], [self.Rrope], [128, S])
            self.dump("xT0", self.xT[0][:], self.Rx[0], [128, S])
            kb.barrier()
            if self.dbg:
                kb.e["act"].wait_ge(self.osem[0], self.osem[1])
                kb.barrier()

    def epilogue(self):
        kb = self.kb
        with contextlib.ExitStack() as ps_:
            ys = [kb.sb(f"ystage{i}", [128, 2, D], F32, ps_) for i in range(2)]
            Rys = [Res(f"ystage{i}") for i in range(2)]
            yv = self.y_d.rearrange("(g t p) d -> g p t d", t=2, p=128)
            for g in range(8):
                q = g // 2
                for t in range(2):
                    tt = g * 2 + t
                    for k in range(8):
                        pb = self.ps[(k // 4) + 2 * (tt % 2)]
                        kb.op("pe", lambda e: e.transpose(out=pb[:, (k % 4) * 128:(k % 4 + 1) * 128], in_=self.xT[k][:, tt * 128:(tt + 1) * 128],
                                                          identity=self.cf("ident")), [self.Rx[k][q], self.Rcf], [self.Rps[(k // 4) + 2 * (tt % 2)]])
                    for hh in range(2):
                        bi = hh + 2 * (tt % 2)
                        if hh == 0:
                            kb.op("act", lambda e: e.activation(out=ys[g % 2][:, t, hh * 512:(hh + 1) * 512], in_=self.ps[bi][:], func=AF.Copy),
                                  [self.Rps[bi]], [Rys[g % 2]])
                        else:
                            kb.op("dve", lambda e: e.tensor_copy(out=ys[g % 2][:, t, hh * 512:(hh + 1) * 512], in_=self.ps[bi][:]),
                                  [self.Rps[bi]], [Rys[g % 2]])
                kb.dma_out("sp", [Rys[g % 2]], lambda e: e.dma_start(out=yv[g], in_=ys[g % 2][:]), self.osem)

    def layer(self, l):
        self.cur = l % 2
        while self.ada_gen is not None:
            self.ada_step()
        self.norm_mod(l, 0)
        kb = self.kb
        with contextlib.ExitStack() as ls:
            self.oT = [None] * 4
            self.RoT = [None] * 4
            for m, fn in self.mixer_order():
                self.oT[m] = [kb.sb(f"oT{m}_{c}", [128, S], BF16, ls) for c in range(2)]
                self.RoT[m] = [[Res(f"oT{m}_{c}_{q}") for q in range(NQ)] for c in range(2)]
                with contextlib.ExitStack() as ms:
                    fn(l, ms)
                    kb.barrier()
                if l == 0:
                    for c in range(2):
                        self.dump(f"o{m}_{c}", self.oT[m][c][:], self.RoT[m][c], [128, S])
            if self.dbg:
                kb.e["act"].wait_ge(self.osem[0], self.osem[1])
                kb.barrier()
            if getattr(self, "mixers", None) is None:
                self.merge(l)
                self.dump(f"x1_l{l}", self.xT[0][:], self.Rx[0], [128, S])
        if getattr(self, "mixers", None) is None:
            self.norm_mod(l, 1)
            self.ffn(l)
            self.dump(f"x2_l{l}", self.xT[0][:], self.Rx[0], [128, S])
            if self.dbg:
                kb.e["act"].wait_ge(self.osem[0], self.osem[1])
                kb.barrier()

    def gate_tile(self, wg, rwg, blk, q, gt, Rgt):
        kb = self.kb
        pb, Rpb = self.ps[3], self.Rps[3]
        self.proj_T(pb, Rpb, slice(0, 128), lambda k: wg[:, k, blk * 128:(blk + 1) * 128], rwg, q)
        kb.op("act", lambda e: e.activation(out=gt[:], in_=pb[:], func=AF.Exp, scale=-1.0), [Rpb], [Rgt])
        kb.op("dve", lambda e: e.tensor_scalar(out=gt[:], in0=gt[:], scalar1=1.0, scalar2=None, op0=ALU.add), [Rgt], [Rgt])
        kb.op("dve", lambda e: e.reciprocal(out=gt[:], in_=gt[:]), [Rgt], [Rgt])

    def nsa(self, l, ms):
        kb = self.kb
        self.mark('nsa_prelude')
        KS2 = kb.sb("KS2", [128, S], BF16, ms)
        KW2 = kb.sb("KW2", [128, S], BF16, ms)
        RKS = [Res(f"KS2_{q}") for q in range(NQ)]
        RKW = [Res(f"KW2_{q}") for q in range(NQ)]
        VSW = kb.sb("VSW", [128, NKT, 384], BF16, ms)
        RVSW = Res("VSW")
        KCMP = kb.sb("KCMP", [128, 128], BF16, ms)
        VCMP = kb.sb("VCMP", [128, 192], BF16, ms)
        RKCMP, RVCMP = Res("KCMP"), Res("VCMP")
        IMP = kb.sb("IMP", [128, 512], F32, ms)
        RIMP = Res("IMP")
        SELM = kb.sb("SELM", [32, S], BF16, ms)
        RSELM = [Res(f"SELM{q}") for q in range(NQ)]
        self.mixer_scratch(ms, nq=2, nk=0, va=False)
        kb.op("pool", lambda e: e.memset(VSW[:, :, 64:128], 1.0), [], [RVSW])
        kb.op("pool", lambda e: e.memset(VSW[:, :, 256:320], 1.0), [], [RVSW])
        kb.op("pool", lambda e: e.memset(VCMP[:], 0.0), [], [RVCMP])
        kb.op("pool", lambda e: e.memset(VCMP[:, 64:128], 1.0), [RVCMP], [RVCMP])
        kb.op("pool", lambda e: e.memset(KCMP[:], 0.0), [], [RKCMP])
        kb.op("pool", lambda e: e.memset(IMP[:], 0.0), [], [RIMP])
        pstat, Rpstat = self.ps[5], self.Rps[5]
        wA, rwA = self.load_w(self.win_cols(l, C_NSA_KC, 384), 384)
        wq, rwq = self.load_w(self.win_cols(l, C_NSA_Q, 256), 256)
        with contextlib.ExitStack() as pscope:
            KVC = self.QTt[1]
            RKVC = [Res(f"KVC{q}") for q in range(NQ)]
            W1 = kb.sb("W1", [128, 32, 64], BF16, pscope)
            W2 = kb.sb("W2", [128, 64], BF16, pscope)
            PEt = kb.sb("PEt", [128, 32], BF16, pscope)
            HID = kb.sb("HID", [128, 128], BF16, pscope)
            RW1, RW2, RPE, RHID = Res("W1"), Res("W2"), Res("PEt"), Res("HID")
            kb.dma_in("pool", RW1, lambda e: [e.dma_start(out=W1[64 * i:64 * i + 64, :, :], in_=self.cmp_w1[l, i].rearrange("(j d) o -> d j o", d=64)) for i in range(2)])
            kb.dma_in("pool", RW2, lambda e: [e.dma_start(out=W2[64 * i:64 * i + 64, :], in_=self.cmp_w2[l, i]) for i in range(2)])
            kb.dma_in("pool", RPE, lambda e: [e.dma_start(out=PEt[64 * i:64 * i + 64, :], in_=self.cmp_pe[l, i]) for i in range(2)])
            for q in range(NQ):
                cs = slice(q * QT, (q + 1) * QT)
                for (c0, dstt, Rd, gname) in ((128, KS2, RKS, "nsa_gks"), (256, KW2, RKW, "nsa_gkw")):
                    for half in range(2):
                        self.proj_T(self.ps[3], self.Rps[3], slice(64 * half, 64 * half + 64), lambda k: wA[:, k, c0:c0 + 64], rwA, q)
                    self.headnorm(3, 128, self.cbf("b64"), self.cf("epsc", 1, 2), self.der(gname), dstt[:, cs], Rd[q], cs, rope=(0, "sw_nsa", [0, 64]))
                self.proj_T(self.ps[4], self.Rps[4], slice(0, 128), lambda k: wA[:, k, 0:128], rwA, q)
                kb.op("act", lambda e: e.activation(out=KVC[:, cs], in_=self.ps[4][:], func=AF.Copy), [self.Rps[4]], [RKVC[q]])
            for g in range(4):
                pb, Rpb = self.ps[3 + g % 2], self.Rps[3 + g % 2]
                for t in range(4):
                    kt = g * 4 + t
                    for b in range(2):
                        for k in range(8):
                            self.mm(pb[:, (t * 2 + b) * 64:(t * 2 + b + 1) * 64], self.hT[k][:, kt * 128:(kt + 1) * 128], wA[:, k, 192 + 128 * b:256 + 128 * b],
                                    k == 0, k == 7, [rwA, self.Rh[k][g]], Rpb)
                src = pb[:].rearrange("p (t b c) -> p t b c", t=4, b=2)
                for sidx in (0, 2):
                    dstv = VSW[:, g * 4:(g + 1) * 4, :].rearrange("p t (b s c) -> p t b s c", b=2, s=3)[:, :, :, sidx, :]
                    kb.op("dve" if sidx == 0 else "act", (lambda e: e.tensor_copy(out=dstv, in_=src)) if sidx == 0 else
                          (lambda e: e.activation(out=dstv, in_=src, func=AF.Copy)), [Rpb], [RVSW])
            pH, RpH = self.ps[6], self.Rps[6]
            for i in range(2):
                rr = slice(64 * i, 64 * i + 64)
                n = 0
                for j in range(32):
                    self.mm(pH[rr, 0:127], W1[rr, j, :], KVC[rr, j:j + 16 * 126 + 1:16], n == 0, False, [RW1] + RKVC, RpH)
                    n += 1
                    self.mm(pH[rr, 0:127], W1[rr, j, :], PEt[rr, j:j + 1].to_broadcast([64, 127]), False, j == 31, [RW1, RPE], RpH)
            kb.op("act", lambda e: e.activation(out=HID[:, 0:127], in_=pH[:, 0:127], func=AF.Silu), [RpH], [RHID])
            for half in range(2):
                self.mm(self.ps[3][64 * half:64 * half + 64, 0:127], W2[0:64, :], HID[0:64, 0:127], True, True, [RW2, RHID], self.Rps[3])
            kb.op("act", lambda e: e.activation(out=self.sqb[0][:, 0:127], in_=self.ps[3][:, 0:127], func=AF.Square), [self.Rps[3]], [self.Rsqb[0]])
            self.mm(pstat[:, 0:127], self.cbf("b64"), self.sqb[0][:, 0:127], True, True, [self.Rsqb[0], self.Rcb], Rpstat)
            self.rstd_from(pstat[:, 0:127], self.rstd[0][:, 0:127], self.cf("epsc", 1, 2), Rpstat, self.Rrstd[0])
            kb.op("dve", lambda e: e.scalar_tensor_tensor(out=KCMP[:, 0:127], in0=self.ps[3][:, 0:127], scalar=self.der("nsa_gkc"), in1=self.rstd[0][:, 0:127],
                                                          op0=ALU.mult, op1=ALU.mult), [self.Rps[3], self.Rrstd[0], self.Rder, RKCMP], [RKCMP])
            self.mm(self.ps[4][0:127, 0:64], HID[64:128, 0:127], W2[64:128, :], True, True, [RW2, RHID], self.Rps[4])
            kb.op("dve", lambda e: e.tensor_copy(out=VCMP[0:127, 0:64], in_=self.ps[4][0:127, 0:64]), [self.Rps[4], RVCMP], [RVCMP])
            kb.op("act", lambda e: e.activation(out=VCMP[0:127, 128:192], in_=self.ps[4][0:127, 0:64], func=AF.Copy), [self.Rps[4], RVCMP], [RVCMP])
            self.dump("nsaKS", KS2[:], RKS, [128, S])
            self.dump("nsaKCMP", KCMP[:], [RKCMP], [128, 128])
            self.dump("nsaVCMP", VCMP[:], [RVCMP], [128, 192])
            kb.barrier()
            if self.dbg:
                kb.e["act"].wait_ge(self.osem[0], self.osem[1])
                kb.barrier()
        GT = [kb.sb(f"GT{i}", [128, QT], F32, ms) for i in range(2)]
        RGT = [Res(f"GT{i}") for i in range(2)]
        self.mark('nsa_q')
        wg0, rwg0 = self.load_w(self.wg_rep[l].rearrange("(k p) n -> p k n", p=128)[:, :, 0:256], 256)
        for u in range(2):
            for q in range(NQ):
                cs = slice(q * QT, (q + 1) * QT)
                self.proj_T(self.ps[3], self.Rps[3], slice(0, 128), lambda k: wq[:, k, u * 128:(u + 1) * 128], rwq, q)
                self.headnorm(3, 128, self.cbf("b64"), self.cf("epsc", 1, 2), self.der("nsa_gq"), self.QTt[u][:, cs], self.RQT[u][q], cs, rope=(0, "sw_nsa", [0, 64]))
        self.mark('nsa_cmp')
        wg1, rwg1 = self.load_w(self.wg_rep[l].rearrange("(k p) n -> p k n", p=128)[:, :, 256:768], 512)
        pI, RpI = self.ps[5], self.Rps[5]
        gi = 0
        for u in range(2):
            for q in range(NQ):
                cs = slice(q * QT, (q + 1) * QT)
                gt, Rgt = GT[gi], RGT[gi]
                gi ^= 1
                self.pend_flush()
                self.gate_tile(wg0, rwg0, u, q, gt, Rgt)
                for hh in range(2):
                    rb = slice(64 * hh, 64 * hh + 64)
                    ob = 6 + hh
                    rins = [self.RQT[u][q], RKCMP, RVCMP]

                    def imp_mm(pi, hh=hh):
                        for t in range(4):
                            self.mm(pI[:, (hh * 4 + t) * 64:(hh * 4 + t + 1) * 64], self.PT[pi][:, t * 128:(t + 1) * 128], self.cbf("ovl"), True, True,
                                    [self.RPT[pi], self.Rcb], RpI)

                    def fin_cmp(ob=ob, hh=hh, u=u, cs=cs, q=q, gt=gt, Rgt=Rgt):
                        i, nr = self.finalize(ob, hh, None, None, eps=1e-30)
                        kb.op("pool", lambda e: e.tensor_tensor(out=self.rec[i][nr, :], in0=self.rec[i][nr, :], in1=gt[nr, :], op=ALU.mult), [self.Rrec[i], Rgt], [self.Rrec[i]])
                        kb.op("dve", lambda e: e.tensor_tensor(out=self.oT[0][u][nr, cs], in0=self.ps[ob][nr, :], in1=self.rec[i][nr, :], op=ALU.mult),
                              [self.Rps[ob], self.Rrec[i]], [self.RoT[0][u][q]])

                    self.attn_map([(0, 0, QT, ("vis", 0, q * QT - 31))],
                                  lambda c0, c1, rb=rb, u=u, q=q: self.QTt[u][rb, q * QT + c0:q * QT + c1],
                                  lambda kt, rb=rb: KCMP[rb, :],
                                  lambda kt, hh=hh: VCMP[:, 64 * hh:64 * hh + 128],
                                  0.125, rins, ob, fin=fin_cmp, after_p=imp_mm)
                for hh in range(2):
                    for t in range(4):
                        tt = q * 4 + t
                        base = (hh * 4 + t) * 64
                        rcol = self.rstd[0][:, 0:1]
                        kb.op("dve", lambda e: e.tensor_scalar(out=rcol, in0=pI[:, base + 32:base + 33], scalar1=1e-30, scalar2=None, op0=ALU.add), [RpI], [self.Rrstd[0]])
                        kb.op("dve", lambda e: e.reciprocal(out=rcol, in_=rcol), [self.Rrstd[0]], [self.Rrstd[0]])
                        kb.op("dve", lambda e: e.scalar_tensor_tensor(out=IMP[:, tt * 32:(tt + 1) * 32], in0=pI[:, base:base + 32], scalar=rcol,
                                                                      in1=IMP[:, tt * 32:(tt + 1) * 32], op0=ALU.mult, op1=ALU.add),
                              [RpI, self.Rrstd[0], RIMP], [RIMP])
        self.pend_flush()
        self.dump("nsaIMP", IMP[:], [RIMP], [128, 512])
        self.mark('nsa_topk')
        SC = self.rec[0]
        RSC = self.Rrec[0]
        kb.op("dve", lambda e: e.tensor_tensor(out=SC[:], in0=IMP[:], in1=self.cbf("tabA"), op=ALU.mult), [RIMP, self.Rcb], [RSC])
        kb.op("dve", lambda e: e.tensor_tensor(out=SC[:], in0=SC[:], in1=self.cbf("tabB"), op=ALU.add), [RSC, self.Rcb], [RSC])
        m8 = self.rstd[1]
        Rm8 = self.Rrstd[1]
        sc2 = self.rec[1]
        Rsc2 = self.Rrec[1]
        selm = self.sqb[0]
        Rselm = self.Rsqb[0]
        pT, RpT = self.ps[5], self.Rps[5]
        for tt in range(16):
            sl = slice(tt * 32, (tt + 1) * 32)
            kb.op("dve", lambda e: e.max(out=m8[:, 0:8], in_=SC[:, sl]), [RSC], [Rm8])
            kb.op("dve", lambda e: e.match_replace(out=sc2[:, 0:32], in_to_replace=m8[:, 0:8], in_values=SC[:, sl], imm_value=-2.0), [RSC, Rm8], [Rsc2])
            kb.op("dve", lambda e: e.max(out=m8[:, 8:16], in_=sc2[:, 0:32]), [Rsc2], [Rm8])
            kb.op("dve", lambda e: e.tensor_scalar(out=selm[:, sl], in0=SC[:, sl], scalar1=m8[:, 15:16], scalar2=-1.0, op0=ALU.is_ge, op1=ALU.add),
                  [RSC, Rm8], [Rselm])
            self.mm(pT[0:32, (tt % 4) * 128:(tt % 4 + 1) * 128], selm[:, sl], self.cbf("ident"), True, True, [Rselm, self.Rcb], RpT)
            if tt % 4 == 3:
                qq = tt // 4
                kb.op("act", lambda e: e.activation(out=SELM[0:32, qq * QT:(qq + 1) * QT], in_=pT[0:32, :], func=AF.Copy), [RpT], [RSELM[qq]])
        self.dump("nsaSELM", SELM[:], RSELM, [32, S])
        self.mark('nsa_slcwin')
        for u in range(2):
            for q in range(NQ):
                cs = slice(q * QT, (q + 1) * QT)
                self.gate_tile(wg1, rwg1, u, q, GT[0], RGT[0])
                self.gate_tile(wg1, rwg1, 2 + u, q, GT[1], RGT[1])
                for hh in range(2):
                    rb = slice(64 * hh, 64 * hh + 64)
                    rins = [self.RQT[u][q], RVSW, RSELM[q], self.Rcb] + RKS
                    shared = {}

                    def fin_s(hh=hh, shared=shared):
                        i_s, nr = self.finalize(6, hh, None, None)
                        kb.op("pool", lambda e: e.tensor_tensor(out=self.rec[i_s][nr, :], in0=self.rec[i_s][nr, :], in1=GT[0][nr, :], op=ALU.mult),
                              [self.Rrec[i_s], RGT[0]], [self.Rrec[i_s]])
                        kb.op("dve", lambda e: e.tensor_tensor(out=self.rec[i_s][nr, :], in0=self.ps[6][nr, :], in1=self.rec[i_s][nr, :], op=ALU.mult),
                              [self.Rps[6], self.Rrec[i_s]], [self.Rrec[i_s]])
                        shared["i_s"] = i_s

                    def fin_w(hh=hh, shared=shared, u=u, cs=cs, q=q):
                        i_s = shared["i_s"]
                        i_w, nr = self.finalize(7, hh, None, None)
                        kb.op("pool", lambda e: e.tensor_tensor(out=self.rec[i_w][nr, :], in0=self.rec[i_w][nr, :], in1=GT[1][nr, :], op=ALU.mult),
                              [self.Rrec[i_w], RGT[1]], [self.Rrec[i_w]])
                        kb.op("dve", lambda e: e.tensor_tensor(out=self.rec[i_w][nr, :], in0=self.ps[7][nr, :], in1=self.rec[i_w][nr, :], op=ALU.mult),
                              [self.Rps[7], self.Rrec[i_w]], [self.Rrec[i_w]])
                        kb.op("pool", lambda e: e.tensor_tensor(out=self.rec[i_w][nr, :], in0=self.rec[i_w][nr, :], in1=self.rec[i_s][nr, :], op=ALU.add),
                              [self.Rrec[i_w], self.Rrec[i_s]], [self.Rrec[i_w]])
                        kb.op("pool", lambda e: e.tensor_tensor(out=self.oT[0][u][nr, cs], in0=self.oT[0][u][nr, cs], in1=self.rec[i_w][nr, :], op=ALU.add),
                              [self.Rrec[i_w], self.RoT[0][u][q]], [self.RoT[0][u][q]])

                    self.attn_map(self.causal_tiles(q),
                                  lambda c0, c1, rb=rb, u=u, q=q: self.QTt[u][rb, q * QT + c0:q * QT + c1],
                                  lambda kt, rb=rb: KS2[rb, kt * 128:(kt + 1) * 128],
                                  lambda kt, hh=hh: VSW[:, kt, 64 * hh:64 * hh + 128],
                                  0.125, rins, 6,
                                  extra=lambda kt, c0, c1, q=q: (self.cbf("esel", slice(0, 32), kt * 128, (kt + 1) * 128), SELM[0:32, q * QT + c0:q * QT + c1]),
                                  fin=fin_s)
                    tiles = [(4 * q + j, 128 * j, QT, ("causal", 0, 0)) for j in range(4)]
                    if q > 0:
                        tiles += [(4 * q - 4 + j, 0, 128 * (j + 1), ("lower", 128 * j, 0)) for j in range(4)]
                    rins = [self.RQT[u][q], RVSW] + RKW
                    self.attn_map(tiles,
                                  lambda c0, c1, rb=rb, u=u, q=q: self.QTt[u][rb, q * QT + c0:q * QT + c1],
                                  lambda kt, rb=rb: KW2[rb, kt * 128:(kt + 1) * 128],
                                  lambda kt, hh=hh: VSW[:, kt, 192 + 64 * hh:192 + 64 * hh + 128],
                                  0.125, rins, 7, fin=fin_w)
                self.pend_flush()

    def mixer_order(self):
        sel = getattr(self, 'mixers', None) or [0, 2, 1, 3]
        fns = {0: self.nsa, 1: self.fox, 2: self.mla, 3: self.diff}
        return [(m, fns[m]) for m in sel]

    def g_ada(self, l):
        kb = self.kb
        par = l % 2
        t_mod, Rmod, Rder = self.t_mods[par], self.Rmods[par], self.Rders[par]
        pb, Rpb = self.ps[7], self.Rps[7]
        av = self.ada_w[l].rearrange("(k p) n -> p k n", p=128)
        avk = self.ada_w[l].rearrange("(k p) n -> k p n", p=128)
        n = 0
        for k in range(8):
            for cb_ in range(12):
                wb, rw = self.adab[n % 2], self.Radab[n % 2]
                kb.dma_in("pool", rw, lambda e: e.dma_start(out=wb[:], in_=avk[k][:, cb_ * 512:(cb_ + 1) * 512]))
                for jj in range(4):
                    j = cb_ * 4 + jj
                    kb.op("pe", lambda e: e.matmul(pb[:, j:j + 1], lhsT=wb[:, jj * 128:(jj + 1) * 128], rhs=self.t_scb[:, k:k + 1],
                                                   start=(k == 0 and j == 0), stop=(k == 7), skip_group_check=True), [rw, self.Rscb], [Rpb])
                n += 1
                yield
        kb.op("dve", lambda e: e.tensor_tensor(out=t_mod[:], in0=pb[:, 0:48], in1=self.sm(l, "ada_b"), op=ALU.add),
              [Rpb, self.Rsm], [Rmod])
        d = lambda *a, **kw: self.der(*a, par=par, **kw)

        def ts(dst, src, s1, s2=None, op0=ALU.mult, op1=None):
            if s2 is None:
                kb.op("dve", lambda e: e.tensor_scalar(out=dst, in0=src, scalar1=s1, scalar2=None, op0=op0),
                      [Rmod, self.Rsm, Rder, self.Rcf], [Rder])
            else:
                kb.op("dve", lambda e: e.tensor_scalar(out=dst, in0=src, scalar1=s1, scalar2=s2, op0=op0, op1=op1),
                      [Rmod, self.Rsm, Rder, self.Rcf], [Rder])

        ts(d("a1"), t_mod[:, 8:16], 1.0, 32.0, ALU.add, ALU.mult)
        ts(d("a2"), t_mod[:, 32:40], 1.0, 32.0, ALU.add, ALU.mult)
        for nm, sc in (("nsa_gq", 8.0), ("nsa_gkc", 8.0), ("nsa_gks", 8.0), ("nsa_gkw", 8.0), ("fox_gq", 8.0), ("fox_gk", 8.0),
                       ("mla_cqg", 16.0), ("mla_ckvg", math.sqrt(128.0)), ("mla_gq", math.sqrt(96.0)), ("mla_gk", math.sqrt(96.0)),
                       ("dif_gq", math.sqrt(32.0)), ("dif_gk", math.sqrt(32.0)), ("dif_og", self.cf("lamc", 2 * l, 2 * l + 1))):
            ts(d(nm), self.sm(l, nm), sc)
        ts(d("negfb"), self.sm(l, "fox_fb"), -1.0)
        yield
        if not hasattr(self, "t_lt"):
            self.t_lt = kb.sb("t_lt", [128, 64], F32)
            self.Rlt = Res("lt")
        lt = self.t_lt
        kb.op("dve", lambda e: e.tensor_tensor(out=lt[:, 0:32], in0=self.sm(l, "dif_lam", 0, 32), in1=self.sm(l, "dif_lam", 32, 64), op=ALU.mult),
              [self.Rsm], [self.Rlt])
        kb.op("dve", lambda e: e.tensor_tensor(out=lt[:, 32:64], in0=self.sm(l, "dif_lam", 64, 96), in1=self.sm(l, "dif_lam", 96, 128), op=ALU.mult),
              [self.Rsm], [self.Rlt])
        kb.op("dve", lambda e: e.tensor_reduce(out=d("t0", 0, 2), in_=lt[:].rearrange("p (a b) -> p a b", a=2), axis=mybir.AxisListType.X, op=ALU.add),
              [self.Rlt], [Rder])
        kb.op("act", lambda e: e.activation(out=d("t1", 0, 2), in_=d("t0", 0, 2), func=AF.Exp), [Rder], [Rder])
        kb.op("dve", lambda e: e.tensor_tensor(out=d("t0", 2, 3), in0=d("t1", 1, 2), in1=d("t1", 0, 1), op=ALU.subtract), [Rder], [Rder])
        kb.op("dve", lambda e: e.tensor_scalar(out=d("neglam"), in0=d("t0", 2, 3), scalar1=self.cf("lamc", 2 * l + 1, 2 * l + 2), scalar2=None, op0=ALU.add),
              [Rder, self.Rcf], [Rder])
        yield

    def ada_step(self, n=1):
        for _ in range(n):
            if self.ada_gen is not None:
                try:
                    next(self.ada_gen)
                except StopIteration:
                    self.ada_gen = None

    def norm_mod(self, l, which):
        self.mark('norm')
        kb = self.kb
        acol = "a1" if which == 0 else "a2"
        shc = 0 if which == 0 else 24
        with contextlib.ExitStack() as ns:
            sq = [kb.sb(f"nsq{i}", [128, QT], BF16, ns) for i in range(2)]
            Rsq = [Res(f"nsq{i}") for i in range(2)]
            rstd = kb.sb("nrstd", [128, QT], F32, ns)
            Rrstd = Res("nrstd")
            tmp = [kb.sb(f"ntmp{i}", [128, QT], F32, ns) for i in range(2)]
            Rtmp = [Res(f"ntmp{i}") for i in range(2)]
            for q in range(NQ):
                cs = slice(q * QT, (q + 1) * QT)
                pb, Rpb = self.ps[q % 2], self.Rps[q % 2]
                for k in range(8):
                    eng = "pool" if k % 2 == 0 else "dve"
                    kb.op(eng, lambda e: e.tensor_tensor(out=sq[k % 2][:], in0=self.xT[k][:, cs], in1=self.xT[k][:, cs], op=ALU.mult),
                          [self.Rx[k][q]], [Rsq[k % 2]])
                    self.mm(pb[:], self.cbf("ones"), sq[k % 2][:], k == 0, k == 7, [Rsq[k % 2], self.Rcb], Rpb)
                kb.op("act", lambda e: e.activation(out=rstd[:], in_=pb[:], func=AF.Ln, bias=self.cf("epsc", 0, 1), scale=1.0), [Rpb, self.Rcf], [Rrstd])
                kb.op("act", lambda e: e.activation(out=rstd[:], in_=rstd[:], func=AF.Exp, scale=-0.5), [Rrstd], [Rrstd])
                for k in range(8):
                    kb.op("dve", lambda e: e.tensor_tensor(out=tmp[k % 2][:], in0=self.xT[k][:, cs], in1=rstd[:], op=ALU.mult),
                          [self.Rx[k][q], Rrstd], [Rtmp[k % 2]])
                    kb.op("act", lambda e: e.activation(out=self.hT[k][:, cs], in_=tmp[k % 2][:], func=AF.Identity,
                                                        scale=self.der(acol, k, k + 1), bias=self.t_mod[:, shc + k:shc + k + 1]),
                          [Rtmp[k % 2], self.Rder, self.Rmod], [self.Rh[k][q]])
            kb.barrier()
        self.dump(f"h{which}_0", self.hT[0][:], self.Rh[0], [128, S])
        self.dump(f"h{which}_7", self.hT[7][:], self.Rh[7], [128, S])

    def mixer_scratch(self, ms, nq=1, nk=1, va=True):
        kb = self.kb
        self.sqb = [kb.sb(f"sqb{i}", [128, QT], BF16, ms) for i in range(2)]
        self.Rsqb = [Res(f"sqb{i}") for i in range(2)]
        self.rstd = [kb.sb(f"rstd{i}", [128, QT], F32, ms) for i in range(2)]
        self.Rrstd = [Res(f"rstd{i}") for i in range(2)]
        self.rt1 = [kb.sb(f"rt1_{i}", [128, QT], BF16, ms) for i in range(2)]
        self.Rrt1 = [Res(f"rt1_{i}") for i in range(2)]
        self.rt2 = [kb.sb(f"rt2_{i}", [128, QT], BF16, ms) for i in range(2)]
        self.Rrt2 = [Res(f"rt2_{i}") for i in range(2)]
        self.PT = [kb.sb(f"PT{i}", [128, QT], BF16, ms) for i in range(6)]
        self.RPT = [Res(f"PT{i}") for i in range(6)]
        self.rec = [kb.sb(f"rec{i}", [128, QT], F32, ms) for i in range(2)]
        self.Rrec = [Res(f"rec{i}") for i in range(2)]
        self.QTt = [kb.sb(f"QTt{i}", [128, S], BF16, ms) for i in range(nq)]
        self.RQT = [[Res(f"QT{i}_{q}") for q in range(NQ)] for i in range(nq)]
        self.KTt = [kb.sb(f"KTt{i}", [128, S], BF16, ms) for i in range(nk)]
        self.RKT = [[Res(f"KT{i}_{q}") for q in range(NQ)] for i in range(nk)]
        self.pend = []
        self.bg = None
        self.hn_i = 0
        self.pt_i = 0
        self.sb_i = 0
        self.rec_i = 0
        if va:
            self.VA = kb.sb("VA", [128, NKT, 256], BF16, ms)
            self.RVA = [Res(f"VA{g}") for g in range(4)]
            kb.op("pool", lambda e: e.memset(self.VA[:, :, 64:192], 1.0), [], self.RVA)

    def headnorm(self, *a, **kw):
        for _ in self.g_headnorm(*a, **kw):
            pass

    def g_headnorm(self, src_bank, nrows, blk, neps, gcol, dst, Rdst, cs, rope=None, stat_bank=5):
        kb = self.kb
        i = self.hn_i
        self.hn_i ^= 1
        rs = slice(0, nrows)
        src, Rsrc = self.ps[src_bank][rs, :], self.Rps[src_bank]
        pstat, Rpstat = self.ps[stat_bank], self.Rps[stat_bank]
        kb.op("act", lambda e: e.activation(out=self.sqb[i][rs, :], in_=src, func=AF.Square), [Rsrc], [self.Rsqb[i]])
        self.mm(pstat[rs, :], blk, self.sqb[i][rs, :], True, True, [self.Rsqb[i], self.Rcb], Rpstat)
        yield
        kb.op("act", lambda e: e.activation(out=self.rstd[i][rs, :], in_=pstat[rs, :], func=AF.Ln, bias=neps, scale=1.0), [Rpstat, self.Rcf], [self.Rrstd[i]])
        kb.op("act", lambda e: e.activation(out=self.rstd[i][rs, :], in_=self.rstd[i][rs, :], func=AF.Exp, scale=-0.5), [self.Rrstd[i]], [self.Rrstd[i]])
        yield
        kb.op("dve", lambda e: e.scalar_tensor_tensor(out=dst, in0=src, scalar=gcol, in1=self.rstd[i][rs, :], op0=ALU.mult, op1=ALU.mult),
              [Rsrc, self.Rrstd[i], self.Rder], [Rdst])
        if rope is None:
            yield
            return
        ci, swname, wins = rope
        pA, RpA = pstat, Rpstat
        pB, RpB = self.ps[src_bank], self.Rps[src_bank]
        self.mm(pA[rs, :], self.cbf("ident", rs, 0, nrows), dst, True, True, [Rdst, self.Rcb], RpA)
        self.mm(pB[rs, :], self.cbf(swname, rs, 0, nrows), dst, True, True, [Rdst, self.Rcb], RpB)
        yield
        tab = slice(32 * ci, 32 * ci + 32)
        for w0 in wins:
            ws = slice(w0, w0 + 32)
            kb.op("dve", lambda e: e.tensor_tensor(out=self.rt1[i][ws, :], in0=pA[ws, :], in1=self.ropeC[tab, cs], op=ALU.mult),
                  [RpA, self.Rrope], [self.Rrt1[i]])
            kb.op("dve", lambda e: e.tensor_tensor(out=self.rt2[i][ws, :], in0=pB[ws, :], in1=self.ropeS[tab, cs], op=ALU.mult),
                  [RpB, self.Rrope], [self.Rrt2[i]])
            kb.op("pool", lambda e: e.tensor_tensor(out=dst[ws, :], in0=self.rt1[i][ws, :], in1=self.rt2[i][ws, :], op=ALU.add),
                  [self.Rrt1[i], self.Rrt2[i]], [Rdst])
        yield

    def rr(self, *gens):
        gens = list(gens)
        while gens:
            for g in list(gens):
                try:
                    next(g)
                except StopIteration:
                    gens.remove(g)
            yield

    def bg_step(self):
        if self.bg is not None:
            try:
                next(self.bg)
            except StopIteration:
                self.bg = None

    def bg_drain(self):
        while self.bg is not None:
            self.bg_step()

    def g_vgroup(self, g, w_ap_k, rw, ncol, evac, nk=8, lhs_fn=None, rl=None):
        pb, Rpb = self.ps[3 + g % 2], self.Rps[3 + g % 2]
        for t in range(4):
            kt = g * 4 + t
            for k in range(nk):
                lhs = self.hT[k][:, kt * 128:(kt + 1) * 128] if lhs_fn is None else lhs_fn(k, kt)
                rr = self.Rh[k][g] if rl is None else rl(k, g)
                self.mm(pb[:, t * ncol:(t + 1) * ncol], lhs, w_ap_k(k), k == 0, k == nk - 1, [rw, rr], Rpb)
            if t % 2 == 1:
                yield
        evac(g, pb, Rpb)
        yield

    def attn_map(self, tiles, q_ap, k_ap, va_ap, scale, rins, o_bank, extra=None, bias=None, fin=None, after_p=None):
        kb = self.kb
        nt = len(tiles)
        for ti, (kt, c0, c1, mask) in enumerate(tiles):
            n = c1 - c0
            si = self.sb_i
            self.sb_i = (self.sb_i + 1) % 5
            pi = self.pt_i
            self.pt_i = (self.pt_i + 1) % 6
            pS, RpS = self.ps[si], self.Rps[si]
            self.mm(pS[:, 0:n], k_ap(kt), q_ap(c0, c1), True, extra is None, rins, RpS)
            if extra is not None:
                l2, r2 = extra(kt, c0, c1)
                self.mm(pS[:, 0:n], l2, r2, False, True, rins, RpS)
            b = bias(kt) if bias is not None else None
            if b is None:
                kb.op("act", lambda e: e.activation(out=self.PT[pi][:, 0:n], in_=pS[:, 0:n], func=AF.Exp, scale=scale), [RpS], [self.RPT[pi]])
            else:
                kb.op("act", lambda e: e.activation(out=self.PT[pi][:, 0:n], in_=pS[:, 0:n], func=AF.Exp, scale=scale, bias=b),
                      [RpS] + list(rins), [self.RPT[pi]])
            if mask is not None:
                kind, m0, base = mask
                if kind == "causal":
                    kb.op("pool", lambda e: e.affine_select(out=self.PT[pi][:, m0:m0 + 128], in_=self.PT[pi][:, m0:m0 + 128], pattern=[[1, 128]],
                                                            compare_op=ALU.is_ge, fill=0.0, base=0, channel_multiplier=-1),
                          [self.RPT[pi]], [self.RPT[pi]])
                elif kind == "lower":
                    kb.op("pool", lambda e: e.affine_select(out=self.PT[pi][:, m0:m0 + 128], in_=self.PT[pi][:, m0:m0 + 128], pattern=[[-1, 128]],
                                                            compare_op=ALU.is_ge, fill=0.0, base=-1, channel_multiplier=1),
                          [self.RPT[pi]], [self.RPT[pi]])
                elif kind == "vis":
                    kb.op("pool", lambda e: e.affine_select(out=self.PT[pi][:, 0:n], in_=self.PT[pi][:, 0:n], pattern=[[1, n]],
                                                            compare_op=ALU.is_ge, fill=0.0, base=base, channel_multiplier=-16),
                          [self.RPT[pi]], [self.RPT[pi]])
            if after_p is not None:
                after_p(pi)

            def pv(kt=kt, c0=c0, c1=c1, n=n, pi=pi, first=(ti == 0), last=(ti == nt - 1)):
                self.mm(self.ps[o_bank][:, c0:c1], va_ap(kt), self.PT[pi][:, 0:n], first, last,
                        [self.RPT[pi]] + list(rins), self.Rps[o_bank])

            self.pend.append((pv, fin if ti == nt - 1 else None))
            while len(self.pend) > self.LA:
                self.pend_pop()

    LA = 4

    def pend_pop(self):
        pv, fin = self.pend.pop(0)
        pv()
        if fin is not None:
            fin()

    def pend_flush(self):
        while self.pend:
            self.pend_pop()

    @staticmethod
    def causal_tiles(q):
        tiles = [(kt, 0, QT, None) for kt in range(4 * q)]
        for j in range(4):
            tiles.append((4 * q + j, 128 * j, QT, ("causal", 0, 0)))
        return tiles

    def finalize(self, o_bank, parity, dst, Rdst, eps=None):
        kb = self.kb
        i = self.rec_i
        self.rec_i ^= 1
        pO, RpO = self.ps[o_bank], self.Rps[o_bank]
        nr = slice(0, 64) if parity == 0 else slice(64, 128)
        dr = slice(64, 128) if parity == 0 else slice(0, 64)
        if eps is None:
            kb.op("dve", lambda e: e.reciprocal(out=self.rec[i][nr, :], in_=pO[dr, :]), [RpO], [self.Rrec[i]])
        else:
            kb.op("dve", lambda e: e.tensor_scalar(out=self.rec[i][nr, :], in0=pO[dr, :], scalar1=eps, scalar2=None, op0=ALU.add), [RpO], [self.Rrec[i]])
            kb.op("dve", lambda e: e.reciprocal(out=self.rec[i][nr, :], in_=self.rec[i][nr, :]), [self.Rrec[i]], [self.Rrec[i]])
        if dst is not None:
            kb.op("dve", lambda e: e.tensor_tensor(out=dst, in0=pO[nr, :], in1=self.rec[i][nr, :], op=ALU.mult), [RpO, self.Rrec[i]], [Rdst])
        return i, nr

    def proj_T(self, pb, Rpb, rows, w_ap_k, rw, q, nk=8, rhs_fn=None, rrhs=None):
        cs = slice(q * QT, (q + 1) * QT)
        for k in range(nk):
            rhs = self.hT[k][:, cs] if rhs_fn is None else rhs_fn(k)
            rr = self.Rh[k][q] if rrhs is None else rrhs(k)
            self.mm(pb[rows, :], w_ap_k(k), rhs, k == 0, k == nk - 1, [rw, rr], Rpb)

    def v_proj(self, w_ap_k, rw, ncol, evac, nk=8, lhs_fn=None, rl=None):
        for g in range(4):
            pb, Rpb = self.ps[3 + g % 2], self.Rps[3 + g % 2]
            for t in range(4):
                kt = g * 4 + t
                for k in range(nk):
                    lhs = self.hT[k][:, kt * 128:(kt + 1) * 128] if lhs_fn is None else lhs_fn(k, kt)
                    rr = self.Rh[k][g] if rl is None else rl(k, g)
                    self.mm(pb[:, t * ncol:(t + 1) * ncol], lhs, w_ap_k(k), k == 0, k == nk - 1, [rw, rr], Rpb)
            evac(g, pb, Rpb)

    def fox(self, l, fs):
        kb = self.kb
        self.mark('fox_prelude')
        fs0 = fs
        DQ = kb.sb("foxDQ", [128, S], BF16, fs)
        RDQ = [Res(f"foxDQ{q}") for q in range(NQ)]
        Dk = kb.sb("foxDk", [128, NKT, 4], F32, fs)
        RDk = Res("foxDk")
        with contextlib.ExitStack() as fs:
            Dt = kb.sb("foxD", [128, S], F32, fs)
            RD = [Res(f"foxD{q}") for q in range(NQ)]
            ft = [kb.sb(f"foxft{i}", [128, QT], F32, fs) for i in range(2)]
            Rft = [Res(f"foxft{i}") for i in range(2)]
            onesf = kb.sb("foxones", [128, QT], F32, fs)
            Rones = Res("foxones")
            kb.op("pool", lambda e: e.memset(onesf[:], 1.0), [], [Rones])
            wf, rwf = self.load_w(self.wf_pad[l].rearrange("(k p) n -> p k n", p=128), 128)
            for q in range(NQ):
                cs = slice(q * QT, (q + 1) * QT)
                pb, Rpb = self.ps[3 + q % 2], self.Rps[3 + q % 2]
                self.proj_T(pb, Rpb, slice(0, 128), lambda k: wf[:, k, 0:128], rwf, q)
                kb.op("act", lambda e: e.activation(out=ft[0][:], in_=pb[:], func=AF.Exp, scale=-1.0, bias=self.der("negfb")),
                      [Rpb, self.Rder], [Rft[0]])
                kb.op("act", lambda e: e.activation(out=ft[1][:], in_=ft[0][:], func=AF.Ln, bias=self.cf("epsc", 6, 7), scale=1.0), [Rft[0], self.Rcf], [Rft[1]])
                if q == 0:
                    kb.op("dve", lambda e: e.tensor_tensor_scan(out=Dt[:, cs], data0=onesf[:], data1=ft[1][:], initial=0.0, op0=ALU.mult, op1=ALU.add),
                          [Rones, Rft[1]], [RD[q]])
                else:
                    kb.op("dve", lambda e: e.tensor_tensor_scan(out=Dt[:, cs], data0=onesf[:], data1=ft[1][:], initial=Dt[:, q * QT - 1:q * QT],
                                                                op0=ALU.mult, op1=ALU.add), [Rones, Rft[1], RD[q - 1]], [RD[q]])
                kb.op("pool", lambda e: e.tensor_scalar(out=DQ[:, cs], in0=Dt[:, cs], scalar1=-8.0, scalar2=None, op0=ALU.mult), [RD[q]], [RDQ[q]])
                pt, Rpt = self.ps[5], self.Rps[5]
                for t in range(4):
                    kb.op("pe", lambda e: e.transpose(out=pt[:, t * 128:(t + 1) * 128], in_=Dt[:, q * QT + t * 128:q * QT + (t + 1) * 128],
                                                      identity=self.cf("ident")), [RD[q], self.Rcf], [Rpt])
                kb.op("dve", lambda e: e.tensor_copy(out=Dk[:, q * 4:(q + 1) * 4, :], in_=pt[:].rearrange("p (t h r) -> p t h r", t=4, h=4)[:, :, :, 0]),
                      [Rpt], [RDk])
            self.dump("foxD", Dt[:], RD, [128, S])
            kb.barrier()
            if self.dbg:
                kb.e["act"].wait_ge(self.osem[0], self.osem[1])
                kb.barrier()
        if True:
            self.mixer_scratch(fs0)
            self.mark('fox_units')
            fox_w = []
            for u in range(2):
                wqk_, rwqk_ = self.next_wb()
                kb.dma_in("pool", rwqk_, lambda e: [e.dma_start(out=wqk_[:, :, 0:128], in_=self.win_cols(l, C_FOX_Q + u * 128, 128)),
                                                    e.dma_start(out=wqk_[:, :, 128:256], in_=self.win_cols(l, C_FOX_K + u * 128, 128)),
                                                    e.dma_start(out=wqk_[:, :, 256:384], in_=self.win_cols(l, C_FOX_V + u * 128, 128))])
                fox_w.append((wqk_, rwqk_))
            for u in range(2):
                wqk, rwqk = fox_w[u]

                def evac(g, pb, Rpb):
                    src = pb[:].rearrange("p (t a c) -> p t a c", t=4, a=2)
                    dstv = self.VA[:, g * 4:(g + 1) * 4, :].rearrange("p t (a c) -> p t a c", a=4)[:, :, 0::3, :]
                    kb.op("dve", lambda e: e.tensor_copy(out=dstv, in_=src), [Rpb], [self.RVA[g]])

                def pre(q):
                    cs = slice(q * QT, (q + 1) * QT)
                    def ch_q():
                        self.proj_T(self.ps[3], self.Rps[3], slice(0, 128), lambda k: wqk[:, k, 0:128], rwqk, q)
                        yield
                        yield from self.g_headnorm(3, 128, self.cbf("b64"), self.cf("epsc", 1, 2), self.der("fox_gq"), self.QTt[0][:, cs], self.RQT[0][q], cs)

                    def ch_k():
                        self.proj_T(self.ps[4], self.Rps[4], slice(0, 128), lambda k: wqk[:, k, 128:256], rwqk, q)
                        yield
                        yield from self.g_headnorm(4, 128, self.cbf("b64"), self.cf("epsc", 1, 2), self.der("fox_gk"), self.KTt[0][:, cs], self.RKT[0][q], cs, stat_bank=2)

                    yield from self.rr(ch_q(), ch_k())
                    yield from self.g_vgroup(q, lambda k: wqk[:, k, 256:384], rwqk, 128, evac)

                self.bg = pre(0)
                self.bg_drain()
                for q in range(NQ):
                    cs = slice(q * QT, (q + 1) * QT)
                    if q + 1 < NQ:
                        self.bg = pre(q + 1)
                        self.bg_drain()
                    for hh in range(2):
                        h = 2 * u + hh
                        rb = slice(64 * hh, 64 * hh + 64)
                        ob = 6 + hh
                        rins = [self.RQT[0][q], RDk, RDQ[q], self.Rcb] + self.RKT[0][:q + 1] + self.RVA[:q + 1]
                        self.attn_map(
                            self.causal_tiles(q),
                            lambda c0, c1, rb=rb, q=q: self.QTt[0][rb, q * QT + c0:q * QT + c1],
                            lambda kt, rb=rb: self.KTt[0][rb, kt * 128:(kt + 1) * 128],
                            lambda kt, hh=hh: self.VA[:, kt, 128 * hh:128 * hh + 128],
                            0.125, rins, ob,
                            extra=lambda kt, c0, c1, h=h, q=q: (self.cbf("selrow", slice(0, 128), 128 * h, 128 * h + 128), DQ[:, q * QT + c0:q * QT + c1]),
                            bias=lambda kt, h=h: Dk[:, kt, h:h + 1],
                            fin=lambda ob=ob, hh=hh, u=u, rb=rb, cs=cs, q=q: self.finalize(ob, hh, self.oT[1][u][rb, cs], self.RoT[1][u][q]))
                    self.bg_drain()
                self.pend_flush()

    def merge(self, l):
        self.mark('merge')
        kb = self.kb
        with contextlib.ExitStack() as ms:
            MT = [kb.sb(f"MT{i}", [128, S], BF16, ms) for i in range(4)]
            RMT = [[Res(f"MT{i}_{q}") for q in range(NQ)] for i in range(4)]
            brw = [kb.sb(f"brw{i}", [128, 2, 4, 128], BF16, ms) for i in range(2)]
            Rbrw = [Res(f"brw{i}") for i in range(2)]
            wo = [kb.sb(f"wo{i}", [128, 4, 128], BF16, ms) for i in range(2)]
            Rwo = [Res(f"wo{i}") for i in range(2)]
            sig = [kb.sb(f"sig{i}", [128, QT], F32, ms) for i in range(2)]
            Rsig = [Res(f"sig{i}") for i in range(2)]
            tmp = [kb.sb(f"mtmp{i}", [128, QT], F32, ms) for i in range(2)]
            Rtmp = [Res(f"mtmp{i}") for i in range(2)]
            acc = [kb.sb(f"macc{i}", [128, QT], F32, ms) for i in range(2)]
            Racc = [Res(f"macc{i}") for i in range(2)]
            gwv = self.gate_w[l].rearrange("(k p) (m n) -> p k m n", p=128, m=4)
            brv = self.br_w[l].rearrange("m (k p) n -> p k m n", p=128)
            wov = self.w_out[l].rearrange("(k p) n -> p k n", p=128)
            n_it = 0
            loaded = {}

            def load_dc(dc):
                if dc in loaded or dc >= 8:
                    return
                wb_, rgw_ = self.next_wb()
                kb.dma_in("pool", rgw_, lambda e: [e.dma_start(out=wb_[:, :, m_ * 128:(m_ + 1) * 128], in_=gwv[:, :, m_, dc * 128:(dc + 1) * 128]) for m_ in range(4)])
                bi_ = dc % 2
                kb.dma_in("pool", Rbrw[bi_], lambda e: [e.dma_start(out=brw[bi_][:, :, m_, :], in_=brv[:, :, m_, dc * 128:(dc + 1) * 128]) for m_ in range(4)])
                loaded[dc] = (wb_, rgw_)

            wo_loaded = {}

            def load_wo(grp, dout):
                key = grp * 8 + dout
                if key in wo_loaded or dout >= 8:
                    return
                wi_ = key % 2
                kb.dma_in("pool", Rwo[wi_], lambda e: e.dma_start(out=wo[wi_][:], in_=wov[:, grp * 4:(grp + 1) * 4, dout * 128:(dout + 1) * 128]))
                wo_loaded[key] = wi_

            for grp in range(2):
                for dcl in range(4):
                    dc = grp * 4 + dcl
                    load_dc(dc)
                    if dcl < 3:
                        load_dc(dc + 1)
                    wb, rgw = loaded[dc]
                    bi = dc % 2
                    for q in range(NQ):
                        cs = slice(q * QT, (q + 1) * QT)
                        ai = n_it % 2
                        n_it += 1
                        for m in range(4):
                            pg, Rpg = self.ps[m % 2], self.Rps[m % 2]
                            py, Rpy = self.ps[2 + m % 2], self.Rps[2 + m % 2]
                            for k in range(8):
                                self.mm(pg[:], wb[:, k, m * 128:(m + 1) * 128], self.hT[k][:, cs], k == 0, k == 7, [rgw, self.Rh[k][q]], Rpg)
                            for k in range(2):
                                self.mm(py[:], brw[bi][:, k, m, :], self.oT[m][k][:, cs], k == 0, k == 1, [Rbrw[bi], self.RoT[m][k][q]], Rpy)
                            si = m % 2
                            kb.op("act", lambda e: e.activation(out=sig[si][:], in_=pg[:], func=AF.Sigmoid, bias=self.sm(l, "gate_b", m * 8 + dc, m * 8 + dc + 1), scale=1.0),
                                  [Rpg, self.Rsm], [Rsig[si]])
                            if m == 0:
                                kb.op("dve", lambda e: e.tensor_tensor(out=acc[ai][:], in0=py[:], in1=sig[si][:], op=ALU.mult), [Rpy, Rsig[si]], [Racc[ai]])
                            else:
                                kb.op("dve", lambda e: e.tensor_tensor(out=tmp[si][:], in0=py[:], in1=sig[si][:], op=ALU.mult), [Rpy, Rsig[si]], [Rtmp[si]])
                                if m < 3:
                                    kb.op("dve", lambda e: e.tensor_tensor(out=acc[ai][:], in0=acc[ai][:], in1=tmp[si][:], op=ALU.add), [Racc[ai], Rtmp[si]], [Racc[ai]])
                                else:
                                    kb.op("dve", lambda e: e.tensor_tensor(out=MT[dcl][:, cs], in0=acc[ai][:], in1=tmp[si][:], op=ALU.add),
                                          [Racc[ai], Rtmp[si]], [RMT[dcl][q]])
                if l == 0 and grp == 0:
                    self.dump("merged0", MT[0][:], RMT[0], [128, S])
                for dout in range(8):
                    load_wo(grp, dout)
                    load_wo(grp, dout + 1)
                    if dout == 7 and grp == 0:
                        load_dc(4)
                    wi = wo_loaded[grp * 8 + dout]
                    for q in range(NQ):
                        cs = slice(q * QT, (q + 1) * QT)
                        pb, Rpb = self.ps[4 + (dout * NQ + q) % 2], self.Rps[4 + (dout * NQ + q) % 2]
                        for dcl in range(4):
                            self.mm(pb[:], wo[wi][:, dcl, :], MT[dcl][:, cs], dcl == 0, dcl == 3, [Rwo[wi], RMT[dcl][q]], Rpb)
                        kb.op("dve", lambda e: e.scalar_tensor_tensor(out=self.xT[dout][:, cs], in0=pb[:], scalar=self.t_mod[:, 16 + dout:17 + dout],
                                                                      in1=self.xT[dout][:, cs], op0=ALU.mult, op1=ALU.add),
                              [Rpb, self.Rmod, self.Rx[dout][q]], [self.Rx[dout][q]])
            kb.barrier()

    def ffn(self, l):
        kb = self.kb
        self.mark('ffn')
        NJ = NFF // 2
        with contextlib.ExitStack() as fs:
            AT = [kb.sb(f"AT{j}", [128, S], BF16, fs) for j in range(NJ)]
            RAT = [[Res(f"AT{j}_{q}") for q in range(NQ)] for j in range(NJ)]
            G = [kb.sb(f"G{i}", [128, QT + 2], F32, fs) for i in range(2)]
            RG = [Res(f"G{i}") for i in range(2)]
            GC = [kb.sb(f"GC{i}", [128, QT], F32, fs) for i in range(2)]
            RGC = [Res(f"GC{i}") for i in range(2)]
            wd = [kb.sb(f"wd{i}", [128, NJ, 128], BF16, fs) for i in range(3)]
            Rwd = [Res(f"wd{i}") for i in range(3)]
            wd_n = 0
            gn = 0
            upv = self.w_up[l].rearrange("(k p) n -> p k n", p=128)
            dnv = self.w_down[l].rearrange("(j p) n -> p j n", p=128)
            if l + 1 < self.n_layers:
                self.ada_gen = self.g_ada(l + 1)
            up_loaded = {}

            def load_up(j0):
                if j0 in up_loaded or j0 >= NFF:
                    return
                nj_ = 1 if (j0 % NJ) == NJ - 1 else 2
                wb_, rwu_ = self.next_wb()
                kb.dma_in("pool", rwu_, lambda e: [e.dma_start(out=wb_[:, :, 0:128 * nj_], in_=upv[:, :, j0 * 128:(j0 + nj_) * 128]),
                                                   e.dma_start(out=wb_[:, :, 256:256 + 128 * nj_], in_=upv[:, :, DFF + j0 * 128:DFF + (j0 + nj_) * 128])])
                up_loaded[j0] = (wb_, rwu_)

            wd_loaded = {}

            def load_wd(ps__, dout):
                key = ps__ * 8 + dout
                if key in wd_loaded or dout >= 8:
                    return
                wi_ = key % 3
                kb.dma_in("pool", Rwd[wi_], lambda e: e.dma_start(out=wd[wi_][:], in_=dnv[:, ps__ * NJ:(ps__ + 1) * NJ, dout * 128:(dout + 1) * 128]))
                wd_loaded[key] = wi_

            for ps_ in range(2):
                wb, rwu = None, None
                for jj in range(NJ):
                    j = ps_ * NJ + jj
                    self.ada_step(3)
                    if jj % 2 == 0:
                        load_up(j)
                        nxt = j + 2
                        if jj + 2 < NJ:
                            load_up(nxt)
                        elif ps_ == 0:
                            pass
                        wb, rwu = up_loaded[j]
                    if jj == NJ - 1:
                        load_wd(ps_, 0)
                    co = 128 * (jj % 2)
                    w0 = self.sm(l, "conv_w", 0 * NFF + j, 0 * NFF + j + 1)
                    w1 = self.sm(l, "conv_w", 1 * NFF + j, 1 * NFF + j + 1)
                    w2 = self.sm(l, "conv_w", 2 * NFF + j, 2 * NFF + j + 1)
                    cb_ = self.sm(l, "conv_b", j, j + 1)
                    for q in range(NQ):
                        gi = gn % 2
                        gn += 1
                        cs = slice(q * QT, (q + 1) * QT)
                        pg, Rpg = self.ps[gi], self.Rps[gi]
                        pv, Rpv = self.ps[2 + gi], self.Rps[2 + gi]
                        for k in range(8):
                            self.mm(pg[:], wb[:, k, co:co + 128], self.hT[k][:, cs], k == 0, k == 7, [rwu, self.Rh[k][q]], Rpg)
                        for k in range(8):
                            self.mm(pv[:], wb[:, k, 256 + co:256 + co + 128], self.hT[k][:, cs], k == 0, k == 7, [rwu, self.Rh[k][q]], Rpv)
                        if q == 0:
                            kb.op("dve", lambda e: e.memset(G[gi][:, 0:2], 0.0), [], [RG[gi]])
                        else:
                            kb.op("dve", lambda e: e.tensor_copy(out=G[gi][:, 0:2], in_=G[1 - gi][:, QT:QT + 2]), [RG[1 - gi]], [RG[gi]])
                        kb.op("act", lambda e: e.activation(out=G[gi][:, 2:2 + QT], in_=pg[:], func=AF.Copy), [Rpg], [RG[gi]])
                        kb.op("act", lambda e: e.activation(out=GC[gi][:], in_=pg[:], func=AF.Identity, scale=w2, bias=cb_),
                              [Rpg, self.Rsm], [RGC[gi]])
                        kb.op("dve", lambda e: e.scalar_tensor_tensor(out=GC[gi][:], in0=G[gi][:, 1:1 + QT], scalar=w1, in1=GC[gi][:], op0=ALU.mult, op1=ALU.add),
                              [RG[gi], self.Rsm, RGC[gi]], [RGC[gi]])
                        kb.op("dve", lambda e: e.scalar_tensor_tensor(out=GC[gi][:], in0=G[gi][:, 0:QT], scalar=w0, in1=GC[gi][:], op0=ALU.mult, op1=ALU.add),
                              [RG[gi], self.Rsm, RGC[gi]], [RGC[gi]])
                        kb.op("act", lambda e: e.activation(out=GC[gi][:], in_=GC[gi][:], func=AF.Silu), [RGC[gi]], [RGC[gi]])
                        kb.op("dve", lambda e: e.tensor_tensor(out=AT[jj][:, cs], in0=pv[:], in1=GC[gi][:], op=ALU.mult),
                              [Rpv, RGC[gi]], [RAT[jj][q]])
                for dout in range(8):
                    self.ada_step(3)
                    load_wd(ps_, dout)
                    load_wd(ps_, dout + 1)
                    if dout == 6 and ps_ == 0:
                        load_up(NJ)
                    wi = wd_loaded[ps_ * 8 + dout]
                    for q in range(NQ):
                        cs = slice(q * QT, (q + 1) * QT)
                        pb, Rpb = self.ps[4 + q % 2], self.Rps[4 + q % 2]
                        for jj in range(NJ):
                            self.mm(pb[:], wd[wi][:, jj, :], AT[jj][:, cs], jj == 0, jj == NJ - 1, [Rwd[wi], RAT[jj][q]], Rpb)
                        kb.op("dve", lambda e: e.scalar_tensor_tensor(out=self.xT[dout][:, cs], in0=pb[:], scalar=self.t_mod[:, 40 + dout:41 + dout],
                                                                      in1=self.xT[dout][:, cs], op0=ALU.mult, op1=ALU.add),
                              [Rpb, self.Rmod, self.Rx[dout][q]], [self.Rx[dout][q]])
            kb.barrier()

    def rstd_from(self, pstat_ap, out_ap, neps, Rin, Rout):
        kb = self.kb
        kb.op("act", lambda e: e.activation(out=out_ap, in_=pstat_ap, func=AF.Ln, bias=neps, scale=1.0), [Rin, self.Rcf], [Rout])
        kb.op("act", lambda e: e.activation(out=out_ap, in_=out_ap, func=AF.Exp, scale=-0.5), [Rout], [Rout])

    def mla(self, l, ms):
        kb = self.kb
        self.mark('mla_prelude')
        cqn = [kb.sb(f"cqn{i}", [128, S], BF16, ms) for i in range(2)]
        Rcqn = [[Res(f"cqn{i}_{q}") for q in range(NQ)] for i in range(2)]
        ckvn = kb.sb("ckvn", [128, S], BF16, ms)
        Rckvn = [Res(f"ckvn{q}") for q in range(NQ)]
        wuq = kb.sb("wuq", [128, 2, 384], BF16, ms)
        wukv = kb.sb("wukv", [128, 512], BF16, ms)
        Rwuq, Rwukv = Res("wuq"), Res("wukv")
        kb.dma_in("pool", Rwuq, lambda e: e.dma_start(out=wuq[:], in_=self.w_uq[l].rearrange("(k p) n -> p k n", p=128)))
        kb.dma_in("pool", Rwukv, lambda e: e.dma_start(out=wukv[:], in_=self.w_ukv[l]))
        self.mixer_scratch(ms)
        wc, rwc = self.load_w(self.win_cols(l, C_MLA_CQ, 416), 416)
        pstat, Rpstat = self.ps[5], self.Rps[5]
        for q in range(NQ):
            cs = slice(q * QT, (q + 1) * QT)
            for c in range(2):
                self.proj_T(self.ps[3 + c], self.Rps[3 + c], slice(0, 128), lambda k: wc[:, k, c * 128:(c + 1) * 128], rwc, q)
                kb.op("act", lambda e: e.activation(out=self.sqb[c][:], in_=self.ps[3 + c][:], func=AF.Square), [self.Rps[3 + c]], [self.Rsqb[c]])
                self.mm(pstat[:], self.cbf("ones"), self.sqb[c][:], c == 0, c == 1, [self.Rsqb[c], self.Rcb], Rpstat)
            self.rstd_from(pstat[:], self.rstd[0][:], self.cf("epsc", 4, 5), Rpstat, self.Rrstd[0])
            for c in range(2):
                kb.op("dve", lambda e: e.scalar_tensor_tensor(out=cqn[c][:, cs], in0=self.ps[3 + c][:], scalar=self.der("mla_cqg", c, c + 1),
                                                              in1=self.rstd[0][:], op0=ALU.mult, op1=ALU.mult),
                      [self.Rps[3 + c], self.Rrstd[0], self.Rder], [Rcqn[c][q]])
            self.proj_T(self.ps[3], self.Rps[3], slice(0, 128), lambda k: wc[:, k, 256:384], rwc, q)
            self.headnorm(3, 128, self.cbf("ones"), self.cf("epsc", 5, 6), self.der("mla_ckvg"), ckvn[:, cs], Rckvn[q], cs)
        self.mark('mla_heads')
        r96 = slice(0, 96)
        for h in range(4):
            hh = h % 2
            vcol = 0 if hh == 0 else 192

            def evac(g, pb, Rpb, vcol=vcol):
                kb.op("dve", lambda e: e.tensor_copy(out=self.VA[:, g * 4:(g + 1) * 4, vcol:vcol + 64], in_=pb[:, 0:256].rearrange("p (t c) -> p t c", t=4)),
                      [Rpb], [self.RVA[g]])

            def pre(q, h=h):
                cs = slice(q * QT, (q + 1) * QT)
                def ch_q():
                    for k in range(2):
                        self.mm(self.ps[3][0:64, :], wuq[:, k, 96 * h + 32:96 * h + 96], cqn[k][:, cs], k == 0, k == 1, [Rwuq, Rcqn[k][q]], self.Rps[3])
                    for k in range(2):
                        self.mm(self.ps[3][64:96, :], wuq[:, k, 96 * h:96 * h + 32], cqn[k][:, cs], k == 0, k == 1, [Rwuq, Rcqn[k][q]], self.Rps[3])
                    yield
                    yield from self.g_headnorm(3, 96, self.cbf("ones", r96, 0, 96), self.cf("epsc", 2, 3, r96), self.der("mla_gq", 0, 1, r96), self.QTt[0][r96, cs],
                                               self.RQT[0][q], cs, rope=(1, "sw_mla", [64]))

                def ch_k():
                    self.mm(self.ps[4][0:64, :], wukv[:, 128 * h:128 * h + 64], ckvn[:, cs], True, True, [Rwukv, Rckvn[q]], self.Rps[4])
                    self.proj_T(self.ps[4], self.Rps[4], slice(64, 96), lambda k: wc[:, k, 384:416], rwc, q)
                    yield
                    yield from self.g_headnorm(4, 96, self.cbf("ones", r96, 0, 96), self.cf("epsc", 2, 3, r96), self.der("mla_gk", 0, 1, r96), self.KTt[0][r96, cs],
                                               self.RKT[0][q], cs, rope=(1, "sw_mla", [64]), stat_bank=2)

                yield from self.rr(ch_q(), ch_k())
                yield from self.g_vgroup(q, lambda k: wukv[:, 128 * h + 64:128 * h + 128], Rwukv, 64, evac, nk=1,
                                         lhs_fn=lambda k, kt: ckvn[:, kt * 128:(kt + 1) * 128], rl=lambda k, g: Rckvn[g])

            self.bg = pre(0)
            self.bg_drain()
            for q in range(NQ):
                cs = slice(q * QT, (q + 1) * QT)
                if q + 1 < NQ:
                    self.bg = pre(q + 1)
                    self.bg_drain()
                rb = slice(64 * hh, 64 * hh + 64)
                ob = 6 + q % 2
                rins = [self.RQT[0][q]] + self.RKT[0][:q + 1] + self.RVA[:q + 1]
                self.attn_map(self.causal_tiles(q),
                              lambda c0, c1, q=q: self.QTt[0][r96, q * QT + c0:q * QT + c1],
                              lambda kt: self.KTt[0][r96, kt * 128:(kt + 1) * 128],
                              lambda kt, hh=hh: self.VA[:, kt, 128 * hh:128 * hh + 128],
                              96.0 ** -0.5, rins, ob,
                              fin=lambda ob=ob, hh=hh, h=h, rb=rb, cs=cs, q=q: self.finalize(ob, hh, self.oT[2][h // 2][rb, cs], self.RoT[2][h // 2][q]))
                self.bg_drain()
            self.pend_flush()

    def diff(self, l, ms):
        kb = self.kb
        self.mark('diff')
        self.mixer_scratch(ms)
        dsq = kb.sb("dsq", [128, QT], BF16, ms)
        Rdsq = Res("dsq")
        r64 = slice(0, 64)
        pstat, Rpstat = self.ps[5], self.Rps[5]
        dif_w = {}

        def load_dif(h_):
            if h_ in dif_w or h_ >= 4:
                return
            w_, r_ = self.next_wb()
            kb.dma_in("pool", r_, lambda e: [e.dma_start(out=w_[:, :, 0:64], in_=self.win_cols(l, C_DIF_Q + 64 * h_, 64)),
                                             e.dma_start(out=w_[:, :, 64:128], in_=self.win_cols(l, C_DIF_K + 64 * h_, 64)),
                                             e.dma_start(out=w_[:, :, 128:192], in_=self.win_cols(l, C_DIF_V + 64 * h_, 64))])
            dif_w[h_] = (w_, r_)

        for h in range(4):
            hh = h % 2
            load_dif(h)
            load_dif(h + 1)
            wqk, rwqk = dif_w[h]
            vcol = 0 if hh == 0 else 192

            def evac(g, pb, Rpb, vcol=vcol):
                kb.op("dve", lambda e: e.tensor_copy(out=self.VA[:, g * 4:(g + 1) * 4, vcol:vcol + 64], in_=pb[:, 0:256].rearrange("p (t c) -> p t c", t=4)),
                      [Rpb], [self.RVA[g]])

            def pre(q, wqk=wqk, rwqk=rwqk):
                cs = slice(q * QT, (q + 1) * QT)
                def ch_q():
                    self.proj_T(self.ps[3], self.Rps[3], r64, lambda k: wqk[:, k, 0:64], rwqk, q)
                    yield
                    yield from self.g_headnorm(3, 64, self.cbf("b32", r64, 0, 64), self.cf("epsc", 3, 4, r64), self.der("dif_gq", 0, 1, r64), self.QTt[0][r64, cs],
                                               self.RQT[0][q], cs, rope=(2, "sw_dif", [0, 32]))

                def ch_k():
                    self.proj_T(self.ps[4], self.Rps[4], r64, lambda k: wqk[:, k, 64:128], rwqk, q)
                    yield
                    yield from self.g_headnorm(4, 64, self.cbf("b32", r64, 0, 64), self.cf("epsc", 3, 4, r64), self.der("dif_gk", 0, 1, r64), self.KTt[0][r64, cs],
                                               self.RKT[0][q], cs, rope=(2, "sw_dif", [0, 32]), stat_bank=2)

                yield from self.rr(ch_q(), ch_k())
                yield from self.g_vgroup(q, lambda k: wqk[:, k, 128:192], rwqk, 64, evac)

            self.bg = pre(0)
            self.bg_drain()
            for q in range(NQ):
                cs = slice(q * QT, (q + 1) * QT)
                if q + 1 < NQ:
                    self.bg = pre(q + 1)
                    self.bg_drain()
                rins = [self.RQT[0][q]] + self.RKT[0][:q + 1] + self.RVA[:q + 1]

                def fin_diff(q=q, cs=cs, hh=hh, h=h):
                    recs = []
                    for a in range(2):
                        i, nr = self.finalize(6 + a, hh, None, None)
                        kb.op("dve", lambda e: e.tensor_tensor(out=self.rec[i][nr, :], in0=self.ps[6 + a][nr, :], in1=self.rec[i][nr, :], op=ALU.mult),
                              [self.Rps[6 + a], self.Rrec[i]], [self.Rrec[i]])
                        recs.append(i)
                    i0, i1 = recs
                    kb.op("dve", lambda e: e.scalar_tensor_tensor(out=self.rec[i0][nr, :], in0=self.rec[i1][nr, :], scalar=self.der("neglam", 0, 1, nr),
                                                                  in1=self.rec[i0][nr, :], op0=ALU.mult, op1=ALU.add),
                          [self.Rrec[i0], self.Rrec[i1], self.Rder], [self.Rrec[i0]])
                    pst, Rpst = self.ps[6], self.Rps[6]
                    kb.op("pool", lambda e: e.tensor_tensor(out=dsq[nr, :], in0=self.rec[i0][nr, :], in1=self.rec[i0][nr, :], op=ALU.mult),
                          [self.Rrec[i0]], [Rdsq])
                    self.mm(pst[nr, :], self.cbf("ones", nr, 0, 64), dsq[nr, :], True, True, [Rdsq, self.Rcb], Rpst)
                    self.rstd_from(pst[nr, :], self.rec[i1][nr, :], self.cf("epsc", 1, 2, nr), Rpst, self.Rrec[i1])
                    kb.op("dve", lambda e: e.scalar_tensor_tensor(out=self.oT[3][h // 2][nr, cs], in0=self.rec[i0][nr, :], scalar=self.der("dif_og", 0, 1, nr),
                                                                  in1=self.rec[i1][nr, :], op0=ALU.mult, op1=ALU.mult),
                          [self.Rrec[i0], self.Rrec[i1], self.Rder], [self.RoT[3][h // 2][q]])

                for a in range(2):
                    ra = slice(32 * a, 32 * a + 32)
                    self.attn_map(self.causal_tiles(q),
                                  lambda c0, c1, ra=ra, q=q: self.QTt[0][ra, q * QT + c0:q * QT + c1],
                                  lambda kt, ra=ra: self.KTt[0][ra, kt * 128:(kt + 1) * 128],
                                  lambda kt, hh=hh: self.VA[:, kt, 128 * hh:128 * hh + 128],
                                  32.0 ** -0.5, rins, 6 + a, fin=(fin_diff if a == 1 else None))
                self.bg_drain()
            self.pend_flush()


def prep_inputs(inp, layer_ids=tuple(range(DEPTH))):
    li = list(layer_ids)
    cb, cf = make_consts(layer_ids)
    sm = make_smalls(inp, layer_ids)

    def W(name):
        return np.ascontiguousarray(np.asarray(inp[name], np.float32)[li])

    w_in = W("w_in")
    gcols = []
    for br in range(3):
        for pr in range(2):
            for hh in range(2):
                gcols += [C_NSA_G + br * 4 + pr * 2 + hh] * 64
    wg_rep = np.ascontiguousarray(w_in[:, :, gcols])
    wf_pad = np.zeros((len(li), D, 128), np.float32)
    for h in range(4):
        wf_pad[:, :, 32 * h] = w_in[:, :, C_FOX_F + h]
    shared = {
        "ada_w": W("ada_w"), "w_in": w_in, "wg_rep": wg_rep, "wf_pad": wf_pad,
        "cmp_w1": W("nsa_cmp_w1"), "cmp_w2": W("nsa_cmp_w2"),
        "cmp_pe": np.ascontiguousarray(np.transpose(W("nsa_cmp_pe"), (0, 1, 3, 2))),
        "w_uq": W("mla_w_uq"), "w_ukv": W("mla_w_ukv"), "br_w": W("br_w"), "gate_w": W("gate_w"),
        "w_out": W("w_out"), "w_up": W("ffn_w_up"), "w_down": W("ffn_w_down"),
        "smalls": sm, "cbf": cb, "cf32": cf,
    }
    maps = []
    for b in range(8):
        m = dict(shared)
        m["x"] = np.ascontiguousarray(inp["x"][b], np.float32)
        m["cT"] = np.ascontiguousarray(np.asarray(inp["c"][b], np.float32).reshape(8, 128).T)
        m["pos"] = np.ascontiguousarray(np.asarray(inp["positions"][b], np.int32).reshape(1, S))
        maps.append(m)
    return maps


FUSED = True


def kernel(**inputs):
    inp = {k: np.asarray(v) for k, v in inputs.items()}
    if FUSED:
        maps = prep_inputs(inp)
        nc = Prog().build()
        res = run_bass_kernel_spmd(nc, maps, core_ids=list(range(8)))
        return np.stack([np.asarray(res.results[b]["y"], np.float32) for b in range(8)], axis=0)
    nc = Prog(n_layers=1, wdepth=1).build()
    x = np.asarray(inp["x"], np.float32)
    for l in range(DEPTH):
        cur = dict(inp)
        cur["x"] = x
        maps = prep_inputs(cur, (l,))
        res = run_bass_kernel_spmd(nc, maps, core_ids=list(range(8)))
        x = np.stack([np.asarray(res.results[b]["y"], np.float32) for b in range(8)], axis=0)
    return x
```

```python
import contextlib
import math
import numpy as np
import ml_dtypes
import concourse.bass as bass
import concourse.mybir as mybir
from concourse.bass_utils import run_bass_kernel_spmd

F32 = mybir.dt.float32
BF16 = mybir.dt.bfloat16
I32 = mybir.dt.int32
AF = mybir.ActivationFunctionType
ALU = mybir.AluOpType

S = 2048
D = 1024
NQ = 4
QT = 512
NKT = 16
DEPTH = 4
EPS = 1e-6
DFF = 2816
NFF = 22
THETA = 500000.0
BIG = 30000.0
IN_W = 2608


class Res:
    __slots__ = ("name", "w", "r", "dsem", "dcnt")

    def __init__(self, name):
        self.name = name
        self.w = None
        self.r = {}
        self.dsem = None
        self.dcnt = 0


class KB:
    ENG = ("pe", "act", "dve", "pool", "sp")

    def __init__(self, nc, stack):
        self.nc = nc
        self.stack = stack
        self.e = {"pe": nc.tensor, "act": nc.scalar, "dve": nc.vector, "pool": nc.gpsimd, "sp": nc.sync}
        self.sem = {k: stack.enter_context(nc.semaphore("s_" + k)) for k in self.ENG}
        self.cnt = {k: 0 for k in self.ENG}
        self.seen = {k: {} for k in self.ENG}
        self.same_engine_raw = True
        self.n_wait = 0

    def sb(self, name, shape, dt, stack=None):
        self.uid = getattr(self, "uid", 0) + 1
        return (stack or self.stack).enter_context(self.nc.sbuf_tensor(f"{name}_u{self.uid}", list(shape), dt))

    def ps(self, name, shape, dt=F32):
        return self.stack.enter_context(self.nc.psum_tensor(name, list(shape), dt))

    def newsem(self, name):
        self.uid = getattr(self, "uid", 0) + 1
        return self.stack.enter_context(self.nc.semaphore(f"{name}_u{self.uid}"))

    def _need(self, eng, deps):
        for key, sem, val in deps:
            if self.seen[eng].get(key, 0) >= val:
                continue
            self.e[eng].wait_ge(sem, val)
            self.n_wait += 1
            self.seen[eng][key] = val

    def _deps(self, eng, reads, writes):
        d = {}

        def add(w):
            if w is None:
                return
            e, i = w
            if e == "dma":
                sem, val = i
                key = ("dma", id(sem))
                if d.get(key, (None, 0))[1] < val:
                    d[key] = (sem, val)
            else:
                if e == eng and (eng == "pe" or not self.same_engine_raw):
                    return
                if d.get(e, (None, 0))[1] < i:
                    d[e] = (self.sem[e], i)

        for r in reads:
            add(r.w)
        for w in writes:
            add(w.w)
            for e, i in w.r.items():
                if e == eng:
                    continue
                if e == "dma":
                    add(("dma", i))
                else:
                    add((e, i))
        return [(k, s, v) for k, (s, v) in d.items()]

    def op(self, eng, fn, reads=(), writes=()):
        self._need(eng, self._deps(eng, reads, writes))
        ins = fn(self.e[eng])
        ins.then_inc(self.sem[eng], 1)
        self.cnt[eng] += 1
        idx = self.cnt[eng]
        for r in reads:
            r.r[eng] = idx
        for w in writes:
            w.w = (eng, idx)
            w.r = {}
        return ins

    def dma_in(self, q, res, fn, reads=()):
        if res.dsem is None:
            res.dsem = self.newsem("d_" + res.name)
        self._need(q, self._deps(q, reads, [res]))
        inss = fn(self.e[q])
        if not isinstance(inss, (list, tuple)):
            inss = [inss]
        for ins in inss:
            ins.then_inc(res.dsem, 16)
            res.dcnt += 16
        res.w = ("dma", (res.dsem, res.dcnt))
        res.r = {}

    def dma_out(self, q, res_list, fn, sem):
        self._need(q, self._deps(q, res_list, []))
        inss = fn(self.e[q])
        if not isinstance(inss, (list, tuple)):
            inss = [inss]
        for ins in inss:
            ins.then_inc(sem[0], 16)
            sem[1] += 16
        for r in res_list:
            r.r["dma"] = (sem[0], sem[1])

    def barrier(self):
        for a in ("pe", "act", "dve", "pool"):
            deps = []
            for b in ("pe", "act", "dve", "pool"):
                if a != b and self.cnt[b] > 0:
                    deps.append((b, self.sem[b], self.cnt[b]))
            self._need(a, deps)


CB = {}
CF = {}
SM = {}


def _alloc(tab, name, n, cur):
    tab[name] = (cur, cur + n)
    return cur + n


def _layout():
    c = 0
    for name, n in (("ident", 128), ("ones", 128), ("b64", 128), ("b32", 128), ("sw_nsa", 128),
                    ("sw_mla", 128), ("sw_dif", 128), ("selrow", 512), ("esel", 2048), ("ovl", 64), ("tabA", 512), ("tabB", 512)):
        c = _alloc(CB, name, n, c)
    CB["_n"] = c
    c = 0
    for name, n in (("ident", 128), ("invf", 1), ("sgn", 1), ("negpi", 1), ("epsc", 8), ("lamc", 2 * DEPTH)):
        c = _alloc(CF, name, n, c)
    CF["_n"] = c
    c = 0
    for name, n in (("ada_b", 48), ("gate_b", 32), ("conv_w", 66), ("conv_b", 22),
                    ("nsa_gq", 1), ("nsa_gkc", 1), ("nsa_gks", 1), ("nsa_gkw", 1),
                    ("fox_gq", 1), ("fox_gk", 1), ("fox_fb", 1),
                    ("mla_cqg", 2), ("mla_ckvg", 1), ("mla_gq", 1), ("mla_gk", 1),
                    ("dif_gq", 1), ("dif_gk", 1), ("dif_og", 1), ("dif_lam", 128)):
        c = _alloc(SM, name, n, c)
    SM["_n"] = c


_layout()

DER = {}
_c = 0
for _name, _n in (("a1", 8), ("a2", 8), ("nsa_gq", 1), ("nsa_gkc", 1), ("nsa_gks", 1), ("nsa_gkw", 1),
                  ("fox_gq", 1), ("fox_gk", 1), ("negfb", 1), ("mla_cqg", 2), ("mla_ckvg", 1), ("mla_gq", 1),
                  ("mla_gk", 1), ("dif_gq", 1), ("dif_gk", 1), ("dif_og", 1), ("neglam", 1), ("t0", 4), ("t1", 4)):
    _c = _alloc(DER, _name, _n, _c)
DER["_n"] = _c


def _rope_partner(r, head, n_rot):
    rr = r % head
    half = n_rot // 2
    base = r - rr
    if rr < half:
        return base + rr + half, rr, -1.0
    if rr < n_rot:
        return base + rr - half, rr - half, 1.0
    return None, None, 0.0


def make_consts(layer_ids=tuple(range(DEPTH))):
    cb = np.zeros((128, CB["_n"]), np.float32)
    cf = np.zeros((128, CF["_n"]), np.float32)
    p = np.arange(128)
    cb[:, CB["ident"][0]:CB["ident"][1]] = np.eye(128)
    cb[:, CB["ones"][0]:CB["ones"][1]] = 1.0
    cb[:, CB["b64"][0]:CB["b64"][1]] = (p[:, None] // 64 == p[None, :] // 64)
    cb[:, CB["b32"][0]:CB["b32"][1]] = (p[:, None] // 32 == p[None, :] // 32)
    for ci, (nm, head, nrot) in enumerate((("sw_nsa", 64, 16), ("sw_mla", 96, 32), ("sw_dif", 32, 8))):
        sw = np.zeros((128, 128), np.float32)
        half = nrot // 2
        inv = (np.float32(THETA) ** (-np.arange(half, dtype=np.float32) / np.float32(half))).astype(np.float32)
        for m in range(128):
            if nm == "sw_mla":
                if 64 <= m < 96:
                    rr = m - 64
                    sw[64 + (rr + 16 if rr < 16 else rr - 16), m] = 1.0
                continue
            pr, fi, sg = _rope_partner(m, head, nrot)
            if pr is not None:
                sw[pr, m] = 1.0
        for r in range(32):
            pr, fi, sg = _rope_partner(r, head, nrot)
            if pr is not None:
                cf[32 * ci + r, CF["invf"][0]] = inv[fi]
                cf[32 * ci + r, CF["sgn"][0]] = sg
        cb[:, CB[nm][0]:CB[nm][1]] = sw
    for h in range(4):
        cb[32 * h, CB["selrow"][0] + 128 * h: CB["selrow"][0] + 128 * (h + 1)] = 1.0
    for kt in range(16):
        for pp in range(128):
            cb[2 * kt + pp // 64, CB["esel"][0] + kt * 128 + pp] = BIG
    cs = np.arange(127)[:, None] * 16
    bs = np.arange(32)[None, :] * 64
    ov = ((cs < bs + 64) & (cs + 32 > bs)).astype(np.float32)
    cb[0:127, CB["ovl"][0]:CB["ovl"][0] + 32] = ov
    cb[0:127, CB["ovl"][0] + 32:CB["ovl"][0] + 64] = 1.0
    cf[:, CF["ident"][0]:CF["ident"][1]] = np.eye(128)
    cf[:, CF["negpi"][0]] = -math.pi
    for i_, v_ in enumerate((1024 * EPS, 64 * EPS, 96 * EPS, 32 * EPS, 256 * EPS, 128 * EPS, 1.0, 1e-30)):
        cf[:, CF["epsc"][0] + i_] = v_
    for i_, l_ in enumerate(layer_ids):
        lam_init = 0.8 - 0.6 * math.exp(-0.3 * l_)
        cf[:, CF["lamc"][0] + 2 * i_] = 8.0 * (1.0 - lam_init)
        cf[:, CF["lamc"][0] + 2 * i_ + 1] = -lam_init
    t = (np.arange(16)[None, :, None] * 128 + p[:, None, None])
    j = np.arange(32)[None, None, :]
    cur = t // 64
    future = (j * 64 > t)
    forced = ((j == 0) | (j == cur) | (j == cur - 1)) & (~future)
    A = (~future & ~forced).astype(np.float32)
    Bt = np.where(future, -1.0, np.where(forced, 1e4, 0.0)).astype(np.float32)
    cb[:, CB["tabA"][0]:CB["tabA"][1]] = A.reshape(128, 512)
    cb[:, CB["tabB"][0]:CB["tabB"][1]] = Bt.reshape(128, 512)
    return cb.astype(ml_dtypes.bfloat16), cf


def make_smalls(inp, layer_ids=tuple(range(DEPTH))):
    sm = np.zeros((128, len(layer_ids) * SM["_n"]), np.float32)
    for li_, l in enumerate(layer_ids):
        o = li_ * SM["_n"]

        def put(name, arr):
            a, b = SM[name]
            arr = np.asarray(arr, np.float32)
            if arr.ndim == 1:
                arr = arr[:, None]
            sm[:arr.shape[0], o + a:o + a + arr.shape[1]] = arr

        put("ada_b", inp["ada_b"][l].reshape(48, 128).T)
        put("gate_b", inp["gate_b"][l].reshape(32, 128).T)
        cw = inp["ffn_conv_w"][l]
        put("conv_w", np.concatenate([cw[t].reshape(22, 128).T for t in range(3)], axis=1))
        put("conv_b", inp["ffn_conv_b"][l].reshape(22, 128).T)
        g = inp["nsa_qk_g"][l]
        put("nsa_gq", np.tile(g[0], 2)); put("nsa_gkc", np.tile(g[1], 2))
        put("nsa_gks", np.tile(g[2], 2)); put("nsa_gkw", np.tile(g[3], 2))
        g = inp["fox_qk_g"][l]
        put("fox_gq", np.tile(g[0], 2)); put("fox_gk", np.tile(g[1], 2))
        fb = np.zeros(128, np.float32)
        fb[0::32] = inp["fox_f_b"][l]
        put("fox_fb", fb)
        put("mla_cqg", inp["mla_cq_g"][l].reshape(2, 128).T)
        put("mla_ckvg", inp["mla_ckv_g"][l])
        gq_, gk_ = inp["mla_qk_g"][l, 0], inp["mla_qk_g"][l, 1]
        put("mla_gq", np.concatenate([gq_[32:], gq_[:32]])); put("mla_gk", np.concatenate([gk_[32:], gk_[:32]]))
        put("dif_gq", np.tile(inp["diff_qk_g"][l, 0], 4)); put("dif_gk", np.tile(inp["diff_qk_g"][l, 1], 4))
        put("dif_og", np.tile(inp["diff_out_g"][l], 2))
        put("dif_lam", np.tile(inp["diff_lambda"][l].reshape(1, 128), (128, 1)))
    return sm


C_NSA_Q, C_NSA_KC, C_NSA_KS, C_NSA_VS, C_NSA_KW, C_NSA_VW, C_NSA_G = 0, 256, 384, 448, 512, 576, 640
C_FOX_Q, C_FOX_K, C_FOX_V, C_FOX_F = 652, 908, 1164, 1420
C_MLA_CQ, C_MLA_CKV, C_MLA_KR = 1424, 1680, 1808
C_DIF_Q, C_DIF_K, C_DIF_V = 1840, 2096, 2352


class Prog:
    def __init__(self, n_layers=DEPTH, dbg=(), wdepth=DEPTH):
        self.n_layers = n_layers
        self.wd = wdepth
        self.dbg = set(dbg)
        self.nc = bass.Bass("TRN2", target_bir_lowering=False)
        self.stack = contextlib.ExitStack()
        self.dbg_out = {}

    def dram_in(self, name, shape, dt=F32):
        return self.nc.dram_tensor(name, list(shape), dt, kind="ExternalInput").ap()

    def dram_out(self, name, shape, dt=F32):
        return self.nc.dram_tensor(name, list(shape), dt, kind="ExternalOutput").ap()

    def mm(self, out, lhsT, rhs, start, stop, rin, rout):
        self.kb.op("pe", lambda e: e.matmul(out, lhsT=lhsT, rhs=rhs, start=start, stop=stop), rin, [rout])

    def cbf(self, name, rows=slice(0, 128), c0=0, c1=None):
        a, b = CB[name]
        if c1 is None:
            c1 = b - a
        return self.t_cb[rows, a + c0:a + c1]

    def cf(self, name, c0=0, c1=None, rows=slice(0, 128)):
        a, b = CF[name]
        if c1 is None:
            c1 = b - a
        return self.t_cf[rows, a + c0:a + c1]

    def sm(self, l, name, c0=0, c1=None, rows=slice(0, 128)):
        a, b = SM[name]
        if c1 is None:
            c1 = b - a
        o = l * SM["_n"]
        return self.t_sm[rows, o + a + c0:o + a + c1]

    def der(self, name, c0=0, c1=None, rows=slice(0, 128), par=None):
        a, b = DER[name]
        if c1 is None:
            c1 = b - a
        return self.t_ders[self.cur if par is None else par][rows, a + c0:a + c1]

    @property
    def t_mod(self):
        return self.t_mods[self.cur]

    @property
    def Rmod(self):
        return self.Rmods[self.cur]

    @property
    def Rder(self):
        return self.Rders[self.cur]

    def dump(self, name, ap, res, shape):
        if name not in self.dbg:
            return
        d = self.dram_out("dbg_" + name, shape, F32 if ap.dtype == F32 else BF16)
        self.dbg_out[name] = d
        self.kb.dma_out("sp", res, lambda e: e.dma_start(out=d, in_=ap), self.osem)

    def build(self):
        nc = self.nc
        st = self.stack
        kb = self.kb = KB(nc, st)
        self.osem = [kb.newsem("osem"), 0]
        self.x_d = self.dram_in("x", [S, D])
        self.cT_d = self.dram_in("cT", [128, 8])
        self.pos_d = self.dram_in("pos", [1, S], I32)
        self.ada_w = self.dram_in("ada_w", [self.wd, D, 6 * D])
        self.w_in = self.dram_in("w_in", [self.wd, D, IN_W])
        self.wg_rep = self.dram_in("wg_rep", [self.wd, D, 768])
        self.wf_pad = self.dram_in("wf_pad", [self.wd, D, 128])
        self.cmp_w1 = self.dram_in("cmp_w1", [self.wd, 2, 2048, 64])
        self.cmp_w2 = self.dram_in("cmp_w2", [self.wd, 2, 64, 64])
        self.cmp_pe = self.dram_in("cmp_pe", [self.wd, 2, 64, 32])
        self.w_uq = self.dram_in("w_uq", [self.wd, 256, 384])
        self.w_ukv = self.dram_in("w_ukv", [self.wd, 128, 512])
        self.br_w = self.dram_in("br_w", [self.wd, 4, 256, D])
        self.gate_w = self.dram_in("gate_w", [self.wd, D, 4 * D])
        self.w_out = self.dram_in("w_out", [self.wd, D, D])
        self.w_up = self.dram_in("w_up", [self.wd, D, 2 * DFF])
        self.w_down = self.dram_in("w_down", [self.wd, DFF, D])
        self.sm_d = self.dram_in("smalls", [128, self.wd * SM["_n"]])
        self.cb_d = self.dram_in("cbf", [128, CB["_n"]], BF16)
        self.cf_d = self.dram_in("cf32", [128, CF["_n"]])
        self.y_d = self.dram_out("y", [S, D])

        self.xT = [kb.sb(f"xT{k}", [128, S], F32) for k in range(8)]
        self.hT = [kb.sb(f"hT{k}", [128, S], BF16) for k in range(8)]
        self.Rx = [[Res(f"x{k}_{q}") for q in range(NQ)] for k in range(8)]
        self.Rh = [[Res(f"h{k}_{q}") for q in range(NQ)] for k in range(8)]
        self.ropeC = kb.sb("ropeC", [128, S], BF16)
        self.ropeS = kb.sb("ropeS", [128, S], BF16)
        self.Rrope = Res("rope")
        self.t_cb = kb.sb("t_cb", [128, CB["_n"]], BF16)
        self.t_cf = kb.sb("t_cf", [128, CF["_n"]], F32)
        self.t_sm = kb.sb("t_sm", [128, self.wd * SM["_n"]], F32)
        self.t_mods = [kb.sb(f"t_mod{i}", [128, 48], F32) for i in range(2)]
        self.t_ders = [kb.sb(f"t_der{i}", [128, DER["_n"]], F32) for i in range(2)]
        self.adab = [kb.sb(f"adab{i}", [128, 512], BF16) for i in range(2)]
        self.Radab = [Res(f"adab{i}") for i in range(2)]
        self.Rmods = [Res("mod0"), Res("mod1")]
        self.Rders = [Res("der0"), Res("der1")]
        self.cur = 0
        self.t_scb = kb.sb("t_scb", [128, 8], BF16)
        self.Rcb, self.Rcf, self.Rsm, self.Rscb = (Res(n) for n in ("cb", "cf", "sm", "scb"))
        self.WB = [kb.sb(f"WB{i}", [128, 8, 512], BF16) for i in range(2)]
        self.RWB = [Res(f"WB{i}") for i in range(2)]
        self.wb_i = 0
        self.ps = [kb.ps(f"ps{i}", [128, 512]) for i in range(8)]
        self.Rps = [Res(f"ps{i}") for i in range(8)]

        kb.dma_in("sp", self.Rcb, lambda e: e.dma_start(out=self.t_cb[:], in_=self.cb_d))
        kb.dma_in("sp", self.Rcf, lambda e: e.dma_start(out=self.t_cf[:], in_=self.cf_d))
        kb.dma_in("sp", self.Rsm, lambda e: e.dma_start(out=self.t_sm[:], in_=self.sm_d))

        self.ada_gen = None
        self.prologue()
        for l in range(self.n_layers):
            self.layer(l)
        self.epilogue()
        kb.e["sp"].wait_ge(self.osem[0], self.osem[1])
        self.stack.close()
        return nc

    def mark(self, name):
        if not hasattr(self, 'marks'):
            self.marks = []
        self.marks.append((name, self.kb.cnt['pe']))

    def next_wb(self):
        i = self.wb_i
        self.wb_i ^= 1
        return self.WB[i], self.RWB[i]

    def load_w(self, dram_ap_pkn, ncols, nk=8):
        wb, r = self.next_wb()
        self.kb.dma_in("pool", r, lambda e: e.dma_start(out=wb[:, 0:nk, 0:ncols], in_=dram_ap_pkn))
        return wb, r

    def win_cols(self, l, c0, n):
        return self.w_in[l].rearrange("(k p) n -> p k n", p=128)[:, :, c0:c0 + n]

    def prologue(self):
        kb = self.kb
        with contextlib.ExitStack() as ps_:
            xs = [kb.sb(f"xstage{i}", [128, 4, D], F32, ps_) for i in range(2)]
            Rxs = [Res(f"xstage{i}") for i in range(2)]
            posi = kb.sb("posi", [128, S], I32, ps_)
            posf = kb.sb("posf", [128, S], F32, ps_)
            tA = kb.sb("tA", [128, S], F32, ps_)
            tB = kb.sb("tB", [128, S], F32, ps_)
            Rposi, Rposf, RtA, RtB = Res("posi"), Res("posf"), Res("tA"), Res("tB")
            ct = kb.sb("ct", [128, 8], F32, ps_)
            Rct = Res("ct")
            kb.dma_in("sp", Rct, lambda e: e.dma_start(out=ct[:], in_=self.cT_d))
            kb.op("act", lambda e: e.activation(out=self.t_scb[:], in_=ct[:], func=AF.Silu), [Rct], [self.Rscb])
            self.ada_gen = self.g_ada(0)
            xv = self.x_d.rearrange("(g t p) d -> g p t d", t=4, p=128)
            for g in range(2):
                kb.dma_in("sp", Rxs[g], lambda e: e.dma_start(out=xs[g][:], in_=xv[g]))
            for g in range(4):
                xg, Rxg = xs[g % 2], Rxs[g % 2]
                for k in range(8):
                    self.ada_step(3)
                    pb = self.ps[k % 4]
                    for t in range(4):
                        kb.op("pe", lambda e: e.transpose(out=pb[:, t * 128:(t + 1) * 128], in_=xg[:, t, k * 128:(k + 1) * 128],
                                                          identity=self.cf("ident")), [Rxg, self.Rcf], [self.Rps[k % 4]])
                    eng = "act" if k % 2 == 0 else "dve"
                    if eng == "act":
                        kb.op("act", lambda e: e.activation(out=self.xT[k][:, g * 512:(g + 1) * 512], in_=pb[:], func=AF.Copy),
                              [self.Rps[k % 4]], [self.Rx[k][g]])
                    else:
                        kb.op("dve", lambda e: e.tensor_copy(out=self.xT[k][:, g * 512:(g + 1) * 512], in_=pb[:]),
                              [self.Rps[k % 4]], [self.Rx[k][g]])
                if g + 2 < 4:
                    kb.dma_in("sp", Rxg, lambda e: e.dma_start(out=xg[:], in_=xv[g + 2]))
            kb.dma_in("sp", Rposi, lambda e: e.dma_start(out=posi[:], in_=self.pos_d.partition_broadcast(128)))
            kb.op("dve", lambda e: e.tensor_copy(out=posf[:], in_=posi[:]), [Rposi], [Rposf])
            twopi = 2.0 * math.pi
            invf = self.cf("invf")
            sgn = self.cf("sgn")
            ti = posi
            for which in range(2):
                kb.op("dve", lambda e: e.tensor_scalar(out=tA[:], in0=posf[:], scalar1=invf, scalar2=1.0 / twopi, op0=ALU.mult, op1=ALU.mult),
                      [Rposf, self.Rcf, RtB], [RtA])
                if which == 1:
                    kb.op("dve", lambda e: e.tensor_scalar(out=tA[:], in0=tA[:], scalar1=0.25, scalar2=None, op0=ALU.add), [RtA], [RtA])
                kb.op("dve", lambda e: e.tensor_copy(out=ti[:], in_=tA[:]), [RtA, Rposf], [Rposi])
                kb.op("dve", lambda e: e.tensor_copy(out=tB[:], in_=ti[:]), [Rposi], [RtB])
                kb.op("dve", lambda e: e.tensor_tensor(out=tA[:], in0=tA[:], in1=tB[:], op=ALU.subtract), [RtA, RtB], [RtA])
                kb.op("act", lambda e: e.activation(out=tB[:], in_=tA[:], func=AF.Sin, scale=twopi), [RtA], [RtB])
                if which == 0:
                    kb.op("dve", lambda e: e.tensor_scalar(out=self.ropeS[:], in0=tB[:], scalar1=sgn, scalar2=None, op0=ALU.mult),
                          [RtB, self.Rcf], [self.Rrope])
                else:
                    kb.op("dve", lambda e: e.tensor_copy(out=self.ropeC[:], in_=tB[:]), [RtB], [self.Rrope])
            self.dump("ropeC", self.ropeC[:], [self.Rrope], [128, S])
            self.dump("ropeS", self.ropeS[:], [self.Rrope], [128, S])
            self.dump("xT0", self.xT[0][:], self.Rx[0], [128, S])
            kb.barrier()
            if self.dbg:
                kb.e["act"].wait_ge(self.osem[0], self.osem[1])
                kb.barrier()

    def epilogue(self):
        kb = self.kb
        with contextlib.ExitStack() as ps_:
            ys = [kb.sb(f"ystage{i}", [128, 2, D], F32, ps_) for i in range(2)]
            Rys = [Res(f"ystage{i}") for i in range(2)]
            yv = self.y_d.rearrange("(g t p) d -> g p t d", t=2, p=128)
            for g in range(8):
                q = g // 2
                for t in range(2):
                    tt = g * 2 + t
                    for k in range(8):
                        pb = self.ps[(k // 4) + 2 * (tt % 2)]
                        kb.op("pe", lambda e: e.transpose(out=pb[:, (k % 4) * 128:(k % 4 + 1) * 128], in_=self.xT[k][:, tt * 128:(tt + 1) * 128],
                                                          identity=self.cf("ident")), [self.Rx[k][q], self.Rcf], [self.Rps[(k // 4) + 2 * (tt % 2)]])
                    for hh in range(2):
                        bi = hh + 2 * (tt % 2)
                        if hh == 0:
                            kb.op("act", lambda e: e.activation(out=ys[g % 2][:, t, hh * 512:(hh + 1) * 512], in_=self.ps[bi][:], func=AF.Copy),
                                  [self.Rps[bi]], [Rys[g % 2]])
                        else:
                            kb.op("dve", lambda e: e.tensor_copy(out=ys[g % 2][:, t, hh * 512:(hh + 1) * 512], in_=self.ps[bi][:]),
                                  [self.Rps[bi]], [Rys[g % 2]])
                kb.dma_out("sp", [Rys[g % 2]], lambda e: e.dma_start(out=yv[g], in_=ys[g % 2][:]), self.osem)

    def layer(self, l):
        self.cur = l % 2
        while self.ada_gen is not None:
            self.ada_step()
        self.norm_mod(l, 0)
        kb = self.kb
        with contextlib.ExitStack() as ls:
            self.oT = [None] * 4
            self.RoT = [None] * 4
            for m, fn in self.mixer_order():
                self.oT[m] = [kb.sb(f"oT{m}_{c}", [128, S], BF16, ls) for c in range(2)]
                self.RoT[m] = [[Res(f"oT{m}_{c}_{q}") for q in range(NQ)] for c in range(2)]
                with contextlib.ExitStack() as ms:
                    fn(l, ms)
                    kb.barrier()
                if l == 0:
                    for c in range(2):
                        self.dump(f"o{m}_{c}", self.oT[m][c][:], self.RoT[m][c], [128, S])
            if self.dbg:
                kb.e["act"].wait_ge(self.osem[0], self.osem[1])
                kb.barrier()
            if getattr(self, "mixers", None) is None:
                self.merge(l)
                self.dump(f"x1_l{l}", self.xT[0][:], self.Rx[0], [128, S])
        if getattr(self, "mixers", None) is None:
            self.norm_mod(l, 1)
            self.ffn(l)
            self.dump(f"x2_l{l}", self.xT[0][:], self.Rx[0], [128, S])
            if self.dbg:
                kb.e["act"].wait_ge(self.osem[0], self.osem[1])
                kb.barrier()

    def gate_tile(self, wg, rwg, blk, q, gt, Rgt):
        kb = self.kb
        pb, Rpb = self.ps[3], self.Rps[3]
        self.proj_T(pb, Rpb, slice(0, 128), lambda k: wg[:, k, blk * 128:(blk + 1) * 128], rwg, q)
        kb.op("act", lambda e: e.activation(out=gt[:], in_=pb[:], func=AF.Exp, scale=-1.0), [Rpb], [Rgt])
        kb.op("dve", lambda e: e.tensor_scalar(out=gt[:], in0=gt[:], scalar1=1.0, scalar2=None, op0=ALU.add), [Rgt], [Rgt])
        kb.op("dve", lambda e: e.reciprocal(out=gt[:], in_=gt[:]), [Rgt], [Rgt])

    def nsa(self, l, ms):
        kb = self.kb
        self.mark('nsa_prelude')
        KS2 = kb.sb("KS2", [128, S], BF16, ms)
        KW2 = kb.sb("KW2", [128, S], BF16, ms)
        RKS = [Res(f"KS2_{q}") for q in range(NQ)]
        RKW = [Res(f"KW2_{q}") for q in range(NQ)]
        VSW = kb.sb("VSW", [128, NKT, 384], BF16, ms)
        RVSW = Res("VSW")
        KCMP = kb.sb("KCMP", [128, 128], BF16, ms)
        VCMP = kb.sb("VCMP", [128, 192], BF16, ms)
        RKCMP, RVCMP = Res("KCMP"), Res("VCMP")
        IMP = kb.sb("IMP", [128, 512], F32, ms)
        RIMP = Res("IMP")
        SELM = kb.sb("SELM", [32, S], BF16, ms)
        RSELM = [Res(f"SELM{q}") for q in range(NQ)]
        self.mixer_scratch(ms, nq=2, nk=0, va=False)
        kb.op("pool", lambda e: e.memset(VSW[:, :, 64:128], 1.0), [], [RVSW])
        kb.op("pool", lambda e: e.memset(VSW[:, :, 256:320], 1.0), [], [RVSW])
        kb.op("pool", lambda e: e.memset(VCMP[:], 0.0), [], [RVCMP])
        kb.op("pool", lambda e: e.memset(VCMP[:, 64:128], 1.0), [RVCMP], [RVCMP])
        kb.op("pool", lambda e: e.memset(KCMP[:], 0.0), [], [RKCMP])
        kb.op("pool", lambda e: e.memset(IMP[:], 0.0), [], [RIMP])
        pstat, Rpstat = self.ps[5], self.Rps[5]
        wA, rwA = self.load_w(self.win_cols(l, C_NSA_KC, 384), 384)
        wq, rwq = self.load_w(self.win_cols(l, C_NSA_Q, 256), 256)
        with contextlib.ExitStack() as pscope:
            KVC = self.QTt[1]
            RKVC = [Res(f"KVC{q}") for q in range(NQ)]
            W1 = kb.sb("W1", [128, 32, 64], BF16, pscope)
            W2 = kb.sb("W2", [128, 64], BF16, pscope)
            PEt = kb.sb("PEt", [128, 32], BF16, pscope)
            HID = kb.sb("HID", [128, 128], BF16, pscope)
            RW1, RW2, RPE, RHID = Res("W1"), Res("W2"), Res("PEt"), Res("HID")
            kb.dma_in("pool", RW1, lambda e: [e.dma_start(out=W1[64 * i:64 * i + 64, :, :], in_=self.cmp_w1[l, i].rearrange("(j d) o -> d j o", d=64)) for i in range(2)])
            kb.dma_in("pool", RW2, lambda e: [e.dma_start(out=W2[64 * i:64 * i + 64, :], in_=self.cmp_w2[l, i]) for i in range(2)])
            kb.dma_in("pool", RPE, lambda e: [e.dma_start(out=PEt[64 * i:64 * i + 64, :], in_=self.cmp_pe[l, i]) for i in range(2)])
            for q in range(NQ):
                cs = slice(q * QT, (q + 1) * QT)
                def ch_kx(c0, dstt, Rd, gname, bank, sbank):
                    for half in range(2):
                        self.proj_T(self.ps[bank], self.Rps[bank], slice(64 * half, 64 * half + 64), lambda k: wA[:, k, c0:c0 + 64], rwA, q)
                    yield
                    yield from self.g_headnorm(bank, 128, self.cbf("b64"), self.cf("epsc", 1, 2), self.der(gname), dstt[:, cs], Rd[q], cs,
                                               rope=(0, "sw_nsa", [0, 64]), stat_bank=sbank)

                for _ in self.rr(ch_kx(128, KS2, RKS, "nsa_gks", 3, 5), ch_kx(256, KW2, RKW, "nsa_gkw", 4, 2)):
                    pass
                self.proj_T(self.ps[4], self.Rps[4], slice(0, 128), lambda k: wA[:, k, 0:128], rwA, q)
                kb.op("act", lambda e: e.activation(out=KVC[:, cs], in_=self.ps[4][:], func=AF.Copy), [self.Rps[4]], [RKVC[q]])
            for g in range(4):
                pb, Rpb = self.ps[3 + g % 2], self.Rps[3 + g % 2]
                for t in range(4):
                    kt = g * 4 + t
                    for b in range(2):
                        for k in range(8):
                            self.mm(pb[:, (t * 2 + b) * 64:(t * 2 + b + 1) * 64], self.hT[k][:, kt * 128:(kt + 1) * 128], wA[:, k, 192 + 128 * b:256 + 128 * b],
                                    k == 0, k == 7, [rwA, self.Rh[k][g]], Rpb)
                src = pb[:].rearrange("p (t b c) -> p t b c", t=4, b=2)
                for sidx in (0, 2):
                    dstv = VSW[:, g * 4:(g + 1) * 4, :].rearrange("p t (b s c) -> p t b s c", b=2, s=3)[:, :, :, sidx, :]
                    kb.op("dve" if sidx == 0 else "act", (lambda e: e.tensor_copy(out=dstv, in_=src)) if sidx == 0 else
                          (lambda e: e.activation(out=dstv, in_=src, func=AF.Copy)), [Rpb], [RVSW])
            pH, RpH = self.ps[6], self.Rps[6]
            for i in range(2):
                rr = slice(64 * i, 64 * i + 64)
                n = 0
                for j in range(32):
                    self.mm(pH[rr, 0:127], W1[rr, j, :], KVC[rr, j:j + 16 * 126 + 1:16], n == 0, False, [RW1] + RKVC, RpH)
                    n += 1
                    self.mm(pH[rr, 0:127], W1[rr, j, :], PEt[rr, j:j + 1].to_broadcast([64, 127]), False, j == 31, [RW1, RPE], RpH)
            kb.op("act", lambda e: e.activation(out=HID[:, 0:127], in_=pH[:, 0:127], func=AF.Silu), [RpH], [RHID])
            for half in range(2):
                self.mm(self.ps[3][64 * half:64 * half + 64, 0:127], W2[0:64, :], HID[0:64, 0:127], True, True, [RW2, RHID], self.Rps[3])
            kb.op("act", lambda e: e.activation(out=self.sqb[0][:, 0:127], in_=self.ps[3][:, 0:127], func=AF.Square), [self.Rps[3]], [self.Rsqb[0]])
            self.mm(pstat[:, 0:127], self.cbf("b64"), self.sqb[0][:, 0:127], True, True, [self.Rsqb[0], self.Rcb], Rpstat)
            self.rstd_from(pstat[:, 0:127], self.rstd[0][:, 0:127], self.cf("epsc", 1, 2), Rpstat, self.Rrstd[0])
            kb.op("dve", lambda e: e.scalar_tensor_tensor(out=KCMP[:, 0:127], in0=self.ps[3][:, 0:127], scalar=self.der("nsa_gkc"), in1=self.rstd[0][:, 0:127],
                                                          op0=ALU.mult, op1=ALU.mult), [self.Rps[3], self.Rrstd[0], self.Rder, RKCMP], [RKCMP])
            self.mm(self.ps[4][0:127, 0:64], HID[64:128, 0:127], W2[64:128, :], True, True, [RW2, RHID], self.Rps[4])
            kb.op("dve", lambda e: e.tensor_copy(out=VCMP[0:127, 0:64], in_=self.ps[4][0:127, 0:64]), [self.Rps[4], RVCMP], [RVCMP])
            kb.op("act", lambda e: e.activation(out=VCMP[0:127, 128:192], in_=self.ps[4][0:127, 0:64], func=AF.Copy), [self.Rps[4], RVCMP], [RVCMP])
            self.dump("nsaKS", KS2[:], RKS, [128, S])
            self.dump("nsaKCMP", KCMP[:], [RKCMP], [128, 128])
            self.dump("nsaVCMP", VCMP[:], [RVCMP], [128, 192])
            kb.barrier()
            if self.dbg:
                kb.e["act"].wait_ge(self.osem[0], self.osem[1])
                kb.barrier()
        GT = [kb.sb(f"GT{i}", [128, QT], F32, ms) for i in range(2)]
        RGT = [Res(f"GT{i}") for i in range(2)]
        self.mark('nsa_q')
        wg0, rwg0 = self.load_w(self.wg_rep[l].rearrange("(k p) n -> p k n", p=128)[:, :, 0:256], 256)
        for q in range(NQ):
            cs = slice(q * QT, (q + 1) * QT)

            def ch_qu(u, bank, sbank):
                self.proj_T(self.ps[bank], self.Rps[bank], slice(0, 128), lambda k: wq[:, k, u * 128:(u + 1) * 128], rwq, q)
                yield
                yield from self.g_headnorm(bank, 128, self.cbf("b64"), self.cf("epsc", 1, 2), self.der("nsa_gq"), self.QTt[u][:, cs], self.RQT[u][q], cs,
                                           rope=(0, "sw_nsa", [0, 64]), stat_bank=sbank)

            for _ in self.rr(ch_qu(0, 3, 5), ch_qu(1, 4, 2)):
                pass
        self.mark('nsa_cmp')
        wg1, rwg1 = self.load_w(self.wg_rep[l].rearrange("(k p) n -> p k n", p=128)[:, :, 256:768], 512)
        pI, RpI = self.ps[5], self.Rps[5]
        gi = 0
        for u in range(2):
            for q in range(NQ):
                cs = slice(q * QT, (q + 1) * QT)
                gt, Rgt = GT[gi], RGT[gi]
                gi ^= 1
                self.pend_flush()
                self.gate_tile(wg0, rwg0, u, q, gt, Rgt)
                for hh in range(2):
                    rb = slice(64 * hh, 64 * hh + 64)
                    ob = 6 + hh
                    rins = [self.RQT[u][q], RKCMP, RVCMP]

                    def imp_mm(pi, hh=hh):
                        for t in range(4):
                            self.mm(pI[:, (hh * 4 + t) * 64:(hh * 4 + t + 1) * 64], self.PT[pi][:, t * 128:(t + 1) * 128], self.cbf("ovl"), True, True,
                                    [self.RPT[pi], self.Rcb], RpI)

                    def fin_cmp(ob=ob, hh=hh, u=u, cs=cs, q=q, gt=gt, Rgt=Rgt):
                        i, nr = self.finalize(ob, hh, None, None, eps=1e-30)
                        kb.op("pool", lambda e: e.tensor_tensor(out=self.rec[i][nr, :], in0=self.rec[i][nr, :], in1=gt[nr, :], op=ALU.mult), [self.Rrec[i], Rgt], [self.Rrec[i]])
                        kb.op("dve", lambda e: e.tensor_tensor(out=self.oT[0][u][nr, cs], in0=self.ps[ob][nr, :], in1=self.rec[i][nr, :], op=ALU.mult),
                              [self.Rps[ob], self.Rrec[i]], [self.RoT[0][u][q]])

                    self.attn_map([(0, 0, QT, ("vis", 0, q * QT - 31))],
                                  lambda c0, c1, rb=rb, u=u, q=q: self.QTt[u][rb, q * QT + c0:q * QT + c1],
                                  lambda kt, rb=rb: KCMP[rb, :],
                                  lambda kt, hh=hh: VCMP[:, 64 * hh:64 * hh + 128],
                                  0.125, rins, ob, fin=fin_cmp, after_p=imp_mm)
                for hh in range(2):
                    for t in range(4):
                        tt = q * 4 + t
                        base = (hh * 4 + t) * 64
                        rcol = self.rstd[0][:, 0:1]
                        kb.op("dve", lambda e: e.tensor_scalar(out=rcol, in0=pI[:, base + 32:base + 33], scalar1=1e-30, scalar2=None, op0=ALU.add), [RpI], [self.Rrstd[0]])
                        kb.op("dve", lambda e: e.reciprocal(out=rcol, in_=rcol), [self.Rrstd[0]], [self.Rrstd[0]])
                        kb.op("dve", lambda e: e.scalar_tensor_tensor(out=IMP[:, tt * 32:(tt + 1) * 32], in0=pI[:, base:base + 32], scalar=rcol,
                                                                      in1=IMP[:, tt * 32:(tt + 1) * 32], op0=ALU.mult, op1=ALU.add),
                              [RpI, self.Rrstd[0], RIMP], [RIMP])
        self.pend_flush()
        self.dump("nsaIMP", IMP[:], [RIMP], [128, 512])
        self.mark('nsa_topk')
        SC = self.rec[0]
        RSC = self.Rrec[0]
        kb.op("dve", lambda e: e.tensor_tensor(out=SC[:], in0=IMP[:], in1=self.cbf("tabA"), op=ALU.mult), [RIMP, self.Rcb], [RSC])
        kb.op("dve", lambda e: e.tensor_tensor(out=SC[:], in0=SC[:], in1=self.cbf("tabB"), op=ALU.add), [RSC, self.Rcb], [RSC])
        m8 = self.rstd[1]
        Rm8 = self.Rrstd[1]
        sc2 = self.rec[1]
        Rsc2 = self.Rrec[1]
        selm = self.sqb[0]
        Rselm = self.Rsqb[0]
        pT, RpT = self.ps[5], self.Rps[5]
        for tt in range(16):
            sl = slice(tt * 32, (tt + 1) * 32)
            kb.op("dve", lambda e: e.max(out=m8[:, 0:8], in_=SC[:, sl]), [RSC], [Rm8])
            kb.op("dve", lambda e: e.match_replace(out=sc2[:, 0:32], in_to_replace=m8[:, 0:8], in_values=SC[:, sl], imm_value=-2.0), [RSC, Rm8], [Rsc2])
            kb.op("dve", lambda e: e.max(out=m8[:, 8:16], in_=sc2[:, 0:32]), [Rsc2], [Rm8])
            kb.op("dve", lambda e: e.tensor_scalar(out=selm[:, sl], in0=SC[:, sl], scalar1=m8[:, 15:16], scalar2=-1.0, op0=ALU.is_ge, op1=ALU.add),
                  [RSC, Rm8], [Rselm])
            self.mm(pT[0:32, (tt % 4) * 128:(tt % 4 + 1) * 128], selm[:, sl], self.cbf("ident"), True, True, [Rselm, self.Rcb], RpT)
            if tt % 4 == 3:
                qq = tt // 4
                kb.op("act", lambda e: e.activation(out=SELM[0:32, qq * QT:(qq + 1) * QT], in_=pT[0:32, :], func=AF.Copy), [RpT], [RSELM[qq]])
        self.dump("nsaSELM", SELM[:], RSELM, [32, S])
        self.mark('nsa_slcwin')
        for u in range(2):
            for q in range(NQ):
                cs = slice(q * QT, (q + 1) * QT)
                self.gate_tile(wg1, rwg1, u, q, GT[0], RGT[0])
                self.gate_tile(wg1, rwg1, 2 + u, q, GT[1], RGT[1])
                for hh in range(2):
                    rb = slice(64 * hh, 64 * hh + 64)
                    rins = [self.RQT[u][q], RVSW, RSELM[q], self.Rcb] + RKS
                    shared = {}

                    def fin_s(hh=hh, shared=shared):
                        i_s, nr = self.finalize(6, hh, None, None)
                        kb.op("pool", lambda e: e.tensor_tensor(out=self.rec[i_s][nr, :], in0=self.rec[i_s][nr, :], in1=GT[0][nr, :], op=ALU.mult),
                              [self.Rrec[i_s], RGT[0]], [self.Rrec[i_s]])
                        kb.op("dve", lambda e: e.tensor_tensor(out=self.rec[i_s][nr, :], in0=self.ps[6][nr, :], in1=self.rec[i_s][nr, :], op=ALU.mult),
                              [self.Rps[6], self.Rrec[i_s]], [self.Rrec[i_s]])
                        shared["i_s"] = i_s

                    def fin_w(hh=hh, shared=shared, u=u, cs=cs, q=q):
                        i_s = shared["i_s"]
                        i_w, nr = self.finalize(7, hh, None, None)
                        kb.op("pool", lambda e: e.tensor_tensor(out=self.rec[i_w][nr, :], in0=self.rec[i_w][nr, :], in1=GT[1][nr, :], op=ALU.mult),
                              [self.Rrec[i_w], RGT[1]], [self.Rrec[i_w]])
                        kb.op("dve", lambda e: e.tensor_tensor(out=self.rec[i_w][nr, :], in0=self.ps[7][nr, :], in1=self.rec[i_w][nr, :], op=ALU.mult),
                              [self.Rps[7], self.Rrec[i_w]], [self.Rrec[i_w]])
                        kb.op("pool", lambda e: e.tensor_tensor(out=self.rec[i_w][nr, :], in0=self.rec[i_w][nr, :], in1=self.rec[i_s][nr, :], op=ALU.add),
                              [self.Rrec[i_w], self.Rrec[i_s]], [self.Rrec[i_w]])
                        kb.op("pool", lambda e: e.tensor_tensor(out=self.oT[0][u][nr, cs], in0=self.oT[0][u][nr, cs], in1=self.rec[i_w][nr, :], op=ALU.add),
                              [self.Rrec[i_w], self.RoT[0][u][q]], [self.RoT[0][u][q]])

                    self.attn_map(self.causal_tiles(q),
                                  lambda c0, c1, rb=rb, u=u, q=q: self.QTt[u][rb, q * QT + c0:q * QT + c1],
                                  lambda kt, rb=rb: KS2[rb, kt * 128:(kt + 1) * 128],
                                  lambda kt, hh=hh: VSW[:, kt, 64 * hh:64 * hh + 128],
                                  0.125, rins, 6,
                                  extra=lambda kt, c0, c1, q=q: (self.cbf("esel", slice(0, 32), kt * 128, (kt + 1) * 128), SELM[0:32, q * QT + c0:q * QT + c1]),
                                  fin=fin_s)
                    tiles = [(4 * q + j, 128 * j, QT, ("causal", 0, 0)) for j in range(4)]
                    if q > 0:
                        tiles += [(4 * q - 4 + j, 0, 128 * (j + 1), ("lower", 128 * j, 0)) for j in range(4)]
                    rins = [self.RQT[u][q], RVSW] + RKW
                    self.attn_map(tiles,
                                  lambda c0, c1, rb=rb, u=u, q=q: self.QTt[u][rb, q * QT + c0:q * QT + c1],
                                  lambda kt, rb=rb: KW2[rb, kt * 128:(kt + 1) * 128],
                                  lambda kt, hh=hh: VSW[:, kt, 192 + 64 * hh:192 + 64 * hh + 128],
                                  0.125, rins, 7, fin=fin_w)
                self.pend_flush()

    def mixer_order(self):
        sel = getattr(self, 'mixers', None) or [0, 2, 1, 3]
        fns = {0: self.nsa, 1: self.fox, 2: self.mla, 3: self.diff}
        return [(m, fns[m]) for m in sel]

    def g_ada(self, l):
        kb = self.kb
        par = l % 2
        t_mod, Rmod, Rder = self.t_mods[par], self.Rmods[par], self.Rders[par]
        pb, Rpb = self.ps[7], self.Rps[7]
        av = self.ada_w[l].rearrange("(k p) n -> p k n", p=128)
        avk = self.ada_w[l].rearrange("(k p) n -> k p n", p=128)
        n = 0
        for k in range(8):
            for cb_ in range(12):
                wb, rw = self.adab[n % 2], self.Radab[n % 2]
                kb.dma_in("pool", rw, lambda e: e.dma_start(out=wb[:], in_=avk[k][:, cb_ * 512:(cb_ + 1) * 512]))
                for jj in range(4):
                    j = cb_ * 4 + jj
                    kb.op("pe", lambda e: e.matmul(pb[:, j:j + 1], lhsT=wb[:, jj * 128:(jj + 1) * 128], rhs=self.t_scb[:, k:k + 1],
                                                   start=(k == 0 and j == 0), stop=(k == 7), skip_group_check=True), [rw, self.Rscb], [Rpb])
                n += 1
                yield
        kb.op("dve", lambda e: e.tensor_tensor(out=t_mod[:], in0=pb[:, 0:48], in1=self.sm(l, "ada_b"), op=ALU.add),
              [Rpb, self.Rsm], [Rmod])
        d = lambda *a, **kw: self.der(*a, par=par, **kw)

        def ts(dst, src, s1, s2=None, op0=ALU.mult, op1=None):
            if s2 is None:
                kb.op("dve", lambda e: e.tensor_scalar(out=dst, in0=src, scalar1=s1, scalar2=None, op0=op0),
                      [Rmod, self.Rsm, Rder, self.Rcf], [Rder])
            else:
                kb.op("dve", lambda e: e.tensor_scalar(out=dst, in0=src, scalar1=s1, scalar2=s2, op0=op0, op1=op1),
                      [Rmod, self.Rsm, Rder, self.Rcf], [Rder])

        ts(d("a1"), t_mod[:, 8:16], 1.0, 32.0, ALU.add, ALU.mult)
        ts(d("a2"), t_mod[:, 32:40], 1.0, 32.0, ALU.add, ALU.mult)
        for nm, sc in (("nsa_gq", 8.0), ("nsa_gkc", 8.0), ("nsa_gks", 8.0), ("nsa_gkw", 8.0), ("fox_gq", 8.0), ("fox_gk", 8.0),
                       ("mla_cqg", 16.0), ("mla_ckvg", math.sqrt(128.0)), ("mla_gq", math.sqrt(96.0)), ("mla_gk", math.sqrt(96.0)),
                       ("dif_gq", math.sqrt(32.0)), ("dif_gk", math.sqrt(32.0)), ("dif_og", self.cf("lamc", 2 * l, 2 * l + 1))):
            ts(d(nm), self.sm(l, nm), sc)
        ts(d("negfb"), self.sm(l, "fox_fb"), -1.0)
        yield
        if not hasattr(self, "t_lt"):
            self.t_lt = kb.sb("t_lt", [128, 64], F32)
            self.Rlt = Res("lt")
        lt = self.t_lt
        kb.op("dve", lambda e: e.tensor_tensor(out=lt[:, 0:32], in0=self.sm(l, "dif_lam", 0, 32), in1=self.sm(l, "dif_lam", 32, 64), op=ALU.mult),
              [self.Rsm], [self.Rlt])
        kb.op("dve", lambda e: e.tensor_tensor(out=lt[:, 32:64], in0=self.sm(l, "dif_lam", 64, 96), in1=self.sm(l, "dif_lam", 96, 128), op=ALU.mult),
              [self.Rsm], [self.Rlt])
        kb.op("dve", lambda e: e.tensor_reduce(out=d("t0", 0, 2), in_=lt[:].rearrange("p (a b) -> p a b", a=2), axis=mybir.AxisListType.X, op=ALU.add),
              [self.Rlt], [Rder])
        kb.op("act", lambda e: e.activation(out=d("t1", 0, 2), in_=d("t0", 0, 2), func=AF.Exp), [Rder], [Rder])
        kb.op("dve", lambda e: e.tensor_tensor(out=d("t0", 2, 3), in0=d("t1", 1, 2), in1=d("t1", 0, 1), op=ALU.subtract), [Rder], [Rder])
        kb.op("dve", lambda e: e.tensor_scalar(out=d("neglam"), in0=d("t0", 2, 3), scalar1=self.cf("lamc", 2 * l + 1, 2 * l + 2), scalar2=None, op0=ALU.add),
              [Rder, self.Rcf], [Rder])
        yield

    def ada_step(self, n=1):
        for _ in range(n):
            if self.ada_gen is not None:
                try:
                    next(self.ada_gen)
                except StopIteration:
                    self.ada_gen = None

    def norm_mod(self, l, which):
        self.mark('norm')
        kb = self.kb
        acol = "a1" if which == 0 else "a2"
        shc = 0 if which == 0 else 24
        with contextlib.ExitStack() as ns:
            sq = [kb.sb(f"nsq{i}", [128, QT], BF16, ns) for i in range(2)]
            Rsq = [Res(f"nsq{i}") for i in range(2)]
            rstd = kb.sb("nrstd", [128, QT], F32, ns)
            Rrstd = Res("nrstd")
            tmp = [kb.sb(f"ntmp{i}", [128, QT], F32, ns) for i in range(2)]
            Rtmp = [Res(f"ntmp{i}") for i in range(2)]
            for q in range(NQ):
                cs = slice(q * QT, (q + 1) * QT)
                pb, Rpb = self.ps[q % 2], self.Rps[q % 2]
                for k in range(8):
                    eng = "pool" if k % 2 == 0 else "dve"
                    kb.op(eng, lambda e: e.tensor_tensor(out=sq[k % 2][:], in0=self.xT[k][:, cs], in1=self.xT[k][:, cs], op=ALU.mult),
                          [self.Rx[k][q]], [Rsq[k % 2]])
                    self.mm(pb[:], self.cbf("ones"), sq[k % 2][:], k == 0, k == 7, [Rsq[k % 2], self.Rcb], Rpb)
                kb.op("act", lambda e: e.activation(out=rstd[:], in_=pb[:], func=AF.Ln, bias=self.cf("epsc", 0, 1), scale=1.0), [Rpb, self.Rcf], [Rrstd])
                kb.op("act", lambda e: e.activation(out=rstd[:], in_=rstd[:], func=AF.Exp, scale=-0.5), [Rrstd], [Rrstd])
                for k in range(8):
                    kb.op("dve", lambda e: e.tensor_tensor(out=tmp[k % 2][:], in0=self.xT[k][:, cs], in1=rstd[:], op=ALU.mult),
                          [self.Rx[k][q], Rrstd], [Rtmp[k % 2]])
                    kb.op("act", lambda e: e.activation(out=self.hT[k][:, cs], in_=tmp[k % 2][:], func=AF.Identity,
                                                        scale=self.der(acol, k, k + 1), bias=self.t_mod[:, shc + k:shc + k + 1]),
                          [Rtmp[k % 2], self.Rder, self.Rmod], [self.Rh[k][q]])
            kb.barrier()
        self.dump(f"h{which}_0", self.hT[0][:], self.Rh[0], [128, S])
        self.dump(f"h{which}_7", self.hT[7][:], self.Rh[7], [128, S])

    def mixer_scratch(self, ms, nq=1, nk=1, va=True):
        kb = self.kb
        self.sqb = [kb.sb(f"sqb{i}", [128, QT], BF16, ms) for i in range(2)]
        self.Rsqb = [Res(f"sqb{i}") for i in range(2)]
        self.rstd = [kb.sb(f"rstd{i}", [128, QT], F32, ms) for i in range(2)]
        self.Rrstd = [Res(f"rstd{i}") for i in range(2)]
        self.rt1 = [kb.sb(f"rt1_{i}", [128, QT], BF16, ms) for i in range(2)]
        self.Rrt1 = [Res(f"rt1_{i}") for i in range(2)]
        self.rt2 = [kb.sb(f"rt2_{i}", [128, QT], BF16, ms) for i in range(2)]
        self.Rrt2 = [Res(f"rt2_{i}") for i in range(2)]
        self.PT = [kb.sb(f"PT{i}", [128, QT], BF16, ms) for i in range(6)]
        self.RPT = [Res(f"PT{i}") for i in range(6)]
        self.rec = [kb.sb(f"rec{i}", [128, QT], F32, ms) for i in range(2)]
        self.Rrec = [Res(f"rec{i}") for i in range(2)]
        self.QTt = [kb.sb(f"QTt{i}", [128, S], BF16, ms) for i in range(nq)]
        self.RQT = [[Res(f"QT{i}_{q}") for q in range(NQ)] for i in range(nq)]
        self.KTt = [kb.sb(f"KTt{i}", [128, S], BF16, ms) for i in range(nk)]
        self.RKT = [[Res(f"KT{i}_{q}") for q in range(NQ)] for i in range(nk)]
        self.pend = []
        self.bg = None
        self.hn_i = 0
        self.pt_i = 0
        self.sb_i = 0
        self.rec_i = 0
        if va:
            self.VA = kb.sb("VA", [128, NKT, 256], BF16, ms)
            self.RVA = [Res(f"VA{g}") for g in range(4)]
            kb.op("pool", lambda e: e.memset(self.VA[:, :, 64:192], 1.0), [], self.RVA)

    def headnorm(self, *a, **kw):
        for _ in self.g_headnorm(*a, **kw):
            pass

    def g_headnorm(self, src_bank, nrows, blk, neps, gcol, dst, Rdst, cs, rope=None, stat_bank=5):
        kb = self.kb
        i = self.hn_i
        self.hn_i ^= 1
        rs = slice(0, nrows)
        src, Rsrc = self.ps[src_bank][rs, :], self.Rps[src_bank]
        pstat, Rpstat = self.ps[stat_bank], self.Rps[stat_bank]
        kb.op("act", lambda e: e.activation(out=self.sqb[i][rs, :], in_=src, func=AF.Square), [Rsrc], [self.Rsqb[i]])
        self.mm(pstat[rs, :], blk, self.sqb[i][rs, :], True, True, [self.Rsqb[i], self.Rcb], Rpstat)
        yield
        kb.op("act", lambda e: e.activation(out=self.rstd[i][rs, :], in_=pstat[rs, :], func=AF.Ln, bias=neps, scale=1.0), [Rpstat, self.Rcf], [self.Rrstd[i]])
        kb.op("act", lambda e: e.activation(out=self.rstd[i][rs, :], in_=self.rstd[i][rs, :], func=AF.Exp, scale=-0.5), [self.Rrstd[i]], [self.Rrstd[i]])
        yield
        kb.op("dve", lambda e: e.scalar_tensor_tensor(out=dst, in0=src, scalar=gcol, in1=self.rstd[i][rs, :], op0=ALU.mult, op1=ALU.mult),
              [Rsrc, self.Rrstd[i], self.Rder], [Rdst])
        if rope is None:
            yield
            return
        ci, swname, wins = rope
        pA, RpA = pstat, Rpstat
        pB, RpB = self.ps[src_bank], self.Rps[src_bank]
        self.mm(pA[rs, :], self.cbf("ident", rs, 0, nrows), dst, True, True, [Rdst, self.Rcb], RpA)
        self.mm(pB[rs, :], self.cbf(swname, rs, 0, nrows), dst, True, True, [Rdst, self.Rcb], RpB)
        yield
        tab = slice(32 * ci, 32 * ci + 32)
        for w0 in wins:
            ws = slice(w0, w0 + 32)
            kb.op("dve", lambda e: e.tensor_tensor(out=self.rt1[i][ws, :], in0=pA[ws, :], in1=self.ropeC[tab, cs], op=ALU.mult),
                  [RpA, self.Rrope], [self.Rrt1[i]])
            kb.op("dve", lambda e: e.tensor_tensor(out=self.rt2[i][ws, :], in0=pB[ws, :], in1=self.ropeS[tab, cs], op=ALU.mult),
                  [RpB, self.Rrope], [self.Rrt2[i]])
            kb.op("pool", lambda e: e.tensor_tensor(out=dst[ws, :], in0=self.rt1[i][ws, :], in1=self.rt2[i][ws, :], op=ALU.add),
                  [self.Rrt1[i], self.Rrt2[i]], [Rdst])
        yield

    def rr(self, *gens):
        gens = list(gens)
        while gens:
            for g in list(gens):
                try:
                    next(g)
                except StopIteration:
                    gens.remove(g)
            yield

    def bg_step(self):
        if self.bg is not None:
            try:
                next(self.bg)
            except StopIteration:
                self.bg = None

    def bg_drain(self):
        while self.bg is not None:
            self.bg_step()

    def g_vgroup(self, g, w_ap_k, rw, ncol, evac, nk=8, lhs_fn=None, rl=None):
        pb, Rpb = self.ps[3 + g % 2], self.Rps[3 + g % 2]
        for t in range(4):
            kt = g * 4 + t
            for k in range(nk):
                lhs = self.hT[k][:, kt * 128:(kt + 1) * 128] if lhs_fn is None else lhs_fn(k, kt)
                rr = self.Rh[k][g] if rl is None else rl(k, g)
                self.mm(pb[:, t * ncol:(t + 1) * ncol], lhs, w_ap_k(k), k == 0, k == nk - 1, [rw, rr], Rpb)
            if t % 2 == 1:
                yield
        evac(g, pb, Rpb)
        yield

    def attn_map(self, tiles, q_ap, k_ap, va_ap, scale, rins, o_bank, extra=None, bias=None, fin=None, after_p=None):
        kb = self.kb
        nt = len(tiles)
        for ti, (kt, c0, c1, mask) in enumerate(tiles):
            n = c1 - c0
            si = self.sb_i
            self.sb_i = (self.sb_i + 1) % 5
            pi = self.pt_i
            self.pt_i = (self.pt_i + 1) % 6
            pS, RpS = self.ps[si], self.Rps[si]
            self.mm(pS[:, 0:n], k_ap(kt), q_ap(c0, c1), True, extra is None, rins, RpS)
            if extra is not None:
                l2, r2 = extra(kt, c0, c1)
                self.mm(pS[:, 0:n], l2, r2, False, True, rins, RpS)
            b = bias(kt) if bias is not None else None
            if b is None:
                kb.op("act", lambda e: e.activation(out=self.PT[pi][:, 0:n], in_=pS[:, 0:n], func=AF.Exp, scale=scale), [RpS], [self.RPT[pi]])
            else:
                kb.op("act", lambda e: e.activation(out=self.PT[pi][:, 0:n], in_=pS[:, 0:n], func=AF.Exp, scale=scale, bias=b),
                      [RpS] + list(rins), [self.RPT[pi]])
            if mask is not None:
                kind, m0, base = mask
                if kind == "causal":
                    kb.op("pool", lambda e: e.affine_select(out=self.PT[pi][:, m0:m0 + 128], in_=self.PT[pi][:, m0:m0 + 128], pattern=[[1, 128]],
                                                            compare_op=ALU.is_ge, fill=0.0, base=0, channel_multiplier=-1),
                          [self.RPT[pi]], [self.RPT[pi]])
                elif kind == "lower":
                    kb.op("pool", lambda e: e.affine_select(out=self.PT[pi][:, m0:m0 + 128], in_=self.PT[pi][:, m0:m0 + 128], pattern=[[-1, 128]],
                                                            compare_op=ALU.is_ge, fill=0.0, base=-1, channel_multiplier=1),
                          [self.RPT[pi]], [self.RPT[pi]])
                elif kind == "vis":
                    kb.op("pool", lambda e: e.affine_select(out=self.PT[pi][:, 0:n], in_=self.PT[pi][:, 0:n], pattern=[[1, n]],
                                                            compare_op=ALU.is_ge, fill=0.0, base=base, channel_multiplier=-16),
                          [self.RPT[pi]], [self.RPT[pi]])
            if after_p is not None:
                after_p(pi)

            def pv(kt=kt, c0=c0, c1=c1, n=n, pi=pi, first=(ti == 0), last=(ti == nt - 1)):
                self.mm(self.ps[o_bank][:, c0:c1], va_ap(kt), self.PT[pi][:, 0:n], first, last,
                        [self.RPT[pi]] + list(rins), self.Rps[o_bank])

            self.pend.append((pv, fin if ti == nt - 1 else None))
            while len(self.pend) > self.LA:
                self.pend_pop()

    LA = 4

    def pend_pop(self):
        pv, fin = self.pend.pop(0)
        pv()
        if fin is not None:
            fin()

    def pend_flush(self):
        while self.pend:
            self.pend_pop()

    @staticmethod
    def causal_tiles(q):
        tiles = [(kt, 0, QT, None) for kt in range(4 * q)]
        for j in range(4):
            tiles.append((4 * q + j, 128 * j, QT, ("causal", 0, 0)))
        return tiles

    def finalize(self, o_bank, parity, dst, Rdst, eps=None):
        kb = self.kb
        i = self.rec_i
        self.rec_i ^= 1
        pO, RpO = self.ps[o_bank], self.Rps[o_bank]
        nr = slice(0, 64) if parity == 0 else slice(64, 128)
        dr = slice(64, 128) if parity == 0 else slice(0, 64)
        if eps is None:
            kb.op("dve", lambda e: e.reciprocal(out=self.rec[i][nr, :], in_=pO[dr, :]), [RpO], [self.Rrec[i]])
        else:
            kb.op("dve", lambda e: e.tensor_scalar(out=self.rec[i][nr, :], in0=pO[dr, :], scalar1=eps, scalar2=None, op0=ALU.add), [RpO], [self.Rrec[i]])
            kb.op("dve", lambda e: e.reciprocal(out=self.rec[i][nr, :], in_=self.rec[i][nr, :]), [self.Rrec[i]], [self.Rrec[i]])
        if dst is not None:
            kb.op("dve", lambda e: e.tensor_tensor(out=dst, in0=pO[nr, :], in1=self.rec[i][nr, :], op=ALU.mult), [RpO, self.Rrec[i]], [Rdst])
        return i, nr

    def proj_T(self, pb, Rpb, rows, w_ap_k, rw, q, nk=8, rhs_fn=None, rrhs=None):
        cs = slice(q * QT, (q + 1) * QT)
        for k in range(nk):
            rhs = self.hT[k][:, cs] if rhs_fn is None else rhs_fn(k)
            rr = self.Rh[k][q] if rrhs is None else rrhs(k)
            self.mm(pb[rows, :], w_ap_k(k), rhs, k == 0, k == nk - 1, [rw, rr], Rpb)

    def v_proj(self, w_ap_k, rw, ncol, evac, nk=8, lhs_fn=None, rl=None):
        for g in range(4):
            pb, Rpb = self.ps[3 + g % 2], self.Rps[3 + g % 2]
            for t in range(4):
                kt = g * 4 + t
                for k in range(nk):
                    lhs = self.hT[k][:, kt * 128:(kt + 1) * 128] if lhs_fn is None else lhs_fn(k, kt)
                    rr = self.Rh[k][g] if rl is None else rl(k, g)
                    self.mm(pb[:, t * ncol:(t + 1) * ncol], lhs, w_ap_k(k), k == 0, k == nk - 1, [rw, rr], Rpb)
            evac(g, pb, Rpb)

    def fox(self, l, fs):
        kb = self.kb
        self.mark('fox_prelude')
        fs0 = fs
        DQ = kb.sb("foxDQ", [128, S], BF16, fs)
        RDQ = [Res(f"foxDQ{q}") for q in range(NQ)]
        Dk = kb.sb("foxDk", [128, NKT, 4], F32, fs)
        RDk = Res("foxDk")
        with contextlib.ExitStack() as fs:
            Dt = kb.sb("foxD", [128, S], F32, fs)
            RD = [Res(f"foxD{q}") for q in range(NQ)]
            ft = [kb.sb(f"foxft{i}", [128, QT], F32, fs) for i in range(2)]
            Rft = [Res(f"foxft{i}") for i in range(2)]
            onesf = kb.sb("foxones", [128, QT], F32, fs)
            Rones = Res("foxones")
            kb.op("pool", lambda e: e.memset(onesf[:], 1.0), [], [Rones])
            wf, rwf = self.load_w(self.wf_pad[l].rearrange("(k p) n -> p k n", p=128), 128)
            for q in range(NQ):
                cs = slice(q * QT, (q + 1) * QT)
                pb, Rpb = self.ps[3 + q % 2], self.Rps[3 + q % 2]
                self.proj_T(pb, Rpb, slice(0, 128), lambda k: wf[:, k, 0:128], rwf, q)
                kb.op("act", lambda e: e.activation(out=ft[0][:], in_=pb[:], func=AF.Exp, scale=-1.0, bias=self.der("negfb")),
                      [Rpb, self.Rder], [Rft[0]])
                kb.op("act", lambda e: e.activation(out=ft[1][:], in_=ft[0][:], func=AF.Ln, bias=self.cf("epsc", 6, 7), scale=1.0), [Rft[0], self.Rcf], [Rft[1]])
                if q == 0:
                    kb.op("dve", lambda e: e.tensor_tensor_scan(out=Dt[:, cs], data0=onesf[:], data1=ft[1][:], initial=0.0, op0=ALU.mult, op1=ALU.add),
                          [Rones, Rft[1]], [RD[q]])
                else:
                    kb.op("dve", lambda e: e.tensor_tensor_scan(out=Dt[:, cs], data0=onesf[:], data1=ft[1][:], initial=Dt[:, q * QT - 1:q * QT],
                                                                op0=ALU.mult, op1=ALU.add), [Rones, Rft[1], RD[q - 1]], [RD[q]])
                kb.op("pool", lambda e: e.tensor_scalar(out=DQ[:, cs], in0=Dt[:, cs], scalar1=-8.0, scalar2=None, op0=ALU.mult), [RD[q]], [RDQ[q]])
                pt, Rpt = self.ps[5], self.Rps[5]
                for t in range(4):
                    kb.op("pe", lambda e: e.transpose(out=pt[:, t * 128:(t + 1) * 128], in_=Dt[:, q * QT + t * 128:q * QT + (t + 1) * 128],
                                                      identity=self.cf("ident")), [RD[q], self.Rcf], [Rpt])
                kb.op("dve", lambda e: e.tensor_copy(out=Dk[:, q * 4:(q + 1) * 4, :], in_=pt[:].rearrange("p (t h r) -> p t h r", t=4, h=4)[:, :, :, 0]),
                      [Rpt], [RDk])
            self.dump("foxD", Dt[:], RD, [128, S])
            kb.barrier()
            if self.dbg:
                kb.e["act"].wait_ge(self.osem[0], self.osem[1])
                kb.barrier()
        if True:
            self.mixer_scratch(fs0)
            self.mark('fox_units')
            fox_w = []
            for u in range(2):
                wqk_, rwqk_ = self.next_wb()
                kb.dma_in("pool", rwqk_, lambda e: [e.dma_start(out=wqk_[:, :, 0:128], in_=self.win_cols(l, C_FOX_Q + u * 128, 128)),
                                                    e.dma_start(out=wqk_[:, :, 128:256], in_=self.win_cols(l, C_FOX_K + u * 128, 128)),
                                                    e.dma_start(out=wqk_[:, :, 256:384], in_=self.win_cols(l, C_FOX_V + u * 128, 128))])
                fox_w.append((wqk_, rwqk_))
            for u in range(2):
                wqk, rwqk = fox_w[u]

                def evac(g, pb, Rpb):
                    src = pb[:].rearrange("p (t a c) -> p t a c", t=4, a=2)
                    dstv = self.VA[:, g * 4:(g + 1) * 4, :].rearrange("p t (a c) -> p t a c", a=4)[:, :, 0::3, :]
                    kb.op("dve", lambda e: e.tensor_copy(out=dstv, in_=src), [Rpb], [self.RVA[g]])

                def pre(q):
                    cs = slice(q * QT, (q + 1) * QT)
                    def ch_q():
                        self.proj_T(self.ps[3], self.Rps[3], slice(0, 128), lambda k: wqk[:, k, 0:128], rwqk, q)
                        yield
                        yield from self.g_headnorm(3, 128, self.cbf("b64"), self.cf("epsc", 1, 2), self.der("fox_gq"), self.QTt[0][:, cs], self.RQT[0][q], cs)

                    def ch_k():
                        self.proj_T(self.ps[4], self.Rps[4], slice(0, 128), lambda k: wqk[:, k, 128:256], rwqk, q)
                        yield
                        yield from self.g_headnorm(4, 128, self.cbf("b64"), self.cf("epsc", 1, 2), self.der("fox_gk"), self.KTt[0][:, cs], self.RKT[0][q], cs, stat_bank=2)

                    yield from self.rr(ch_q(), ch_k())
                    yield from self.g_vgroup(q, lambda k: wqk[:, k, 256:384], rwqk, 128, evac)

                self.bg = pre(0)
                self.bg_drain()
                for q in range(NQ):
                    cs = slice(q * QT, (q + 1) * QT)
                    if q + 1 < NQ:
                        self.bg = pre(q + 1)
                        self.bg_drain()
                    for hh in range(2):
                        h = 2 * u + hh
                        rb = slice(64 * hh, 64 * hh + 64)
                        ob = 6 + hh
                        rins = [self.RQT[0][q], RDk, RDQ[q], self.Rcb] + self.RKT[0][:q + 1] + self.RVA[:q + 1]
                        self.attn_map(
                            self.causal_tiles(q),
                            lambda c0, c1, rb=rb, q=q: self.QTt[0][rb, q * QT + c0:q * QT + c1],
                            lambda kt, rb=rb: self.KTt[0][rb, kt * 128:(kt + 1) * 128],
                            lambda kt, hh=hh: self.VA[:, kt, 128 * hh:128 * hh + 128],
                            0.125, rins, ob,
                            extra=lambda kt, c0, c1, h=h, q=q: (self.cbf("selrow", slice(0, 128), 128 * h, 128 * h + 128), DQ[:, q * QT + c0:q * QT + c1]),
                            bias=lambda kt, h=h: Dk[:, kt, h:h + 1],
                            fin=lambda ob=ob, hh=hh, u=u, rb=rb, cs=cs, q=q: self.finalize(ob, hh, self.oT[1][u][rb, cs], self.RoT[1][u][q]))
                    self.bg_drain()
                self.pend_flush()

    def merge(self, l):
        self.mark('merge')
        kb = self.kb
        with contextlib.ExitStack() as ms:
            MT = [kb.sb(f"MT{i}", [128, S], BF16, ms) for i in range(4)]
            RMT = [[Res(f"MT{i}_{q}") for q in range(NQ)] for i in range(4)]
            brw = [kb.sb(f"brw{i}", [128, 2, 4, 128], BF16, ms) for i in range(2)]
            Rbrw = [Res(f"brw{i}") for i in range(2)]
            wo = [kb.sb(f"wo{i}", [128, 4, 128], BF16, ms) for i in range(2)]
            Rwo = [Res(f"wo{i}") for i in range(2)]
            sig = [kb.sb(f"sig{i}", [128, QT], F32, ms) for i in range(2)]
            Rsig = [Res(f"sig{i}") for i in range(2)]
            tmp = [kb.sb(f"mtmp{i}", [128, QT], F32, ms) for i in range(2)]
            Rtmp = [Res(f"mtmp{i}") for i in range(2)]
            acc = [kb.sb(f"macc{i}", [128, QT], F32, ms) for i in range(2)]
            Racc = [Res(f"macc{i}") for i in range(2)]
            gwv = self.gate_w[l].rearrange("(k p) (m n) -> p k m n", p=128, m=4)
            brv = self.br_w[l].rearrange("m (k p) n -> p k m n", p=128)
            wov = self.w_out[l].rearrange("(k p) n -> p k n", p=128)
            n_it = 0
            loaded = {}

            def load_dc(dc):
                if dc in loaded or dc >= 8:
                    return
                wb_, rgw_ = self.next_wb()
                kb.dma_in("pool", rgw_, lambda e: [e.dma_start(out=wb_[:, :, m_ * 128:(m_ + 1) * 128], in_=gwv[:, :, m_, dc * 128:(dc + 1) * 128]) for m_ in range(4)])
                bi_ = dc % 2
                kb.dma_in("pool", Rbrw[bi_], lambda e: [e.dma_start(out=brw[bi_][:, :, m_, :], in_=brv[:, :, m_, dc * 128:(dc + 1) * 128]) for m_ in range(4)])
                loaded[dc] = (wb_, rgw_)

            wo_loaded = {}

            def load_wo(grp, dout):
                key = grp * 8 + dout
                if key in wo_loaded or dout >= 8:
                    return
                wi_ = key % 2
                kb.dma_in("pool", Rwo[wi_], lambda e: e.dma_start(out=wo[wi_][:], in_=wov[:, grp * 4:(grp + 1) * 4, dout * 128:(dout + 1) * 128]))
                wo_loaded[key] = wi_

            for grp in range(2):
                for dcl in range(4):
                    dc = grp * 4 + dcl
                    load_dc(dc)
                    if dcl < 3:
                        load_dc(dc + 1)
                    wb, rgw = loaded[dc]
                    bi = dc % 2
                    for q in range(NQ):
                        cs = slice(q * QT, (q + 1) * QT)
                        ai = n_it % 2
                        n_it += 1
                        for m in range(4):
                            pg, Rpg = self.ps[m % 2], self.Rps[m % 2]
                            py, Rpy = self.ps[2 + m % 2], self.Rps[2 + m % 2]
                            for k in range(8):
                                self.mm(pg[:], wb[:, k, m * 128:(m + 1) * 128], self.hT[k][:, cs], k == 0, k == 7, [rgw, self.Rh[k][q]], Rpg)
                            for k in range(2):
                                self.mm(py[:], brw[bi][:, k, m, :], self.oT[m][k][:, cs], k == 0, k == 1, [Rbrw[bi], self.RoT[m][k][q]], Rpy)
                            si = m % 2
                            kb.op("act", lambda e: e.activation(out=sig[si][:], in_=pg[:], func=AF.Sigmoid, bias=self.sm(l, "gate_b", m * 8 + dc, m * 8 + dc + 1), scale=1.0),
                                  [Rpg, self.Rsm], [Rsig[si]])
                            if m == 0:
                                kb.op("dve", lambda e: e.tensor_tensor(out=acc[ai][:], in0=py[:], in1=sig[si][:], op=ALU.mult), [Rpy, Rsig[si]], [Racc[ai]])
                            else:
                                kb.op("dve", lambda e: e.tensor_tensor(out=tmp[si][:], in0=py[:], in1=sig[si][:], op=ALU.mult), [Rpy, Rsig[si]], [Rtmp[si]])
                                if m < 3:
                                    kb.op("dve", lambda e: e.tensor_tensor(out=acc[ai][:], in0=acc[ai][:], in1=tmp[si][:], op=ALU.add), [Racc[ai], Rtmp[si]], [Racc[ai]])
                                else:
                                    kb.op("dve", lambda e: e.tensor_tensor(out=MT[dcl][:, cs], in0=acc[ai][:], in1=tmp[si][:], op=ALU.add),
                                          [Racc[ai], Rtmp[si]], [RMT[dcl][q]])
                if l == 0 and grp == 0:
                    self.dump("merged0", MT[0][:], RMT[0], [128, S])
                for dout in range(8):
                    load_wo(grp, dout)
                    load_wo(grp, dout + 1)
                    if dout == 7 and grp == 0:
                        load_dc(4)
                    wi = wo_loaded[grp * 8 + dout]
                    for q in range(NQ):
                        cs = slice(q * QT, (q + 1) * QT)
                        pb, Rpb = self.ps[4 + (dout * NQ + q) % 2], self.Rps[4 + (dout * NQ + q) % 2]
                        for dcl in range(4):
                            self.mm(pb[:], wo[wi][:, dcl, :], MT[dcl][:, cs], dcl == 0, dcl == 3, [Rwo[wi], RMT[dcl][q]], Rpb)
                        kb.op("dve", lambda e: e.scalar_tensor_tensor(out=self.xT[dout][:, cs], in0=pb[:], scalar=self.t_mod[:, 16 + dout:17 + dout],
                                                                      in1=self.xT[dout][:, cs], op0=ALU.mult, op1=ALU.add),
                              [Rpb, self.Rmod, self.Rx[dout][q]], [self.Rx[dout][q]])
            kb.barrier()

    def ffn(self, l):
        kb = self.kb
        self.mark('ffn')
        NJ = NFF // 2
        with contextlib.ExitStack() as fs:
            AT = [kb.sb(f"AT{j}", [128, S], BF16, fs) for j in range(NJ)]
            RAT = [[Res(f"AT{j}_{q}") for q in range(NQ)] for j in range(NJ)]
            G = [kb.sb(f"G{i}", [128, QT + 2], F32, fs) for i in range(2)]
            RG = [Res(f"G{i}") for i in range(2)]
            GC = [kb.sb(f"GC{i}", [128, QT], F32, fs) for i in range(2)]
            RGC = [Res(f"GC{i}") for i in range(2)]
            wd = [kb.sb(f"wd{i}", [128, NJ, 128], BF16, fs) for i in range(3)]
            Rwd = [Res(f"wd{i}") for i in range(3)]
            wd_n = 0
            gn = 0
            upv = self.w_up[l].rearrange("(k p) n -> p k n", p=128)
            dnv = self.w_down[l].rearrange("(j p) n -> p j n", p=128)
            if l + 1 < self.n_layers:
                self.ada_gen = self.g_ada(l + 1)
            up_loaded = {}

            def load_up(j0):
                if j0 in up_loaded or j0 >= NFF:
                    return
                nj_ = 1 if (j0 % NJ) == NJ - 1 else 2
                wb_, rwu_ = self.next_wb()
                kb.dma_in("pool", rwu_, lambda e: [e.dma_start(out=wb_[:, :, 0:128 * nj_], in_=upv[:, :, j0 * 128:(j0 + nj_) * 128]),
                                                   e.dma_start(out=wb_[:, :, 256:256 + 128 * nj_], in_=upv[:, :, DFF + j0 * 128:DFF + (j0 + nj_) * 128])])
                up_loaded[j0] = (wb_, rwu_)

            wd_loaded = {}

            def load_wd(ps__, dout):
                key = ps__ * 8 + dout
                if key in wd_loaded or dout >= 8:
                    return
                wi_ = key % 3
                kb.dma_in("pool", Rwd[wi_], lambda e: e.dma_start(out=wd[wi_][:], in_=dnv[:, ps__ * NJ:(ps__ + 1) * NJ, dout * 128:(dout + 1) * 128]))
                wd_loaded[key] = wi_

            for ps_ in range(2):
                wb, rwu = None, None
                for jj in range(NJ):
                    j = ps_ * NJ + jj
                    self.ada_step(3)
                    if jj % 2 == 0:
                        load_up(j)
                        nxt = j + 2
                        if jj + 2 < NJ:
                            load_up(nxt)
                        elif ps_ == 0:
                            pass
                        wb, rwu = up_loaded[j]
                    if jj == NJ - 1:
                        load_wd(ps_, 0)
                    co = 128 * (jj % 2)
                    w0 = self.sm(l, "conv_w", 0 * NFF + j, 0 * NFF + j + 1)
                    w1 = self.sm(l, "conv_w", 1 * NFF + j, 1 * NFF + j + 1)
                    w2 = self.sm(l, "conv_w", 2 * NFF + j, 2 * NFF + j + 1)
                    cb_ = self.sm(l, "conv_b", j, j + 1)
                    for q in range(NQ):
                        gi = gn % 2
                        gn += 1
                        cs = slice(q * QT, (q + 1) * QT)
                        pg, Rpg = self.ps[gi], self.Rps[gi]
                        pv, Rpv = self.ps[2 + gi], self.Rps[2 + gi]
                        for k in range(8):
                            self.mm(pg[:], wb[:, k, co:co + 128], self.hT[k][:, cs], k == 0, k == 7, [rwu, self.Rh[k][q]], Rpg)
                        for k in range(8):
                            self.mm(pv[:], wb[:, k, 256 + co:256 + co + 128], self.hT[k][:, cs], k == 0, k == 7, [rwu, self.Rh[k][q]], Rpv)
                        if q == 0:
                            kb.op("dve", lambda e: e.memset(G[gi][:, 0:2], 0.0), [], [RG[gi]])
                        else:
                            kb.op("dve", lambda e: e.tensor_copy(out=G[gi][:, 0:2], in_=G[1 - gi][:, QT:QT + 2]), [RG[1 - gi]], [RG[gi]])
                        kb.op("act", lambda e: e.activation(out=G[gi][:, 2:2 + QT], in_=pg[:], func=AF.Copy), [Rpg], [RG[gi]])
                        kb.op("act", lambda e: e.activation(out=GC[gi][:], in_=pg[:], func=AF.Identity, scale=w2, bias=cb_),
                              [Rpg, self.Rsm], [RGC[gi]])
                        kb.op("dve", lambda e: e.scalar_tensor_tensor(out=GC[gi][:], in0=G[gi][:, 1:1 + QT], scalar=w1, in1=GC[gi][:], op0=ALU.mult, op1=ALU.add),
                              [RG[gi], self.Rsm, RGC[gi]], [RGC[gi]])
                        kb.op("dve", lambda e: e.scalar_tensor_tensor(out=GC[gi][:], in0=G[gi][:, 0:QT], scalar=w0, in1=GC[gi][:], op0=ALU.mult, op1=ALU.add),
                              [RG[gi], self.Rsm, RGC[gi]], [RGC[gi]])
                        kb.op("act", lambda e: e.activation(out=GC[gi][:], in_=GC[gi][:], func=AF.Silu), [RGC[gi]], [RGC[gi]])
                        kb.op("dve", lambda e: e.tensor_tensor(out=AT[jj][:, cs], in0=pv[:], in1=GC[gi][:], op=ALU.mult),
                              [Rpv, RGC[gi]], [RAT[jj][q]])
                for dout in range(8):
                    self.ada_step(3)
                    load_wd(ps_, dout)
                    load_wd(ps_, dout + 1)
                    if dout == 6 and ps_ == 0:
                        load_up(NJ)
                    wi = wd_loaded[ps_ * 8 + dout]
                    for q in range(NQ):
                        cs = slice(q * QT, (q + 1) * QT)
                        pb, Rpb = self.ps[4 + q % 2], self.Rps[4 + q % 2]
                        for jj in range(NJ):
                            self.mm(pb[:], wd[wi][:, jj, :], AT[jj][:, cs], jj == 0, jj == NJ - 1, [Rwd[wi], RAT[jj][q]], Rpb)
                        kb.op("dve", lambda e: e.scalar_tensor_tensor(out=self.xT[dout][:, cs], in0=pb[:], scalar=self.t_mod[:, 40 + dout:41 + dout],
                                                                      in1=self.xT[dout][:, cs], op0=ALU.mult, op1=ALU.add),
                              [Rpb, self.Rmod, self.Rx[dout][q]], [self.Rx[dout][q]])
            kb.barrier()

    def rstd_from(self, pstat_ap, out_ap, neps, Rin, Rout):
        kb = self.kb
        kb.op("act", lambda e: e.activation(out=out_ap, in_=pstat_ap, func=AF.Ln, bias=neps, scale=1.0), [Rin, self.Rcf], [Rout])
        kb.op("act", lambda e: e.activation(out=out_ap, in_=out_ap, func=AF.Exp, scale=-0.5), [Rout], [Rout])

    def mla(self, l, ms):
        kb = self.kb
        self.mark('mla_prelude')
        cqn = [kb.sb(f"cqn{i}", [128, S], BF16, ms) for i in range(2)]
        Rcqn = [[Res(f"cqn{i}_{q}") for q in range(NQ)] for i in range(2)]
        ckvn = kb.sb("ckvn", [128, S], BF16, ms)
        Rckvn = [Res(f"ckvn{q}") for q in range(NQ)]
        wuq = kb.sb("wuq", [128, 2, 384], BF16, ms)
        wukv = kb.sb("wukv", [128, 512], BF16, ms)
        Rwuq, Rwukv = Res("wuq"), Res("wukv")
        kb.dma_in("pool", Rwuq, lambda e: e.dma_start(out=wuq[:], in_=self.w_uq[l].rearrange("(k p) n -> p k n", p=128)))
        kb.dma_in("pool", Rwukv, lambda e: e.dma_start(out=wukv[:], in_=self.w_ukv[l]))
        self.mixer_scratch(ms)
        wc, rwc = self.load_w(self.win_cols(l, C_MLA_CQ, 416), 416)
        pstat, Rpstat = self.ps[5], self.Rps[5]
        for q in range(NQ):
            cs = slice(q * QT, (q + 1) * QT)
            for c in range(2):
                self.proj_T(self.ps[3 + c], self.Rps[3 + c], slice(0, 128), lambda k: wc[:, k, c * 128:(c + 1) * 128], rwc, q)
                kb.op("act", lambda e: e.activation(out=self.sqb[c][:], in_=self.ps[3 + c][:], func=AF.Square), [self.Rps[3 + c]], [self.Rsqb[c]])
                self.mm(pstat[:], self.cbf("ones"), self.sqb[c][:], c == 0, c == 1, [self.Rsqb[c], self.Rcb], Rpstat)
            self.rstd_from(pstat[:], self.rstd[0][:], self.cf("epsc", 4, 5), Rpstat, self.Rrstd[0])
            for c in range(2):
                kb.op("dve", lambda e: e.scalar_tensor_tensor(out=cqn[c][:, cs], in0=self.ps[3 + c][:], scalar=self.der("mla_cqg", c, c + 1),
                                                              in1=self.rstd[0][:], op0=ALU.mult, op1=ALU.mult),
                      [self.Rps[3 + c], self.Rrstd[0], self.Rder], [Rcqn[c][q]])
            self.proj_T(self.ps[3], self.Rps[3], slice(0, 128), lambda k: wc[:, k, 256:384], rwc, q)
            self.headnorm(3, 128, self.cbf("ones"), self.cf("epsc", 5, 6), self.der("mla_ckvg"), ckvn[:, cs], Rckvn[q], cs)
        self.mark('mla_heads')
        r96 = slice(0, 96)
        for h in range(4):
            hh = h % 2
            vcol = 0 if hh == 0 else 192

            def evac(g, pb, Rpb, vcol=vcol):
                kb.op("dve", lambda e: e.tensor_copy(out=self.VA[:, g * 4:(g + 1) * 4, vcol:vcol + 64], in_=pb[:, 0:256].rearrange("p (t c) -> p t c", t=4)),
                      [Rpb], [self.RVA[g]])

            def pre(q, h=h):
                cs = slice(q * QT, (q + 1) * QT)
                def ch_q():
                    for k in range(2):
                        self.mm(self.ps[3][0:64, :], wuq[:, k, 96 * h + 32:96 * h + 96], cqn[k][:, cs], k == 0, k == 1, [Rwuq, Rcqn[k][q]], self.Rps[3])
                    for k in range(2):
                        self.mm(self.ps[3][64:96, :], wuq[:, k, 96 * h:96 * h + 32], cqn[k][:, cs], k == 0, k == 1, [Rwuq, Rcqn[k][q]], self.Rps[3])
                    yield
                    yield from self.g_headnorm(3, 96, self.cbf("ones", r96, 0, 96), self.cf("epsc", 2, 3, r96), self.der("mla_gq", 0, 1, r96), self.QTt[0][r96, cs],
                                               self.RQT[0][q], cs, rope=(1, "sw_mla", [64]))

                def ch_k():
                    self.mm(self.ps[4][0:64, :], wukv[:, 128 * h:128 * h + 64], ckvn[:, cs], True, True, [Rwukv, Rckvn[q]], self.Rps[4])
                    self.proj_T(self.ps[4], self.Rps[4], slice(64, 96), lambda k: wc[:, k, 384:416], rwc, q)
                    yield
                    yield from self.g_headnorm(4, 96, self.cbf("ones", r96, 0, 96), self.cf("epsc", 2, 3, r96), self.der("mla_gk", 0, 1, r96), self.KTt[0][r96, cs],
                                               self.RKT[0][q], cs, rope=(1, "sw_mla", [64]), stat_bank=2)

                yield from self.rr(ch_q(), ch_k())
                yield from self.g_vgroup(q, lambda k: wukv[:, 128 * h + 64:128 * h + 128], Rwukv, 64, evac, nk=1,
                                         lhs_fn=lambda k, kt: ckvn[:, kt * 128:(kt + 1) * 128], rl=lambda k, g: Rckvn[g])

            self.bg = pre(0)
            self.bg_drain()
            for q in range(NQ):
                cs = slice(q * QT, (q + 1) * QT)
                if q + 1 < NQ:
                    self.bg = pre(q + 1)
                    self.bg_drain()
                rb = slice(64 * hh, 64 * hh + 64)
                ob = 6 + q % 2
                rins = [self.RQT[0][q]] + self.RKT[0][:q + 1] + self.RVA[:q + 1]
                self.attn_map(self.causal_tiles(q),
                              lambda c0, c1, q=q: self.QTt[0][r96, q * QT + c0:q * QT + c1],
                              lambda kt: self.KTt[0][r96, kt * 128:(kt + 1) * 128],
                              lambda kt, hh=hh: self.VA[:, kt, 128 * hh:128 * hh + 128],
                              96.0 ** -0.5, rins, ob,
                              fin=lambda ob=ob, hh=hh, h=h, rb=rb, cs=cs, q=q: self.finalize(ob, hh, self.oT[2][h // 2][rb, cs], self.RoT[2][h // 2][q]))
                self.bg_drain()
            self.pend_flush()

    def diff(self, l, ms):
        kb = self.kb
        self.mark('diff')
        self.mixer_scratch(ms)
        dsq = kb.sb("dsq", [128, QT], BF16, ms)
        Rdsq = Res("dsq")
        r64 = slice(0, 64)
        pstat, Rpstat = self.ps[5], self.Rps[5]
        dif_w = {}

        def load_dif(h_):
            if h_ in dif_w or h_ >= 4:
                return
            w_, r_ = self.next_wb()
            kb.dma_in("pool", r_, lambda e: [e.dma_start(out=w_[:, :, 0:64], in_=self.win_cols(l, C_DIF_Q + 64 * h_, 64)),
                                             e.dma_start(out=w_[:, :, 64:128], in_=self.win_cols(l, C_DIF_K + 64 * h_, 64)),
                                             e.dma_start(out=w_[:, :, 128:192], in_=self.win_cols(l, C_DIF_V + 64 * h_, 64))])
            dif_w[h_] = (w_, r_)

        for h in range(4):
            hh = h % 2
            load_dif(h)
            load_dif(h + 1)
            wqk, rwqk = dif_w[h]
            vcol = 0 if hh == 0 else 192

            def evac(g, pb, Rpb, vcol=vcol):
                kb.op("dve", lambda e: e.tensor_copy(out=self.VA[:, g * 4:(g + 1) * 4, vcol:vcol + 64], in_=pb[:, 0:256].rearrange("p (t c) -> p t c", t=4)),
                      [Rpb], [self.RVA[g]])

            def pre(q, wqk=wqk, rwqk=rwqk):
                cs = slice(q * QT, (q + 1) * QT)
                def ch_q():
                    self.proj_T(self.ps[3], self.Rps[3], r64, lambda k: wqk[:, k, 0:64], rwqk, q)
                    yield
                    yield from self.g_headnorm(3, 64, self.cbf("b32", r64, 0, 64), self.cf("epsc", 3, 4, r64), self.der("dif_gq", 0, 1, r64), self.QTt[0][r64, cs],
                                               self.RQT[0][q], cs, rope=(2, "sw_dif", [0, 32]))

                def ch_k():
                    self.proj_T(self.ps[4], self.Rps[4], r64, lambda k: wqk[:, k, 64:128], rwqk, q)
                    yield
                    yield from self.g_headnorm(4, 64, self.cbf("b32", r64, 0, 64), self.cf("epsc", 3, 4, r64), self.der("dif_gk", 0, 1, r64), self.KTt[0][r64, cs],
                                               self.RKT[0][q], cs, rope=(2, "sw_dif", [0, 32]), stat_bank=2)

                yield from self.rr(ch_q(), ch_k())
                yield from self.g_vgroup(q, lambda k: wqk[:, k, 128:192], rwqk, 64, evac)

            self.bg = pre(0)
            self.bg_drain()
            for q in range(NQ):
                cs = slice(q * QT, (q + 1) * QT)
                if q + 1 < NQ:
                    self.bg = pre(q + 1)
                    self.bg_drain()
                rins = [self.RQT[0][q]] + self.RKT[0][:q + 1] + self.RVA[:q + 1]

                def fin_diff(q=q, cs=cs, hh=hh, h=h):
                    recs = []
                    for a in range(2):
                        i, nr = self.finalize(6 + a, hh, None, None)
                        kb.op("dve", lambda e: e.tensor_tensor(out=self.rec[i][nr, :], in0=self.ps[6 + a][nr, :], in1=self.rec[i][nr, :], op=ALU.mult),
                              [self.Rps[6 + a], self.Rrec[i]], [self.Rrec[i]])
                        recs.append(i)
                    i0, i1 = recs
                    kb.op("dve", lambda e: e.scalar_tensor_tensor(out=self.rec[i0][nr, :], in0=self.rec[i1][nr, :], scalar=self.der("neglam", 0, 1, nr),
                                                                  in1=self.rec[i0][nr, :], op0=ALU.mult, op1=ALU.add),
                          [self.Rrec[i0], self.Rrec[i1], self.Rder], [self.Rrec[i0]])
                    pst, Rpst = self.ps[6], self.Rps[6]
                    kb.op("pool", lambda e: e.tensor_tensor(out=dsq[nr, :], in0=self.rec[i0][nr, :], in1=self.rec[i0][nr, :], op=ALU.mult),
                          [self.Rrec[i0]], [Rdsq])
                    self.mm(pst[nr, :], self.cbf("ones", nr, 0, 64), dsq[nr, :], True, True, [Rdsq, self.Rcb], Rpst)
                    self.rstd_from(pst[nr, :], self.rec[i1][nr, :], self.cf("epsc", 1, 2, nr), Rpst, self.Rrec[i1])
                    kb.op("dve", lambda e: e.scalar_tensor_tensor(out=self.oT[3][h // 2][nr, cs], in0=self.rec[i0][nr, :], scalar=self.der("dif_og", 0, 1, nr),
                                                                  in1=self.rec[i1][nr, :], op0=ALU.mult, op1=ALU.mult),
                          [self.Rrec[i0], self.Rrec[i1], self.Rder], [self.RoT[3][h // 2][q]])

                for a in range(2):
                    ra = slice(32 * a, 32 * a + 32)
                    self.attn_map(self.causal_tiles(q),
                                  lambda c0, c1, ra=ra, q=q: self.QTt[0][ra, q * QT + c0:q * QT + c1],
                                  lambda kt, ra=ra: self.KTt[0][ra, kt * 128:(kt + 1) * 128],
                                  lambda kt, hh=hh: self.VA[:, kt, 128 * hh:128 * hh + 128],
                                  32.0 ** -0.5, rins, 6 + a, fin=(fin_diff if a == 1 else None))
                self.bg_drain()
            self.pend_flush()


def prep_inputs(inp, layer_ids=tuple(range(DEPTH))):
    li = list(layer_ids)
    cb, cf = make_consts(layer_ids)
    sm = make_smalls(inp, layer_ids)

    def W(name):
        return np.ascontiguousarray(np.asarray(inp[name], np.float32)[li])

    w_in = W("w_in")
    gcols = []
    for br in range(3):
        for pr in range(2):
            for hh in range(2):
                gcols += [C_NSA_G + br * 4 + pr * 2 + hh] * 64
    wg_rep = np.ascontiguousarray(w_in[:, :, gcols])
    wf_pad = np.zeros((len(li), D, 128), np.float32)
    for h in range(4):
        wf_pad[:, :, 32 * h] = w_in[:, :, C_FOX_F + h]
    shared = {
        "ada_w": W("ada_w"), "w_in": w_in, "wg_rep": wg_rep, "wf_pad": wf_pad,
        "cmp_w1": W("nsa_cmp_w1"), "cmp_w2": W("nsa_cmp_w2"),
        "cmp_pe": np.ascontiguousarray(np.transpose(W("nsa_cmp_pe"), (0, 1, 3, 2))),
        "w_uq": W("mla_w_uq"), "w_ukv": W("mla_w_ukv"), "br_w": W("br_w"), "gate_w": W("gate_w"),
        "w_out": W("w_out"), "w_up": W("ffn_w_up"), "w_down": W("ffn_w_down"),
        "smalls": sm, "cbf": cb, "cf32": cf,
    }
    maps = []
    for b in range(8):
        m = dict(shared)
        m["x"] = np.ascontiguousarray(inp["x"][b], np.float32)
        m["cT"] = np.ascontiguousarray(np.asarray(inp["c"][b], np.float32).reshape(8, 128).T)
        m["pos"] = np.ascontiguousarray(np.asarray(inp["positions"][b], np.int32).reshape(1, S))
        maps.append(m)
    return maps


FUSED = True


def kernel(**inputs):
    inp = {k: np.asarray(v) for k, v in inputs.items()}
    if FUSED:
        maps = prep_inputs(inp)
        nc = Prog().build()
        res = run_bass_kernel_spmd(nc, maps, core_ids=list(range(8)))
        return np.stack([np.asarray(res.results[b]["y"], np.float32) for b in range(8)], axis=0)
    nc = Prog(n_layers=1, wdepth=1).build()
    x = np.asarray(inp["x"], np.float32)
    for l in range(DEPTH):
        cur = dict(inp)
        cur["x"] = x
        maps = prep_inputs(cur, (l,))
        res = run_bass_kernel_spmd(nc, maps, core_ids=list(range(8)))
        x = np.stack([np.asarray(res.results[b]["y"], np.float32) for b in range(8)], axis=0)
    return x
```

```python
import contextlib
import math
import numpy as np
import ml_dtypes
import concourse.bass as bass
import concourse.mybir as mybir
from concourse.bass_utils import run_bass_kernel_spmd

F32 = mybir.dt.float32
BF16 = mybir.dt.bfloat16
I32 = mybir.dt.int32
AF = mybir.ActivationFunctionType
ALU = mybir.AluOpType

S = 2048
D = 1024
NQ = 4
QT = 512
NKT = 16
DEPTH = 4
EPS = 1e-6
DFF = 2816
NFF = 22
THETA = 500000.0
BIG = 30000.0
IN_W = 2608


class Res:
    __slots__ = ("name", "w", "r", "dsem", "dcnt")

    def __init__(self, name):
        self.name = name
        self.w = None
        self.r = {}
        self.dsem = None
        self.dcnt = 0


class KB:
    ENG = ("pe", "act", "dve", "pool", "sp")

    def __init__(self, nc, stack):
        self.nc = nc
        self.stack = stack
        self.e = {"pe": nc.tensor, "act": nc.scalar, "dve": nc.vector, "pool": nc.gpsimd, "sp": nc.sync}
        self.sem = {k: stack.enter_context(nc.semaphore("s_" + k)) for k in self.ENG}
        self.cnt = {k: 0 for k in self.ENG}
        self.seen = {k: {} for k in self.ENG}
        self.same_engine_raw = True
        self.n_wait = 0

    def sb(self, name, shape, dt, stack=None):
        self.uid = getattr(self, "uid", 0) + 1
        return (stack or self.stack).enter_context(self.nc.sbuf_tensor(f"{name}_u{self.uid}", list(shape), dt))

    def ps(self, name, shape, dt=F32):
        return self.stack.enter_context(self.nc.psum_tensor(name, list(shape), dt))

    def newsem(self, name):
        self.uid = getattr(self, "uid", 0) + 1
        return self.stack.enter_context(self.nc.semaphore(f"{name}_u{self.uid}"))

    def _need(self, eng, deps):
        for key, sem, val in deps:
            if self.seen[eng].get(key, 0) >= val:
                continue
            self.e[eng].wait_ge(sem, val)
            self.n_wait += 1
            self.seen[eng][key] = val

    def _deps(self, eng, reads, writes):
        d = {}

        def add(w):
            if w is None:
                return
            e, i = w
            if e == "dma":
                sem, val = i
                key = ("dma", id(sem))
                if d.get(key, (None, 0))[1] < val:
                    d[key] = (sem, val)
            else:
                if e == eng and (eng == "pe" or not self.same_engine_raw):
                    return
                if d.get(e, (None, 0))[1] < i:
                    d[e] = (self.sem[e], i)

        for r in reads:
            add(r.w)
        for w in writes:
            add(w.w)
            for e, i in w.r.items():
                if e == eng:
                    continue
                if e == "dma":
                    add(("dma", i))
                else:
                    add((e, i))
        return [(k, s, v) for k, (s, v) in d.items()]

    def op(self, eng, fn, reads=(), writes=()):
        self._need(eng, self._deps(eng, reads, writes))
        ins = fn(self.e[eng])
        ins.then_inc(self.sem[eng], 1)
        self.cnt[eng] += 1
        idx = self.cnt[eng]
        for r in reads:
            r.r[eng] = idx
        for w in writes:
            w.w = (eng, idx)
            w.r = {}
        return ins

    def dma_in(self, q, res, fn, reads=()):
        if res.dsem is None:
            res.dsem = self.newsem("d_" + res.name)
        self._need(q, self._deps(q, reads, [res]))
        inss = fn(self.e[q])
        if not isinstance(inss, (list, tuple)):
            inss = [inss]
        for ins in inss:
            ins.then_inc(res.dsem, 16)
            res.dcnt += 16
        res.w = ("dma", (res.dsem, res.dcnt))
        res.r = {}

    def dma_out(self, q, res_list, fn, sem):
        self._need(q, self._deps(q, res_list, []))
        inss = fn(self.e[q])
        if not isinstance(inss, (list, tuple)):
            inss = [inss]
        for ins in inss:
            ins.then_inc(sem[0], 16)
            sem[1] += 16
        for r in res_list:
            r.r["dma"] = (sem[0], sem[1])

    def barrier(self):
        for a in ("pe", "act", "dve", "pool"):
            deps = []
            for b in ("pe", "act", "dve", "pool"):
                if a != b and self.cnt[b] > 0:
                    deps.append((b, self.sem[b], self.cnt[b]))
            self._need(a, deps)


CB = {}
CF = {}
SM = {}


def _alloc(tab, name, n, cur):
    tab[name] = (cur, cur + n)
    return cur + n


def _layout():
    c = 0
    for name, n in (("ident", 128), ("ones", 128), ("b64", 128), ("b32", 128), ("sw_nsa", 128),
                    ("sw_mla", 128), ("sw_dif", 128), ("selrow", 512), ("esel", 2048), ("ovl", 64), ("tabA", 512), ("tabB", 512)):
        c = _alloc(CB, name, n, c)
    CB["_n"] = c
    c = 0
    for name, n in (("ident", 128), ("invf", 1), ("sgn", 1), ("negpi", 1), ("epsc", 8), ("lamc", 2 * DEPTH)):
        c = _alloc(CF, name, n, c)
    CF["_n"] = c
    c = 0
    for name, n in (("ada_b", 48), ("gate_b", 32), ("conv_w", 66), ("conv_b", 22),
                    ("nsa_gq", 1), ("nsa_gkc", 1), ("nsa_gks", 1), ("nsa_gkw", 1),
                    ("fox_gq", 1), ("fox_gk", 1), ("fox_fb", 1),
                    ("mla_cqg", 2), ("mla_ckvg", 1), ("mla_gq", 1), ("mla_gk", 1),
                    ("dif_gq", 1), ("dif_gk", 1), ("dif_og", 1), ("dif_lam", 128)):
        c = _alloc(SM, name, n, c)
    SM["_n"] = c


_layout()

DER = {}
_c = 0
for _name, _n in (("a1", 8), ("a2", 8), ("nsa_gq", 1), ("nsa_gkc", 1), ("nsa_gks", 1), ("nsa_gkw", 1),
                  ("fox_gq", 1), ("fox_gk", 1), ("negfb", 1), ("mla_cqg", 2), ("mla_ckvg", 1), ("mla_gq", 1),
                  ("mla_gk", 1), ("dif_gq", 1), ("dif_gk", 1), ("dif_og", 1), ("neglam", 1), ("t0", 4), ("t1", 4)):
    _c = _alloc(DER, _name, _n, _c)
DER["_n"] = _c


def _rope_partner(r, head, n_rot):
    rr = r % head
    half = n_rot // 2
    base = r - rr
    if rr < half:
        return base + rr + half, rr, -1.0
    if rr < n_rot:
        return base + rr - half, rr - half, 1.0
    return None, None, 0.0


def make_consts(layer_ids=tuple(range(DEPTH))):
    cb = np.zeros((128, CB["_n"]), np.float32)
    cf = np.zeros((128, CF["_n"]), np.float32)
    p = np.arange(128)
    cb[:, CB["ident"][0]:CB["ident"][1]] = np.eye(128)
    cb[:, CB["ones"][0]:CB["ones"][1]] = 1.0
    cb[:, CB["b64"][0]:CB["b64"][1]] = (p[:, None] // 64 == p[None, :] // 64)
    cb[:, CB["b32"][0]:CB["b32"][1]] = (p[:, None] // 32 == p[None, :] // 32)
    for ci, (nm, head, nrot) in enumerate((("sw_nsa", 64, 16), ("sw_mla", 96, 32), ("sw_dif", 32, 8))):
        sw = np.zeros((128, 128), np.float32)
        half = nrot // 2
        inv = (np.float32(THETA) ** (-np.arange(half, dtype=np.float32) / np.float32(half))).astype(np.float32)
        for m in range(128):
            if nm == "sw_mla":
                if 64 <= m < 96:
                    rr = m - 64
                    sw[64 + (rr + 16 if rr < 16 else rr - 16), m] = 1.0
                continue
            pr, fi, sg = _rope_partner(m, head, nrot)
            if pr is not None:
                sw[pr, m] = 1.0
        for r in range(32):
            pr, fi, sg = _rope_partner(r, head, nrot)
            if pr is not None:
                cf[32 * ci + r, CF["invf"][0]] = inv[fi]
                cf[32 * ci + r, CF["sgn"][0]] = sg
        cb[:, CB[nm][0]:CB[nm][1]] = sw
    for h in range(4):
        cb[32 * h, CB["selrow"][0] + 128 * h: CB["selrow"][0] + 128 * (h + 1)] = 1.0
    for kt in range(16):
        for pp in range(128):
            cb[2 * kt + pp // 64, CB["esel"][0] + kt * 128 + pp] = BIG
    cs = np.arange(127)[:, None] * 16
    bs = np.arange(32)[None, :] * 64
    ov = ((cs < bs + 64) & (cs + 32 > bs)).astype(np.float32)
    cb[0:127, CB["ovl"][0]:CB["ovl"][0] + 32] = ov
    cb[0:127, CB["ovl"][0] + 32:CB["ovl"][0] + 64] = 1.0
    cf[:, CF["ident"][0]:CF["ident"][1]] = np.eye(128)
    cf[:, CF["negpi"][0]] = -math.pi
    for i_, v_ in enumerate((1024 * EPS, 64 * EPS, 96 * EPS, 32 * EPS, 256 * EPS, 128 * EPS, 1.0, 1e-30)):
        cf[:, CF["epsc"][0] + i_] = v_
    for i_, l_ in enumerate(layer_ids):
        lam_init = 0.8 - 0.6 * math.exp(-0.3 * l_)
        cf[:, CF["lamc"][0] + 2 * i_] = 8.0 * (1.0 - lam_init)
        cf[:, CF["lamc"][0] + 2 * i_ + 1] = -lam_init
    t = (np.arange(16)[None, :, None] * 128 + p[:, None, None])
    j = np.arange(32)[None, None, :]
    cur = t // 64
    future = (j * 64 > t)
    forced = ((j == 0) | (j == cur) | (j == cur - 1)) & (~future)
    A = (~future & ~forced).astype(np.float32)
    Bt = np.where(future, -1.0, np.where(forced, 1e4, 0.0)).astype(np.float32)
    cb[:, CB["tabA"][0]:CB["tabA"][1]] = A.reshape(128, 512)
    cb[:, CB["tabB"][0]:CB["tabB"][1]] = Bt.reshape(128, 512)
    return cb.astype(ml_dtypes.bfloat16), cf


def make_smalls(inp, layer_ids=tuple(range(DEPTH))):
    sm = np.zeros((128, len(layer_ids) * SM["_n"]), np.float32)
    for li_, l in enumerate(layer_ids):
        o = li_ * SM["_n"]

        def put(name, arr):
            a, b = SM[name]
            arr = np.asarray(arr, np.float32)
            if arr.ndim == 1:
                arr = arr[:, None]
            sm[:arr.shape[0], o + a:o + a + arr.shape[1]] = arr

        put("ada_b", inp["ada_b"][l].reshape(48, 128).T)
        put("gate_b", inp["gate_b"][l].reshape(32, 128).T)
        cw = inp["ffn_conv_w"][l]
        put("conv_w", np.concatenate([cw[t].reshape(22, 128).T for t in range(3)], axis=1))
        put("conv_b", inp["ffn_conv_b"][l].reshape(22, 128).T)
        g = inp["nsa_qk_g"][l]
        put("nsa_gq", np.tile(g[0], 2)); put("nsa_gkc", np.tile(g[1], 2))
        put("nsa_gks", np.tile(g[2], 2)); put("nsa_gkw", np.tile(g[3], 2))
        g = inp["fox_qk_g"][l]
        put("fox_gq", np.tile(g[0], 2)); put("fox_gk", np.tile(g[1], 2))
        fb = np.zeros(128, np.float32)
        fb[0::32] = inp["fox_f_b"][l]
        put("fox_fb", fb)
        put("mla_cqg", inp["mla_cq_g"][l].reshape(2, 128).T)
        put("mla_ckvg", inp["mla_ckv_g"][l])
        gq_, gk_ = inp["mla_qk_g"][l, 0], inp["mla_qk_g"][l, 1]
        put("mla_gq", np.concatenate([gq_[32:], gq_[:32]])); put("mla_gk", np.concatenate([gk_[32:], gk_[:32]]))
        put("dif_gq", np.tile(inp["diff_qk_g"][l, 0], 4)); put("dif_gk", np.tile(inp["diff_qk_g"][l, 1], 4))
        put("dif_og", np.tile(inp["diff_out_g"][l], 2))
        put("dif_lam", np.tile(inp["diff_lambda"][l].reshape(1, 128), (128, 1)))
    return sm


C_NSA_Q, C_NSA_KC, C_NSA_KS, C_NSA_VS, C_NSA_KW, C_NSA_VW, C_NSA_G = 0, 256, 384, 448, 512, 576, 640
C_FOX_Q, C_FOX_K, C_FOX_V, C_FOX_F = 652, 908, 1164, 1420
C_MLA_CQ, C_MLA_CKV, C_MLA_KR = 1424, 1680, 1808
C_DIF_Q, C_DIF_K, C_DIF_V = 1840, 2096, 2352


class Prog:
    def __init__(self, n_layers=DEPTH, dbg=(), wdepth=DEPTH):
        self.n_layers = n_layers
        self.wd = wdepth
        self.dbg = set(dbg)
        self.nc = bass.Bass("TRN2", target_bir_lowering=False)
        self.stack = contextlib.ExitStack()
        self.dbg_out = {}

    def dram_in(self, name, shape, dt=F32):
        return self.nc.dram_tensor(name, list(shape), dt, kind="ExternalInput").ap()

    def dram_out(self, name, shape, dt=F32):
        return self.nc.dram_tensor(name, list(shape), dt, kind="ExternalOutput").ap()

    def mm(self, out, lhsT, rhs, start, stop, rin, rout):
        self.kb.op("pe", lambda e: e.matmul(out, lhsT=lhsT, rhs=rhs, start=start, stop=stop), rin, [rout])

    def cbf(self, name, rows=slice(0, 128), c0=0, c1=None):
        a, b = CB[name]
        if c1 is None:
            c1 = b - a
        return self.t_cb[rows, a + c0:a + c1]

    def cf(self, name, c0=0, c1=None, rows=slice(0, 128)):
        a, b = CF[name]
        if c1 is None:
            c1 = b - a
        return self.t_cf[rows, a + c0:a + c1]

    def sm(self, l, name, c0=0, c1=None, rows=slice(0, 128)):
        a, b = SM[name]
        if c1 is None:
            c1 = b - a
        o = l * SM["_n"]
        return self.t_sm[rows, o + a + c0:o + a + c1]

    def der(self, name, c0=0, c1=None, rows=slice(0, 128), par=None):
        a, b = DER[name]
        if c1 is None:
            c1 = b - a
        return self.t_ders[self.cur if par is None else par][rows, a + c0:a + c1]

    @property
    def t_mod(self):
        return self.t_mods[self.cur]

    @property
    def Rmod(self):
        return self.Rmods[self.cur]

    @property
    def Rder(self):
        return self.Rders[self.cur]

    def dump(self, name, ap, res, shape):
        if name not in self.dbg:
            return
        d = self.dram_out("dbg_" + name, shape, F32 if ap.dtype == F32 else BF16)
        self.dbg_out[name] = d
        self.kb.dma_out("sp", res, lambda e: e.dma_start(out=d, in_=ap), self.osem)

    def build(self):
        nc = self.nc
        st = self.stack
        kb = self.kb = KB(nc, st)
        self.osem = [kb.newsem("osem"), 0]
        self.x_d = self.dram_in("x", [S, D])
        self.cT_d = self.dram_in("cT", [128, 8])
        self.pos_d = self.dram_in("pos", [1, S], I32)
        self.ada_w = self.dram_in("ada_w", [self.wd, D, 6 * D])
        self.w_in = self.dram_in("w_in", [self.wd, D, IN_W])
        self.wg_rep = self.dram_in("wg_rep", [self.wd, D, 768])
        self.wf_pad = self.dram_in("wf_pad", [self.wd, D, 128])
        self.cmp_w1 = self.dram_in("cmp_w1", [self.wd, 2, 2048, 64])
        self.cmp_w2 = self.dram_in("cmp_w2", [self.wd, 2, 64, 64])
        self.cmp_pe = self.dram_in("cmp_pe", [self.wd, 2, 64, 32])
        self.w_uq = self.dram_in("w_uq", [self.wd, 256, 384])
        self.w_ukv = self.dram_in("w_ukv", [self.wd, 128, 512])
        self.br_w = self.dram_in("br_w", [self.wd, 4, 256, D])
        self.gate_w = self.dram_in("gate_w", [self.wd, D, 4 * D])
        self.w_out = self.dram_in("w_out", [self.wd, D, D])
        self.w_up = self.dram_in("w_up", [self.wd, D, 2 * DFF])
        self.w_down = self.dram_in("w_down", [self.wd, DFF, D])
        self.sm_d = self.dram_in("smalls", [128, self.wd * SM["_n"]])
        self.cb_d = self.dram_in("cbf", [128, CB["_n"]], BF16)
        self.cf_d = self.dram_in("cf32", [128, CF["_n"]])
        self.y_d = self.dram_out("y", [S, D])

        self.xT = [kb.sb(f"xT{k}", [128, S], F32) for k in range(8)]
        self.hT = [kb.sb(f"hT{k}", [128, S], BF16) for k in range(8)]
        self.Rx = [[Res(f"x{k}_{q}") for q in range(NQ)] for k in range(8)]
        self.Rh = [[Res(f"h{k}_{q}") for q in range(NQ)] for k in range(8)]
        self.ropeC = kb.sb("ropeC", [128, S], BF16)
        self.ropeS = kb.sb("ropeS", [128, S], BF16)
        self.Rrope = Res("rope")
        self.t_cb = kb.sb("t_cb", [128, CB["_n"]], BF16)
        self.t_cf = kb.sb("t_cf", [128, CF["_n"]], F32)
        self.t_sm = kb.sb("t_sm", [128, self.wd * SM["_n"]], F32)
        self.t_mods = [kb.sb(f"t_mod{i}", [128, 48], F32) for i in range(2)]
        self.t_ders = [kb.sb(f"t_der{i}", [128, DER["_n"]], F32) for i in range(2)]
        self.adab = [kb.sb(f"adab{i}", [128, 512], BF16) for i in range(2)]
        self.Radab = [Res(f"adab{i}") for i in range(2)]
        self.Rmods = [Res("mod0"), Res("mod1")]
        self.Rders = [Res("der0"), Res("der1")]
        self.cur = 0
        self.t_scb = kb.sb("t_scb", [128, 8], BF16)
        self.Rcb, self.Rcf, self.Rsm, self.Rscb = (Res(n) for n in ("cb", "cf", "sm", "scb"))
        self.WB = [kb.sb(f"WB{i}", [128, 8, 512], BF16) for i in range(2)]
        self.RWB = [Res(f"WB{i}") for i in range(2)]
        self.wb_i = 0
        self.ps = [kb.ps(f"ps{i}", [128, 512]) for i in range(8)]
        self.Rps = [Res(f"ps{i}") for i in range(8)]

        kb.dma_in("sp", self.Rcb, lambda e: e.dma_start(out=self.t_cb[:], in_=self.cb_d))
        kb.dma_in("sp", self.Rcf, lambda e: e.dma_start(out=self.t_cf[:], in_=self.cf_d))
        kb.dma_in("sp", self.Rsm, lambda e: e.dma_start(out=self.t_sm[:], in_=self.sm_d))

        self.ada_gen = None
        self.prologue()
        for l in range(self.n_layers):
            self.layer(l)
        self.epilogue()
        kb.e["sp"].wait_ge(self.osem[0], self.osem[1])
        self.stack.close()
        return nc

    def mark(self, name):
        if not hasattr(self, 'marks'):
            self.marks = []
        self.marks.append((name, self.kb.cnt['pe']))

    def next_wb(self):
        i = self.wb_i
        self.wb_i ^= 1
        return self.WB[i], self.RWB[i]

    def load_w(self, dram_ap_pkn, ncols, nk=8):
        wb, r = self.next_wb()
        self.kb.dma_in("pool", r, lambda e: e.dma_start(out=wb[:, 0:nk, 0:ncols], in_=dram_ap_pkn))
        return wb, r

    def win_cols(self, l, c0, n):
        return self.w_in[l].rearrange("(k p) n -> p k n", p=128)[:, :, c0:c0 + n]

    def prologue(self):
        kb = self.kb
        with contextlib.ExitStack() as ps_:
            xs = [kb.sb(f"xstage{i}", [128, 4, D], F32, ps_) for i in range(2)]
            Rxs = [Res(f"xstage{i}") for i in range(2)]
            posi = kb.sb("posi", [128, S], I32, ps_)
            posf = kb.sb("posf", [128, S], F32, ps_)
            tA = kb.sb("tA", [128, S], F32, ps_)
            tB = kb.sb("tB", [128, S], F32, ps_)
            Rposi, Rposf, RtA, RtB = Res("posi"), Res("posf"), Res("tA"), Res("tB")
            ct = kb.sb("ct", [128, 8], F32, ps_)
            Rct = Res("ct")
            kb.dma_in("sp", Rct, lambda e: e.dma_start(out=ct[:], in_=self.cT_d))
            kb.op("act", lambda e: e.activation(out=self.t_scb[:], in_=ct[:], func=AF.Silu), [Rct], [self.Rscb])
            self.ada_gen = self.g_ada(0)
            xv = self.x_d.rearrange("(g t p) d -> g p t d", t=4, p=128)
            for g in range(2):
                kb.dma_in("sp", Rxs[g], lambda e: e.dma_start(out=xs[g][:], in_=xv[g]))
            for g in range(4):
                xg, Rxg = xs[g % 2], Rxs[g % 2]
                for k in range(8):
                    self.ada_step(3)
                    pb = self.ps[k % 4]
                    for t in range(4):
                        kb.op("pe", lambda e: e.transpose(out=pb[:, t * 128:(t + 1) * 128], in_=xg[:, t, k * 128:(k + 1) * 128],
                                                          identity=self.cf("ident")), [Rxg, self.Rcf], [self.Rps[k % 4]])
                    eng = "act" if k % 2 == 0 else "dve"
                    if eng == "act":
                        kb.op("act", lambda e: e.activation(out=self.xT[k][:, g * 512:(g + 1) * 512], in_=pb[:], func=AF.Copy),
                              [self.Rps[k % 4]], [self.Rx[k][g]])
                    else:
                        kb.op("dve", lambda e: e.tensor_copy(out=self.xT[k][:, g * 512:(g + 1) * 512], in_=pb[:]),
                              [self.Rps[k % 4]], [self.Rx[k][g]])
                if g + 2 < 4:
                    kb.dma_in("sp", Rxg, lambda e: e.dma_start(out=xg[:], in_=xv[g + 2]))
            kb.dma_in("sp", Rposi, lambda e: e.dma_start(out=posi[:], in_=self.pos_d.partition_broadcast(128)))
            kb.op("dve", lambda e: e.tensor_copy(out=posf[:], in_=posi[:]), [Rposi], [Rposf])
            twopi = 2.0 * math.pi
            invf = self.cf("invf")
            sgn = self.cf("sgn")
            ti = posi
            for which in range(2):
                kb.op("dve", lambda e: e.tensor_scalar(out=tA[:], in0=posf[:], scalar1=invf, scalar2=1.0 / twopi, op0=ALU.mult, op1=ALU.mult),
                      [Rposf, self.Rcf, RtB], [RtA])
                if which == 1:
                    kb.op("dve", lambda e: e.tensor_scalar(out=tA[:], in0=tA[:], scalar1=0.25, scalar2=None, op0=ALU.add), [RtA], [RtA])
                kb.op("dve", lambda e: e.tensor_copy(out=ti[:], in_=tA[:]), [RtA, Rposf], [Rposi])
                kb.op("dve", lambda e: e.tensor_copy(out=tB[:], in_=ti[:]), [Rposi], [RtB])
                kb.op("dve", lambda e: e.tensor_tensor(out=tA[:], in0=tA[:], in1=tB[:], op=ALU.subtract), [RtA, RtB], [RtA])
                kb.op("act", lambda e: e.activation(out=tB[:], in_=tA[:], func=AF.Sin, scale=twopi), [RtA], [RtB])
                if which == 0:
                    kb.op("dve", lambda e: e.tensor_scalar(out=self.ropeS[:], in0=tB[:], scalar1=sgn, scalar2=None, op0=ALU.mult),
                          [RtB, self.Rcf], [self.Rrope])
                else:
                    kb.op("dve", lambda e: e.tensor_copy(out=self.ropeC[:], in_=tB[:]), [RtB], [self.Rrope])
            self.dump("ropeC", self.ropeC[:], [self.Rrope], [128, S])
            self.dump("ropeS", self.ropeS[:], [self.Rrope], [128, S])
            self.dump("xT0", self.xT[0][:], self.Rx[0], [128, S])
            kb.barrier()
            if self.dbg:
                kb.e["act"].wait_ge(self.osem[0], self.osem[1])
                kb.barrier()

    def epilogue(self):
        kb = self.kb
        with contextlib.ExitStack() as ps_:
            ys = [kb.sb(f"ystage{i}", [128, 2, D], F32, ps_) for i in range(2)]
            Rys = [Res(f"ystage{i}") for i in range(2)]
            yv = self.y_d.rearrange("(g t p) d -> g p t d", t=2, p=128)
            for g in range(8):
                q = g // 2
                for t in range(2):
                    tt = g * 2 + t
                    for k in range(8):
                        pb = self.ps[(k // 4) + 2 * (tt % 2)]
                        kb.op("pe", lambda e: e.transpose(out=pb[:, (k % 4) * 128:(k % 4 + 1) * 128], in_=self.xT[k][:, tt * 128:(tt + 1) * 128],
                                                          identity=self.cf("ident")), [self.Rx[k][q], self.Rcf], [self.Rps[(k // 4) + 2 * (tt % 2)]])
                    for hh in range(2):
                        bi = hh + 2 * (tt % 2)
                        if hh == 0:
                            kb.op("act", lambda e: e.activation(out=ys[g % 2][:, t, hh * 512:(hh + 1) * 512], in_=self.ps[bi][:], func=AF.Copy),
                                  [self.Rps[bi]], [Rys[g % 2]])
                        else:
                            kb.op("dve", lambda e: e.tensor_copy(out=ys[g % 2][:, t, hh * 512:(hh + 1) * 512], in_=self.ps[bi][:]),
                                  [self.Rps[bi]], [Rys[g % 2]])
                kb.dma_out("sp", [Rys[g % 2]], lambda e: e.dma_start(out=yv[g], in_=ys[g % 2][:]), self.osem)

    def layer(self, l):
        self.cur = l % 2
        while self.ada_gen is not None:
            self.ada_step()
        self.norm_mod(l, 0)
        kb = self.kb
        with contextlib.ExitStack() as ls:
            self.oT = [None] * 4
            self.RoT = [None] * 4
            for m, fn in self.mixer_order():
                self.oT[m] = [kb.sb(f"oT{m}_{c}", [128, S], BF16, ls) for c in range(2)]
                self.RoT[m] = [[Res(f"oT{m}_{c}_{q}") for q in range(NQ)] for c in range(2)]
                with contextlib.ExitStack() as ms:
                    fn(l, ms)
                    kb.barrier()
                if l == 0:
                    for c in range(2):
                        self.dump(f"o{m}_{c}", self.oT[m][c][:], self.RoT[m][c], [128, S])
            if self.dbg:
                kb.e["act"].wait_ge(self.osem[0], self.osem[1])
                kb.barrier()
            if getattr(self, "mixers", None) is None:
                self.merge(l)
                self.dump(f"x1_l{l}", self.xT[0][:], self.Rx[0], [128, S])
        if getattr(self, "mixers", None) is None:
            self.norm_mod(l, 1)
            self.ffn(l)
            self.dump(f"x2_l{l}", self.xT[0][:], self.Rx[0], [128, S])
            if self.dbg:
                kb.e["act"].wait_ge(self.osem[0], self.osem[1])
                kb.barrier()

    def gate_tile(self, wg, rwg, blk, q, gt, Rgt):
        kb = self.kb
        pb, Rpb = self.ps[3], self.Rps[3]
        self.proj_T(pb, Rpb, slice(0, 128), lambda k: wg[:, k, blk * 128:(blk + 1) * 128], rwg, q)
        kb.op("act", lambda e: e.activation(out=gt[:], in_=pb[:], func=AF.Exp, scale=-1.0), [Rpb], [Rgt])
        kb.op("dve", lambda e: e.tensor_scalar(out=gt[:], in0=gt[:], scalar1=1.0, scalar2=None, op0=ALU.add), [Rgt], [Rgt])
        kb.op("dve", lambda e: e.reciprocal(out=gt[:], in_=gt[:]), [Rgt], [Rgt])

    def nsa(self, l, ms):
        kb = self.kb
        self.mark('nsa_prelude')
        KS2 = kb.sb("KS2", [128, S], BF16, ms)
        KW2 = kb.sb("KW2", [128, S], BF16, ms)
        RKS = [Res(f"KS2_{q}") for q in range(NQ)]
        RKW = [Res(f"KW2_{q}") for q in range(NQ)]
        VSW = kb.sb("VSW", [128, NKT, 384], BF16, ms)
        RVSW = Res("VSW")
        KCMP = kb.sb("KCMP", [128, 128], BF16, ms)
        VCMP = kb.sb("VCMP", [128, 192], BF16, ms)
        RKCMP, RVCMP = Res("KCMP"), Res("VCMP")
        IMP = kb.sb("IMP", [128, 512], F32, ms)
        RIMP = Res("IMP")
        SELM = kb.sb("SELM", [32, S], BF16, ms)
        RSELM = [Res(f"SELM{q}") for q in range(NQ)]
        self.mixer_scratch(ms, nq=2, nk=0, va=False)
        kb.op("pool", lambda e: e.memset(VSW[:, :, 64:128], 1.0), [], [RVSW])
        kb.op("pool", lambda e: e.memset(VSW[:, :, 256:320], 1.0), [], [RVSW])
        kb.op("pool", lambda e: e.memset(VCMP[:], 0.0), [], [RVCMP])
        kb.op("pool", lambda e: e.memset(VCMP[:, 64:128], 1.0), [RVCMP], [RVCMP])
        kb.op("pool", lambda e: e.memset(KCMP[:], 0.0), [], [RKCMP])
        kb.op("pool", lambda e: e.memset(IMP[:], 0.0), [], [RIMP])
        pstat, Rpstat = self.ps[5], self.Rps[5]
        wA, rwA = self.load_w(self.win_cols(l, C_NSA_KC, 384), 384)
        wq, rwq = self.load_w(self.win_cols(l, C_NSA_Q, 256), 256)
        with contextlib.ExitStack() as pscope:
            KVC = self.QTt[1]
            RKVC = [Res(f"KVC{q}") for q in range(NQ)]
            W1 = kb.sb("W1", [128, 32, 64], BF16, pscope)
            W2 = kb.sb("W2", [128, 64], BF16, pscope)
            PEt = kb.sb("PEt", [128, 32], BF16, pscope)
            HID = kb.sb("HID", [128, 128], BF16, pscope)
            RW1, RW2, RPE, RHID = Res("W1"), Res("W2"), Res("PEt"), Res("HID")
            kb.dma_in("pool", RW1, lambda e: [e.dma_start(out=W1[64 * i:64 * i + 64, :, :], in_=self.cmp_w1[l, i].rearrange("(j d) o -> d j o", d=64)) for i in range(2)])
            kb.dma_in("pool", RW2, lambda e: [e.dma_start(out=W2[64 * i:64 * i + 64, :], in_=self.cmp_w2[l, i]) for i in range(2)])
            kb.dma_in("pool", RPE, lambda e: [e.dma_start(out=PEt[64 * i:64 * i + 64, :], in_=self.cmp_pe[l, i]) for i in range(2)])
            for q in range(NQ):
                cs = slice(q * QT, (q + 1) * QT)
                def ch_kx(c0, dstt, Rd, gname, bank, sbank):
                    for half in range(2):
                        self.proj_T(self.ps[bank], self.Rps[bank], slice(64 * half, 64 * half + 64), lambda k: wA[:, k, c0:c0 + 64], rwA, q)
                    yield
                    yield from self.g_headnorm(bank, 128, self.cbf("b64"), self.cf("epsc", 1, 2), self.der(gname), dstt[:, cs], Rd[q], cs,
                                               rope=(0, "sw_nsa", [0, 64]), stat_bank=sbank)

                for _ in self.rr(ch_kx(128, KS2, RKS, "nsa_gks", 3, 5), ch_kx(256, KW2, RKW, "nsa_gkw", 4, 2)):
                    pass
                self.proj_T(self.ps[4], self.Rps[4], slice(0, 128), lambda k: wA[:, k, 0:128], rwA, q)
                kb.op("act", lambda e: e.activation(out=KVC[:, cs], in_=self.ps[4][:], func=AF.Copy), [self.Rps[4]], [RKVC[q]])
            for g in range(4):
                pb, Rpb = self.ps[3 + g % 2], self.Rps[3 + g % 2]
                for t in range(4):
                    kt = g * 4 + t
                    for b in range(2):
                        for k in range(8):
                            self.mm(pb[:, (t * 2 + b) * 64:(t * 2 + b + 1) * 64], self.hT[k][:, kt * 128:(kt + 1) * 128], wA[:, k, 192 + 128 * b:256 + 128 * b],
                                    k == 0, k == 7, [rwA, self.Rh[k][g]], Rpb)
                src = pb[:].rearrange("p (t b c) -> p t b c", t=4, b=2)
                for sidx in (0, 2):
                    dstv = VSW[:, g * 4:(g + 1) * 4, :].rearrange("p t (b s c) -> p t b s c", b=2, s=3)[:, :, :, sidx, :]
                    kb.op("dve" if sidx == 0 else "act", (lambda e: e.tensor_copy(out=dstv, in_=src)) if sidx == 0 else
                          (lambda e: e.activation(out=dstv, in_=src, func=AF.Copy)), [Rpb], [RVSW])
            pH, RpH = self.ps[6], self.Rps[6]
            for i in range(2):
                rr = slice(64 * i, 64 * i + 64)
                n = 0
                for j in range(32):
                    self.mm(pH[rr, 0:127], W1[rr, j, :], KVC[rr, j:j + 16 * 126 + 1:16], n == 0, False, [RW1] + RKVC, RpH)
                    n += 1
                    self.mm(pH[rr, 0:127], W1[rr, j, :], PEt[rr, j:j + 1].to_broadcast([64, 127]), False, j == 31, [RW1, RPE], RpH)
            kb.op("act", lambda e: e.activation(out=HID[:, 0:127], in_=pH[:, 0:127], func=AF.Silu), [RpH], [RHID])
            for half in range(2):
                self.mm(self.ps[3][64 * half:64 * half + 64, 0:127], W2[0:64, :], HID[0:64, 0:127], True, True, [RW2, RHID], self.Rps[3])
            kb.op("act", lambda e: e.activation(out=self.sqb[0][:, 0:127], in_=self.ps[3][:, 0:127], func=AF.Square), [self.Rps[3]], [self.Rsqb[0]])
            self.mm(pstat[:, 0:127], self.cbf("b64"), self.sqb[0][:, 0:127], True, True, [self.Rsqb[0], self.Rcb], Rpstat)
            self.rstd_from(pstat[:, 0:127], self.rstd[0][:, 0:127], self.cf("epsc", 1, 2), Rpstat, self.Rrstd[0])
            kb.op("dve", lambda e: e.scalar_tensor_tensor(out=KCMP[:, 0:127], in0=self.ps[3][:, 0:127], scalar=self.der("nsa_gkc"), in1=self.rstd[0][:, 0:127],
                                                          op0=ALU.mult, op1=ALU.mult), [self.Rps[3], self.Rrstd[0], self.Rder, RKCMP], [RKCMP])
            self.mm(self.ps[4][0:127, 0:64], HID[64:128, 0:127], W2[64:128, :], True, True, [RW2, RHID], self.Rps[4])
            kb.op("dve", lambda e: e.tensor_copy(out=VCMP[0:127, 0:64], in_=self.ps[4][0:127, 0:64]), [self.Rps[4], RVCMP], [RVCMP])
            kb.op("act", lambda e: e.activation(out=VCMP[0:127, 128:192], in_=self.ps[4][0:127, 0:64], func=AF.Copy), [self.Rps[4], RVCMP], [RVCMP])
            self.dump("nsaKS", KS2[:], RKS, [128, S])
            self.dump("nsaKCMP", KCMP[:], [RKCMP], [128, 128])
            self.dump("nsaVCMP", VCMP[:], [RVCMP], [128, 192])
            kb.barrier()
            if self.dbg:
                kb.e["act"].wait_ge(self.osem[0], self.osem[1])
                kb.barrier()
        GT = [kb.sb(f"GT{i}", [128, QT], F32, ms) for i in range(2)]
        RGT = [Res(f"GT{i}") for i in range(2)]
        self.mark('nsa_q')
        wg0, rwg0 = self.load_w(self.wg_rep[l].rearrange("(k p) n -> p k n", p=128)[:, :, 0:256], 256)
        for q in range(NQ):
            cs = slice(q * QT, (q + 1) * QT)

            def ch_qu(u, bank, sbank):
                self.proj_T(self.ps[bank], self.Rps[bank], slice(0, 128), lambda k: wq[:, k, u * 128:(u + 1) * 128], rwq, q)
                yield
                yield from self.g_headnorm(bank, 128, self.cbf("b64"), self.cf("epsc", 1, 2), self.der("nsa_gq"), self.QTt[u][:, cs], self.RQT[u][q], cs,
                                           rope=(0, "sw_nsa", [0, 64]), stat_bank=sbank)

            for _ in self.rr(ch_qu(0, 3, 5), ch_qu(1, 4, 2)):
                pass
        self.mark('nsa_cmp')
        wg1, rwg1 = self.load_w(self.wg_rep[l].rearrange("(k p) n -> p k n", p=128)[:, :, 256:768], 512)
        pI, RpI = self.ps[5], self.Rps[5]
        gi = 0
        for u in range(2):
            for q in range(NQ):
                cs = slice(q * QT, (q + 1) * QT)
                gt, Rgt = GT[gi], RGT[gi]
                gi ^= 1
                self.pend_flush()
                self.gate_tile(wg0, rwg0, u, q, gt, Rgt)
                for hh in range(2):
                    rb = slice(64 * hh, 64 * hh + 64)
                    ob = 6 + hh
                    rins = [self.RQT[u][q], RKCMP, RVCMP]

                    def imp_mm(pi, hh=hh):
                        for t in range(4):
                            self.mm(pI[:, (hh * 4 + t) * 64:(hh * 4 + t + 1) * 64], self.PT[pi][:, t * 128:(t + 1) * 128], self.cbf("ovl"), True, True,
                                    [self.RPT[pi], self.Rcb], RpI)

                    def fin_cmp(ob=ob, hh=hh, u=u, cs=cs, q=q, gt=gt, Rgt=Rgt):
                        i, nr = self.finalize(ob, hh, None, None, eps=1e-30)
                        kb.op("pool", lambda e: e.tensor_tensor(out=self.rec[i][nr, :], in0=self.rec[i][nr, :], in1=gt[nr, :], op=ALU.mult), [self.Rrec[i], Rgt], [self.Rrec[i]])
                        kb.op("dve", lambda e: e.tensor_tensor(out=self.oT[0][u][nr, cs], in0=self.ps[ob][nr, :], in1=self.rec[i][nr, :], op=ALU.mult),
                              [self.Rps[ob], self.Rrec[i]], [self.RoT[0][u][q]])

                    self.attn_map([(0, 0, QT, ("vis", 0, q * QT - 31))],
                                  lambda c0, c1, rb=rb, u=u, q=q: self.QTt[u][rb, q * QT + c0:q * QT + c1],
                                  lambda kt, rb=rb: KCMP[rb, :],
                                  lambda kt, hh=hh: VCMP[:, 64 * hh:64 * hh + 128],
                                  0.125, rins, ob, fin=fin_cmp, after_p=imp_mm)
                for hh in range(2):
                    for t in range(4):
                        tt = q * 4 + t
                        base = (hh * 4 + t) * 64
                        rcol = self.rstd[0][:, 0:1]
                        kb.op("dve", lambda e: e.tensor_scalar(out=rcol, in0=pI[:, base + 32:base + 33], scalar1=1e-30, scalar2=None, op0=ALU.add), [RpI], [self.Rrstd[0]])
                        kb.op("dve", lambda e: e.reciprocal(out=rcol, in_=rcol), [self.Rrstd[0]], [self.Rrstd[0]])
                        kb.op("dve", lambda e: e.scalar_tensor_tensor(out=IMP[:, tt * 32:(tt + 1) * 32], in0=pI[:, base:base + 32], scalar=rcol,
                                                                      in1=IMP[:, tt * 32:(tt + 1) * 32], op0=ALU.mult, op1=ALU.add),
                              [RpI, self.Rrstd[0], RIMP], [RIMP])
        self.pend_flush()
        self.dump("nsaIMP", IMP[:], [RIMP], [128, 512])
        self.mark('nsa_topk')
        SC = self.rec[0]
        RSC = self.Rrec[0]
        kb.op("dve", lambda e: e.tensor_tensor(out=SC[:], in0=IMP[:], in1=self.cbf("tabA"), op=ALU.mult), [RIMP, self.Rcb], [RSC])
        kb.op("dve", lambda e: e.tensor_tensor(out=SC[:], in0=SC[:], in1=self.cbf("tabB"), op=ALU.add), [RSC, self.Rcb], [RSC])
        m8 = self.rstd[1]
        Rm8 = self.Rrstd[1]
        sc2 = self.rec[1]
        Rsc2 = self.Rrec[1]
        selm = self.sqb[0]
        Rselm = self.Rsqb[0]
        pT, RpT = self.ps[5], self.Rps[5]
        for tt in range(16):
            sl = slice(tt * 32, (tt + 1) * 32)
            kb.op("dve", lambda e: e.max(out=m8[:, 0:8], in_=SC[:, sl]), [RSC], [Rm8])
            kb.op("dve", lambda e: e.match_replace(out=sc2[:, 0:32], in_to_replace=m8[:, 0:8], in_values=SC[:, sl], imm_value=-2.0), [RSC, Rm8], [Rsc2])
            kb.op("dve", lambda e: e.max(out=m8[:, 8:16], in_=sc2[:, 0:32]), [Rsc2], [Rm8])
            kb.op("dve", lambda e: e.tensor_scalar(out=selm[:, sl], in0=SC[:, sl], scalar1=m8[:, 15:16], scalar2=-1.0, op0=ALU.is_ge, op1=ALU.add),
                  [RSC, Rm8], [Rselm])
            self.mm(pT[0:32, (tt % 4) * 128:(tt % 4 + 1) * 128], selm[:, sl], self.cbf("ident"), True, True, [Rselm, self.Rcb], RpT)
            if tt % 4 == 3:
                qq = tt // 4
                kb.op("act", lambda e: e.activation(out=SELM[0:32, qq * QT:(qq + 1) * QT], in_=pT[0:32, :], func=AF.Copy), [RpT], [RSELM[qq]])
        self.dump("nsaSELM", SELM[:], RSELM, [32, S])
        self.mark('nsa_slcwin')
        for u in range(2):
            for q in range(NQ):
                cs = slice(q * QT, (q + 1) * QT)
                self.gate_tile(wg1, rwg1, u, q, GT[0], RGT[0])
                self.gate_tile(wg1, rwg1, 2 + u, q, GT[1], RGT[1])
                for hh in range(2):
                    rb = slice(64 * hh, 64 * hh + 64)
                    rins = [self.RQT[u][q], RVSW, RSELM[q], self.Rcb] + RKS
                    shared = {}

                    def fin_s(hh=hh, shared=shared):
                        i_s, nr = self.finalize(6, hh, None, None)
                        kb.op("pool", lambda e: e.tensor_tensor(out=self.rec[i_s][nr, :], in0=self.rec[i_s][nr, :], in1=GT[0][nr, :], op=ALU.mult),
                              [self.Rrec[i_s], RGT[0]], [self.Rrec[i_s]])
                        kb.op("dve", lambda e: e.tensor_tensor(out=self.rec[i_s][nr, :], in0=self.ps[6][nr, :], in1=self.rec[i_s][nr, :], op=ALU.mult),
                              [self.Rps[6], self.Rrec[i_s]], [self.Rrec[i_s]])
                        shared["i_s"] = i_s

                    def fin_w(hh=hh, shared=shared, u=u, cs=cs, q=q):
                        i_s = shared["i_s"]
                        i_w, nr = self.finalize(7, hh, None, None)
                        kb.op("pool", lambda e: e.tensor_tensor(out=self.rec[i_w][nr, :], in0=self.rec[i_w][nr, :], in1=GT[1][nr, :], op=ALU.mult),
                              [self.Rrec[i_w], RGT[1]], [self.Rrec[i_w]])
                        kb.op("dve", lambda e: e.tensor_tensor(out=self.rec[i_w][nr, :], in0=self.ps[7][nr, :], in1=self.rec[i_w][nr, :], op=ALU.mult),
                              [self.Rps[7], self.Rrec[i_w]], [self.Rrec[i_w]])
                        kb.op("pool", lambda e: e.tensor_tensor(out=self.rec[i_w][nr, :], in0=self.rec[i_w][nr, :], in1=self.rec[i_s][nr, :], op=ALU.add),
                              [self.Rrec[i_w], self.Rrec[i_s]], [self.Rrec[i_w]])
                        kb.op("pool", lambda e: e.tensor_tensor(out=self.oT[0][u][nr, cs], in0=self.oT[0][u][nr, cs], in1=self.rec[i_w][nr, :], op=ALU.add),
                              [self.Rrec[i_w], self.RoT[0][u][q]], [self.RoT[0][u][q]])

                    self.attn_map(self.causal_tiles(q),
                                  lambda c0, c1, rb=rb, u=u, q=q: self.QTt[u][rb, q * QT + c0:q * QT + c1],
                                  lambda kt, rb=rb: KS2[rb, kt * 128:(kt + 1) * 128],
                                  lambda kt, hh=hh: VSW[:, kt, 64 * hh:64 * hh + 128],
                                  0.125, rins, 6,
                                  extra=lambda kt, c0, c1, q=q: (self.cbf("esel", slice(0, 32), kt * 128, (kt + 1) * 128), SELM[0:32, q * QT + c0:q * QT + c1]),
                                  fin=fin_s)
                    tiles = [(4 * q + j, 128 * j, QT, ("causal", 0, 0)) for j in range(4)]
                    if q > 0:
                        tiles += [(4 * q - 4 + j, 0, 128 * (j + 1), ("lower", 128 * j, 0)) for j in range(4)]
                    rins = [self.RQT[u][q], RVSW] + RKW
                    self.attn_map(tiles,
                                  lambda c0, c1, rb=rb, u=u, q=q: self.QTt[u][rb, q * QT + c0:q * QT + c1],
                                  lambda kt, rb=rb: KW2[rb, kt * 128:(kt + 1) * 128],
                                  lambda kt, hh=hh: VSW[:, kt, 192 + 64 * hh:192 + 64 * hh + 128],
                                  0.125, rins, 7, fin=fin_w)
                self.pend_flush()

    def mixer_order(self):
        sel = getattr(self, 'mixers', None) or [0, 2, 1, 3]
        fns = {0: self.nsa, 1: self.fox, 2: self.mla, 3: self.diff}
        return [(m, fns[m]) for m in sel]

    def g_ada(self, l):
        kb = self.kb
        par = l % 2
        t_mod, Rmod, Rder = self.t_mods[par], self.Rmods[par], self.Rders[par]
        pb, Rpb = self.ps[7], self.Rps[7]
        av = self.ada_w[l].rearrange("(k p) n -> p k n", p=128)
        avk = self.ada_w[l].rearrange("(k p) n -> k p n", p=128)
        n = 0
        for k in range(8):
            for cb_ in range(12):
                wb, rw = self.adab[n % 2], self.Radab[n % 2]
                kb.dma_in("pool", rw, lambda e: e.dma_start(out=wb[:], in_=avk[k][:, cb_ * 512:(cb_ + 1) * 512]))
                for jj in range(4):
                    j = cb_ * 4 + jj
                    kb.op("pe", lambda e: e.matmul(pb[:, j:j + 1], lhsT=wb[:, jj * 128:(jj + 1) * 128], rhs=self.t_scb[:, k:k + 1],
                                                   start=(k == 0 and j == 0), stop=(k == 7), skip_group_check=True), [rw, self.Rscb], [Rpb])
                n += 1
                yield
        kb.op("dve", lambda e: e.tensor_tensor(out=t_mod[:], in0=pb[:, 0:48], in1=self.sm(l, "ada_b"), op=ALU.add),
              [Rpb, self.Rsm], [Rmod])
        d = lambda *a, **kw: self.der(*a, par=par, **kw)

        def ts(dst, src, s1, s2=None, op0=ALU.mult, op1=None):
            if s2 is None:
                kb.op("dve", lambda e: e.tensor_scalar(out=dst, in0=src, scalar1=s1, scalar2=None, op0=op0),
                      [Rmod, self.Rsm, Rder, self.Rcf], [Rder])
            else:
                kb.op("dve", lambda e: e.tensor_scalar(out=dst, in0=src, scalar1=s1, scalar2=s2, op0=op0, op1=op1),
                      [Rmod, self.Rsm, Rder, self.Rcf], [Rder])

        ts(d("a1"), t_mod[:, 8:16], 1.0, 32.0, ALU.add, ALU.mult)
        ts(d("a2"), t_mod[:, 32:40], 1.0, 32.0, ALU.add, ALU.mult)
        for nm, sc in (("nsa_gq", 8.0), ("nsa_gkc", 8.0), ("nsa_gks", 8.0), ("nsa_gkw", 8.0), ("fox_gq", 8.0), ("fox_gk", 8.0),
                       ("mla_cqg", 16.0), ("mla_ckvg", math.sqrt(128.0)), ("mla_gq", math.sqrt(96.0)), ("mla_gk", math.sqrt(96.0)),
                       ("dif_gq", math.sqrt(32.0)), ("dif_gk", math.sqrt(32.0)), ("dif_og", self.cf("lamc", 2 * l, 2 * l + 1))):
            ts(d(nm), self.sm(l, nm), sc)
        ts(d("negfb"), self.sm(l, "fox_fb"), -1.0)
        yield
        if not hasattr(self, "t_lt"):
            self.t_lt = kb.sb("t_lt", [128, 64], F32)
            self.Rlt = Res("lt")
        lt = self.t_lt
        kb.op("dve", lambda e: e.tensor_tensor(out=lt[:, 0:32], in0=self.sm(l, "dif_lam", 0, 32), in1=self.sm(l, "dif_lam", 32, 64), op=ALU.mult),
              [self.Rsm], [self.Rlt])
        kb.op("dve", lambda e: e.tensor_tensor(out=lt[:, 32:64], in0=self.sm(l, "dif_lam", 64, 96), in1=self.sm(l, "dif_lam", 96, 128), op=ALU.mult),
              [self.Rsm], [self.Rlt])
        kb.op("dve", lambda e: e.tensor_reduce(out=d("t0", 0, 2), in_=lt[:].rearrange("p (a b) -> p a b", a=2), axis=mybir.AxisListType.X, op=ALU.add),
              [self.Rlt], [Rder])
        kb.op("act", lambda e: e.activation(out=d("t1", 0, 2), in_=d("t0", 0, 2), func=AF.Exp), [Rder], [Rder])
        kb.op("dve", lambda e: e.tensor_tensor(out=d("t0", 2, 3), in0=d("t1", 1, 2), in1=d("t1", 0, 1), op=ALU.subtract), [Rder], [Rder])
        kb.op("dve", lambda e: e.tensor_scalar(out=d("neglam"), in0=d("t0", 2, 3), scalar1=self.cf("lamc", 2 * l + 1, 2 * l + 2), scalar2=None, op0=ALU.add),
              [Rder, self.Rcf], [Rder])
        yield

    def ada_step(self, n=1):
        for _ in range(n):
            if self.ada_gen is not None:
                try:
                    next(self.ada_gen)
                except StopIteration:
                    self.ada_gen = None

    def norm_mod(self, l, which):
        self.mark('norm')
        kb = self.kb
        acol = "a1" if which == 0 else "a2"
        shc = 0 if which == 0 else 24
        with contextlib.ExitStack() as ns:
            sq = [kb.sb(f"nsq{i}", [128, QT], BF16, ns) for i in range(2)]
            Rsq = [Res(f"nsq{i}") for i in range(2)]
            rstd = kb.sb("nrstd", [128, QT], F32, ns)
            Rrstd = Res("nrstd")
            tmp = [kb.sb(f"ntmp{i}", [128, QT], F32, ns) for i in range(2)]
            Rtmp = [Res(f"ntmp{i}") for i in range(2)]
            for q in range(NQ):
                cs = slice(q * QT, (q + 1) * QT)
                pb, Rpb = self.ps[q % 2], self.Rps[q % 2]
                for k in range(8):
                    eng = "pool" if k % 2 == 0 else "dve"
                    kb.op(eng, lambda e: e.tensor_tensor(out=sq[k % 2][:], in0=self.xT[k][:, cs], in1=self.xT[k][:, cs], op=ALU.mult),
                          [self.Rx[k][q]], [Rsq[k % 2]])
                    self.mm(pb[:], self.cbf("ones"), sq[k % 2][:], k == 0, k == 7, [Rsq[k % 2], self.Rcb], Rpb)
                kb.op("act", lambda e: e.activation(out=rstd[:], in_=pb[:], func=AF.Ln, bias=self.cf("epsc", 0, 1), scale=1.0), [Rpb, self.Rcf], [Rrstd])
                kb.op("act", lambda e: e.activation(out=rstd[:], in_=rstd[:], func=AF.Exp, scale=-0.5), [Rrstd], [Rrstd])
                for k in range(8):
                    kb.op("dve", lambda e: e.tensor_tensor(out=tmp[k % 2][:], in0=self.xT[k][:, cs], in1=rstd[:], op=ALU.mult),
                          [self.Rx[k][q], Rrstd], [Rtmp[k % 2]])
                    kb.op("act", lambda e: e.activation(out=self.hT[k][:, cs], in_=tmp[k % 2][:], func=AF.Identity,
                                                        scale=self.der(acol, k, k + 1), bias=self.t_mod[:, shc + k:shc + k + 1]),
                          [Rtmp[k % 2], self.Rder, self.Rmod], [self.Rh[k][q]])
            kb.barrier()
        self.dump(f"h{which}_0", self.hT[0][:], self.Rh[0], [128, S])
        self.dump(f"h{which}_7", self.hT[7][:], self.Rh[7], [128, S])

    def mixer_scratch(self, ms, nq=1, nk=1, va=True):
        kb = self.kb
        self.sqb = [kb.sb(f"sqb{i}", [128, QT], BF16, ms) for i in range(2)]
        self.Rsqb = [Res(f"sqb{i}") for i in range(2)]
        self.rstd = [kb.sb(f"rstd{i}", [128, QT], F32, ms) for i in range(2)]
        self.Rrstd = [Res(f"rstd{i}") for i in range(2)]
        self.rt1 = [kb.sb(f"rt1_{i}", [128, QT], BF16, ms) for i in range(2)]
        self.Rrt1 = [Res(f"rt1_{i}") for i in range(2)]
        self.rt2 = [kb.sb(f"rt2_{i}", [128, QT], BF16, ms) for i in range(2)]
        self.Rrt2 = [Res(f"rt2_{i}") for i in range(2)]
        self.PT = [kb.sb(f"PT{i}", [128, QT], BF16, ms) for i in range(6)]
        self.RPT = [Res(f"PT{i}") for i in range(6)]
        self.rec = [kb.sb(f"rec{i}", [128, QT], F32, ms) for i in range(2)]
        self.Rrec = [Res(f"rec{i}") for i in range(2)]
        self.QTt = [kb.sb(f"QTt{i}", [128, S], BF16, ms) for i in range(nq)]
        self.RQT = [[Res(f"QT{i}_{q}") for q in range(NQ)] for i in range(nq)]
        self.KTt = [kb.sb(f"KTt{i}", [128, S], BF16, ms) for i in range(nk)]
        self.RKT = [[Res(f"KT{i}_{q}") for q in range(NQ)] for i in range(nk)]
        self.pend = []
        self.bg = None
        self.hn_i = 0
        self.pt_i = 0
        self.sb_i = 0
        self.rec_i = 0
        if va:
            self.VA = kb.sb("VA", [128, NKT, 256], BF16, ms)
            self.RVA = [Res(f"VA{g}") for g in range(4)]
            kb.op("pool", lambda e: e.memset(self.VA[:, :, 64:192], 1.0), [], self.RVA)

    def headnorm(self, *a, **kw):
        for _ in self.g_headnorm(*a, **kw):
            pass

    def g_headnorm(self, src_bank, nrows, blk, neps, gcol, dst, Rdst, cs, rope=None, stat_bank=5):
        kb = self.kb
        i = self.hn_i
        self.hn_i ^= 1
        rs = slice(0, nrows)
        src, Rsrc = self.ps[src_bank][rs, :], self.Rps[src_bank]
        pstat, Rpstat = self.ps[stat_bank], self.Rps[stat_bank]
        kb.op("act", lambda e: e.activation(out=self.sqb[i][rs, :], in_=src, func=AF.Square), [Rsrc], [self.Rsqb[i]])
        self.mm(pstat[rs, :], blk, self.sqb[i][rs, :], True, True, [self.Rsqb[i], self.Rcb], Rpstat)
        yield
        kb.op("act", lambda e: e.activation(out=self.rstd[i][rs, :], in_=pstat[rs, :], func=AF.Ln, bias=neps, scale=1.0), [Rpstat, self.Rcf], [self.Rrstd[i]])
        kb.op("act", lambda e: e.activation(out=self.rstd[i][rs, :], in_=self.rstd[i][rs, :], func=AF.Exp, scale=-0.5), [self.Rrstd[i]], [self.Rrstd[i]])
        yield
        kb.op("dve", lambda e: e.scalar_tensor_tensor(out=dst, in0=src, scalar=gcol, in1=self.rstd[i][rs, :], op0=ALU.mult, op1=ALU.mult),
              [Rsrc, self.Rrstd[i], self.Rder], [Rdst])
        if rope is None:
            yield
            return
        ci, swname, wins = rope
        pA, RpA = pstat, Rpstat
        pB, RpB = self.ps[src_bank], self.Rps[src_bank]
        self.mm(pA[rs, :], self.cbf("ident", rs, 0, nrows), dst, True, True, [Rdst, self.Rcb], RpA)
        self.mm(pB[rs, :], self.cbf(swname, rs, 0, nrows), dst, True, True, [Rdst, self.Rcb], RpB)
        yield
        tab = slice(32 * ci, 32 * ci + 32)
        for w0 in wins:
            ws = slice(w0, w0 + 32)
            kb.op("dve", lambda e: e.tensor_tensor(out=self.rt1[i][ws, :], in0=pA[ws, :], in1=self.ropeC[tab, cs], op=ALU.mult),
                  [RpA, self.Rrope], [self.Rrt1[i]])
            kb.op("dve", lambda e: e.tensor_tensor(out=self.rt2[i][ws, :], in0=pB[ws, :], in1=self.ropeS[tab, cs], op=ALU.mult),
                  [RpB, self.Rrope], [self.Rrt2[i]])
            kb.op("pool", lambda e: e.tensor_tensor(out=dst[ws, :], in0=self.rt1[i][ws, :], in1=self.rt2[i][ws, :], op=ALU.add),
                  [self.Rrt1[i], self.Rrt2[i]], [Rdst])
        yield

    def rr(self, *gens):
        gens = list(gens)
        while gens:
            for g in list(gens):
                try:
                    next(g)
                except StopIteration:
                    gens.remove(g)
            yield

    def bg_step(self):
        if self.bg is not None:
            try:
                next(self.bg)
            except StopIteration:
                self.bg = None

    def bg_drain(self):
        while self.bg is not None:
            self.bg_step()

    def g_vgroup(self, g, w_ap_k, rw, ncol, evac, nk=8, lhs_fn=None, rl=None, bank=None):
        bank = (3 + g % 2) if bank is None else bank
        pb, Rpb = self.ps[bank], self.Rps[bank]
        for t in range(4):
            kt = g * 4 + t
            for k in range(nk):
                lhs = self.hT[k][:, kt * 128:(kt + 1) * 128] if lhs_fn is None else lhs_fn(k, kt)
                rr = self.Rh[k][g] if rl is None else rl(k, g)
                self.mm(pb[:, t * ncol:(t + 1) * ncol], lhs, w_ap_k(k), k == 0, k == nk - 1, [rw, rr], Rpb)
            if t % 2 == 1:
                yield
        evac(g, pb, Rpb)
        yield

    def attn_map(self, tiles, q_ap, k_ap, va_ap, scale, rins, o_bank, extra=None, bias=None, fin=None, after_p=None):
        kb = self.kb
        nt = len(tiles)
        for ti, (kt, c0, c1, mask) in enumerate(tiles):
            n = c1 - c0
            si = self.sb_i
            self.sb_i = (self.sb_i + 1) % 5
            pi = self.pt_i
            self.pt_i = (self.pt_i + 1) % 6
            pS, RpS = self.ps[si], self.Rps[si]
            self.mm(pS[:, 0:n], k_ap(kt), q_ap(c0, c1), True, extra is None, rins, RpS)
            if extra is not None:
                l2, r2 = extra(kt, c0, c1)
                self.mm(pS[:, 0:n], l2, r2, False, True, rins, RpS)
            b = bias(kt) if bias is not None else None
            if b is None:
                kb.op("act", lambda e: e.activation(out=self.PT[pi][:, 0:n], in_=pS[:, 0:n], func=AF.Exp, scale=scale), [RpS], [self.RPT[pi]])
            else:
                kb.op("act", lambda e: e.activation(out=self.PT[pi][:, 0:n], in_=pS[:, 0:n], func=AF.Exp, scale=scale, bias=b),
                      [RpS] + list(rins), [self.RPT[pi]])
            if mask is not None:
                kind, m0, base = mask
                if kind == "causal":
                    kb.op("pool", lambda e: e.affine_select(out=self.PT[pi][:, m0:m0 + 128], in_=self.PT[pi][:, m0:m0 + 128], pattern=[[1, 128]],
                                                            compare_op=ALU.is_ge, fill=0.0, base=0, channel_multiplier=-1),
                          [self.RPT[pi]], [self.RPT[pi]])
                elif kind == "lower":
                    kb.op("pool", lambda e: e.affine_select(out=self.PT[pi][:, m0:m0 + 128], in_=self.PT[pi][:, m0:m0 + 128], pattern=[[-1, 128]],
                                                            compare_op=ALU.is_ge, fill=0.0, base=-1, channel_multiplier=1),
                          [self.RPT[pi]], [self.RPT[pi]])
                elif kind == "vis":
                    kb.op("pool", lambda e: e.affine_select(out=self.PT[pi][:, 0:n], in_=self.PT[pi][:, 0:n], pattern=[[1, n]],
                                                            compare_op=ALU.is_ge, fill=0.0, base=base, channel_multiplier=-16),
                          [self.RPT[pi]], [self.RPT[pi]])
            if after_p is not None:
                after_p(pi)

            def pv(kt=kt, c0=c0, c1=c1, n=n, pi=pi, first=(ti == 0), last=(ti == nt - 1)):
                self.mm(self.ps[o_bank][:, c0:c1], va_ap(kt), self.PT[pi][:, 0:n], first, last,
                        [self.RPT[pi]] + list(rins), self.Rps[o_bank])

            self.pend.append((pv, fin if ti == nt - 1 else None))
            while len(self.pend) > self.LA:
                self.pend_pop()

    LA = 4

    def pend_pop(self):
        pv, fin = self.pend.pop(0)
        pv()
        if fin is not None:
            fin()

    def pend_flush(self):
        while self.pend:
            self.pend_pop()

    @staticmethod
    def causal_tiles(q):
        tiles = [(kt, 0, QT, None) for kt in range(4 * q)]
        for j in range(4):
            tiles.append((4 * q + j, 128 * j, QT, ("causal", 0, 0)))
        return tiles

    def finalize(self, o_bank, parity, dst, Rdst, eps=None):
        kb = self.kb
        i = self.rec_i
        self.rec_i ^= 1
        pO, RpO = self.ps[o_bank], self.Rps[o_bank]
        nr = slice(0, 64) if parity == 0 else slice(64, 128)
        dr = slice(64, 128) if parity == 0 else slice(0, 64)
        if eps is None:
            kb.op("dve", lambda e: e.reciprocal(out=self.rec[i][nr, :], in_=pO[dr, :]), [RpO], [self.Rrec[i]])
        else:
            kb.op("dve", lambda e: e.tensor_scalar(out=self.rec[i][nr, :], in0=pO[dr, :], scalar1=eps, scalar2=None, op0=ALU.add), [RpO], [self.Rrec[i]])
            kb.op("dve", lambda e: e.reciprocal(out=self.rec[i][nr, :], in_=self.rec[i][nr, :]), [self.Rrec[i]], [self.Rrec[i]])
        if dst is not None:
            kb.op("dve", lambda e: e.tensor_tensor(out=dst, in0=pO[nr, :], in1=self.rec[i][nr, :], op=ALU.mult), [RpO, self.Rrec[i]], [Rdst])
        return i, nr

    def proj_T(self, pb, Rpb, rows, w_ap_k, rw, q, nk=8, rhs_fn=None, rrhs=None):
        cs = slice(q * QT, (q + 1) * QT)
        for k in range(nk):
            rhs = self.hT[k][:, cs] if rhs_fn is None else rhs_fn(k)
            rr = self.Rh[k][q] if rrhs is None else rrhs(k)
            self.mm(pb[rows, :], w_ap_k(k), rhs, k == 0, k == nk - 1, [rw, rr], Rpb)

    def v_proj(self, w_ap_k, rw, ncol, evac, nk=8, lhs_fn=None, rl=None):
        for g in range(4):
            pb, Rpb = self.ps[3 + g % 2], self.Rps[3 + g % 2]
            for t in range(4):
                kt = g * 4 + t
                for k in range(nk):
                    lhs = self.hT[k][:, kt * 128:(kt + 1) * 128] if lhs_fn is None else lhs_fn(k, kt)
                    rr = self.Rh[k][g] if rl is None else rl(k, g)
                    self.mm(pb[:, t * ncol:(t + 1) * ncol], lhs, w_ap_k(k), k == 0, k == nk - 1, [rw, rr], Rpb)
            evac(g, pb, Rpb)

    def fox(self, l, fs):
        kb = self.kb
        self.mark('fox_prelude')
        fs0 = fs
        DQ = kb.sb("foxDQ", [128, S], BF16, fs)
        RDQ = [Res(f"foxDQ{q}") for q in range(NQ)]
        Dk = kb.sb("foxDk", [128, NKT, 4], F32, fs)
        RDk = Res("foxDk")
        with contextlib.ExitStack() as fs:
            Dt = kb.sb("foxD", [128, S], F32, fs)
            RD = [Res(f"foxD{q}") for q in range(NQ)]
            ft = [kb.sb(f"foxft{i}", [128, QT], F32, fs) for i in range(2)]
            Rft = [Res(f"foxft{i}") for i in range(2)]
            onesf = kb.sb("foxones", [128, QT], F32, fs)
            Rones = Res("foxones")
            kb.op("pool", lambda e: e.memset(onesf[:], 1.0), [], [Rones])
            wf, rwf = self.load_w(self.wf_pad[l].rearrange("(k p) n -> p k n", p=128), 128)
            for q in range(NQ):
                cs = slice(q * QT, (q + 1) * QT)
                pb, Rpb = self.ps[3 + q % 2], self.Rps[3 + q % 2]
                self.proj_T(pb, Rpb, slice(0, 128), lambda k: wf[:, k, 0:128], rwf, q)
                kb.op("act", lambda e: e.activation(out=ft[0][:], in_=pb[:], func=AF.Exp, scale=-1.0, bias=self.der("negfb")),
                      [Rpb, self.Rder], [Rft[0]])
                kb.op("act", lambda e: e.activation(out=ft[1][:], in_=ft[0][:], func=AF.Ln, bias=self.cf("epsc", 6, 7), scale=1.0), [Rft[0], self.Rcf], [Rft[1]])
                if q == 0:
                    kb.op("dve", lambda e: e.tensor_tensor_scan(out=Dt[:, cs], data0=onesf[:], data1=ft[1][:], initial=0.0, op0=ALU.mult, op1=ALU.add),
                          [Rones, Rft[1]], [RD[q]])
                else:
                    kb.op("dve", lambda e: e.tensor_tensor_scan(out=Dt[:, cs], data0=onesf[:], data1=ft[1][:], initial=Dt[:, q * QT - 1:q * QT],
                                                                op0=ALU.mult, op1=ALU.add), [Rones, Rft[1], RD[q - 1]], [RD[q]])
                kb.op("pool", lambda e: e.tensor_scalar(out=DQ[:, cs], in0=Dt[:, cs], scalar1=-8.0, scalar2=None, op0=ALU.mult), [RD[q]], [RDQ[q]])
                pt, Rpt = self.ps[5], self.Rps[5]
                for t in range(4):
                    kb.op("pe", lambda e: e.transpose(out=pt[:, t * 128:(t + 1) * 128], in_=Dt[:, q * QT + t * 128:q * QT + (t + 1) * 128],
                                                      identity=self.cf("ident")), [RD[q], self.Rcf], [Rpt])
                kb.op("dve", lambda e: e.tensor_copy(out=Dk[:, q * 4:(q + 1) * 4, :], in_=pt[:].rearrange("p (t h r) -> p t h r", t=4, h=4)[:, :, :, 0]),
                      [Rpt], [RDk])
            self.dump("foxD", Dt[:], RD, [128, S])
            kb.barrier()
            if self.dbg:
                kb.e["act"].wait_ge(self.osem[0], self.osem[1])
                kb.barrier()
        if True:
            self.mixer_scratch(fs0)
            self.mark('fox_units')
            fox_w = []
            for u in range(2):
                wqk_, rwqk_ = self.next_wb()
                kb.dma_in("pool", rwqk_, lambda e: [e.dma_start(out=wqk_[:, :, 0:128], in_=self.win_cols(l, C_FOX_Q + u * 128, 128)),
                                                    e.dma_start(out=wqk_[:, :, 128:256], in_=self.win_cols(l, C_FOX_K + u * 128, 128)),
                                                    e.dma_start(out=wqk_[:, :, 256:384], in_=self.win_cols(l, C_FOX_V + u * 128, 128))])
                fox_w.append((wqk_, rwqk_))
            for u in range(2):
                wqk, rwqk = fox_w[u]

                def evac(g, pb, Rpb):
                    src = pb[:].rearrange("p (t a c) -> p t a c", t=4, a=2)
                    dstv = self.VA[:, g * 4:(g + 1) * 4, :].rearrange("p t (a c) -> p t a c", a=4)[:, :, 0::3, :]
                    kb.op("dve", lambda e: e.tensor_copy(out=dstv, in_=src), [Rpb], [self.RVA[g]])

                def pre(q):
                    cs = slice(q * QT, (q + 1) * QT)
                    def ch_q():
                        self.proj_T(self.ps[3], self.Rps[3], slice(0, 128), lambda k: wqk[:, k, 0:128], rwqk, q)
                        yield
                        yield from self.g_headnorm(3, 128, self.cbf("b64"), self.cf("epsc", 1, 2), self.der("fox_gq"), self.QTt[0][:, cs], self.RQT[0][q], cs)

                    def ch_k():
                        self.proj_T(self.ps[4], self.Rps[4], slice(0, 128), lambda k: wqk[:, k, 128:256], rwqk, q)
                        yield
                        yield from self.g_headnorm(4, 128, self.cbf("b64"), self.cf("epsc", 1, 2), self.der("fox_gk"), self.KTt[0][:, cs], self.RKT[0][q], cs, stat_bank=2)

                    yield from self.rr(ch_q(), ch_k(), self.g_vgroup(q, lambda k: wqk[:, k, 256:384], rwqk, 128, evac, bank=q % 2))

                self.bg = pre(0)
                self.bg_drain()
                for q in range(NQ):
                    cs = slice(q * QT, (q + 1) * QT)
                    if q + 1 < NQ:
                        self.bg = pre(q + 1)
                        self.bg_drain()
                    for hh in range(2):
                        h = 2 * u + hh
                        rb = slice(64 * hh, 64 * hh + 64)
                        ob = 6 + hh
                        rins = [self.RQT[0][q], RDk, RDQ[q], self.Rcb] + self.RKT[0][:q + 1] + self.RVA[:q + 1]
                        self.attn_map(
                            self.causal_tiles(q),
                            lambda c0, c1, rb=rb, q=q: self.QTt[0][rb, q * QT + c0:q * QT + c1],
                            lambda kt, rb=rb: self.KTt[0][rb, kt * 128:(kt + 1) * 128],
                            lambda kt, hh=hh: self.VA[:, kt, 128 * hh:128 * hh + 128],
                            0.125, rins, ob,
                            extra=lambda kt, c0, c1, h=h, q=q: (self.cbf("selrow", slice(0, 128), 128 * h, 128 * h + 128), DQ[:, q * QT + c0:q * QT + c1]),
                            bias=lambda kt, h=h: Dk[:, kt, h:h + 1],
                            fin=lambda ob=ob, hh=hh, u=u, rb=rb, cs=cs, q=q: self.finalize(ob, hh, self.oT[1][u][rb, cs], self.RoT[1][u][q]))
                    self.bg_drain()
                self.pend_flush()

    def merge(self, l):
        self.mark('merge')
        kb = self.kb
        with contextlib.ExitStack() as ms:
            MT = [kb.sb(f"MT{i}", [128, S], BF16, ms) for i in range(4)]
            RMT = [[Res(f"MT{i}_{q}") for q in range(NQ)] for i in range(4)]
            brw = [kb.sb(f"brw{i}", [128, 2, 4, 128], BF16, ms) for i in range(2)]
            Rbrw = [Res(f"brw{i}") for i in range(2)]
            wo = [kb.sb(f"wo{i}", [128, 4, 128], BF16, ms) for i in range(2)]
            Rwo = [Res(f"wo{i}") for i in range(2)]
            sig = [kb.sb(f"sig{i}", [128, QT], F32, ms) for i in range(2)]
            Rsig = [Res(f"sig{i}") for i in range(2)]
            tmp = [kb.sb(f"mtmp{i}", [128, QT], F32, ms) for i in range(2)]
            Rtmp = [Res(f"mtmp{i}") for i in range(2)]
            acc = [kb.sb(f"macc{i}", [128, QT], F32, ms) for i in range(2)]
            Racc = [Res(f"macc{i}") for i in range(2)]
            gwv = self.gate_w[l].rearrange("(k p) (m n) -> p k m n", p=128, m=4)
            brv = self.br_w[l].rearrange("m (k p) n -> p k m n", p=128)
            wov = self.w_out[l].rearrange("(k p) n -> p k n", p=128)
            n_it = 0
            loaded = {}

            def load_dc(dc):
                if dc in loaded or dc >= 8:
                    return
                wb_, rgw_ = self.next_wb()
                kb.dma_in("pool", rgw_, lambda e: [e.dma_start(out=wb_[:, :, m_ * 128:(m_ + 1) * 128], in_=gwv[:, :, m_, dc * 128:(dc + 1) * 128]) for m_ in range(4)])
                bi_ = dc % 2
                kb.dma_in("pool", Rbrw[bi_], lambda e: [e.dma_start(out=brw[bi_][:, :, m_, :], in_=brv[:, :, m_, dc * 128:(dc + 1) * 128]) for m_ in range(4)])
                loaded[dc] = (wb_, rgw_)

            wo_loaded = {}

            def load_wo(grp, dout):
                key = grp * 8 + dout
                if key in wo_loaded or dout >= 8:
                    return
                wi_ = key % 2
                kb.dma_in("pool", Rwo[wi_], lambda e: e.dma_start(out=wo[wi_][:], in_=wov[:, grp * 4:(grp + 1) * 4, dout * 128:(dout + 1) * 128]))
                wo_loaded[key] = wi_

            for grp in range(2):
                for dcl in range(4):
                    dc = grp * 4 + dcl
                    load_dc(dc)
                    if dcl < 3:
                        load_dc(dc + 1)
                    wb, rgw = loaded[dc]
                    bi = dc % 2
                    for q in range(NQ):
                        cs = slice(q * QT, (q + 1) * QT)
                        ai = n_it % 2
                        n_it += 1
                        for m in range(4):
                            pg, Rpg = self.ps[m % 2], self.Rps[m % 2]
                            py, Rpy = self.ps[2 + m % 2], self.Rps[2 + m % 2]
                            for k in range(8):
                                self.mm(pg[:], wb[:, k, m * 128:(m + 1) * 128], self.hT[k][:, cs], k == 0, k == 7, [rgw, self.Rh[k][q]], Rpg)
                            for k in range(2):
                                self.mm(py[:], brw[bi][:, k, m, :], self.oT[m][k][:, cs], k == 0, k == 1, [Rbrw[bi], self.RoT[m][k][q]], Rpy)
                            si = m % 2
                            kb.op("act", lambda e: e.activation(out=sig[si][:], in_=pg[:], func=AF.Sigmoid, bias=self.sm(l, "gate_b", m * 8 + dc, m * 8 + dc + 1), scale=1.0),
                                  [Rpg, self.Rsm], [Rsig[si]])
                            if m == 0:
                                kb.op("dve", lambda e: e.tensor_tensor(out=acc[ai][:], in0=py[:], in1=sig[si][:], op=ALU.mult), [Rpy, Rsig[si]], [Racc[ai]])
                            else:
                                kb.op("dve", lambda e: e.tensor_tensor(out=tmp[si][:], in0=py[:], in1=sig[si][:], op=ALU.mult), [Rpy, Rsig[si]], [Rtmp[si]])
                                if m < 3:
                                    kb.op("dve", lambda e: e.tensor_tensor(out=acc[ai][:], in0=acc[ai][:], in1=tmp[si][:], op=ALU.add), [Racc[ai], Rtmp[si]], [Racc[ai]])
                                else:
                                    kb.op("dve", lambda e: e.tensor_tensor(out=MT[dcl][:, cs], in0=acc[ai][:], in1=tmp[si][:], op=ALU.add),
                                          [Racc[ai], Rtmp[si]], [RMT[dcl][q]])
                if l == 0 and grp == 0:
                    self.dump("merged0", MT[0][:], RMT[0], [128, S])
                for dout in range(8):
                    load_wo(grp, dout)
                    load_wo(grp, dout + 1)
                    if dout == 7 and grp == 0:
                        load_dc(4)
                    wi = wo_loaded[grp * 8 + dout]
                    for q in range(NQ):
                        cs = slice(q * QT, (q + 1) * QT)
                        pb, Rpb = self.ps[4 + (dout * NQ + q) % 2], self.Rps[4 + (dout * NQ + q) % 2]
                        for dcl in range(4):
                            self.mm(pb[:], wo[wi][:, dcl, :], MT[dcl][:, cs], dcl == 0, dcl == 3, [Rwo[wi], RMT[dcl][q]], Rpb)
                        kb.op("dve", lambda e: e.scalar_tensor_tensor(out=self.xT[dout][:, cs], in0=pb[:], scalar=self.t_mod[:, 16 + dout:17 + dout],
                                                                      in1=self.xT[dout][:, cs], op0=ALU.mult, op1=ALU.add),
                              [Rpb, self.Rmod, self.Rx[dout][q]], [self.Rx[dout][q]])
            kb.barrier()

    def ffn(self, l):
        kb = self.kb
        self.mark('ffn')
        NJ = NFF // 2
        with contextlib.ExitStack() as fs:
            AT = [kb.sb(f"AT{j}", [128, S], BF16, fs) for j in range(NJ)]
            RAT = [[Res(f"AT{j}_{q}") for q in range(NQ)] for j in range(NJ)]
            G = [kb.sb(f"G{i}", [128, QT + 2], F32, fs) for i in range(2)]
            RG = [Res(f"G{i}") for i in range(2)]
            GC = [kb.sb(f"GC{i}", [128, QT], F32, fs) for i in range(2)]
            RGC = [Res(f"GC{i}") for i in range(2)]
            wd = [kb.sb(f"wd{i}", [128, NJ, 128], BF16, fs) for i in range(3)]
            Rwd = [Res(f"wd{i}") for i in range(3)]
            wd_n = 0
            gn = 0
            upv = self.w_up[l].rearrange("(k p) n -> p k n", p=128)
            dnv = self.w_down[l].rearrange("(j p) n -> p j n", p=128)
            if l + 1 < self.n_layers:
                self.ada_gen = self.g_ada(l + 1)
            up_loaded = {}

            def load_up(j0):
                if j0 in up_loaded or j0 >= NFF:
                    return
                nj_ = 1 if (j0 % NJ) == NJ - 1 else 2
                wb_, rwu_ = self.next_wb()
                kb.dma_in("pool", rwu_, lambda e: [e.dma_start(out=wb_[:, :, 0:128 * nj_], in_=upv[:, :, j0 * 128:(j0 + nj_) * 128]),
                                                   e.dma_start(out=wb_[:, :, 256:256 + 128 * nj_], in_=upv[:, :, DFF + j0 * 128:DFF + (j0 + nj_) * 128])])
                up_loaded[j0] = (wb_, rwu_)

            wd_loaded = {}

            def load_wd(ps__, dout):
                key = ps__ * 8 + dout
                if key in wd_loaded or dout >= 8:
                    return
                wi_ = key % 3
                kb.dma_in("pool", Rwd[wi_], lambda e: e.dma_start(out=wd[wi_][:], in_=dnv[:, ps__ * NJ:(ps__ + 1) * NJ, dout * 128:(dout + 1) * 128]))
                wd_loaded[key] = wi_

            for ps_ in range(2):
                wb, rwu = None, None
                for jj in range(NJ):
                    j = ps_ * NJ + jj
                    self.ada_step(3)
                    if jj % 2 == 0:
                        load_up(j)
                        nxt = j + 2
                        if jj + 2 < NJ:
                            load_up(nxt)
                        elif ps_ == 0:
                            pass
                        wb, rwu = up_loaded[j]
                    if jj == NJ - 1:
                        load_wd(ps_, 0)
                    co = 128 * (jj % 2)
                    w0 = self.sm(l, "conv_w", 0 * NFF + j, 0 * NFF + j + 1)
                    w1 = self.sm(l, "conv_w", 1 * NFF + j, 1 * NFF + j + 1)
                    w2 = self.sm(l, "conv_w", 2 * NFF + j, 2 * NFF + j + 1)
                    cb_ = self.sm(l, "conv_b", j, j + 1)
                    for q in range(NQ):
                        gi = gn % 2
                        gn += 1
                        cs = slice(q * QT, (q + 1) * QT)
                        pg, Rpg = self.ps[gi], self.Rps[gi]
                        pv, Rpv = self.ps[2 + gi], self.Rps[2 + gi]
                        for k in range(8):
                            self.mm(pg[:], wb[:, k, co:co + 128], self.hT[k][:, cs], k == 0, k == 7, [rwu, self.Rh[k][q]], Rpg)
                        for k in range(8):
                            self.mm(pv[:], wb[:, k, 256 + co:256 + co + 128], self.hT[k][:, cs], k == 0, k == 7, [rwu, self.Rh[k][q]], Rpv)
                        if q == 0:
                            kb.op("dve", lambda e: e.memset(G[gi][:, 0:2], 0.0), [], [RG[gi]])
                        else:
                            kb.op("dve", lambda e: e.tensor_copy(out=G[gi][:, 0:2], in_=G[1 - gi][:, QT:QT + 2]), [RG[1 - gi]], [RG[gi]])
                        kb.op("act", lambda e: e.activation(out=G[gi][:, 2:2 + QT], in_=pg[:], func=AF.Copy), [Rpg], [RG[gi]])
                        kb.op("act", lambda e: e.activation(out=GC[gi][:], in_=pg[:], func=AF.Identity, scale=w2, bias=cb_),
                              [Rpg, self.Rsm], [RGC[gi]])
                        kb.op("dve", lambda e: e.scalar_tensor_tensor(out=GC[gi][:], in0=G[gi][:, 1:1 + QT], scalar=w1, in1=GC[gi][:], op0=ALU.mult, op1=ALU.add),
                              [RG[gi], self.Rsm, RGC[gi]], [RGC[gi]])
                        kb.op("dve", lambda e: e.scalar_tensor_tensor(out=GC[gi][:], in0=G[gi][:, 0:QT], scalar=w0, in1=GC[gi][:], op0=ALU.mult, op1=ALU.add),
                              [RG[gi], self.Rsm, RGC[gi]], [RGC[gi]])
                        kb.op("act", lambda e: e.activation(out=GC[gi][:], in_=GC[gi][:], func=AF.Silu), [RGC[gi]], [RGC[gi]])
                        kb.op("dve", lambda e: e.tensor_tensor(out=AT[jj][:, cs], in0=pv[:], in1=GC[gi][:], op=ALU.mult),
                              [Rpv, RGC[gi]], [RAT[jj][q]])
                for dout in range(8):
                    self.ada_step(3)
                    load_wd(ps_, dout)
                    load_wd(ps_, dout + 1)
                    if dout == 6 and ps_ == 0:
                        load_up(NJ)
                    wi = wd_loaded[ps_ * 8 + dout]
                    for q in range(NQ):
                        cs = slice(q * QT, (q + 1) * QT)
                        pb, Rpb = self.ps[4 + q % 2], self.Rps[4 + q % 2]
                        for jj in range(NJ):
                            self.mm(pb[:], wd[wi][:, jj, :], AT[jj][:, cs], jj == 0, jj == NJ - 1, [Rwd[wi], RAT[jj][q]], Rpb)
                        kb.op("dve", lambda e: e.scalar_tensor_tensor(out=self.xT[dout][:, cs], in0=pb[:], scalar=self.t_mod[:, 40 + dout:41 + dout],
                                                                      in1=self.xT[dout][:, cs], op0=ALU.mult, op1=ALU.add),
                              [Rpb, self.Rmod, self.Rx[dout][q]], [self.Rx[dout][q]])
            kb.barrier()

    def rstd_from(self, pstat_ap, out_ap, neps, Rin, Rout):
        kb = self.kb
        kb.op("act", lambda e: e.activation(out=out_ap, in_=pstat_ap, func=AF.Ln, bias=neps, scale=1.0), [Rin, self.Rcf], [Rout])
        kb.op("act", lambda e: e.activation(out=out_ap, in_=out_ap, func=AF.Exp, scale=-0.5), [Rout], [Rout])

    def mla(self, l, ms):
        kb = self.kb
        self.mark('mla_prelude')
        cqn = [kb.sb(f"cqn{i}", [128, S], BF16, ms) for i in range(2)]
        Rcqn = [[Res(f"cqn{i}_{q}") for q in range(NQ)] for i in range(2)]
        ckvn = kb.sb("ckvn", [128, S], BF16, ms)
        Rckvn = [Res(f"ckvn{q}") for q in range(NQ)]
        wuq = kb.sb("wuq", [128, 2, 384], BF16, ms)
        wukv = kb.sb("wukv", [128, 512], BF16, ms)
        Rwuq, Rwukv = Res("wuq"), Res("wukv")
        kb.dma_in("pool", Rwuq, lambda e: e.dma_start(out=wuq[:], in_=self.w_uq[l].rearrange("(k p) n -> p k n", p=128)))
        kb.dma_in("pool", Rwukv, lambda e: e.dma_start(out=wukv[:], in_=self.w_ukv[l]))
        self.mixer_scratch(ms)
        wc, rwc = self.load_w(self.win_cols(l, C_MLA_CQ, 416), 416)
        pstat, Rpstat = self.ps[5], self.Rps[5]
        for q in range(NQ):
            cs = slice(q * QT, (q + 1) * QT)
            for c in range(2):
                self.proj_T(self.ps[3 + c], self.Rps[3 + c], slice(0, 128), lambda k: wc[:, k, c * 128:(c + 1) * 128], rwc, q)
                kb.op("act", lambda e: e.activation(out=self.sqb[c][:], in_=self.ps[3 + c][:], func=AF.Square), [self.Rps[3 + c]], [self.Rsqb[c]])
                self.mm(pstat[:], self.cbf("ones"), self.sqb[c][:], c == 0, c == 1, [self.Rsqb[c], self.Rcb], Rpstat)
            self.rstd_from(pstat[:], self.rstd[0][:], self.cf("epsc", 4, 5), Rpstat, self.Rrstd[0])
            for c in range(2):
                kb.op("dve", lambda e: e.scalar_tensor_tensor(out=cqn[c][:, cs], in0=self.ps[3 + c][:], scalar=self.der("mla_cqg", c, c + 1),
                                                              in1=self.rstd[0][:], op0=ALU.mult, op1=ALU.mult),
                      [self.Rps[3 + c], self.Rrstd[0], self.Rder], [Rcqn[c][q]])
            self.proj_T(self.ps[3], self.Rps[3], slice(0, 128), lambda k: wc[:, k, 256:384], rwc, q)
            self.headnorm(3, 128, self.cbf("ones"), self.cf("epsc", 5, 6), self.der("mla_ckvg"), ckvn[:, cs], Rckvn[q], cs)
        self.mark('mla_heads')
        r96 = slice(0, 96)
        for h in range(4):
            hh = h % 2
            vcol = 0 if hh == 0 else 192

            def evac(g, pb, Rpb, vcol=vcol):
                kb.op("dve", lambda e: e.tensor_copy(out=self.VA[:, g * 4:(g + 1) * 4, vcol:vcol + 64], in_=pb[:, 0:256].rearrange("p (t c) -> p t c", t=4)),
                      [Rpb], [self.RVA[g]])

            def pre(q, h=h):
                cs = slice(q * QT, (q + 1) * QT)
                def ch_q():
                    for k in range(2):
                        self.mm(self.ps[3][0:64, :], wuq[:, k, 96 * h + 32:96 * h + 96], cqn[k][:, cs], k == 0, k == 1, [Rwuq, Rcqn[k][q]], self.Rps[3])
                    for k in range(2):
                        self.mm(self.ps[3][64:96, :], wuq[:, k, 96 * h:96 * h + 32], cqn[k][:, cs], k == 0, k == 1, [Rwuq, Rcqn[k][q]], self.Rps[3])
                    yield
                    yield from self.g_headnorm(3, 96, self.cbf("ones", r96, 0, 96), self.cf("epsc", 2, 3, r96), self.der("mla_gq", 0, 1, r96), self.QTt[0][r96, cs],
                                               self.RQT[0][q], cs, rope=(1, "sw_mla", [64]))

                def ch_k():
                    self.mm(self.ps[4][0:64, :], wukv[:, 128 * h:128 * h + 64], ckvn[:, cs], True, True, [Rwukv, Rckvn[q]], self.Rps[4])
                    self.proj_T(self.ps[4], self.Rps[4], slice(64, 96), lambda k: wc[:, k, 384:416], rwc, q)
                    yield
                    yield from self.g_headnorm(4, 96, self.cbf("ones", r96, 0, 96), self.cf("epsc", 2, 3, r96), self.der("mla_gk", 0, 1, r96), self.KTt[0][r96, cs],
                                               self.RKT[0][q], cs, rope=(1, "sw_mla", [64]), stat_bank=2)

                yield from self.rr(ch_q(), ch_k(),
                                   self.g_vgroup(q, lambda k: wukv[:, 128 * h + 64:128 * h + 128], Rwukv, 64, evac, nk=1,
                                                 lhs_fn=lambda k, kt: ckvn[:, kt * 128:(kt + 1) * 128], rl=lambda k, g: Rckvn[g], bank=q % 2))

            self.bg = pre(0)
            self.bg_drain()
            for q in range(NQ):
                cs = slice(q * QT, (q + 1) * QT)
                if q + 1 < NQ:
                    self.bg = pre(q + 1)
                    self.bg_drain()
                rb = slice(64 * hh, 64 * hh + 64)
                ob = 6 + q % 2
                rins = [self.RQT[0][q]] + self.RKT[0][:q + 1] + self.RVA[:q + 1]
                self.attn_map(self.causal_tiles(q),
                              lambda c0, c1, q=q: self.QTt[0][r96, q * QT + c0:q * QT + c1],
                              lambda kt: self.KTt[0][r96, kt * 128:(kt + 1) * 128],
                              lambda kt, hh=hh: self.VA[:, kt, 128 * hh:128 * hh + 128],
                              96.0 ** -0.5, rins, ob,
                              fin=lambda ob=ob, hh=hh, h=h, rb=rb, cs=cs, q=q: self.finalize(ob, hh, self.oT[2][h // 2][rb, cs], self.RoT[2][h // 2][q]))
                self.bg_drain()
            self.pend_flush()

    def diff(self, l, ms):
        kb = self.kb
        self.mark('diff')
        self.mixer_scratch(ms)
        dsq = kb.sb("dsq", [128, QT], BF16, ms)
        Rdsq = Res("dsq")
        r64 = slice(0, 64)
        pstat, Rpstat = self.ps[5], self.Rps[5]
        dif_w = {}

        def load_dif(h_):
            if h_ in dif_w or h_ >= 4:
                return
            w_, r_ = self.next_wb()
            kb.dma_in("pool", r_, lambda e: [e.dma_start(out=w_[:, :, 0:64], in_=self.win_cols(l, C_DIF_Q + 64 * h_, 64)),
                                             e.dma_start(out=w_[:, :, 64:128], in_=self.win_cols(l, C_DIF_K + 64 * h_, 64)),
                                             e.dma_start(out=w_[:, :, 128:192], in_=self.win_cols(l, C_DIF_V + 64 * h_, 64))])
            dif_w[h_] = (w_, r_)

        for h in range(4):
            hh = h % 2
            load_dif(h)
            load_dif(h + 1)
            wqk, rwqk = dif_w[h]
            vcol = 0 if hh == 0 else 192

            def evac(g, pb, Rpb, vcol=vcol):
                kb.op("dve", lambda e: e.tensor_copy(out=self.VA[:, g * 4:(g + 1) * 4, vcol:vcol + 64], in_=pb[:, 0:256].rearrange("p (t c) -> p t c", t=4)),
                      [Rpb], [self.RVA[g]])

            def pre(q, wqk=wqk, rwqk=rwqk):
                cs = slice(q * QT, (q + 1) * QT)
                def ch_q():
                    self.proj_T(self.ps[3], self.Rps[3], r64, lambda k: wqk[:, k, 0:64], rwqk, q)
                    yield
                    yield from self.g_headnorm(3, 64, self.cbf("b32", r64, 0, 64), self.cf("epsc", 3, 4, r64), self.der("dif_gq", 0, 1, r64), self.QTt[0][r64, cs],
                                               self.RQT[0][q], cs, rope=(2, "sw_dif", [0, 32]))

                def ch_k():
                    self.proj_T(self.ps[4], self.Rps[4], r64, lambda k: wqk[:, k, 64:128], rwqk, q)
                    yield
                    yield from self.g_headnorm(4, 64, self.cbf("b32", r64, 0, 64), self.cf("epsc", 3, 4, r64), self.der("dif_gk", 0, 1, r64), self.KTt[0][r64, cs],
                                               self.RKT[0][q], cs, rope=(2, "sw_dif", [0, 32]), stat_bank=2)

                yield from self.rr(ch_q(), ch_k(), self.g_vgroup(q, lambda k: wqk[:, k, 128:192], rwqk, 64, evac, bank=q % 2))

            self.bg = pre(0)
            self.bg_drain()
            for q in range(NQ):
                cs = slice(q * QT, (q + 1) * QT)
                if q + 1 < NQ:
                    self.bg = pre(q + 1)
                    self.bg_drain()
                rins = [self.RQT[0][q]] + self.RKT[0][:q + 1] + self.RVA[:q + 1]

                def fin_diff(q=q, cs=cs, hh=hh, h=h):
                    recs = []
                    for a in range(2):
                        i, nr = self.finalize(6 + a, hh, None, None)
                        kb.op("dve", lambda e: e.tensor_tensor(out=self.rec[i][nr, :], in0=self.ps[6 + a][nr, :], in1=self.rec[i][nr, :], op=ALU.mult),
                              [self.Rps[6 + a], self.Rrec[i]], [self.Rrec[i]])
                        recs.append(i)
                    i0, i1 = recs
                    kb.op("dve", lambda e: e.scalar_tensor_tensor(out=self.rec[i0][nr, :], in0=self.rec[i1][nr, :], scalar=self.der("neglam", 0, 1, nr),
                                                                  in1=self.rec[i0][nr, :], op0=ALU.mult, op1=ALU.add),
                          [self.Rrec[i0], self.Rrec[i1], self.Rder], [self.Rrec[i0]])
                    pst, Rpst = self.ps[6], self.Rps[6]
                    kb.op("pool", lambda e: e.tensor_tensor(out=dsq[nr, :], in0=self.rec[i0][nr, :], in1=self.rec[i0][nr, :], op=ALU.mult),
                          [self.Rrec[i0]], [Rdsq])
                    self.mm(pst[nr, :], self.cbf("ones", nr, 0, 64), dsq[nr, :], True, True, [Rdsq, self.Rcb], Rpst)
                    self.rstd_from(pst[nr, :], self.rec[i1][nr, :], self.cf("epsc", 1, 2, nr), Rpst, self.Rrec[i1])
                    kb.op("dve", lambda e: e.scalar_tensor_tensor(out=self.oT[3][h // 2][nr, cs], in0=self.rec[i0][nr, :], scalar=self.der("dif_og", 0, 1, nr),
                                                                  in1=self.rec[i1][nr, :], op0=ALU.mult, op1=ALU.mult),
                          [self.Rrec[i0], self.Rrec[i1], self.Rder], [self.RoT[3][h // 2][q]])

                for a in range(2):
                    ra = slice(32 * a, 32 * a + 32)
                    self.attn_map(self.causal_tiles(q),
                                  lambda c0, c1, ra=ra, q=q: self.QTt[0][ra, q * QT + c0:q * QT + c1],
                                  lambda kt, ra=ra: self.KTt[0][ra, kt * 128:(kt + 1) * 128],
                                  lambda kt, hh=hh: self.VA[:, kt, 128 * hh:128 * hh + 128],
                                  32.0 ** -0.5, rins, 6 + a, fin=(fin_diff if a == 1 else None))
                self.bg_drain()
            self.pend_flush()


def prep_inputs(inp, layer_ids=tuple(range(DEPTH))):
    li = list(layer_ids)
    cb, cf = make_consts(layer_ids)
    sm = make_smalls(inp, layer_ids)

    def W(name):
        return np.ascontiguousarray(np.asarray(inp[name], np.float32)[li])

    w_in = W("w_in")
    gcols = []
    for br in range(3):
        for pr in range(2):
            for hh in range(2):
                gcols += [C_NSA_G + br * 4 + pr * 2 + hh] * 64
    wg_rep = np.ascontiguousarray(w_in[:, :, gcols])
    wf_pad = np.zeros((len(li), D, 128), np.float32)
    for h in range(4):
        wf_pad[:, :, 32 * h] = w_in[:, :, C_FOX_F + h]
    shared = {
        "ada_w": W("ada_w"), "w_in": w_in, "wg_rep": wg_rep, "wf_pad": wf_pad,
        "cmp_w1": W("nsa_cmp_w1"), "cmp_w2": W("nsa_cmp_w2"),
        "cmp_pe": np.ascontiguousarray(np.transpose(W("nsa_cmp_pe"), (0, 1, 3, 2))),
        "w_uq": W("mla_w_uq"), "w_ukv": W("mla_w_ukv"), "br_w": W("br_w"), "gate_w": W("gate_w"),
        "w_out": W("w_out"), "w_up": W("ffn_w_up"), "w_down": W("ffn_w_down"),
        "smalls": sm, "cbf": cb, "cf32": cf,
    }
    maps = []
    for b in range(8):
        m = dict(shared)
        m["x"] = np.ascontiguousarray(inp["x"][b], np.float32)
        m["cT"] = np.ascontiguousarray(np.asarray(inp["c"][b], np.float32).reshape(8, 128).T)
        m["pos"] = np.ascontiguousarray(np.asarray(inp["positions"][b], np.int32).reshape(1, S))
        maps.append(m)
    return maps


FUSED = True


def kernel(**inputs):
    inp = {k: np.asarray(v) for k, v in inputs.items()}
    if FUSED:
        maps = prep_inputs(inp)
        nc = Prog().build()
        res = run_bass_kernel_spmd(nc, maps, core_ids=list(range(8)))
        return np.stack([np.asarray(res.results[b]["y"], np.float32) for b in range(8)], axis=0)
    nc = Prog(n_layers=1, wdepth=1).build()
    x = np.asarray(inp["x"], np.float32)
    for l in range(DEPTH):
        cur = dict(inp)
        cur["x"] = x
        maps = prep_inputs(cur, (l,))
        res = run_bass_kernel_spmd(nc, maps, core_ids=list(range(8)))
        x = np.stack([np.asarray(res.results[b]["y"], np.float32) for b in range(8)], axis=0)
    return x
```
